# Optimizing a Trainium2 kernel written in Bass

```python
import math
import jax, jax.numpy as jnp
from jax import lax
import numpy as np

D_MODEL = 1024
BATCH = 32
SEQ = 2048
DEPTH = 2
DEC_BATCH = 16
DEC_SEQ = 2048
PAST_LEN = 128

MLA_HEADS = 8
MLA_Q_LORA = 384
MLA_KV_LORA = 256
MLA_NOPE = 64
MLA_ROPE = 32
MLA_V = 64
MLA_QK = MLA_NOPE + MLA_ROPE
ROPE_THETA = 10000.0
Q_BLOCK = 128
RG_WIDTH = 512
RG_BLOCKS = 8
RG_BW = RG_WIDTH // RG_BLOCKS
RG_CONV = 4
RG_C = 8.0
GDN_HEADS = 4
GDN_DK = 128
GDN_DV = 128
GDN_CONV = 4
GDN_CHUNK = 64
N_BRANCH = 3
BRANCH_WIDTH = 512
D_FF = 2816
FFN_CONV = 3
EPS = 1e-6

IN_SPLITS = (MLA_Q_LORA, MLA_KV_LORA + MLA_ROPE, RG_WIDTH, RG_WIDTH,
             GDN_HEADS * (2 * GDN_DK + GDN_DV), GDN_HEADS * GDN_DV,
             2 * GDN_HEADS, 2 * GDN_HEADS, N_BRANCH * D_MODEL)
N_IN = sum(IN_SPLITS)

kernel_name = 'hybrid_mla_rglru_gdn_encoder'


def rmsnorm(x, g):
    xf = x.astype(jnp.float32)
    y = xf * lax.rsqrt(jnp.mean(xf * xf, axis=-1, keepdims=True) + EPS)
    return (y * g.astype(jnp.float32)).astype(x.dtype)


def l2norm(t):
    return t * lax.rsqrt(jnp.sum(t * t, axis=-1, keepdims=True) + EPS)


def split_cols(t, sizes):
    out, start = [], 0
    for n in sizes:
        out.append(t[..., start:start + n])
        start += n
    return out


def dwconv_centred(x, w, b=None):
    K = w.shape[0]
    lo = (K - 1) // 2
    hi = K - 1 - lo
    y = lax.conv_general_dilated(x, w[:, None, :].astype(x.dtype), window_strides=(1,),
                                 padding=[(lo, hi)], dimension_numbers=('NWC', 'WIO', 'NWC'),
                                 feature_group_count=x.shape[-1])
    if b is not None:
        y = y + b
    return y


def rope_tables(S):
    inv = 1.0 / (ROPE_THETA ** (jnp.arange(0, MLA_ROPE, 2, dtype=jnp.float32) / MLA_ROPE))
    ang = jnp.arange(S, dtype=jnp.float32)[:, None] * inv[None, :]
    return jnp.cos(ang), jnp.sin(ang)


def apply_rope(x, cos, sin):
    half = x.shape[-1] // 2
    x1 = x[..., :half].astype(jnp.float32)
    x2 = x[..., half:].astype(jnp.float32)
    return jnp.concatenate([x1 * cos - x2 * sin, x2 * cos + x1 * sin], axis=-1).astype(x.dtype)


def mla_branch(q_down, kv_down, cos, sin, q_norm_g, w_uq, kv_norm_g, w_ukv):
    B, S, _ = q_down.shape
    cq = rmsnorm(q_down, q_norm_g)
    q = (cq @ w_uq).reshape(B, S, MLA_HEADS, MLA_QK)
    q_rope = apply_rope(q[..., MLA_NOPE:], cos[:, None, :], sin[:, None, :])
    q = jnp.concatenate([q[..., :MLA_NOPE], q_rope], axis=-1) * (MLA_QK ** -0.5)
    ckv = rmsnorm(kv_down[..., :MLA_KV_LORA], kv_norm_g)
    k_rope = apply_rope(kv_down[..., MLA_KV_LORA:], cos, sin)
    kv = (ckv @ w_ukv).reshape(B, S, MLA_HEADS, MLA_NOPE + MLA_V)
    v = kv[..., MLA_NOPE:]
    k = jnp.concatenate([kv[..., :MLA_NOPE],
                         jnp.broadcast_to(k_rope[:, :, None, :], (B, S, MLA_HEADS, MLA_ROPE))], axis=-1)
    nq = S // Q_BLOCK
    qb = jnp.moveaxis(q.reshape(B, nq, Q_BLOCK, MLA_HEADS, MLA_QK), 1, 0)

    def attend(q_blk):
        s = jnp.einsum('bqhd,bkhd->bhqk', q_blk, k).astype(jnp.float32)
        p = jax.nn.softmax(s, axis=-1).astype(v.dtype)
        return jnp.einsum('bhqk,bkhd->bqhd', p, v)

    o = lax.map(attend, qb)
    return jnp.moveaxis(o, 0, 1).reshape(B, S, MLA_HEADS * MLA_V)


def lin_combine(e1, e2):
    a1, b1 = e1
    a2, b2 = e2
    return a1 * a2, a2 * b1 + b2


def rglru_branch(x_in, gate_in, conv_w, conv_b, w_a, b_a, w_i, b_i, lam):
    B, S, _ = x_in.shape
    xc = dwconv_centred(x_in, conv_w, conv_b)
    xb = xc.reshape(B, S, RG_BLOCKS, RG_BW)
    xf = xc.astype(jnp.float32)
    pos = jnp.arange(S)

    def direction(d, reverse):
        r = jax.nn.sigmoid((jnp.einsum('bsnc,ncd->bsnd', xb, w_a[d]).reshape(B, S, RG_WIDTH) + b_a[d]).astype(jnp.float32))
        i = jax.nn.sigmoid((jnp.einsum('bsnc,ncd->bsnd', xb, w_i[d]).reshape(B, S, RG_WIDTH) + b_i[d]).astype(jnp.float32))
        log_a = -RG_C * r * jax.nn.softplus(-lam[d].astype(jnp.float32))
        a = jnp.exp(log_a)
        first = (pos == ((S - 1) if reverse else 0))[None, :, None]
        mult = jnp.where(first, 1.0, jnp.sqrt(jnp.maximum(-jnp.expm1(2.0 * log_a), 0.0)))
        _, h = lax.associative_scan(lin_combine, (a, mult * (i * xf)), reverse=reverse, axis=1)
        return h

    h = direction(0, False) + direction(1, True)
    return (h * jax.nn.gelu(gate_in.astype(jnp.float32))).astype(x_in.dtype)


def gated_delta_chunked(q, k, v, g, beta):
    B, S, H, DK = q.shape
    DV = v.shape[-1]
    C = GDN_CHUNK
    N = S // C

    def chunks(t):
        t = t.reshape((B, N, C, H) + t.shape[3:])
        return jnp.moveaxis(t, (1, 3), (0, 2))

    q = chunks(q) * (DK ** -0.5)
    k = chunks(k)
    v = chunks(v)
    beta = chunks(beta)
    g = jnp.cumsum(chunks(g), axis=-1)
    tril = jnp.tril(jnp.ones((C, C), dtype=bool))
    strict = jnp.tril(jnp.ones((C, C), dtype=bool), -1)
    decay = jnp.exp(jnp.where(tril, g[..., :, None] - g[..., None, :], -jnp.inf))
    kb = k * beta[..., None]
    A = jnp.where(strict, jnp.einsum('nbhid,nbhjd->nbhij', kb, k) * decay, 0.0)
    eye = jnp.eye(C, dtype=jnp.float32)
    T = lax.linalg.triangular_solve(eye + A, jnp.broadcast_to(eye, A.shape), left_side=True,
                                    lower=True, unit_diagonal=True)
    u = T @ (v * beta[..., None])
    w = T @ (kb * jnp.exp(g)[..., None])
    attn = jnp.einsum('nbhid,nbhjd->nbhij', q, k) * decay
    q_dec = q * jnp.exp(g)[..., None]
    k_dec = k * jnp.exp(g[..., -1:] - g)[..., None]
    g_last = jnp.exp(g[..., -1])

    def step(state, xs):
        u_n, w_n, qd_n, kd_n, attn_n, gl_n = xs
        v_new = u_n - w_n @ state
        o_n = qd_n @ state + attn_n @ v_new
        state = state * gl_n[..., None, None] + jnp.swapaxes(kd_n, -1, -2) @ v_new
        return state, o_n

    s0 = jnp.zeros((B, H, DK, DV), jnp.float32)
    _, o = lax.scan(step, s0, (u, w, q_dec, k_dec, attn, g_last))
    return jnp.moveaxis(o, (0, 2), (1, 3)).reshape(B, S, H, DV)


def gdn_branch(qkv, z, alpha_raw, beta_raw, conv_w, a_log, dt_bias, norm_g):
    B, S, _ = qkv.shape
    qkv_c = jax.nn.silu(dwconv_centred(qkv, conv_w)).astype(jnp.float32)
    q, k, v = split_cols(qkv_c, (GDN_HEADS * GDN_DK, GDN_HEADS * GDN_DK, GDN_HEADS * GDN_DV))
    q = l2norm(q.reshape(B, S, GDN_HEADS, GDN_DK))
    k = l2norm(k.reshape(B, S, GDN_HEADS, GDN_DK))
    v = v.reshape(B, S, GDN_HEADS, GDN_DV)
    alpha_raw = alpha_raw.astype(jnp.float32).reshape(B, S, 2, GDN_HEADS)
    beta = jax.nn.sigmoid(beta_raw.astype(jnp.float32).reshape(B, S, 2, GDN_HEADS))
    g = -jnp.exp(a_log.astype(jnp.float32)) * jax.nn.softplus(alpha_raw + dt_bias.astype(jnp.float32))
    o_f = gated_delta_chunked(q, k, v, g[:, :, 0], beta[:, :, 0])
    fl = lambda t: jnp.flip(t, axis=1)
    o_b = fl(gated_delta_chunked(fl(q), fl(k), fl(v), fl(g[:, :, 1]), fl(beta[:, :, 1])))
    o = rmsnorm(o_f + o_b, norm_g) * jax.nn.silu(z.astype(jnp.float32).reshape(B, S, GDN_HEADS, GDN_DV))
    return o.reshape(B, S, GDN_HEADS * GDN_DV).astype(qkv.dtype)


def encoder_layer(x, c, cos, sin, p):
    B, S, D = x.shape
    mod = (jax.nn.silu(c) @ p['w_mod'] + p['b_mod'])[:, None, :]
    sh1, sc1, gt1, sh2, sc2, gt2 = jnp.split(mod, 6, axis=-1)
    h = rmsnorm(x, p['ln1_g']) * (1 + sc1) + sh1
    (q_down, kv_down, rg_x, rg_gate, gdn_qkv, gdn_z,
     gdn_alpha, gdn_beta, gates) = split_cols(h @ p['w_in'], IN_SPLITS)
    o_mla = mla_branch(q_down, kv_down, cos, sin, p['mla_q_norm_g'], p['mla_w_uq'],
                       p['mla_kv_norm_g'], p['mla_w_ukv'])
    o_rg = rglru_branch(rg_x, rg_gate, p['rg_conv_w'], p['rg_conv_b'], p['rg_w_a'], p['rg_b_a'],
                        p['rg_w_i'], p['rg_b_i'], p['rg_lam'])
    o_gdn = gdn_branch(gdn_qkv, gdn_z, gdn_alpha, gdn_beta, p['gdn_conv_w'], p['gdn_a_log'],
                       p['gdn_dt_bias'], p['gdn_norm_g'])
    gates = jax.nn.sigmoid(gates.reshape(B, S, N_BRANCH, D).astype(jnp.float32)).astype(x.dtype)
    mixed = None
    for n, o_n in enumerate((o_mla, o_rg, o_gdn)):
        term = gates[:, :, n] * (o_n @ p['w_branch'][n])
        mixed = term if mixed is None else mixed + term
    x = x + gt1 * (mixed @ p['w_out'])
    h2 = rmsnorm(x, p['ln2_g']) * (1 + sc2) + sh2
    up = dwconv_centred(h2 @ p['ffn_w_up'], p['ffn_conv_w'], p['ffn_conv_b'])
    u_lin, u_gate = jnp.split(up, 2, axis=-1)
    x = x + gt2 * ((jax.nn.silu(u_gate) * u_lin) @ p['ffn_w_down'])
    return x


def trunk(x, c, stacked, final_norm_g):
    cos, sin = rope_tables(x.shape[1])
    for l in range(DEPTH):
        p = {name: arr[l] for name, arr in stacked.items()}
        x = encoder_layer(x, c, cos, sin, p)
    return rmsnorm(x, final_norm_g)


def setup_inputs(seed: int = 0) -> dict:
    key = jax.random.key(seed)
    ks = iter(jax.random.split(key, 48))
    f32 = jnp.float32
    L, D = DEPTH, D_MODEL

    def nrm(shape, scale):
        return jax.random.normal(next(ks), shape, f32) * scale

    def gain(shape):
        return 1.0 + 0.02 * jax.random.normal(next(ks), shape, f32)

    u = jax.random.uniform(next(ks), (L, 2, RG_WIDTH), f32, 0.9, 0.999)
    rg_lam = jnp.log(u) - jnp.log1p(-u)
    gdn_a_log = jnp.log(jax.random.uniform(next(ks), (L, 2, GDN_HEADS), f32, 1.0, 16.0))
    dt = jnp.exp(jax.random.uniform(next(ks), (L, 2, GDN_HEADS), f32, math.log(1e-3), math.log(1e-1)))
    gdn_dt_bias = dt + jnp.log(-jnp.expm1(-dt))
    return {
        'x_prompt': nrm((BATCH, SEQ, D), 1.0),
        'x_sample': nrm((DEC_BATCH, DEC_SEQ, D), 1.0),
        'c_prompt': nrm((BATCH, D), 1.0),
        'c_sample': nrm((DEC_BATCH, D), 1.0),
        'ln1_g': gain((L, D)),
        'w_mod': nrm((L, D, 6 * D), D ** -0.5),
        'b_mod': nrm((L, 6 * D), 0.02),
        'w_in': nrm((L, D, N_IN), D ** -0.5),
        'mla_q_norm_g': gain((L, MLA_Q_LORA)),
        'mla_w_uq': nrm((L, MLA_Q_LORA, MLA_HEADS * MLA_QK), MLA_Q_LORA ** -0.5),
        'mla_kv_norm_g': gain((L, MLA_KV_LORA)),
        'mla_w_ukv': nrm((L, MLA_KV_LORA, MLA_HEADS * (MLA_NOPE + MLA_V)), MLA_KV_LORA ** -0.5),
        'rg_conv_w': nrm((L, RG_CONV, RG_WIDTH), RG_CONV ** -0.5),
        'rg_conv_b': nrm((L, RG_WIDTH), 0.02),
        'rg_w_a': nrm((L, 2, RG_BLOCKS, RG_BW, RG_BW), RG_BW ** -0.5),
        'rg_b_a': nrm((L, 2, RG_WIDTH), 0.02),
        'rg_w_i': nrm((L, 2, RG_BLOCKS, RG_BW, RG_BW), RG_BW ** -0.5),
        'rg_b_i': nrm((L, 2, RG_WIDTH), 0.02),
        'rg_lam': rg_lam,
        'gdn_conv_w': nrm((L, GDN_CONV, GDN_HEADS * (2 * GDN_DK + GDN_DV)), GDN_CONV ** -0.5),
        'gdn_a_log': gdn_a_log,
        'gdn_dt_bias': gdn_dt_bias,
        'gdn_norm_g': gain((L, GDN_DV)),
        'w_branch': nrm((L, N_BRANCH, BRANCH_WIDTH, D), BRANCH_WIDTH ** -0.5),
        'w_out': nrm((L, D, D), D ** -0.5),
        'ln2_g': gain((L, D)),
        'ffn_w_up': nrm((L, D, 2 * D_FF), D ** -0.5),
        'ffn_conv_w': nrm((L, FFN_CONV, 2 * D_FF), FFN_CONV ** -0.5),
        'ffn_conv_b': nrm((L, 2 * D_FF), 0.02),
        'ffn_w_down': nrm((L, D_FF, D), D_FF ** -0.5),
        'final_norm_g': gain((D,)),
    }


def reference(x_prompt, x_sample, c_prompt, c_sample, ln1_g, w_mod, b_mod, w_in,
              mla_q_norm_g, mla_w_uq, mla_kv_norm_g, mla_w_ukv,
              rg_conv_w, rg_conv_b, rg_w_a, rg_b_a, rg_w_i, rg_b_i, rg_lam,
              gdn_conv_w, gdn_a_log, gdn_dt_bias, gdn_norm_g,
              w_branch, w_out, ln2_g, ffn_w_up, ffn_conv_w, ffn_conv_b, ffn_w_down,
              final_norm_g):
    stacked = dict(ln1_g=ln1_g, w_mod=w_mod, b_mod=b_mod, w_in=w_in,
                   mla_q_norm_g=mla_q_norm_g, mla_w_uq=mla_w_uq,
                   mla_kv_norm_g=mla_kv_norm_g, mla_w_ukv=mla_w_ukv,
                   rg_conv_w=rg_conv_w, rg_conv_b=rg_conv_b, rg_w_a=rg_w_a, rg_b_a=rg_b_a,
                   rg_w_i=rg_w_i, rg_b_i=rg_b_i, rg_lam=rg_lam,
                   gdn_conv_w=gdn_conv_w, gdn_a_log=gdn_a_log, gdn_dt_bias=gdn_dt_bias,
                   gdn_norm_g=gdn_norm_g, w_branch=w_branch, w_out=w_out, ln2_g=ln2_g,
                   ffn_w_up=ffn_w_up, ffn_conv_w=ffn_conv_w, ffn_conv_b=ffn_conv_b,
                   ffn_w_down=ffn_w_down)
    y_prompt = trunk(x_prompt, c_prompt, stacked, final_norm_g)
    y_sample = trunk(x_sample, c_sample, stacked, final_norm_g)
    return (y_prompt, y_sample)
```

```python
import contextlib
import math
import numpy as np
import concourse.bass as bass
import concourse.mybir as mybir
from concourse.bass_utils import run_bass_kernel_spmd

F32 = mybir.dt.float32
BF16 = mybir.dt.bfloat16
AF = mybir.ActivationFunctionType
ALU = mybir.AluOpType
AX = mybir.AxisListType

D = 1024
T = 2048
NTB = 4
NTL = 16
L_ALL = 2
NCORES = 8
EPS = 1e-6
N_IN = 6832
KR0 = N_IN
NA = N_IN + 192
C_RGX, C_RGG, C_GQKV, C_GZ, C_AB, C_GATE = 672, 1184, 1696, 3232, 3744, 3760
DFF = 2816
NJ = 22
BIG = 30000.0
SEM_LIMIT = 30000

VOFF = {}
_o = 0
for _n, _w in [("ln1", 8), ("ln2", 8), ("bmod", 48), ("qg", 3), ("kvg", 2), ("rgcw", 16), ("rgcb", 4),
               ("rgba", 8), ("rgbi", 8), ("rglam", 8), ("gcw", 48), ("gng", 128), ("galog", 128), ("gdtb", 128),
               ("fcw", 132), ("fcb", 44), ("fng", 8)]:
    VOFF[_n] = (_o, _w)
    _o += _w
NV = _o
COFF = {}
_o = 0
for _n in ["ident", "ones", "negones", "mt0", "mt1", "negs0", "negs1", "negst0", "negst1", "bd16", "m32", "m64", "m128"]:
    COFF[_n] = _o
    _o += 128
NC = _o


class Tk:
    __slots__ = ("name", "w", "r", "t", "dk")

    def __init__(self, name, t=None):
        self.name = name
        self.w = None
        self.r = {}
        self.t = t
        self.dk = name

    def __getitem__(self, idx):
        return self.t[idx]


class Prog:
    def __init__(self, nc):
        self.nc = nc
        self.es = contextlib.ExitStack()
        self.eng = {"pe": nc.tensor, "act": nc.scalar, "dve": nc.vector, "pool": nc.gpsimd, "sp": nc.sync}
        self.sems = {}
        self.cnt = {}
        self.cur = {}
        self.epoch = {e: 0 for e in self.eng}
        self.seen = {e: {} for e in self.eng}
        self.ninst = 0
        self.uid = 0
        self.namectr = {}
        for e in self.eng:
            self._new_epoch(e)

    def _mk_sem(self, key):
        s = self.es.enter_context(self.nc.semaphore("s_%s" % (key,)))
        self.sems[key] = s
        self.cnt[key] = 0
        return s

    def _new_epoch(self, e):
        key = "%s%d" % (e, self.epoch[e])
        self.epoch[e] += 1
        self._mk_sem(key)
        self.cur[e] = key

    def sbuf(self, name, shape, dt, es=None):
        self.uid += 1
        t = (es or self.es).enter_context(self.nc.sbuf_tensor("%s_%d" % (name, self.uid), list(shape), dt))
        tk = Tk(name + str(self.uid), t)
        c = self.namectr.get(name, 0)
        self.namectr[name] = c + 1
        tk.dk = "%s_%d" % (name, c % 4)
        return tk

    def dram(self, name, shape, dt, kind="Internal"):
        t = self.nc.dram_tensor(name, list(shape), dt, kind=kind).ap()
        return Tk(name, t)

    def _deps(self, reads, writes):
        deps = {}
        for t in reads:
            if t.w is not None:
                k, v = t.w
                if deps.get(k, 0) < v:
                    deps[k] = v
        for t in writes:
            if t.w is not None:
                k, v = t.w
                if deps.get(k, 0) < v:
                    deps[k] = v
            for k, v in t.r.items():
                if deps.get(k, 0) < v:
                    deps[k] = v
        return deps

    def _wait(self, e, deps):
        eng = self.eng[e]
        seen = self.seen[e]
        for k, v in deps.items():
            isdma = k.startswith("dma")
            if (not isdma) and k.startswith(e) and e in ("pe", "sp"):
                continue
            if isdma:
                v = self.cnt[k]
            if seen.get(k, 0) < v:
                eng.wait_ge(self.sems[k], v)
                seen[k] = v
                self.ninst += 1

    def op(self, e, fn, reads=(), writes=()):
        self._wait(e, self._deps(reads, writes))
        if self.cnt[self.cur[e]] >= SEM_LIMIT:
            self._new_epoch(e)
        key = self.cur[e]
        inst = fn()
        self.cnt[key] += 1
        v = self.cnt[key]
        inst.then_inc(self.sems[key], 1)
        self.ninst += 1
        for t in reads:
            t.r[key] = v
        for t in writes:
            t.w = (key, v)
            t.r = {}
        return inst

    def dma(self, out_ts, out_ap, in_ts, in_ap, q="sp", sem_t=None):
        st = sem_t if sem_t is not None else out_ts[0]
        key = "dma_" + st.dk
        if key not in self.sems:
            self._mk_sem(key)
        self._wait(q, self._deps(in_ts, out_ts))
        inst = self.eng[q].dma_start(out=out_ap, in_=in_ap)
        self.cnt[key] += 16
        v = self.cnt[key]
        inst.then_inc(self.sems[key], 16)
        self.ninst += 1
        for t in in_ts:
            t.r[key] = v
        for t in out_ts:
            t.w = (key, v)
            t.r = {}
        return inst

    def barrier(self):
        allk = {k: v for k, v in self.cnt.items() if v > 0}
        for e in self.eng:
            self._wait(e, allk)

    def finish(self, outs, q="sp"):
        deps = {}
        for t in outs:
            if t.w is not None:
                k, v = t.w
                deps[k] = max(deps.get(k, 0), v)
        self._wait(q, deps)


OUTSTAGE = 3
GDNSTAGE = 9
GDN_G = 2
GDN_POOL = False
GDN_BF16INV = True
GSUB = 9
RGDBG = 0
PHASES = set(['N1', 'MLA', 'RG', 'GDN', 'MERGE', 'FFN'])


def build(nseq, nlayer=L_ALL, dbg=False):
    nc = bass.Bass("TRN2", target_bir_lowering=False)
    P = Prog(nc)
    L = L_ALL
    xin = P.dram("xin", [nseq, T, D], F32, "ExternalInput")
    cT = P.dram("cT", [128, 8, nseq], F32, "ExternalInput")
    wA = P.dram("wA", [L, 128, 8, NA], F32, "ExternalInput")
    wUQ = P.dram("wUQ", [L, 128, 3, 1536], F32, "ExternalInput")
    wKN = P.dram("wKN", [L, 128, 2, 768], F32, "ExternalInput")
    wV = P.dram("wV", [L, 128, 2, 512], F32, "ExternalInput")
    wRG = P.dram("wRG", [L, 128, 16, 128], F32, "ExternalInput")
    wBR = P.dram("wBR", [L, 128, 12, 1024], F32, "ExternalInput")
    wO = P.dram("wO", [L, 128, 8, 1024], F32, "ExternalInput")
    wUP = P.dram("wUP", [L, 128, 8, 2 * DFF], F32, "ExternalInput")
    wDN = P.dram("wDN", [L, 128, NJ, 1024], F32, "ExternalInput")
    wMOD = P.dram("wMOD", [L, 128, 8, 6144], F32, "ExternalInput")
    vecs_d = P.dram("vecs", [128, L, NV], F32, "ExternalInput")
    consts_d = P.dram("consts", [128, NC], F32, "ExternalInput")
    ropet = P.dram("ropet", [2, 96, T], F32, "ExternalInput")
    yout = P.dram("yout", [nseq, T, D], F32, "ExternalOutput")
    xT_d = nc.dram_tensor("xT_scr", [128, 8, T], F32, kind="Internal").ap()
    xT_tk = [[Tk("xT_%d_%d" % (c, tb)) for tb in range(NTB)] for c in range(8)]
    dbg_d = None
    if dbg:
        dbg_d = P.dram("dbg", [3, 128, 4, T], F32, "ExternalOutput")

    V = nc.vector
    A = nc.scalar
    PE = nc.tensor
    G = nc.gpsimd

    with P.es:
        base = P.es
        consts = P.sbuf("consts", [128, NC], F32)
        vecs = P.sbuf("vecs", [128, L, NV], F32)
        identb = P.sbuf("identb", [128, 128], BF16)
        hT = P.sbuf("hT", [128, 8, T], BF16)
        mod = P.sbuf("mod", [128, L, 48, nseq], F32)
        gs1 = P.sbuf("gs1", [128, L, 8, nseq], F32)
        gs2 = P.sbuf("gs2", [128, L, 8, nseq], F32)
        cneg = P.sbuf("cneg", [128, L, 8], F32)
        cneg2 = P.sbuf("cneg2", [128, L, 8], F32)
        negA = P.sbuf("negA", [128, L, 128], F32)
        NSLOT = 3
        wslots = [P.sbuf("wslot%d" % i, [128, 4096], BF16) for i in range(NSLOT)]
        wctr = [0]
        pbank = []
        for i in range(8):
            t = P.es.enter_context(nc.psum_tensor("psb%d" % i, [128, 512], F32))
            pbank.append((t, [Tk("psq%d_%d" % (i, q)) for q in range(4)]))
        pctr = [0]
        qctr = [0]

        def nextbank(pool=None):
            if pool is None:
                b = pbank[pctr[0] % 8]
            else:
                b = pbank[pool[pctr[0] % len(pool)]]
            pctr[0] += 1
            return b

        def nextq():
            t, qs = pbank[pctr[0] % 8]
            pctr[0] += 1
            return t[:, 0:128], qs[0]

        def cst(name):
            o = COFF[name]
            return consts[:, o:o + 128]

        def vec(l, name, j=0, w=1):
            o, _ = VOFF[name]
            return vecs[:, l, o + j:o + j + w]

        def wslot():
            s = wslots[wctr[0] % NSLOT]
            wctr[0] += 1
            return s

        def wload(slot, dst_ap, src_tk, src_ap):
            P.dma([slot], dst_ap, [src_tk], src_ap, q="pool", sem_t=slot)

        def mm(ps_ap, ps_tks, lhsT, rhs, start, stop, reads):
            P.op("pe", lambda: PE.matmul(ps_ap, lhsT=lhsT, rhs=rhs, start=start, stop=stop), reads, ps_tks)

        def act(out_ap, in_ap, func, reads, writes, bias=None, scale=None):
            kw = {}
            if bias is not None:
                kw["bias"] = bias
            if scale is not None:
                kw["scale"] = scale
            P.op("act", lambda: A.activation(out=out_ap, in_=in_ap, func=func, **kw), reads, writes)

        def vtt(out_ap, in0, in1, op, reads, writes, e="dve"):
            eng = V if e == "dve" else G
            P.op(e, lambda: eng.tensor_tensor(out=out_ap, in0=in0, in1=in1, op=op), reads, writes)

        def vts(out_ap, in0, s1, s2, op0, op1, reads, writes, e="dve"):
            eng = V if e == "dve" else G
            if op1 is None:
                P.op(e, lambda: eng.tensor_scalar(out=out_ap, in0=in0, scalar1=s1, scalar2=None, op0=op0), reads, writes)
            else:
                P.op(e, lambda: eng.tensor_scalar(out=out_ap, in0=in0, scalar1=s1, scalar2=s2, op0=op0, op1=op1), reads, writes)

        def vstt(out_ap, in0, scalar, in1, op0, op1, reads, writes, e="dve"):
            eng = V if e == "dve" else G
            P.op(e, lambda: eng.scalar_tensor_tensor(out=out_ap, in0=in0, scalar=scalar, in1=in1, op0=op0, op1=op1), reads, writes)

        def vcopy(out_ap, in_ap, reads, writes, e="dve"):
            eng = V if e == "dve" else G
            P.op(e, lambda: eng.tensor_copy(out=out_ap, in_=in_ap), reads, writes)

        def vmemset(ap, val, writes, e="dve"):
            eng = V if e == "dve" else G
            P.op(e, lambda: eng.memset(ap, val), [], writes)

        def rsqrt_from(out_ap, in_ap, scale, reads, writes):
            act(out_ap, in_ap, AF.Ln, reads, writes, bias=epsc[:, 0:1], scale=scale)
            act(out_ap, out_ap, AF.Exp, writes, writes, scale=-0.5)

        def softplus_acc(out_ap, x_ap, t1, t2, t3, rd, wr):
            vts(t1, x_ap, -1.0, None, ALU.mult, None, rd, wr)
            vtt(t1, t1, x_ap, ALU.max, rd + wr, wr)
            act(t1, t1, AF.Exp, wr, wr, scale=-1.0)
            vts(t2, t1, 2.0, None, ALU.add, None, wr, wr)
            P.op("dve", lambda: V.reciprocal(out=t2, in_=t2), wr, wr)
            vtt(t1, t1, t2, ALU.mult, wr, wr)
            vtt(t2, t1, t1, ALU.mult, wr, wr)
            vts(t3, t2, 1.0 / 11, 1.0 / 9, ALU.mult, ALU.add, wr, wr)
            for cf in (1.0 / 7, 1.0 / 5, 1.0 / 3, 1.0):
                vtt(t3, t3, t2, ALU.mult, wr, wr)
                vts(t3, t3, cf, None, ALU.add, None, wr, wr)
            vtt(t3, t3, t1, ALU.mult, wr, wr)
            vts(t1, x_ap, 0.0, None, ALU.max, None, rd + wr, wr)
            vstt(out_ap, t3, 2.0, t1, ALU.mult, ALU.add, wr, wr)

        epsc = P.sbuf("epsc", [128, 1], F32)
        vmemset(epsc[:], EPS, [epsc])
        P.dma([consts], consts[:], [consts_d], consts_d.t)
        P.dma([vecs], vecs[:], [vecs_d], vecs_d.t)
        vcopy(identb[:], cst("ident"), [consts], [identb])
        with contextlib.ExitStack() as sc:
            craw = P.sbuf("craw", [128, 8, nseq], F32, sc)
            csil = P.sbuf("csil", [128, 8, nseq], BF16, sc)
            tmpm = P.sbuf("tmpm", [128, 8, nseq], F32, sc)
            spx = P.sbuf("spx", [128, 8], F32, sc)
            spt = P.sbuf("spt", [128, 3, 8], F32, sc)
            P.dma([craw], craw[:], [cT], cT.t)
            act(csil[:], craw[:], AF.Silu, [craw], [csil])
            for l in range(nlayer):
                for pn in range(12):
                    sl = wslot()
                    wv = sl[:, 0:4096].rearrange("p (k n) -> p k n", k=8)
                    wload(sl, wv, wMOD, wMOD.t[l, :, :, pn * 512:(pn + 1) * 512])
                    for jj in range(4):
                        j = pn * 4 + jj
                        pt, pq = nextbank()
                        for k in range(8):
                            mm(pt[:, 0:nseq], pq, wv[:, k, jj * 128:(jj + 1) * 128], csil[:, k, :], k == 0, k == 7, [sl, csil])
                        act(mod[:, l, j, :], pt[:, 0:nseq], AF.Identity, pq, [mod], bias=vec(l, "bmod", j))
                for (gs, lnn, j0) in ((gs1, "ln1", 8), (gs2, "ln2", 32)):
                    vts(tmpm[:], mod[:, l, j0:j0 + 8, :], 1.0, None, ALU.add, None, [mod], [tmpm])
                    for s in range(nseq):
                        vtt(gs[:, l, :, s], tmpm[:, :, s], vec(l, lnn, 0, 8), ALU.mult, [tmpm, vecs], [gs])
                vts(spx[:], vec(l, "rglam", 0, 8), -1.0, None, ALU.mult, None, [vecs], [spx])
                softplus_acc(cneg[:, l, :], spx[:], spt[:, 0, :], spt[:, 1, :], spt[:, 2, :], [spx], [spt, cneg])
                vts(cneg2[:, l, :], cneg[:, l, :], -16.0, None, ALU.mult, None, [cneg], [cneg2])
                vts(cneg[:, l, :], cneg[:, l, :], -8.0, None, ALU.mult, None, [cneg], [cneg])
                act(negA[:, l, :], vec(l, "galog", 0, 128), AF.Exp, [vecs], [negA])
                vts(negA[:, l, :], negA[:, l, :], -1.0, None, ALU.mult, None, [negA], [negA])
            P.barrier()

        def proj_fm(wv, kc, M, rhs_fn, rhs_tks, slot, consume):
            for tb in range(NTB):
                pt, pq = nextbank()
                for k in range(kc):
                    mm(pt[0:M, :], pq, wv[:, k, 0:M], rhs_fn(k, tb), k == 0, k == kc - 1, [slot] + rhs_tks)
                consume(tb, pt, pq)

        def hrhs(k, tb):
            return hT[:, k, tb * 512:(tb + 1) * 512]

        def load_xblock(xb, tb):
            for c in range(8):
                pass
            P.dma([xb], xb[:], [xT_tk[c][tb] for c in range(8)], xT_d[:, :, tb * 512:(tb + 1) * 512])

        def store_xblock(xb, tb):
            P.dma([xT_tk[c][tb] for c in range(8)], xT_d[:, :, tb * 512:(tb + 1) * 512], [xb], xb[:], sem_t=xb)

        def norm_block(xb, tb, l, s, which, sq, rstd, tmp):
            gs = gs1 if which == 1 else gs2
            shj = 0 if which == 1 else 24
            vtt(sq[:], xb[:], xb[:], ALU.mult, [xb], [sq])
            pt, pq = nextbank()
            for c in range(8):
                mm(pt[:, :], pq, cst("ones"), sq[:, c, :], c == 0, c == 7, [consts, sq])
            rsqrt_from(rstd[:], pt[:, :], 1.0 / D, pq, [rstd])
            for c in range(8):
                vtt(tmp[:], xb[:, c, :], rstd[:], ALU.mult, [xb, rstd], [tmp])
                act(hT[:, c, tb * 512:(tb + 1) * 512], tmp[:], AF.Identity, [tmp, gs, mod], [hT],
                    bias=mod[:, l, shj + c, s:s + 1], scale=gs[:, l, c, s:s + 1])

        for s in range(nseq if 'NOSEQ' not in PHASES else 0):
            with contextlib.ExitStack() as sc:
                xtok = [P.sbuf("xtok", [128, D], F32, sc) for _ in range(2)]
                xblk = [P.sbuf("xblk", [128, 8, 128], F32, sc) for _ in range(2)]
                for tl in range(NTL if 'NOL' not in PHASES else 0):
                    xt = xtok[tl % 2]
                    xk = xblk[tl % 2]
                    P.dma([xt], xt[:], [xin], xin.t[s, tl * 128:(tl + 1) * 128, :])
                    for half in range(2):
                        pt, pq = nextbank()
                        for q in range(4):
                            c = half * 4 + q
                            P.op("pe", lambda: PE.transpose(out=pt[:, q * 128:(q + 1) * 128], in_=xt[:, c * 128:(c + 1) * 128],
                                                            identity=cst("ident")), [xt, consts], pq)
                        act(xk[:, half * 4:half * 4 + 4, :], pt[:, :].rearrange("p (q n) -> p q n", q=4), AF.Copy, pq, [xk])
                    tb = tl // 4
                    P.dma([xT_tk[c][tb] for c in range(8)], xT_d[:, :, tl * 128:(tl + 1) * 128], [xk], xk[:], sem_t=xk)
                P.barrier()

            for l in range(nlayer):
                with contextlib.ExitStack() as sc:
                    xb = P.sbuf("xb", [128, 8, 512], F32, sc)
                    sq = P.sbuf("sq", [128, 8, 512], F32, sc)
                    rstd = P.sbuf("rstd", [128, 512], F32, sc)
                    tmp = P.sbuf("tmp", [128, 512], F32, sc)
                    for tb in range(NTB if 'N1' in PHASES else 0):
                        load_xblock(xb, tb)
                        norm_block(xb, tb, l, s, 1, sq, rstd, tmp)
                    P.barrier()

                mixsc = contextlib.ExitStack()
                obT = []
                obT.append(P.sbuf("obT_mla", [128, 4, T], BF16, mixsc))
                with contextlib.ExitStack() as sc:
                    if 'MLA' in PHASES:
                        emit_mla(P, nc, locals(), sc, l, s)
                    P.barrier()
                obT.append(P.sbuf("obT_rg", [128, 4, T], BF16, mixsc))
                with contextlib.ExitStack() as sc:
                    if 'RG' in PHASES:
                        emit_rg(P, nc, locals(), sc, l, s)
                    P.barrier()
                obT.append(P.sbuf("obT_gdn", [128, 4, T], BF16, mixsc))
                with contextlib.ExitStack() as sc:
                    if 'GDN' in PHASES:
                        emit_gdn(P, nc, locals(), sc, l, s)
                    P.barrier()
                if dbg and l == 0 and s == 0:
                    with contextlib.ExitStack() as sc:
                        dtmp = P.sbuf("dtmp", [128, 4, T], F32, sc)
                        for b in range(3):
                            vcopy(dtmp[:], obT[b][:], [obT[b]], [dtmp])
                            P.dma([dbg_d], dbg_d.t[b], [dtmp], dtmp[:], sem_t=dtmp)
                        P.barrier()
                with contextlib.ExitStack() as sc:
                    if 'MERGE' in PHASES:
                        emit_merge(P, nc, locals(), sc, l, s)
                    P.barrier()
                mixsc.close()
                with contextlib.ExitStack() as sc:
                    if 'FFN' in PHASES:
                        emit_ffn(P, nc, locals(), sc, l, s)
                    P.barrier()

            with contextlib.ExitStack() as sc:
                xb = P.sbuf("xb", [128, 8, 512], F32, sc)
                sq = P.sbuf("sq", [128, 8, 512], F32, sc)
                rstd = P.sbuf("rstd", [128, 512], F32, sc)
                ytok = [P.sbuf("ytok", [128, D], F32, sc) for _ in range(2)]
                yc = 0
                for tb in range(NTB if 'NOOUT' not in PHASES else 0):
                    load_xblock(xb, tb)
                    vtt(sq[:], xb[:], xb[:], ALU.mult, [xb], [sq])
                    if OUTSTAGE >= 1:
                        pt, pq = nextbank()
                        for c in range(8):
                            mm(pt[:, :], pq, cst("ones"), sq[:, c, :], c == 0, c == 7, [consts, sq])
                        if OUTSTAGE >= 2:
                            rsqrt_from(rstd[:], pt[:, :], 1.0 / D, pq, [rstd])
                        else:
                            act(rstd[:], pt[:, :], AF.Copy, pq, [rstd])
                        if OUTSTAGE >= 3:
                            for c in range(8):
                                vstt(sq[:, c, :], xb[:, c, :], vec(0, "fng", c), rstd[:], ALU.mult, ALU.mult, [xb, rstd, vecs], [sq])
                    for t4 in range(4):
                        yt = ytok[yc % 2]
                        yc += 1
                        for half in range(2):
                            pt, pq = nextbank()
                            for q in range(4):
                                c = half * 4 + q
                                P.op("pe", lambda: PE.transpose(out=pt[:, q * 128:(q + 1) * 128], in_=sq[:, c, t4 * 128:(t4 + 1) * 128],
                                                                identity=cst("ident")), [sq, consts], pq)
                            act(yt[:, half * 512:(half + 1) * 512], pt[:, :], AF.Copy, pq, [yt])
                        tl = tb * 4 + t4
                        P.dma([yout], yout.t[s, tl * 128:(tl + 1) * 128, :], [yt], yt[:], sem_t=yt)
                P.barrier()
        outs = [yout] + ([dbg_d] if dbg else [])
        P.finish(outs)
        P.barrier()
    return nc, P


def _h(env):
    class E:
        pass
    e = E()
    e.__dict__.update(env)
    return e


def emit_mla(P, nc, env, sc, l, s):
    e = _h(env)
    hT, obT, consts, identb = e.hT, e.obT, e.consts, e.identb
    mm, act, vtt, vts, vstt, vcopy, vmemset = e.mm, e.act, e.vtt, e.vts, e.vstt, e.vcopy, e.vmemset
    nextbank, cst, vec, wslot, wload, rsqrt_from, hrhs = e.nextbank, e.cst, e.vec, e.wslot, e.wload, e.rsqrt_from, e.hrhs
    wA, wUQ, wKN, wV, ropet, vecs = e.wA, e.wUQ, e.wKN, e.wV, e.ropet, e.vecs
    ob = obT[0]
    cosT = P.sbuf("cosT", [96, T], BF16, sc)
    sinT = P.sbuf("sinT", [96, T], BF16, sc)
    cq = P.sbuf("cq", [128, 3, T], BF16, sc)
    ckv = P.sbuf("ckv", [128, 2, T], BF16, sc)
    krT = P.sbuf("krT", [96, T], BF16, sc)
    Vall = P.sbuf("Vall", [128, NTL, 8, 128], BF16, sc)
    qhs = [P.sbuf("qh", [96, T], BF16, sc) for _ in range(2)]
    khs = [P.sbuf("kh", [96, T], BF16, sc) for _ in range(2)]
    PTs = [P.sbuf("PT", [128, 512], BF16, sc) for _ in range(4)]
    sqs = [P.sbuf("sqm", [128, 512], F32, sc) for _ in range(3)]
    rstd = P.sbuf("rstdm", [128, 512], F32, sc)
    t1 = P.sbuf("t1m", [128, 512], F32, sc)
    t2 = P.sbuf("t2m", [128, 512], F32, sc)
    dsh = P.sbuf("dsh", [128, 512], F32, sc)
    P.dma([cosT], cosT[:], [ropet], ropet.t[0], q="pool")
    P.dma([sinT], sinT[:], [ropet], ropet.t[1], q="pool")
    vmemset(Vall[:], 1.0, [Vall])

    for (dst, col0, nch, gname) in ((cq, 0, 3, "qg"), (ckv, 384, 2, "kvg")):
        sl = wslot()
        wv = sl[:, 0:8 * nch * 128].rearrange("p (k n) -> p k n", k=8)
        wload(sl, wv, wA, wA.t[l, :, :, col0:col0 + nch * 128])
        for tb in range(NTB):
            pcs = []
            for c in range(nch):
                pt, pq = nextbank()
                for k in range(8):
                    mm(pt[:, :], pq, wv[:, k, c * 128:(c + 1) * 128], hrhs(k, tb), k == 0, k == 7, [sl, hT])
                act(sqs[c][:], pt[:, :], AF.Square, pq, [sqs[c]])
                pcs.append((pt, pq))
            pt2, pq2 = nextbank()
            for c in range(nch):
                mm(pt2[:, :], pq2, cst("ones"), sqs[c][:], c == 0, c == nch - 1, [consts, sqs[c]])
            rsqrt_from(rstd[:], pt2[:, :], 1.0 / (nch * 128), pq2, [rstd])
            for c in range(nch):
                pt, pq = pcs[c]
                vtt(t1[:], pt[:, :], rstd[:], ALU.mult, pq + [rstd], [t1])
                act(dst[:, c, tb * 512:(tb + 1) * 512], t1[:], AF.Identity, [t1, vecs], [dst], scale=vec(l, gname, c))
    sl = wslot()
    wv = sl[:, 0:8 * 192].rearrange("p (k n) -> p k n", k=8)
    wload(sl, wv, wA, wA.t[l, :, :, KR0:KR0 + 192])
    for tb in range(NTB):
        blk = slice(tb * 512, (tb + 1) * 512)
        pa, qa = nextbank()
        pb, qb_ = nextbank()
        for k in range(8):
            mm(pa[0:96, :], qa, wv[:, k, 0:96], hrhs(k, tb), k == 0, k == 7, [sl, hT])
        for k in range(8):
            mm(pb[0:96, :], qb_, wv[:, k, 96:192], hrhs(k, tb), k == 0, k == 7, [sl, hT])
        vtt(t1[0:96, :], pa[0:96, :], cosT[:, blk], ALU.mult, qa + [cosT], [t1])
        vtt(t2[0:96, :], pb[0:96, :], sinT[:, blk], ALU.mult, qb_ + [sinT], [t2])
        vtt(krT[:, blk], t1[0:96, :], t2[0:96, :], ALU.add, [t1, t2], [krT])
    sl = wslot()
    wv = sl[:, 0:1024].rearrange("p (k n) -> p k n", k=2)
    wload(sl, wv, wV, wV.t[l])
    for tl in range(NTL):
        pt, pq = nextbank()
        for c in range(2):
            mm(pt[:, :], pq, ckv[:, c, tl * 128:(tl + 1) * 128], wv[:, c, :], c == 0, c == 1, [ckv, sl])
        pv = pt[:, :].rearrange("p (h d) -> p h d", h=8)
        for par in range(2):
            act(Vall[:, tl, par::2, par * 64:par * 64 + 64], pv[:, par::2, :], AF.Copy, pq, [Vall])
    scale = 96.0 ** -0.5
    for h in range(8):
        qh, kh = qhs[h % 2], khs[h % 2]
        sl = wslot()
        wq = sl[:, 0:3 * 192].rearrange("p (k n) -> p k n", k=3)
        wload(sl, wq[:, :, 0:96], wUQ, wUQ.t[l, :, :, h * 96:(h + 1) * 96])
        wload(sl, wq[:, :, 96:192], wUQ, wUQ.t[l, :, :, 768 + h * 96:768 + (h + 1) * 96])
        wk = sl[:, 1024:1024 + 2 * 96].rearrange("p (k n) -> p k n", k=2)
        wload(sl, wk, wKN, wKN.t[l, :, :, h * 96:(h + 1) * 96])
        for tb in range(NTB):
            blk = slice(tb * 512, (tb + 1) * 512)
            pa, qa = nextbank()
            pb, qb_ = nextbank()
            for c in range(3):
                mm(pa[0:96, :], qa, wq[:, c, 0:96], cq[:, c, blk], c == 0, c == 2, [sl, cq])
            for c in range(3):
                mm(pb[0:96, :], qb_, wq[:, c, 96:192], cq[:, c, blk], c == 0, c == 2, [sl, cq])
            vtt(t1[0:96, :], pa[0:96, :], cosT[:, blk], ALU.mult, qa + [cosT], [t1])
            vtt(t2[0:96, :], pb[0:96, :], sinT[:, blk], ALU.mult, qb_ + [sinT], [t2])
            vtt(qh[:, blk], t1[0:96, :], t2[0:96, :], ALU.add, [t1, t2], [qh])
            pk, qk = nextbank()
            for c in range(2):
                mm(pk[0:96, :], qk, wk[:, c, 0:96], ckv[:, c, blk], c == 0, False, [sl, ckv])
            mm(pk[0:96, :], qk, identb[0:96, 0:96], krT[:, blk], False, True, [identb, krT])
            act(kh[:, blk], pk[0:96, :], AF.Copy, qk, [kh])
        par = h % 2
        num = slice(par * 64, par * 64 + 64)
        den = slice((1 - par) * 64, (1 - par) * 64 + 64)
        pti = 0
        for qb in range(NTB):
            qblk = slice(qb * 512, (qb + 1) * 512)
            po, qo = nextbank([0, 1])
            LOOK = 2
            pend = []
            for kt in range(NTL + LOOK):
                if kt < NTL:
                    ps_, qs_ = nextbank([2, 3, 4, 5, 6, 7])
                    mm(ps_[:, :], qs_, kh[:, kt * 128:(kt + 1) * 128], qh[:, qblk], True, True, [kh, qh])
                    PT = PTs[pti % 4]
                    pti += 1
                    act(PT[:], ps_[:, :], AF.Exp, qs_, [PT], scale=scale)
                    pend.append(PT)
                if kt >= LOOK:
                    k2 = kt - LOOK
                    PT2 = pend[k2]
                    mm(po[:, :], qo, Vall[:, k2, h, :], PT2[:], k2 == 0, k2 == NTL - 1, [Vall, PT2])
            act(dsh[num, :], po[den, :], AF.Copy, qo, [dsh])
            P.op("dve", lambda: nc.vector.reciprocal(out=dsh[num, :], in_=dsh[num, :]), [dsh], [dsh])
            vtt(ob[num, h // 2, qblk], po[num, :], dsh[num, :], ALU.mult, qo + [dsh], [ob])


def emit_rg(P, nc, env, sc, l, s):
    e = _h(env)
    hT, obT, consts = e.hT, e.obT, e.consts
    mm, act, vtt, vts, vstt, vcopy, vmemset = e.mm, e.act, e.vtt, e.vts, e.vstt, e.vcopy, e.vmemset
    nextbank, cst, vec, wslot, wload, hrhs = e.nextbank, e.cst, e.vec, e.wslot, e.wload, e.hrhs
    wA, wRG, vecs, cneg, cneg2 = e.wA, e.wRG, e.vecs, e.cneg, e.cneg2
    ob = obT[1]
    xpad = P.sbuf("xpad", [128, T + 4], F32, sc)
    xc = P.sbuf("xc", [128, T], F32, sc)
    xcb = P.sbuf("xcb", [128, T], BF16, sc)
    R = P.sbuf("R", [128, T], F32, sc)
    I = P.sbuf("I", [128, T], F32, sc)
    Ab = P.sbuf("Ab", [128, T], F32, sc)
    HF = P.sbuf("HF", [128, T], F32, sc)
    HB = P.sbuf("HB", [128, T], F32, sc)
    gt = P.sbuf("gt", [128, T], F32, sc)
    wrg = P.sbuf("wrg", [128, 16, 128], BF16, sc)
    P.dma([wrg], wrg[:], [wRG], wRG.t[l], q="pool")
    vmemset(xpad[:], 0.0, [xpad])
    for c in range(4):
        sl = wslot()
        wv = sl[:, 0:2048].rearrange("p (k n) -> p k n", k=8)
        wload(sl, wv[:, :, 0:128], wA, wA.t[l, :, :, C_RGX + c * 128:C_RGX + (c + 1) * 128])
        wload(sl, wv[:, :, 128:256], wA, wA.t[l, :, :, C_RGG + c * 128:C_RGG + (c + 1) * 128])
        for tb in range(NTB):
            pt, pq = nextbank()
            for k in range(8):
                mm(pt[:, :], pq, wv[:, k, 0:128], hrhs(k, tb), k == 0, k == 7, [sl, hT])
            act(xpad[:, 1 + tb * 512:1 + (tb + 1) * 512], pt[:, :], AF.Copy, pq, [xpad])
        for tb in range(NTB):
            pt, pq = nextbank()
            for k in range(8):
                mm(pt[:, :], pq, wv[:, k, 128:256], hrhs(k, tb), k == 0, k == 7, [sl, hT])
            act(gt[:, tb * 512:(tb + 1) * 512], pt[:, :], AF.Copy, pq, [gt])
        vtt(HB[:], gt[:], gt[:], ALU.mult, [gt], [HB])
        vts(HB[:], HB[:], 0.044715, 1.0, ALU.mult, ALU.add, [HB], [HB])
        vtt(HB[:], HB[:], gt[:], ALU.mult, [HB, gt], [HB])
        act(HB[:], HB[:], AF.Sigmoid, [HB], [HB], scale=2.0 * math.sqrt(2.0 / math.pi))
        vtt(gt[:], gt[:], HB[:], ALU.mult, [gt, HB], [gt])
        act(xc[:], xpad[:, 0:T], AF.Identity, [xpad, vecs], [xc], bias=vec(l, "rgcb", c), scale=vec(l, "rgcw", c * 4 + 0))
        for k in range(1, 4):
            vstt(xc[:], xpad[:, k:k + T], vec(l, "rgcw", c * 4 + k), xc[:], ALU.mult, ALU.add, [xpad, xc, vecs], [xc])
        vcopy(xcb[:], xc[:], [xc], [xcb])
        if RGDBG == 1:
            vcopy(ob[:, c, :], xc[:], [xc], [ob])
            continue
        if RGDBG == 2:
            vcopy(ob[:, c, :], gt[:], [gt], [ob])
            continue
        for d in range(2):
            for (ai, dst, bname) in ((0, R, "rgba"), (1, I, "rgbi")):
                wi = (d * 2 + ai) * 4 + c
                for tb in range(NTB):
                    blk = slice(tb * 512, (tb + 1) * 512)
                    pt, pq = nextbank()
                    mm(pt[:, :], pq, wrg[:, wi, :], xcb[:, blk], True, True, [wrg, xcb])
                    act(dst[:, blk], pt[:, :], AF.Sigmoid, pq + [vecs], [dst], bias=vec(l, bname, d * 4 + c))
            if RGDBG == 3 and d == 0:
                vcopy(ob[:, c, :], R[:], [R], [ob])
            if RGDBG == 4 and d == 0:
                vcopy(ob[:, c, :], I[:], [I], [ob])
            act(Ab[:], R[:], AF.Exp, [R, cneg], [Ab], scale=cneg[:, l, d * 4 + c:d * 4 + c + 1])
            if RGDBG == 5 and d == 0:
                vcopy(ob[:, c, :], Ab[:], [Ab], [ob])
            act(R[:], R[:], AF.Exp, [R, cneg2], [R], scale=cneg2[:, l, d * 4 + c:d * 4 + c + 1])
            vts(R[:], R[:], -1.0, 1.0, ALU.mult, ALU.add, [R], [R])
            vts(R[:], R[:], 0.0, None, ALU.max, None, [R], [R])
            act(R[:], R[:], AF.Sqrt, [R], [R])
            edge = 0 if d == 0 else T - 1
            vmemset(R[:, edge:edge + 1], 1.0, [R])
            vtt(I[:], I[:], xc[:], ALU.mult, [I, xc], [I])
            vtt(I[:], I[:], R[:], ALU.mult, [I, R], [I])
            if d == 0:
                P.op("dve", lambda: nc.vector.tensor_tensor_scan(out=HF[:], data0=Ab[:], data1=I[:], initial=0.0,
                                                                 op0=ALU.mult, op1=ALU.add), [Ab, I], [HF])
            else:
                P.op("dve", lambda: nc.vector.tensor_tensor_scan(out=HB[:, ::-1], data0=Ab[:, ::-1], data1=I[:, ::-1], initial=0.0,
                                                                 op0=ALU.mult, op1=ALU.add), [Ab, I], [HB])
        if RGDBG in (3, 4, 5):
            continue
        if RGDBG == 6:
            vcopy(ob[:, c, :], HF[:], [HF], [ob])
            continue
        if RGDBG == 7:
            vcopy(ob[:, c, :], HB[:], [HB], [ob])
            continue
        vtt(HF[:], HF[:], HB[:], ALU.add, [HF, HB], [HF])
        vtt(ob[:, c, :], HF[:], gt[:], ALU.mult, [HF, gt], [ob])


def emit_gdn(P, nc, env, sc, l, s):
    e = _h(env)
    hT, obT, consts, identb = e.hT, e.obT, e.consts, e.identb
    mm, act, vtt, vts, vstt, vcopy, vmemset = e.mm, e.act, e.vtt, e.vts, e.vstt, e.vcopy, e.vmemset
    nextbank, nextq, cst, vec, wslot, wload, rsqrt_from, hrhs = e.nextbank, e.nextq, e.cst, e.vec, e.wslot, e.wload, e.rsqrt_from, e.hrhs
    wA, vecs, negA, epsc = e.wA, e.vecs, e.negA, e.epsc
    ob = obT[2]
    qT = P.sbuf("gqT", [128, T], BF16, sc)
    kT = P.sbuf("gkT", [128, T], BF16, sc)
    ktok = P.sbuf("gktok", [128, NTL, 128], BF16, sc)
    vtok = P.sbuf("gvtok", [128, NTL, 128], BF16, sc)
    otok = P.sbuf("gotok", [128, NTL, 128], F32, sc)
    gtok = P.sbuf("ggtok", [128, NTL, 8], F32, sc)
    btok = P.sbuf("gbtok", [128, NTL, 8], F32, sc)
    rtmp = P.sbuf("grtmp", [128, 512], F32, sc)
    S = [P.sbuf("gS", [128, 128], F32, sc) for _ in range(2)]
    Sb = [P.sbuf("gSb", [128, 128], BF16, sc) for _ in range(2)]
    szs = [P.sbuf("gsz", [128, 128], F32, sc) for _ in range(3)]
    ons = [P.sbuf("gon", [128, 128], F32, sc) for _ in range(3)]
    ssqs = [P.sbuf("gssq", [128, 4], F32, sc) for _ in range(3)]

    sl = wslot()
    wab = sl[:, 0:128].rearrange("p (k n) -> p k n", k=8)
    wload(sl, wab, wA, wA.t[l, :, :, C_AB:C_AB + 16])
    for tl in range(NTL):
        pt, pq = nextbank()
        for k in range(8):
            mm(pt[:, 0:16], pq, hT[:, k, tl * 128:(tl + 1) * 128], wab[:, k, :], k == 0, k == 7, [hT, sl])
        vtt(gtok[:, tl, :], pt[:, 0:8], vec(l, "gdtb", 0, 8), ALU.add, pq + [vecs], [gtok])
        act(btok[:, tl, :], pt[:, 8:16], AF.Sigmoid, pq, [btok])
    gfl = gtok[:].rearrange("p a b -> p (a b)")
    e.softplus_acc(rtmp[:, 0:128], gfl, rtmp[:, 128:256], rtmp[:, 256:384], rtmp[:, 384:512], [gtok], [rtmp])
    vtt(gfl, rtmp[:, 0:128], negA[:, l, :], ALU.mult, [rtmp, negA], [gtok])

    ident, ones, negones, bd = cst("ident"), cst("ones"), cst("negones"), cst("bd16")
    offmasks = [cst("m32"), cst("m64"), cst("m128")]
    G = GDN_G
    NH = G + 4
    PL = "pool" if GDN_POOL else "dve"

    for hh in range(4):
        with contextlib.ExitStack() as sc2:
            xpad = P.sbuf("gxpad", [128, T + 4], F32, sc2)
            xc = P.sbuf("gxc", [128, T], F32, sc2)
            vmemset(xpad[:], 0.0, [xpad])
            for role in range(3):
                ch = role * 4 + hh
                col0 = C_GQKV + ch * 128
                sl = wslot()
                wv = sl[:, 0:1024].rearrange("p (k n) -> p k n", k=8)
                wload(sl, wv, wA, wA.t[l, :, :, col0:col0 + 128])
                for tb in range(NTB):
                    pt, pq = nextbank()
                    for k in range(8):
                        mm(pt[:, :], pq, wv[:, k, :], hrhs(k, tb), k == 0, k == 7, [sl, hT])
                    act(xpad[:, 1 + tb * 512:1 + (tb + 1) * 512], pt[:, :], AF.Copy, pq, [xpad])
                act(xc[:], xpad[:, 0:T], AF.Identity, [xpad, vecs], [xc], scale=vec(l, "gcw", ch * 4 + 0))
                for k in range(1, 4):
                    vstt(xc[:], xpad[:, k:k + T], vec(l, "gcw", ch * 4 + k), xc[:], ALU.mult, ALU.add, [xpad, xc, vecs], [xc])
                act(xc[:], xc[:], AF.Silu, [xc], [xc])
                if role < 2:
                    for tb in range(NTB):
                        blk = slice(tb * 512, (tb + 1) * 512)
                        vtt(rtmp[:], xc[:, blk], xc[:, blk], ALU.mult, [xc], [rtmp])
                        pt, pq = nextbank()
                        mm(pt[:, :], pq, ones, rtmp[:], True, True, [consts, rtmp])
                        rsqrt_from(rtmp[:], pt[:, :], 1.0, pq, [rtmp])
                        if role == 0:
                            vstt(xc[:, blk], xc[:, blk], 128.0 ** -0.5, rtmp[:], ALU.mult, ALU.mult, [xc, rtmp], [xc])
                        else:
                            vtt(xc[:, blk], xc[:, blk], rtmp[:], ALU.mult, [xc, rtmp], [xc])
                    vcopy((qT if role == 0 else kT)[:], xc[:], [xc], [qT if role == 0 else kT])
                if role >= 1:
                    dst = ktok if role == 1 else vtok
                    for tl in range(NTL):
                        qa, qk = nextq()
                        mm(qa, [qk], xc[:, tl * 128:(tl + 1) * 128], ident, True, True, [xc, consts])
                        act(dst[:, tl, :], qa, AF.Copy, [qk], [dst])
            P.barrier()
        vmemset(otok[:], 0.0, [otok])
        for d in range(2):
            vmemset(S[d][:], 0.0, [S[d]])
            vmemset(Sb[d][:], 0.0, [Sb[d]])

        with contextlib.ExitStack() as sc3:
            def mk(name, n, dt=F32, w=128):
                return [P.sbuf(name, [128, w], dt, sc3) for _ in range(n)]
            GM, decS, decI, Af = (mk(nm, G) for nm in ("GM", "decS", "decI", "Af"))
            IDT = BF16 if GDN_BF16INV else F32
            identi = identb[:] if GDN_BF16INV else ident
            Ad, Aoff, Bm = (mk(nm, G, IDT) for nm in ("Ad", "Aoff", "Bm"))
            Y = [mk("Ya", G, IDT), mk("Yb", G, IDT)]
            YT = [mk("YTa", G, IDT), mk("YTb", G, IDT)]
            Pm, PTm, Zm, TTb = mk("Pm", G, IDT), mk("PTm", G, IDT), mk("Zm", G, IDT), mk("TTb", G, BF16)
            vb, kbg = mk("vb", G, BF16), mk("kbg", G, BF16)
            um, wTm, kdec, attnT = mk("um", NH), mk("wTm", NH, BF16), mk("kdec", NH, BF16), mk("attnT", NH, BF16)
            cols = mk("cols", NH, F32, 8)
            vnew = mk("vnew", NH, BF16)
            tmpo = mk("tmpo", NH)

            def prep_gen(d, n, i, hi):
                col = d * 4 + hh
                tsl = slice(n * 128, (n + 1) * 128)
                g_col = gtok[:, n, col:col + 1]
                b_col = btok[:, n, col:col + 1]
                mt, negs, negst = cst("mt%d" % d), cst("negs%d" % d), cst("negst%d" % d)
                cl = cols[hi]
                vts(GM[i][:], mt, g_col, None, ALU.mult, None, [consts, gtok], [GM[i]], e=PL)
                vts(cl[:, 6:8], ones[:, 0:2], g_col, None, ALU.mult, None, [consts, gtok], [cl])
                yield
                pD, qD = nextq()
                mm(pD, [qD], GM[i][:], ones, True, False, [GM[i], consts])
                mm(pD, [qD], negones, GM[i][:], False, False, [GM[i], consts])
                mm(pD, [qD], ident, negs, False, True, [consts])
                act(decS[i][:], pD, AF.Exp, [qD], [decS[i]])
                yield
                pDT, qDT = nextq()
                mm(pDT, [qDT], ones, GM[i][:], True, False, [GM[i], consts])
                mm(pDT, [qDT], GM[i][:], negones, False, False, [GM[i], consts])
                mm(pDT, [qDT], ident, negst, False, True, [consts])
                act(decI[i][:], pDT, AF.Exp, [qDT], [decI[i]])
                yield
                pc, qc = nextq()
                mm(pc[:, 0:2], [qc], GM[i][:], ones[:, 0:2], True, True, [GM[i], consts])
                mm(pc[:, 2:4], [qc], ones, cl[:, 6:8], True, True, [cl, consts])
                vcopy(cl[:, 0:2], pc[:, 1:3], [qc], [cl])
                act(cl[:, 2:4], cl[:, 0:2], AF.Exp, [cl], [cl])
                act(cl[:, 4:5], cl[:, 0:1], AF.Exp, [cl], [cl], bias=cl[:, 1:2], scale=-1.0)
                vtt(cl[:, 5:6], cl[:, 2:3], b_col, ALU.mult, [cl, btok], [cl])
                yield
                pK, qK = nextq()
                mm(pK, [qK], kT[:, tsl], kT[:, tsl], True, True, [kT])
                vstt(Af[i][:], pK, b_col, decS[i][:], ALU.mult, ALU.mult, [qK, btok, decS[i]], [Af[i]])
                vtt(Ad[i][:], Af[i][:], bd, ALU.mult, [Af[i], consts], [Ad[i]], e=PL)
                yield
                pQ, qQ = nextq()
                mm(pQ, [qQ], kT[:, tsl], qT[:, tsl], True, True, [kT, qT])
                vtt(decI[i][:], decI[i][:], ident, ALU.add, [decI[i], consts], [decI[i]])
                vtt(attnT[hi][:], pQ, decI[i][:], ALU.mult, [qQ, decI[i]], [attnT[hi]])
                yield
                pB, qB = nextq()
                mm(pB, [qB], Ad[i][:], identi, True, True, [Ad[i], consts, identb])
                act(Bm[i][:], pB, AF.Copy, [qB], [Bm[i]])
                vtt(Pm[i][:], ident, Bm[i][:], ALU.subtract, [consts, Bm[i]], [Pm[i]])
                yield
                yprev, ytprev = Bm[i], Ad[i]
                for lev in range(3):
                    ycur, ytcur = Y[lev % 2][i], YT[lev % 2][i]
                    pyt, qyt = nextq()
                    mm(pyt, [qyt], yprev[:], ytprev[:], True, True, [yprev, ytprev])
                    act(ytcur[:], pyt, AF.Copy, [qyt], [ytcur])
                    if lev < 2:
                        py, qy = nextq()
                        mm(py, [qy], ytprev[:], yprev[:], True, True, [yprev, ytprev])
                        vcopy(ycur[:], py, [qy], [ycur])
                    yield
                    pp, qp = nextq()
                    mm(pp, [qp], ytcur[:], Pm[i][:], True, True, [ytcur, Pm[i]])
                    vtt(Pm[i][:], Pm[i][:], pp, ALU.add, [Pm[i], qp], [Pm[i]])
                    yield
                    yprev, ytprev = ycur, ytcur
                for mi, msk in enumerate(offmasks):
                    ppt, qpt = nextq()
                    mm(ppt, [qpt], Pm[i][:], identi, True, True, [Pm[i], consts, identb])
                    act(PTm[i][:], ppt, AF.Copy, [qpt], [PTm[i]])
                    vtt(Aoff[i][:], Af[i][:], msk, ALU.mult, [Af[i], consts], [Aoff[i]], e=PL)
                    pz, qz = nextq()
                    mm(pz, [qz], Aoff[i][:], Pm[i][:], True, True, [Aoff[i], Pm[i]])
                    vcopy(Zm[i][:], pz, [qz], [Zm[i]])
                    yield
                    pt2, qt2 = nextq()
                    mm(pt2, [qt2], PTm[i][:], Zm[i][:], True, True, [PTm[i], Zm[i]])
                    if mi < 2:
                        vtt(Pm[i][:], Pm[i][:], pt2, ALU.subtract, [Pm[i], qt2], [Pm[i]])
                    else:
                        vtt(TTb[i][:], Pm[i][:], pt2, ALU.subtract, [Pm[i], qt2], [TTb[i]])
                    yield
                vts(vb[i][:], vtok[:, n, :], b_col, None, ALU.mult, None, [vtok, btok], [vb[i]], e=PL)
                vts(kbg[i][:], ktok[:, n, :], cl[:, 5:6], None, ALU.mult, None, [ktok, cl], [kbg[i]], e=PL)
                vts(kdec[hi][:], ktok[:, n, :], cl[:, 4:5], None, ALU.mult, None, [ktok, cl], [kdec[hi]], e=PL)
                pu, qu = nextq()
                mm(pu, [qu], TTb[i][:], vb[i][:], True, True, [TTb[i], vb[i]])
                act(um[hi][:], pu, AF.Copy, [qu], [um[hi]])
                yield
                pw, qw = nextq()
                mm(pw, [qw], kbg[i][:], TTb[i][:], True, True, [TTb[i], kbg[i]])
                act(wTm[hi][:], pw, AF.Copy, [qw], [wTm[hi]])

            def step_gen(d, n, i):
                tsl = slice(n * 128, (n + 1) * 128)
                cl = cols[i]
                p1, q1 = nextq()
                mm(p1, [q1], wTm[i][:], Sb[d][:], True, True, [wTm[i], Sb[d]])
                vstt(vnew[i][:], p1, -1.0, um[i][:], ALU.mult, ALU.add, [q1, um[i]], [vnew[i]])
                p2, q2 = nextq()
                mm(p2, [q2], qT[:, tsl], Sb[d][:], True, True, [qT, Sb[d]])
                vts(tmpo[i][:], p2, cl[:, 2:3], None, ALU.mult, None, [q2, cl], [tmpo[i]])
                yield
                p4, q4 = nextq()
                mm(p4, [q4], kdec[i][:], vnew[i][:], True, True, [kdec[i], vnew[i]])
                vstt(S[d][:], S[d][:], cl[:, 3:4], p4, ALU.mult, ALU.add, [S[d], cl, q4], [S[d]])
                act(Sb[d][:], S[d][:], AF.Copy, [S[d]], [Sb[d]])
                yield
                p3, q3 = nextq()
                mm(p3, [q3], attnT[i][:], vnew[i][:], True, True, [attnT[i], vnew[i]])
                vtt(tmpo[i][:], tmpo[i][:], p3, ALU.add, [q3, tmpo[i]], [tmpo[i]])
                vtt(otok[:, n, :], otok[:, n, :], tmpo[i][:], ALU.add, [otok, tmpo[i]], [otok], e=PL)

            tn = lambda d, it: it if d == 0 else NTL - 1 - it
            order = [(d, it) for it in range(NTL) for d in range(2)]
            nxt = 0
            free_s = list(range(G))
            free_h = list(range(NH))
            hmap = {}
            prep_done = set()
            step_done = {0: -1, 1: -1}
            step_started = {0: -1, 1: -1}
            active = []
            while step_done[0] < NTL - 1 or step_done[1] < NTL - 1:
                while nxt < len(order) and free_s and free_h:
                    d, it = order[nxt]
                    nxt += 1
                    si = free_s.pop(0)
                    hi = free_h.pop(0)
                    hmap[(d, it)] = hi
                    active.append(("P", d, it, prep_gen(d, tn(d, it), si, hi), si))
                for d in range(2):
                    it = step_started[d] + 1
                    if it < NTL and (d, it) in prep_done and step_done[d] == it - 1:
                        active.append(("S", d, it, step_gen(d, tn(d, it), hmap[(d, it)]), None))
                        step_started[d] = it
                for task in list(active):
                    kind, d, it, gen, si = task
                    try:
                        next(gen)
                    except StopIteration:
                        active.remove(task)
                        if kind == "P":
                            prep_done.add((d, it))
                            free_s.append(si)
                        else:
                            step_done[d] = it
                            free_h.append(hmap[(d, it)])
            P.barrier()
        sl = wslot()
        wz = sl[:, 0:1024].rearrange("p (k n) -> p k n", k=8)
        wload(sl, wz, wA, wA.t[l, :, :, C_GZ + hh * 128:C_GZ + (hh + 1) * 128])
        def outA(tl):
            sz, on, ssq = szs[tl % 3], ons[tl % 3], ssqs[tl % 3]
            pt, pq = nextbank()
            for k in range(8):
                mm(pt[:, 0:128], pq, hT[:, k, tl * 128:(tl + 1) * 128], wz[:, k, :], k == 0, k == 7, [hT, sl])
            act(sz[:], pt[:, 0:128], AF.Silu, pq, [sz])
            P.op("dve", lambda: nc.vector.tensor_tensor(out=on[:], in0=otok[:, tl, :], in1=otok[:, tl, :], op=ALU.mult), [otok], [on])
            P.op("dve", lambda: nc.vector.tensor_reduce(out=ssq[:, 0:1], in_=on[:], axis=AX.X, op=ALU.add), [on], [ssq])
            act(ssq[:, 1:2], ssq[:, 0:1], AF.Ln, [ssq], [ssq], bias=epsc[:, 0:1], scale=1.0 / 128)
            act(ssq[:, 1:2], ssq[:, 1:2], AF.Exp, [ssq], [ssq], scale=-0.5)
            vstt(on[:], otok[:, tl, :], ssq[:, 1:2], vec(l, "gng", 0, 128), ALU.mult, ALU.mult, [otok, ssq, vecs], [on])
            vtt(on[:], on[:], sz[:], ALU.mult, [on, sz], [on])

        def outB(tl):
            on = ons[tl % 3]
            qa, qk = nextq()
            mm(qa, [qk], on[:], ident, True, True, [on, consts])
            act(ob[:, hh, tl * 128:(tl + 1) * 128], qa, AF.Copy, [qk], [ob])

        for tl in range(NTL + 2):
            if tl < NTL:
                outA(tl)
            if tl >= 2:
                outB(tl - 2)


def emit_merge(P, nc, env, sc, l, s):
    e = _h(env)
    hT, obT, consts = e.hT, e.obT, e.consts
    mm, act, vtt, vts, vstt, vcopy = e.mm, e.act, e.vtt, e.vts, e.vstt, e.vcopy
    nextbank, cst, vec, wslot, wload, hrhs = e.nextbank, e.cst, e.vec, e.wslot, e.wload, e.hrhs
    wA, wBR, wO, mod = e.wA, e.wBR, e.wO, e.mod
    mT = P.sbuf("mT", [128, 8, T], BF16, sc)
    Gs = [P.sbuf("Gs", [128, 512], F32, sc) for _ in range(2)]
    macc = P.sbuf("macc", [128, 512], F32, sc)
    mtmp = P.sbuf("mtmp", [128, 512], F32, sc)
    xb = P.sbuf("xb", [128, 8, 512], F32, sc)
    sq = P.sbuf("sq", [128, 8, 512], F32, sc)
    rstd = P.sbuf("rstd", [128, 512], F32, sc)
    tmp = P.sbuf("tmp", [128, 512], F32, sc)
    gi = 0
    for j in range(8):
        sl = wslot()
        wg = sl[:, 0:3072].rearrange("p (b k n) -> p b k n", b=3, k=8)
        for b in range(3):
            wload(sl, wg[:, b], wA, wA.t[l, :, :, C_GATE + b * 1024 + j * 128:C_GATE + b * 1024 + (j + 1) * 128])
        sl2 = wslot()
        wb = sl2[:, 0:1536].rearrange("p (k n) -> p k n", k=12)
        wload(sl2, wb, wBR, wBR.t[l, :, :, j * 128:(j + 1) * 128])
        for tb in range(NTB):
            blk = slice(tb * 512, (tb + 1) * 512)
            for b in range(3):
                pg, qg = nextbank()
                for k in range(8):
                    mm(pg[:, :], qg, wg[:, b, k, :], hrhs(k, tb), k == 0, k == 7, [sl, hT])
                Gt = Gs[gi % 2]
                gi += 1
                act(Gt[:], pg[:, :], AF.Sigmoid, qg, [Gt])
                py, qy = nextbank()
                for k in range(4):
                    mm(py[:, :], qy, wb[:, b * 4 + k, :], obT[b][:, k, blk], k == 0, k == 3, [sl2, obT[b]])
                if b == 0:
                    vtt(macc[:], py[:, :], Gt[:], ALU.mult, qy + [Gt], [macc])
                elif b == 1:
                    vtt(mtmp[:], py[:, :], Gt[:], ALU.mult, qy + [Gt], [mtmp])
                    vtt(macc[:], macc[:], mtmp[:], ALU.add, [macc, mtmp], [macc])
                else:
                    vtt(mtmp[:], py[:, :], Gt[:], ALU.mult, qy + [Gt], [mtmp])
                    vtt(mT[:, j, blk], macc[:], mtmp[:], ALU.add, [macc, mtmp], [mT])
    for tb in range(NTB):
        blk = slice(tb * 512, (tb + 1) * 512)
        e.load_xblock(xb, tb)
        for half in range(2):
            sl = wslot()
            wv = sl[:, 0:4096].rearrange("p (k n) -> p k n", k=8)
            wload(sl, wv, wO, wO.t[l, :, :, half * 512:(half + 1) * 512])
            for ii in range(4):
                i = half * 4 + ii
                pt, pq = nextbank()
                for k in range(8):
                    mm(pt[:, :], pq, wv[:, k, ii * 128:(ii + 1) * 128], mT[:, k, blk], k == 0, k == 7, [sl, mT])
                vstt(xb[:, i, :], pt[:, :], mod[:, l, 16 + i, s:s + 1], xb[:, i, :], ALU.mult, ALU.add, pq + [mod, xb], [xb])
        e.store_xblock(xb, tb)
        e.norm_block(xb, tb, l, s, 2, sq, rstd, tmp)


def emit_ffn(P, nc, env, sc, l, s):
    e = _h(env)
    hT, consts = e.hT, e.consts
    mm, act, vtt, vts, vstt, vcopy, vmemset = e.mm, e.act, e.vtt, e.vts, e.vstt, e.vcopy, e.vmemset
    nextbank, cst, vec, wslot, wload, hrhs = e.nextbank, e.cst, e.vec, e.wslot, e.wload, e.hrhs
    wUP, wDN, mod, vecs, xT_d, xT_tk = e.wUP, e.wDN, e.mod, e.vecs, e.xT_d, e.xT_tk
    actT = P.sbuf("actT", [128, NJ, T], BF16, sc)
    upad = [P.sbuf("upad", [128, T + 2], F32, sc) for _ in range(2)]
    cv = [P.sbuf("cv", [128, T], F32, sc) for _ in range(2)]
    xs = [P.sbuf("xs", [128, 512], F32, sc) for _ in range(2)]
    for u in upad:
        vmemset(u[:], 0.0, [u])
    for jp in range(NJ // 2):
        sl = wslot()
        wv = sl[:, 0:4096].rearrange("p (k n) -> p k n", k=8)
        for jj in range(2):
            j = jp * 2 + jj
            wload(sl, wv[:, :, jj * 128:(jj + 1) * 128], wUP, wUP.t[l, :, :, j * 128:(j + 1) * 128])
            wload(sl, wv[:, :, 256 + jj * 128:256 + (jj + 1) * 128], wUP, wUP.t[l, :, :, DFF + j * 128:DFF + (j + 1) * 128])
        for jj in range(2):
            j = jp * 2 + jj
            for lg in range(2):
                ch = lg * NJ + j
                wc = lg * 256 + jj * 128
                for tb in range(NTB):
                    pt, pq = nextbank()
                    for k in range(8):
                        mm(pt[:, :], pq, wv[:, k, wc:wc + 128], hrhs(k, tb), k == 0, k == 7, [sl, hT])
                    act(upad[lg][:, 1 + tb * 512:1 + (tb + 1) * 512], pt[:, :], AF.Copy, pq, [upad[lg]])
                act(cv[lg][:], upad[lg][:, 0:T], AF.Identity, [upad[lg], vecs], [cv[lg]],
                    bias=vec(l, "fcb", ch), scale=vec(l, "fcw", ch * 3 + 0))
                for k in range(1, 3):
                    vstt(cv[lg][:], upad[lg][:, k:k + T], vec(l, "fcw", ch * 3 + k), cv[lg][:], ALU.mult, ALU.add,
                         [upad[lg], cv[lg], vecs], [cv[lg]])
            act(cv[1][:], cv[1][:], AF.Silu, [cv[1]], [cv[1]])
            vtt(actT[:, j, :], cv[1][:], cv[0][:], ALU.mult, [cv[0], cv[1]], [actT])
    xi = 0
    for i in range(8):
        sl = wslot()
        wv = sl[:, 0:NJ * 128].rearrange("p (k n) -> p k n", k=NJ)
        wload(sl, wv, wDN, wDN.t[l, :, :, i * 128:(i + 1) * 128])
        for tb in range(NTB):
            blk = slice(tb * 512, (tb + 1) * 512)
            pt, pq = nextbank()
            for k in range(NJ):
                mm(pt[:, :], pq, wv[:, k, :], actT[:, k, blk], k == 0, k == NJ - 1, [sl, actT])
            x1 = xs[xi % 2]
            xi += 1
            P.dma([x1], x1[:], [xT_tk[i][tb]], xT_d[:, i, blk])
            vstt(x1[:], pt[:, :], mod[:, l, 40 + i, s:s + 1], x1[:], ALU.mult, ALU.add, pq + [mod, x1], [x1])
            P.dma([xT_tk[i][tb]], xT_d[:, i, blk], [x1], x1[:], sem_t=x1)


def _fm(w, kc):
    K, N = w.shape
    return np.ascontiguousarray(w.reshape(kc, 128, N).transpose(1, 0, 2))


def _colv(v, nch):
    return np.ascontiguousarray(np.asarray(v).reshape(nch, 128).T)


def _consts():
    c = np.zeros((128, NC), np.float32)
    i = np.arange(128)
    ident = np.eye(128, dtype=np.float32)
    c[:, COFF["ident"]:COFF["ident"] + 128] = ident
    c[:, COFF["ones"]:COFF["ones"] + 128] = 1.0
    c[:, COFF["negones"]:COFF["negones"] + 128] = -1.0
    m, ii = np.meshgrid(i, i, indexing="ij")
    c[:, COFF["mt0"]:COFF["mt0"] + 128] = (m <= ii)
    c[:, COFF["mt1"]:COFF["mt1"] + 128] = (m >= ii)
    a, b = np.meshgrid(i, i, indexing="ij")
    v0 = (b < a)
    v1 = (b > a)
    c[:, COFF["negs0"]:COFF["negs0"] + 128] = np.where(v0, 0.0, -BIG)
    c[:, COFF["negs1"]:COFF["negs1"] + 128] = np.where(v1, 0.0, -BIG)
    c[:, COFF["negst0"]:COFF["negst0"] + 128] = np.where(v0.T, 0.0, -BIG)
    c[:, COFF["negst1"]:COFF["negst1"] + 128] = np.where(v1.T, 0.0, -BIG)
    bdm = lambda n: ((a // n) == (b // n)).astype(np.float32)
    c[:, COFF["bd16"]:COFF["bd16"] + 128] = bdm(16)
    c[:, COFF["m32"]:COFF["m32"] + 128] = bdm(32) - bdm(16)
    c[:, COFF["m64"]:COFF["m64"] + 128] = bdm(64) - bdm(32)
    c[:, COFF["m128"]:COFF["m128"] + 128] = 1.0 - bdm(64)
    return c


def _rope_tables():
    inv = 1.0 / (10000.0 ** (np.arange(0, 32, 2, dtype=np.float32) / 32.0))
    ang = np.arange(T, dtype=np.float32)[None, :] * inv[:, None].astype(np.float32)
    cos, sin = np.cos(ang).astype(np.float32), np.sin(ang).astype(np.float32)
    r = np.zeros((2, 96, T), np.float32)
    r[0, 0:64] = 1.0
    r[0, 64:80] = cos
    r[0, 80:96] = cos
    r[1, 64:80] = -sin
    r[1, 80:96] = sin
    return r


def prep_weights(inp):
    L = L_ALL
    f = lambda k: np.asarray(inp[k], np.float32)
    w_in = f("w_in")
    wA = np.zeros((L, 128, 8, NA), np.float32)
    sw = np.concatenate([np.arange(16, 32), np.arange(0, 16)])
    wUQ = np.zeros((L, 128, 3, 1536), np.float32)
    wKN = np.zeros((L, 128, 2, 768), np.float32)
    wV = np.zeros((L, 128, 2, 512), np.float32)
    wRG = np.zeros((L, 128, 16, 128), np.float32)
    vecs = np.zeros((128, L, NV), np.float32)
    for l in range(L):
        ext = np.zeros((D, 192), np.float32)
        kr = w_in[l][:, 640:672]
        ext[:, 64:96] = kr
        ext[:, 96 + 64:96 + 96] = kr[:, sw]
        wA[l] = _fm(np.concatenate([w_in[l], ext], axis=1), 8)
        uq = f("mla_w_uq")[l]
        uqs = uq.reshape(384, 8, 96).copy()
        uqs[:, :, 64:96] = uqs[:, :, 64:96][:, :, sw]
        wUQ[l] = _fm(np.concatenate([uq, uqs.reshape(384, 768)], axis=1), 3)
        ukv = f("mla_w_ukv")[l].reshape(256, 8, 128)
        kn = np.zeros((256, 8, 96), np.float32)
        kn[:, :, 0:64] = ukv[:, :, 0:64]
        wKN[l] = _fm(kn.reshape(256, 768), 2)
        wV[l] = _fm(np.ascontiguousarray(ukv[:, :, 64:128]).reshape(256, 512), 2)
        for d in range(2):
            for ai, nm in enumerate(("rg_w_a", "rg_w_i")):
                w = f(nm)[l, d]
                for c in range(4):
                    m = np.zeros((128, 128), np.float32)
                    m[0:64, 0:64] = w[2 * c]
                    m[64:128, 64:128] = w[2 * c + 1]
                    wRG[l, :, (d * 2 + ai) * 4 + c, :] = m

        def put(name, arr):
            o, w = VOFF[name]
            vecs[:, l, o:o + w] = arr
        put("ln1", _colv(f("ln1_g")[l], 8))
        put("ln2", _colv(f("ln2_g")[l], 8))
        put("bmod", _colv(f("b_mod")[l], 48))
        put("qg", _colv(f("mla_q_norm_g")[l], 3))
        put("kvg", _colv(f("mla_kv_norm_g")[l], 2))
        put("rgcw", np.ascontiguousarray(f("rg_conv_w")[l].reshape(4, 4, 128).transpose(2, 1, 0)).reshape(128, 16))
        put("rgcb", _colv(f("rg_conv_b")[l], 4))
        put("rgba", np.ascontiguousarray(f("rg_b_a")[l].reshape(2, 4, 128).transpose(2, 0, 1)).reshape(128, 8))
        put("rgbi", np.ascontiguousarray(f("rg_b_i")[l].reshape(2, 4, 128).transpose(2, 0, 1)).reshape(128, 8))
        put("rglam", np.ascontiguousarray(f("rg_lam")[l].reshape(2, 4, 128).transpose(2, 0, 1)).reshape(128, 8))
        put("gcw", np.ascontiguousarray(f("gdn_conv_w")[l].reshape(4, 12, 128).transpose(2, 1, 0)).reshape(128, 48))
        put("gng", np.broadcast_to(f("gdn_norm_g")[l][None, :], (128, 128)))
        put("galog", np.broadcast_to(np.tile(f("gdn_a_log")[l].reshape(8), 16)[None, :], (128, 128)))
        put("gdtb", np.broadcast_to(np.tile(f("gdn_dt_bias")[l].reshape(8), 16)[None, :], (128, 128)))
        put("fcw", np.ascontiguousarray(f("ffn_conv_w")[l].reshape(3, 44, 128).transpose(2, 1, 0)).reshape(128, 132))
        put("fcb", _colv(f("ffn_conv_b")[l], 44))
        put("fng", _colv(f("final_norm_g"), 8))
    wd = {
        "wA": wA, "wUQ": wUQ, "wKN": wKN, "wV": wV, "wRG": wRG,
        "wBR": np.stack([_fm(f("w_branch")[l].reshape(1536, 1024), 12) for l in range(L)]),
        "wO": np.stack([_fm(f("w_out")[l], 8) for l in range(L)]),
        "wUP": np.stack([_fm(f("ffn_w_up")[l], 8) for l in range(L)]),
        "wDN": np.stack([_fm(f("ffn_w_down")[l], NJ) for l in range(L)]),
        "wMOD": np.stack([_fm(f("w_mod")[l], 8) for l in range(L)]),
        "vecs": vecs, "consts": _consts(), "ropet": _rope_tables(),
    }
    return wd


def core_inputs(wd, xs, cs):
    m = dict(wd)
    m["xin"] = np.ascontiguousarray(xs, dtype=np.float32)
    nseq = xs.shape[0]
    m["cT"] = np.ascontiguousarray(np.asarray(cs, np.float32).reshape(nseq, 8, 128).transpose(2, 1, 0))
    return m


_CACHE = {}


def kernel(**inputs):
    wd = prep_weights(inputs)
    xp = np.asarray(inputs["x_prompt"], np.float32)
    xsm = np.asarray(inputs["x_sample"], np.float32)
    cp = np.asarray(inputs["c_prompt"], np.float32)
    csm = np.asarray(inputs["c_sample"], np.float32)
    xall = np.concatenate([xp, xsm], axis=0)
    call = np.concatenate([cp, csm], axis=0)
    nseq = xall.shape[0] // NCORES
    if "nc" not in _CACHE:
        _CACHE["nc"] = build(nseq)[0]
    nc = _CACHE["nc"]
    in_maps = [core_inputs(wd, xall[i * nseq:(i + 1) * nseq], call[i * nseq:(i + 1) * nseq]) for i in range(NCORES)]
    res = run_bass_kernel_spmd(nc, in_maps, core_ids=list(range(NCORES)))
    y = np.concatenate([np.asarray(r["yout"], np.float32) for r in res.results], axis=0)
    nb = xp.shape[0]
    return (np.ascontiguousarray(y[:nb]), np.ascontiguousarray(y[nb:]))
```

```python
import contextlib
import math
import numpy as np
import concourse.bass as bass
import concourse.mybir as mybir
from concourse.bass_utils import run_bass_kernel_spmd

F32 = mybir.dt.float32
BF16 = mybir.dt.bfloat16
AF = mybir.ActivationFunctionType
ALU = mybir.AluOpType
AX = mybir.AxisListType

D = 1024
T = 2048
NTB = 4
NTL = 16
L_ALL = 2
NCORES = 8
EPS = 1e-6
N_IN = 6832
KR0 = N_IN
NA = N_IN + 192
C_RGX, C_RGG, C_GQKV, C_GZ, C_AB, C_GATE = 672, 1184, 1696, 3232, 3744, 3760
DFF = 2816
NJ = 22
BIG = 30000.0
SEM_LIMIT = 30000

VOFF = {}
_o = 0
for _n, _w in [("ln1", 8), ("ln2", 8), ("bmod", 48), ("qg", 3), ("kvg", 2), ("rgcw", 16), ("rgcb", 4),
               ("rgba", 8), ("rgbi", 8), ("rglam", 8), ("gcw", 48), ("gng", 128), ("galog", 128), ("gdtb", 128),
               ("fcw", 132), ("fcb", 44), ("fng", 8)]:
    VOFF[_n] = (_o, _w)
    _o += _w
NV = _o
COFF = {}
_o = 0
for _n in ["ident", "ident_b", "ones", "ones_b", "negones", "negones_b", "mt0", "mt1", "negs0", "negs1", "negst0", "negst1",
           "bd16", "bd16_b", "m32", "m32_b", "m64", "m64_b", "m128", "m128_b"]:
    COFF[_n] = _o
    _o += 128
NC = _o


class Tk:
    __slots__ = ("name", "w", "r", "t", "dk")

    def __init__(self, name, t=None):
        self.name = name
        self.w = None
        self.r = {}
        self.t = t
        self.dk = name

    def __getitem__(self, idx):
        return self.t[idx]


class Prog:
    def __init__(self, nc):
        self.nc = nc
        self.es = contextlib.ExitStack()
        self.eng = {"pe": nc.tensor, "act": nc.scalar, "dve": nc.vector, "pool": nc.gpsimd, "sp": nc.sync}
        self.sems = {}
        self.cnt = {}
        self.cur = {}
        self.epoch = {e: 0 for e in self.eng}
        self.seen = {e: {} for e in self.eng}
        self.ninst = 0
        self.uid = 0
        self.namectr = {}
        for e in self.eng:
            self._new_epoch(e)

    def _mk_sem(self, key):
        s = self.es.enter_context(self.nc.semaphore("s_%s" % (key,)))
        self.sems[key] = s
        self.cnt[key] = 0
        return s

    def _new_epoch(self, e):
        key = "%s%d" % (e, self.epoch[e])
        self.epoch[e] += 1
        self._mk_sem(key)
        self.cur[e] = key

    def sbuf(self, name, shape, dt, es=None):
        self.uid += 1
        t = (es or self.es).enter_context(self.nc.sbuf_tensor("%s_%d" % (name, self.uid), list(shape), dt))
        tk = Tk(name + str(self.uid), t)
        c = self.namectr.get(name, 0)
        self.namectr[name] = c + 1
        tk.dk = "%s_%d" % (name, c % 4)
        return tk

    def dram(self, name, shape, dt, kind="Internal"):
        t = self.nc.dram_tensor(name, list(shape), dt, kind=kind).ap()
        return Tk(name, t)

    def _deps(self, reads, writes):
        deps = {}
        for t in reads:
            if t.w is not None:
                k, v = t.w
                if deps.get(k, 0) < v:
                    deps[k] = v
        for t in writes:
            if t.w is not None:
                k, v = t.w
                if deps.get(k, 0) < v:
                    deps[k] = v
            for k, v in t.r.items():
                if deps.get(k, 0) < v:
                    deps[k] = v
        return deps

    def _wait(self, e, deps):
        eng = self.eng[e]
        seen = self.seen[e]
        for k, v in deps.items():
            isdma = k.startswith("dma")
            if (not isdma) and k.startswith(e) and e in ("pe", "sp"):
                continue
            if isdma:
                v = self.cnt[k]
            if seen.get(k, 0) < v:
                eng.wait_ge(self.sems[k], v)
                seen[k] = v
                self.ninst += 1

    def op(self, e, fn, reads=(), writes=()):
        self._wait(e, self._deps(reads, writes))
        if self.cnt[self.cur[e]] >= SEM_LIMIT:
            self._new_epoch(e)
        key = self.cur[e]
        inst = fn()
        self.cnt[key] += 1
        v = self.cnt[key]
        inst.then_inc(self.sems[key], 1)
        self.ninst += 1
        for t in reads:
            t.r[key] = v
        for t in writes:
            t.w = (key, v)
            t.r = {}
        return inst

    def dma(self, out_ts, out_ap, in_ts, in_ap, q="sp", sem_t=None):
        st = sem_t if sem_t is not None else out_ts[0]
        key = "dma_" + st.dk
        if key not in self.sems:
            self._mk_sem(key)
        self._wait(q, self._deps(in_ts, out_ts))
        inst = self.eng[q].dma_start(out=out_ap, in_=in_ap)
        self.cnt[key] += 16
        v = self.cnt[key]
        inst.then_inc(self.sems[key], 16)
        self.ninst += 1
        for t in in_ts:
            t.r[key] = v
        for t in out_ts:
            t.w = (key, v)
            t.r = {}
        return inst

    def barrier(self):
        allk = {k: v for k, v in self.cnt.items() if v > 0}
        for e in self.eng:
            self._wait(e, allk)

    def finish(self, outs, q="sp"):
        deps = {}
        for t in outs:
            if t.w is not None:
                k, v = t.w
                deps[k] = max(deps.get(k, 0), v)
        self._wait(q, deps)


OUTSTAGE = 3
GDNSTAGE = 9
GDN_G = 2
GDN_POOL = False
GDN_BF16INV = True
GSUB = 9
RGDBG = 0
PHASES = set(['N1', 'MLA', 'RG', 'GDN', 'MERGE', 'FFN'])


def build(nseq, nlayer=L_ALL, dbg=False):
    nc = bass.Bass("TRN2", target_bir_lowering=False)
    P = Prog(nc)
    L = L_ALL
    xin = P.dram("xin", [nseq, T, D], F32, "ExternalInput")
    cT = P.dram("cT", [128, 8, nseq], F32, "ExternalInput")
    wA = P.dram("wA", [L, 128, 8, NA], F32, "ExternalInput")
    wUQ = P.dram("wUQ", [L, 128, 3, 1536], F32, "ExternalInput")
    wKN = P.dram("wKN", [L, 128, 2, 768], F32, "ExternalInput")
    wV = P.dram("wV", [L, 128, 2, 512], F32, "ExternalInput")
    wRG = P.dram("wRG", [L, 128, 16, 128], F32, "ExternalInput")
    wBR = P.dram("wBR", [L, 128, 12, 1024], F32, "ExternalInput")
    wO = P.dram("wO", [L, 128, 8, 1024], F32, "ExternalInput")
    wUP = P.dram("wUP", [L, 128, 8, 2 * DFF], F32, "ExternalInput")
    wDN = P.dram("wDN", [L, 128, NJ, 1024], F32, "ExternalInput")
    wMOD = P.dram("wMOD", [L, 128, 8, 6144], F32, "ExternalInput")
    vecs_d = P.dram("vecs", [128, L, NV], F32, "ExternalInput")
    consts_d = P.dram("consts", [128, NC], F32, "ExternalInput")
    ropet = P.dram("ropet", [2, 96, T], F32, "ExternalInput")
    yout = P.dram("yout", [nseq, T, D], F32, "ExternalOutput")
    xT_d = nc.dram_tensor("xT_scr", [128, 8, T], F32, kind="Internal").ap()
    xT_tk = [[Tk("xT_%d_%d" % (c, tb)) for tb in range(NTB)] for c in range(8)]
    dbg_d = None
    if dbg:
        dbg_d = P.dram("dbg", [3, 128, 4, T], F32, "ExternalOutput")

    V = nc.vector
    A = nc.scalar
    PE = nc.tensor
    G = nc.gpsimd

    with P.es:
        base = P.es
        consts = P.sbuf("consts", [128, NC], F32)
        vecs = P.sbuf("vecs", [128, L, NV], F32)
        identb = P.sbuf("identb", [128, 256], BF16)
        hT = P.sbuf("hT", [128, 8, T], BF16)
        mod = P.sbuf("mod", [128, L, 48, nseq], F32)
        gs1 = P.sbuf("gs1", [128, L, 8, nseq], F32)
        gs2 = P.sbuf("gs2", [128, L, 8, nseq], F32)
        cneg = P.sbuf("cneg", [128, L, 8], F32)
        cneg2 = P.sbuf("cneg2", [128, L, 8], F32)
        negA = P.sbuf("negA", [128, L, 128], F32)
        NSLOT = 3
        wslots = [P.sbuf("wslot%d" % i, [128, 4096], BF16) for i in range(NSLOT)]
        wctr = [0]
        pbank = []
        for i in range(8):
            t = P.es.enter_context(nc.psum_tensor("psb%d" % i, [128, 512], F32))
            pbank.append((t, [Tk("psq%d_%d" % (i, q)) for q in range(4)]))
        pctr = [0]
        qctr = [0]

        def nextbank(pool=None):
            if pool is None:
                b = pbank[pctr[0] % 8]
            else:
                b = pbank[pool[pctr[0] % len(pool)]]
            pctr[0] += 1
            return b

        def nextq():
            t, qs = pbank[pctr[0] % 8]
            pctr[0] += 1
            return t[:, 0:128], qs[0]

        def nextq2():
            t, qs = pbank[pctr[0] % 8]
            pctr[0] += 1
            return t[:, 0:256], qs[0]

        def cst(name, w=128):
            o = COFF[name]
            return consts[:, o:o + w]

        def vec(l, name, j=0, w=1):
            o, _ = VOFF[name]
            return vecs[:, l, o + j:o + j + w]

        def wslot():
            s = wslots[wctr[0] % NSLOT]
            wctr[0] += 1
            return s

        def wload(slot, dst_ap, src_tk, src_ap):
            P.dma([slot], dst_ap, [src_tk], src_ap, q="pool", sem_t=slot)

        def mm(ps_ap, ps_tks, lhsT, rhs, start, stop, reads):
            P.op("pe", lambda: PE.matmul(ps_ap, lhsT=lhsT, rhs=rhs, start=start, stop=stop), reads, ps_tks)

        def act(out_ap, in_ap, func, reads, writes, bias=None, scale=None):
            kw = {}
            if bias is not None:
                kw["bias"] = bias
            if scale is not None:
                kw["scale"] = scale
            P.op("act", lambda: A.activation(out=out_ap, in_=in_ap, func=func, **kw), reads, writes)

        def vtt(out_ap, in0, in1, op, reads, writes, e="dve"):
            eng = V if e == "dve" else G
            P.op(e, lambda: eng.tensor_tensor(out=out_ap, in0=in0, in1=in1, op=op), reads, writes)

        def vts(out_ap, in0, s1, s2, op0, op1, reads, writes, e="dve"):
            eng = V if e == "dve" else G
            if op1 is None:
                P.op(e, lambda: eng.tensor_scalar(out=out_ap, in0=in0, scalar1=s1, scalar2=None, op0=op0), reads, writes)
            else:
                P.op(e, lambda: eng.tensor_scalar(out=out_ap, in0=in0, scalar1=s1, scalar2=s2, op0=op0, op1=op1), reads, writes)

        def vstt(out_ap, in0, scalar, in1, op0, op1, reads, writes, e="dve"):
            eng = V if e == "dve" else G
            P.op(e, lambda: eng.scalar_tensor_tensor(out=out_ap, in0=in0, scalar=scalar, in1=in1, op0=op0, op1=op1), reads, writes)

        def vcopy(out_ap, in_ap, reads, writes, e="dve"):
            eng = V if e == "dve" else G
            P.op(e, lambda: eng.tensor_copy(out=out_ap, in_=in_ap), reads, writes)

        def vmemset(ap, val, writes, e="dve"):
            eng = V if e == "dve" else G
            P.op(e, lambda: eng.memset(ap, val), [], writes)

        def rsqrt_from(out_ap, in_ap, scale, reads, writes):
            act(out_ap, in_ap, AF.Ln, reads, writes, bias=epsc[:, 0:1], scale=scale)
            act(out_ap, out_ap, AF.Exp, writes, writes, scale=-0.5)

        def softplus_acc(out_ap, x_ap, t1, t2, t3, rd, wr):
            vts(t1, x_ap, -1.0, None, ALU.mult, None, rd, wr)
            vtt(t1, t1, x_ap, ALU.max, rd + wr, wr)
            act(t1, t1, AF.Exp, wr, wr, scale=-1.0)
            vts(t2, t1, 2.0, None, ALU.add, None, wr, wr)
            P.op("dve", lambda: V.reciprocal(out=t2, in_=t2), wr, wr)
            vtt(t1, t1, t2, ALU.mult, wr, wr)
            vtt(t2, t1, t1, ALU.mult, wr, wr)
            vts(t3, t2, 1.0 / 11, 1.0 / 9, ALU.mult, ALU.add, wr, wr)
            for cf in (1.0 / 7, 1.0 / 5, 1.0 / 3, 1.0):
                vtt(t3, t3, t2, ALU.mult, wr, wr)
                vts(t3, t3, cf, None, ALU.add, None, wr, wr)
            vtt(t3, t3, t1, ALU.mult, wr, wr)
            vts(t1, x_ap, 0.0, None, ALU.max, None, rd + wr, wr)
            vstt(out_ap, t3, 2.0, t1, ALU.mult, ALU.add, wr, wr)

        epsc = P.sbuf("epsc", [128, 1], F32)
        vmemset(epsc[:], EPS, [epsc])
        P.dma([consts], consts[:], [consts_d], consts_d.t)
        P.dma([vecs], vecs[:], [vecs_d], vecs_d.t)
        vcopy(identb[:], cst("ident", 256), [consts], [identb])
        with contextlib.ExitStack() as sc:
            craw = P.sbuf("craw", [128, 8, nseq], F32, sc)
            csil = P.sbuf("csil", [128, 8, nseq], BF16, sc)
            tmpm = P.sbuf("tmpm", [128, 8, nseq], F32, sc)
            spx = P.sbuf("spx", [128, 8], F32, sc)
            spt = P.sbuf("spt", [128, 3, 8], F32, sc)
            P.dma([craw], craw[:], [cT], cT.t)
            act(csil[:], craw[:], AF.Silu, [craw], [csil])
            for l in range(nlayer):
                for pn in range(12):
                    sl = wslot()
                    wv = sl[:, 0:4096].rearrange("p (k n) -> p k n", k=8)
                    wload(sl, wv, wMOD, wMOD.t[l, :, :, pn * 512:(pn + 1) * 512])
                    for jj in range(4):
                        j = pn * 4 + jj
                        pt, pq = nextbank()
                        for k in range(8):
                            mm(pt[:, 0:nseq], pq, wv[:, k, jj * 128:(jj + 1) * 128], csil[:, k, :], k == 0, k == 7, [sl, csil])
                        act(mod[:, l, j, :], pt[:, 0:nseq], AF.Identity, pq, [mod], bias=vec(l, "bmod", j))
                for (gs, lnn, j0) in ((gs1, "ln1", 8), (gs2, "ln2", 32)):
                    vts(tmpm[:], mod[:, l, j0:j0 + 8, :], 1.0, None, ALU.add, None, [mod], [tmpm])
                    for s in range(nseq):
                        vtt(gs[:, l, :, s], tmpm[:, :, s], vec(l, lnn, 0, 8), ALU.mult, [tmpm, vecs], [gs])
                vts(spx[:], vec(l, "rglam", 0, 8), -1.0, None, ALU.mult, None, [vecs], [spx])
                softplus_acc(cneg[:, l, :], spx[:], spt[:, 0, :], spt[:, 1, :], spt[:, 2, :], [spx], [spt, cneg])
                vts(cneg2[:, l, :], cneg[:, l, :], -16.0, None, ALU.mult, None, [cneg], [cneg2])
                vts(cneg[:, l, :], cneg[:, l, :], -8.0, None, ALU.mult, None, [cneg], [cneg])
                act(negA[:, l, :], vec(l, "galog", 0, 128), AF.Exp, [vecs], [negA])
                vts(negA[:, l, :], negA[:, l, :], -1.0, None, ALU.mult, None, [negA], [negA])
            P.barrier()

        def proj_fm(wv, kc, M, rhs_fn, rhs_tks, slot, consume):
            for tb in range(NTB):
                pt, pq = nextbank()
                for k in range(kc):
                    mm(pt[0:M, :], pq, wv[:, k, 0:M], rhs_fn(k, tb), k == 0, k == kc - 1, [slot] + rhs_tks)
                consume(tb, pt, pq)

        def hrhs(k, tb):
            return hT[:, k, tb * 512:(tb + 1) * 512]

        def load_xblock(xb, tb):
            for c in range(8):
                pass
            P.dma([xb], xb[:], [xT_tk[c][tb] for c in range(8)], xT_d[:, :, tb * 512:(tb + 1) * 512])

        def store_xblock(xb, tb):
            P.dma([xT_tk[c][tb] for c in range(8)], xT_d[:, :, tb * 512:(tb + 1) * 512], [xb], xb[:], sem_t=xb)

        def norm_block(xb, tb, l, s, which, sq, rstd, tmp):
            gs = gs1 if which == 1 else gs2
            shj = 0 if which == 1 else 24
            vtt(sq[:], xb[:], xb[:], ALU.mult, [xb], [sq])
            pt, pq = nextbank()
            for c in range(8):
                mm(pt[:, :], pq, cst("ones"), sq[:, c, :], c == 0, c == 7, [consts, sq])
            rsqrt_from(rstd[:], pt[:, :], 1.0 / D, pq, [rstd])
            for c in range(8):
                vtt(tmp[:], xb[:, c, :], rstd[:], ALU.mult, [xb, rstd], [tmp])
                act(hT[:, c, tb * 512:(tb + 1) * 512], tmp[:], AF.Identity, [tmp, gs, mod], [hT],
                    bias=mod[:, l, shj + c, s:s + 1], scale=gs[:, l, c, s:s + 1])

        for s in range(nseq if 'NOSEQ' not in PHASES else 0):
            with contextlib.ExitStack() as sc:
                xtok = [P.sbuf("xtok", [128, D], F32, sc) for _ in range(2)]
                xblk = [P.sbuf("xblk", [128, 8, 128], F32, sc) for _ in range(2)]
                for tl in range(NTL if 'NOL' not in PHASES else 0):
                    xt = xtok[tl % 2]
                    xk = xblk[tl % 2]
                    P.dma([xt], xt[:], [xin], xin.t[s, tl * 128:(tl + 1) * 128, :])
                    for half in range(2):
                        pt, pq = nextbank()
                        for q in range(4):
                            c = half * 4 + q
                            P.op("pe", lambda: PE.transpose(out=pt[:, q * 128:(q + 1) * 128], in_=xt[:, c * 128:(c + 1) * 128],
                                                            identity=cst("ident")), [xt, consts], pq)
                        act(xk[:, half * 4:half * 4 + 4, :], pt[:, :].rearrange("p (q n) -> p q n", q=4), AF.Copy, pq, [xk])
                    tb = tl // 4
                    P.dma([xT_tk[c][tb] for c in range(8)], xT_d[:, :, tl * 128:(tl + 1) * 128], [xk], xk[:], sem_t=xk)
                P.barrier()

            for l in range(nlayer):
                with contextlib.ExitStack() as sc:
                    xb = P.sbuf("xb", [128, 8, 512], F32, sc)
                    sq = P.sbuf("sq", [128, 8, 512], F32, sc)
                    rstd = P.sbuf("rstd", [128, 512], F32, sc)
                    tmp = P.sbuf("tmp", [128, 512], F32, sc)
                    for tb in range(NTB if 'N1' in PHASES else 0):
                        load_xblock(xb, tb)
                        norm_block(xb, tb, l, s, 1, sq, rstd, tmp)
                    P.barrier()

                mixsc = contextlib.ExitStack()
                obT = []
                obT.append(P.sbuf("obT_mla", [128, 4, T], BF16, mixsc))
                with contextlib.ExitStack() as sc:
                    if 'MLA' in PHASES:
                        emit_mla(P, nc, locals(), sc, l, s)
                    P.barrier()
                obT.append(P.sbuf("obT_rg", [128, 4, T], BF16, mixsc))
                with contextlib.ExitStack() as sc:
                    if 'RG' in PHASES:
                        emit_rg(P, nc, locals(), sc, l, s)
                    P.barrier()
                obT.append(P.sbuf("obT_gdn", [128, 4, T], BF16, mixsc))
                with contextlib.ExitStack() as sc:
                    if 'GDN' in PHASES:
                        emit_gdn(P, nc, locals(), sc, l, s)
                    P.barrier()
                if dbg and l == 0 and s == 0:
                    with contextlib.ExitStack() as sc:
                        dtmp = P.sbuf("dtmp", [128, 4, T], F32, sc)
                        for b in range(3):
                            vcopy(dtmp[:], obT[b][:], [obT[b]], [dtmp])
                            P.dma([dbg_d], dbg_d.t[b], [dtmp], dtmp[:], sem_t=dtmp)
                        P.barrier()
                with contextlib.ExitStack() as sc:
                    if 'MERGE' in PHASES:
                        emit_merge(P, nc, locals(), sc, l, s)
                    P.barrier()
                mixsc.close()
                with contextlib.ExitStack() as sc:
                    if 'FFN' in PHASES:
                        emit_ffn(P, nc, locals(), sc, l, s)
                    P.barrier()

            with contextlib.ExitStack() as sc:
                xb = P.sbuf("xb", [128, 8, 512], F32, sc)
                sq = P.sbuf("sq", [128, 8, 512], F32, sc)
                rstd = P.sbuf("rstd", [128, 512], F32, sc)
                ytok = [P.sbuf("ytok", [128, D], F32, sc) for _ in range(2)]
                yc = 0
                for tb in range(NTB if 'NOOUT' not in PHASES else 0):
                    load_xblock(xb, tb)
                    vtt(sq[:], xb[:], xb[:], ALU.mult, [xb], [sq])
                    if OUTSTAGE >= 1:
                        pt, pq = nextbank()
                        for c in range(8):
                            mm(pt[:, :], pq, cst("ones"), sq[:, c, :], c == 0, c == 7, [consts, sq])
                        if OUTSTAGE >= 2:
                            rsqrt_from(rstd[:], pt[:, :], 1.0 / D, pq, [rstd])
                        else:
                            act(rstd[:], pt[:, :], AF.Copy, pq, [rstd])
                        if OUTSTAGE >= 3:
                            for c in range(8):
                                vstt(sq[:, c, :], xb[:, c, :], vec(0, "fng", c), rstd[:], ALU.mult, ALU.mult, [xb, rstd, vecs], [sq])
                    for t4 in range(4):
                        yt = ytok[yc % 2]
                        yc += 1
                        for half in range(2):
                            pt, pq = nextbank()
                            for q in range(4):
                                c = half * 4 + q
                                P.op("pe", lambda: PE.transpose(out=pt[:, q * 128:(q + 1) * 128], in_=sq[:, c, t4 * 128:(t4 + 1) * 128],
                                                                identity=cst("ident")), [sq, consts], pq)
                            act(yt[:, half * 512:(half + 1) * 512], pt[:, :], AF.Copy, pq, [yt])
                        tl = tb * 4 + t4
                        P.dma([yout], yout.t[s, tl * 128:(tl + 1) * 128, :], [yt], yt[:], sem_t=yt)
                P.barrier()
        outs = [yout] + ([dbg_d] if dbg else [])
        P.finish(outs)
        P.barrier()
    return nc, P


def _h(env):
    class E:
        pass
    e = E()
    e.__dict__.update(env)
    return e


def emit_mla(P, nc, env, sc, l, s):
    e = _h(env)
    hT, obT, consts, identb = e.hT, e.obT, e.consts, e.identb
    mm, act, vtt, vts, vstt, vcopy, vmemset = e.mm, e.act, e.vtt, e.vts, e.vstt, e.vcopy, e.vmemset
    nextbank, cst, vec, wslot, wload, rsqrt_from, hrhs = e.nextbank, e.cst, e.vec, e.wslot, e.wload, e.rsqrt_from, e.hrhs
    wA, wUQ, wKN, wV, ropet, vecs = e.wA, e.wUQ, e.wKN, e.wV, e.ropet, e.vecs
    ob = obT[0]
    cosT = P.sbuf("cosT", [96, T], BF16, sc)
    sinT = P.sbuf("sinT", [96, T], BF16, sc)
    cq = P.sbuf("cq", [128, 3, T], BF16, sc)
    ckv = P.sbuf("ckv", [128, 2, T], BF16, sc)
    krT = P.sbuf("krT", [96, T], BF16, sc)
    Vall = P.sbuf("Vall", [128, NTL, 8, 128], BF16, sc)
    qhs = [P.sbuf("qh", [96, T], BF16, sc) for _ in range(2)]
    khs = [P.sbuf("kh", [96, T], BF16, sc) for _ in range(2)]
    PTs = [P.sbuf("PT", [128, 512], BF16, sc) for _ in range(4)]
    sqs = [P.sbuf("sqm", [128, 512], F32, sc) for _ in range(3)]
    rstd = P.sbuf("rstdm", [128, 512], F32, sc)
    t1 = P.sbuf("t1m", [128, 512], F32, sc)
    t2 = P.sbuf("t2m", [128, 512], F32, sc)
    dsh = P.sbuf("dsh", [128, 512], F32, sc)
    P.dma([cosT], cosT[:], [ropet], ropet.t[0], q="pool")
    P.dma([sinT], sinT[:], [ropet], ropet.t[1], q="pool")
    vmemset(Vall[:], 1.0, [Vall])

    for (dst, col0, nch, gname) in ((cq, 0, 3, "qg"), (ckv, 384, 2, "kvg")):
        sl = wslot()
        wv = sl[:, 0:8 * nch * 128].rearrange("p (k n) -> p k n", k=8)
        wload(sl, wv, wA, wA.t[l, :, :, col0:col0 + nch * 128])
        for tb in range(NTB):
            pcs = []
            for c in range(nch):
                pt, pq = nextbank()
                for k in range(8):
                    mm(pt[:, :], pq, wv[:, k, c * 128:(c + 1) * 128], hrhs(k, tb), k == 0, k == 7, [sl, hT])
                act(sqs[c][:], pt[:, :], AF.Square, pq, [sqs[c]])
                pcs.append((pt, pq))
            pt2, pq2 = nextbank()
            for c in range(nch):
                mm(pt2[:, :], pq2, cst("ones"), sqs[c][:], c == 0, c == nch - 1, [consts, sqs[c]])
            rsqrt_from(rstd[:], pt2[:, :], 1.0 / (nch * 128), pq2, [rstd])
            for c in range(nch):
                pt, pq = pcs[c]
                vtt(t1[:], pt[:, :], rstd[:], ALU.mult, pq + [rstd], [t1])
                act(dst[:, c, tb * 512:(tb + 1) * 512], t1[:], AF.Identity, [t1, vecs], [dst], scale=vec(l, gname, c))
    sl = wslot()
    wv = sl[:, 0:8 * 192].rearrange("p (k n) -> p k n", k=8)
    wload(sl, wv, wA, wA.t[l, :, :, KR0:KR0 + 192])
    for tb in range(NTB):
        blk = slice(tb * 512, (tb + 1) * 512)
        pa, qa = nextbank()
        pb, qb_ = nextbank()
        for k in range(8):
            mm(pa[0:96, :], qa, wv[:, k, 0:96], hrhs(k, tb), k == 0, k == 7, [sl, hT])
        for k in range(8):
            mm(pb[0:96, :], qb_, wv[:, k, 96:192], hrhs(k, tb), k == 0, k == 7, [sl, hT])
        vtt(t1[0:96, :], pa[0:96, :], cosT[:, blk], ALU.mult, qa + [cosT], [t1])
        vtt(t2[0:96, :], pb[0:96, :], sinT[:, blk], ALU.mult, qb_ + [sinT], [t2])
        vtt(krT[:, blk], t1[0:96, :], t2[0:96, :], ALU.add, [t1, t2], [krT])
    sl = wslot()
    wv = sl[:, 0:1024].rearrange("p (k n) -> p k n", k=2)
    wload(sl, wv, wV, wV.t[l])
    for tl in range(NTL):
        pt, pq = nextbank()
        for c in range(2):
            mm(pt[:, :], pq, ckv[:, c, tl * 128:(tl + 1) * 128], wv[:, c, :], c == 0, c == 1, [ckv, sl])
        pv = pt[:, :].rearrange("p (h d) -> p h d", h=8)
        for par in range(2):
            act(Vall[:, tl, par::2, par * 64:par * 64 + 64], pv[:, par::2, :], AF.Copy, pq, [Vall])
    scale = 96.0 ** -0.5
    for h in range(8):
        qh, kh = qhs[h % 2], khs[h % 2]
        sl = wslot()
        wq = sl[:, 0:3 * 192].rearrange("p (k n) -> p k n", k=3)
        wload(sl, wq[:, :, 0:96], wUQ, wUQ.t[l, :, :, h * 96:(h + 1) * 96])
        wload(sl, wq[:, :, 96:192], wUQ, wUQ.t[l, :, :, 768 + h * 96:768 + (h + 1) * 96])
        wk = sl[:, 1024:1024 + 2 * 96].rearrange("p (k n) -> p k n", k=2)
        wload(sl, wk, wKN, wKN.t[l, :, :, h * 96:(h + 1) * 96])
        for tb in range(NTB):
            blk = slice(tb * 512, (tb + 1) * 512)
            pa, qa = nextbank()
            pb, qb_ = nextbank()
            for c in range(3):
                mm(pa[0:96, :], qa, wq[:, c, 0:96], cq[:, c, blk], c == 0, c == 2, [sl, cq])
            for c in range(3):
                mm(pb[0:96, :], qb_, wq[:, c, 96:192], cq[:, c, blk], c == 0, c == 2, [sl, cq])
            vtt(t1[0:96, :], pa[0:96, :], cosT[:, blk], ALU.mult, qa + [cosT], [t1])
            vtt(t2[0:96, :], pb[0:96, :], sinT[:, blk], ALU.mult, qb_ + [sinT], [t2])
            vtt(qh[:, blk], t1[0:96, :], t2[0:96, :], ALU.add, [t1, t2], [qh])
            pk, qk = nextbank()
            for c in range(2):
                mm(pk[0:96, :], qk, wk[:, c, 0:96], ckv[:, c, blk], c == 0, False, [sl, ckv])
            mm(pk[0:96, :], qk, identb[0:96, 0:96], krT[:, blk], False, True, [identb, krT])
            act(kh[:, blk], pk[0:96, :], AF.Copy, qk, [kh])
        par = h % 2
        num = slice(par * 64, par * 64 + 64)
        den = slice((1 - par) * 64, (1 - par) * 64 + 64)
        pti = 0
        for qb in range(NTB):
            qblk = slice(qb * 512, (qb + 1) * 512)
            po, qo = nextbank([0, 1])
            LOOK = 2
            pend = []
            for kt in range(NTL + LOOK):
                if kt < NTL:
                    ps_, qs_ = nextbank([2, 3, 4, 5, 6, 7])
                    mm(ps_[:, :], qs_, kh[:, kt * 128:(kt + 1) * 128], qh[:, qblk], True, True, [kh, qh])
                    PT = PTs[pti % 4]
                    pti += 1
                    act(PT[:], ps_[:, :], AF.Exp, qs_, [PT], scale=scale)
                    pend.append(PT)
                if kt >= LOOK:
                    k2 = kt - LOOK
                    PT2 = pend[k2]
                    mm(po[:, :], qo, Vall[:, k2, h, :], PT2[:], k2 == 0, k2 == NTL - 1, [Vall, PT2])
            act(dsh[num, :], po[den, :], AF.Copy, qo, [dsh])
            P.op("dve", lambda: nc.vector.reciprocal(out=dsh[num, :], in_=dsh[num, :]), [dsh], [dsh])
            vtt(ob[num, h // 2, qblk], po[num, :], dsh[num, :], ALU.mult, qo + [dsh], [ob])


def emit_rg(P, nc, env, sc, l, s):
    e = _h(env)
    hT, obT, consts = e.hT, e.obT, e.consts
    mm, act, vtt, vts, vstt, vcopy, vmemset = e.mm, e.act, e.vtt, e.vts, e.vstt, e.vcopy, e.vmemset
    nextbank, cst, vec, wslot, wload, hrhs = e.nextbank, e.cst, e.vec, e.wslot, e.wload, e.hrhs
    wA, wRG, vecs, cneg, cneg2 = e.wA, e.wRG, e.vecs, e.cneg, e.cneg2
    ob = obT[1]
    xpad = P.sbuf("xpad", [128, T + 4], F32, sc)
    xc = P.sbuf("xc", [128, T], F32, sc)
    xcb = P.sbuf("xcb", [128, T], BF16, sc)
    R = P.sbuf("R", [128, T], F32, sc)
    I = P.sbuf("I", [128, T], F32, sc)
    Ab = P.sbuf("Ab", [128, T], F32, sc)
    HF = P.sbuf("HF", [128, T], F32, sc)
    HB = P.sbuf("HB", [128, T], F32, sc)
    gt = P.sbuf("gt", [128, T], F32, sc)
    wrg = P.sbuf("wrg", [128, 16, 128], BF16, sc)
    P.dma([wrg], wrg[:], [wRG], wRG.t[l], q="pool")
    vmemset(xpad[:], 0.0, [xpad])
    for c in range(4):
        sl = wslot()
        wv = sl[:, 0:2048].rearrange("p (k n) -> p k n", k=8)
        wload(sl, wv[:, :, 0:128], wA, wA.t[l, :, :, C_RGX + c * 128:C_RGX + (c + 1) * 128])
        wload(sl, wv[:, :, 128:256], wA, wA.t[l, :, :, C_RGG + c * 128:C_RGG + (c + 1) * 128])
        for tb in range(NTB):
            pt, pq = nextbank()
            for k in range(8):
                mm(pt[:, :], pq, wv[:, k, 0:128], hrhs(k, tb), k == 0, k == 7, [sl, hT])
            act(xpad[:, 1 + tb * 512:1 + (tb + 1) * 512], pt[:, :], AF.Copy, pq, [xpad])
        for tb in range(NTB):
            pt, pq = nextbank()
            for k in range(8):
                mm(pt[:, :], pq, wv[:, k, 128:256], hrhs(k, tb), k == 0, k == 7, [sl, hT])
            act(gt[:, tb * 512:(tb + 1) * 512], pt[:, :], AF.Copy, pq, [gt])
        vtt(HB[:], gt[:], gt[:], ALU.mult, [gt], [HB])
        vts(HB[:], HB[:], 0.044715, 1.0, ALU.mult, ALU.add, [HB], [HB])
        vtt(HB[:], HB[:], gt[:], ALU.mult, [HB, gt], [HB])
        act(HB[:], HB[:], AF.Sigmoid, [HB], [HB], scale=2.0 * math.sqrt(2.0 / math.pi))
        vtt(gt[:], gt[:], HB[:], ALU.mult, [gt, HB], [gt])
        act(xc[:], xpad[:, 0:T], AF.Identity, [xpad, vecs], [xc], bias=vec(l, "rgcb", c), scale=vec(l, "rgcw", c * 4 + 0))
        for k in range(1, 4):
            vstt(xc[:], xpad[:, k:k + T], vec(l, "rgcw", c * 4 + k), xc[:], ALU.mult, ALU.add, [xpad, xc, vecs], [xc])
        vcopy(xcb[:], xc[:], [xc], [xcb])
        if RGDBG == 1:
            vcopy(ob[:, c, :], xc[:], [xc], [ob])
            continue
        if RGDBG == 2:
            vcopy(ob[:, c, :], gt[:], [gt], [ob])
            continue
        for d in range(2):
            for (ai, dst, bname) in ((0, R, "rgba"), (1, I, "rgbi")):
                wi = (d * 2 + ai) * 4 + c
                for tb in range(NTB):
                    blk = slice(tb * 512, (tb + 1) * 512)
                    pt, pq = nextbank()
                    mm(pt[:, :], pq, wrg[:, wi, :], xcb[:, blk], True, True, [wrg, xcb])
                    act(dst[:, blk], pt[:, :], AF.Sigmoid, pq + [vecs], [dst], bias=vec(l, bname, d * 4 + c))
            if RGDBG == 3 and d == 0:
                vcopy(ob[:, c, :], R[:], [R], [ob])
            if RGDBG == 4 and d == 0:
                vcopy(ob[:, c, :], I[:], [I], [ob])
            act(Ab[:], R[:], AF.Exp, [R, cneg], [Ab], scale=cneg[:, l, d * 4 + c:d * 4 + c + 1])
            if RGDBG == 5 and d == 0:
                vcopy(ob[:, c, :], Ab[:], [Ab], [ob])
            act(R[:], R[:], AF.Exp, [R, cneg2], [R], scale=cneg2[:, l, d * 4 + c:d * 4 + c + 1])
            vts(R[:], R[:], -1.0, 1.0, ALU.mult, ALU.add, [R], [R])
            vts(R[:], R[:], 0.0, None, ALU.max, None, [R], [R])
            act(R[:], R[:], AF.Sqrt, [R], [R])
            edge = 0 if d == 0 else T - 1
            vmemset(R[:, edge:edge + 1], 1.0, [R])
            vtt(I[:], I[:], xc[:], ALU.mult, [I, xc], [I])
            vtt(I[:], I[:], R[:], ALU.mult, [I, R], [I])
            if d == 0:
                P.op("dve", lambda: nc.vector.tensor_tensor_scan(out=HF[:], data0=Ab[:], data1=I[:], initial=0.0,
                                                                 op0=ALU.mult, op1=ALU.add), [Ab, I], [HF])
            else:
                P.op("dve", lambda: nc.vector.tensor_tensor_scan(out=HB[:, ::-1], data0=Ab[:, ::-1], data1=I[:, ::-1], initial=0.0,
                                                                 op0=ALU.mult, op1=ALU.add), [Ab, I], [HB])
        if RGDBG in (3, 4, 5):
            continue
        if RGDBG == 6:
            vcopy(ob[:, c, :], HF[:], [HF], [ob])
            continue
        if RGDBG == 7:
            vcopy(ob[:, c, :], HB[:], [HB], [ob])
            continue
        vtt(HF[:], HF[:], HB[:], ALU.add, [HF, HB], [HF])
        vtt(ob[:, c, :], HF[:], gt[:], ALU.mult, [HF, gt], [ob])


def emit_gdn(P, nc, env, sc, l, s):
    e = _h(env)
    hT, obT, consts, identb = e.hT, e.obT, e.consts, e.identb
    mm, act, vtt, vts, vstt, vcopy, vmemset = e.mm, e.act, e.vtt, e.vts, e.vstt, e.vcopy, e.vmemset
    nextbank, nextq, cst, vec, wslot, wload, rsqrt_from, hrhs = e.nextbank, e.nextq, e.cst, e.vec, e.wslot, e.wload, e.rsqrt_from, e.hrhs
    nextq2 = e.nextq2
    wA, vecs, negA, epsc = e.wA, e.vecs, e.negA, e.epsc
    ob = obT[2]
    qT = P.sbuf("gqT", [128, T], BF16, sc)
    kT = P.sbuf("gkT", [128, T], BF16, sc)
    ktok = P.sbuf("gktok", [128, NTL, 128], BF16, sc)
    vtok = P.sbuf("gvtok", [128, NTL, 128], BF16, sc)
    otok = P.sbuf("gotok", [128, NTL, 128], F32, sc)
    gtok = P.sbuf("ggtok", [128, NTL, 8], F32, sc)
    btok = P.sbuf("gbtok", [128, NTL, 8], F32, sc)
    rtmp = P.sbuf("grtmp", [128, 512], F32, sc)
    S = [P.sbuf("gS", [128, 128], F32, sc) for _ in range(2)]
    Sb = [P.sbuf("gSb", [128, 128], BF16, sc) for _ in range(2)]
    szs = [P.sbuf("gsz", [128, 128], F32, sc) for _ in range(3)]
    ons = [P.sbuf("gon", [128, 128], F32, sc) for _ in range(3)]
    ssqs = [P.sbuf("gssq", [128, 4], F32, sc) for _ in range(3)]

    sl = wslot()
    wab = sl[:, 0:128].rearrange("p (k n) -> p k n", k=8)
    wload(sl, wab, wA, wA.t[l, :, :, C_AB:C_AB + 16])
    for tl in range(NTL):
        pt, pq = nextbank()
        for k in range(8):
            mm(pt[:, 0:16], pq, hT[:, k, tl * 128:(tl + 1) * 128], wab[:, k, :], k == 0, k == 7, [hT, sl])
        vtt(gtok[:, tl, :], pt[:, 0:8], vec(l, "gdtb", 0, 8), ALU.add, pq + [vecs], [gtok])
        act(btok[:, tl, :], pt[:, 8:16], AF.Sigmoid, pq, [btok])
    gfl = gtok[:].rearrange("p a b -> p (a b)")
    e.softplus_acc(rtmp[:, 0:128], gfl, rtmp[:, 128:256], rtmp[:, 256:384], rtmp[:, 384:512], [gtok], [rtmp])
    vtt(gfl, rtmp[:, 0:128], negA[:, l, :], ALU.mult, [rtmp, negA], [gtok])

    ident, ones, negones, bd = cst("ident"), cst("ones"), cst("negones"), cst("bd16")
    offmasks = [cst("m32"), cst("m64"), cst("m128")]
    G = GDN_G
    NH = G + 4
    PL = "pool" if GDN_POOL else "dve"

    for hh in range(4):
        with contextlib.ExitStack() as sc2:
            xpad = P.sbuf("gxpad", [128, T + 4], F32, sc2)
            xc = P.sbuf("gxc", [128, T], F32, sc2)
            vmemset(xpad[:], 0.0, [xpad])
            for role in range(3):
                ch = role * 4 + hh
                col0 = C_GQKV + ch * 128
                sl = wslot()
                wv = sl[:, 0:1024].rearrange("p (k n) -> p k n", k=8)
                wload(sl, wv, wA, wA.t[l, :, :, col0:col0 + 128])
                for tb in range(NTB):
                    pt, pq = nextbank()
                    for k in range(8):
                        mm(pt[:, :], pq, wv[:, k, :], hrhs(k, tb), k == 0, k == 7, [sl, hT])
                    act(xpad[:, 1 + tb * 512:1 + (tb + 1) * 512], pt[:, :], AF.Copy, pq, [xpad])
                act(xc[:], xpad[:, 0:T], AF.Identity, [xpad, vecs], [xc], scale=vec(l, "gcw", ch * 4 + 0))
                for k in range(1, 4):
                    vstt(xc[:], xpad[:, k:k + T], vec(l, "gcw", ch * 4 + k), xc[:], ALU.mult, ALU.add, [xpad, xc, vecs], [xc])
                act(xc[:], xc[:], AF.Silu, [xc], [xc])
                if role < 2:
                    for tb in range(NTB):
                        blk = slice(tb * 512, (tb + 1) * 512)
                        vtt(rtmp[:], xc[:, blk], xc[:, blk], ALU.mult, [xc], [rtmp])
                        pt, pq = nextbank()
                        mm(pt[:, :], pq, ones, rtmp[:], True, True, [consts, rtmp])
                        rsqrt_from(rtmp[:], pt[:, :], 1.0, pq, [rtmp])
                        if role == 0:
                            vstt(xc[:, blk], xc[:, blk], 128.0 ** -0.5, rtmp[:], ALU.mult, ALU.mult, [xc, rtmp], [xc])
                        else:
                            vtt(xc[:, blk], xc[:, blk], rtmp[:], ALU.mult, [xc, rtmp], [xc])
                    vcopy((qT if role == 0 else kT)[:], xc[:], [xc], [qT if role == 0 else kT])
                if role >= 1:
                    dst = ktok if role == 1 else vtok
                    for tl in range(NTL):
                        qa, qk = nextq()
                        mm(qa, [qk], xc[:, tl * 128:(tl + 1) * 128], ident, True, True, [xc, consts])
                        act(dst[:, tl, :], qa, AF.Copy, [qk], [dst])
            P.barrier()
        vmemset(otok[:], 0.0, [otok])

        with contextlib.ExitStack() as sc3:
            W2 = 256
            def mk(name, n, dt=F32, w=W2):
                return [P.sbuf(name, [128, w], dt, sc3) for _ in range(n)]
            GM, decS, decI, Af = (mk(nm, G) for nm in ("GM", "decS", "decI", "Af"))
            IDT = BF16
            Ad, Aoff, Bm = (mk(nm, G, IDT) for nm in ("Ad", "Aoff", "Bm"))
            Y = [mk("Ya", G, IDT), mk("Yb", G, IDT)]
            YT = [mk("YTa", G, IDT), mk("YTb", G, IDT)]
            Pm, PTm, Zm, TTb = mk("Pm", G, IDT), mk("PTm", G, IDT), mk("Zm", G, IDT), mk("TTb", G, BF16)
            vb, kbg = mk("vb", G, BF16), mk("kbg", G, BF16)
            um, wTm, kdec, attnT = mk("um", NH), mk("wTm", NH, BF16), mk("kdec", NH, BF16), mk("attnT", NH, BF16)
            cols = mk("cols", NH, F32, 16)
            vnew = mk("vnew", NH, BF16)
            tmpo = mk("tmpo", NH)
            S2 = P.sbuf("gS2", [128, W2], F32, sc3)
            Sb2 = P.sbuf("gSb2", [128, W2], BF16, sc3)
            vmemset(S2[:], 0.0, [S2])
            vmemset(Sb2[:], 0.0, [Sb2])
            ident2, identb2 = cst("ident", 256), identb[:, 0:256]
            ones2, negones2 = cst("ones", 256), cst("negones", 256)
            mt01, negs01, negst01 = cst("mt0", 256), cst("negs0", 256), cst("negst0", 256)
            bd2 = cst("bd16", 256)
            offmasks2 = [cst("m32", 256), cst("m64", 256), cst("m128", 256)]
            hs = [slice(0, 128), slice(128, 256)]
            tn = lambda d, it: it if d == 0 else NTL - 1 - it

            def prep_gen(it, i, hi):
                ns = [tn(0, it), tn(1, it)]
                colj = [0 * 4 + hh, 1 * 4 + hh]
                tsl = [slice(n * 128, (n + 1) * 128) for n in ns]
                g_col = [gtok[:, ns[d], colj[d]:colj[d] + 1] for d in range(2)]
                b_col = [btok[:, ns[d], colj[d]:colj[d] + 1] for d in range(2)]
                cl = cols[hi]
                for d in range(2):
                    vts(GM[i][:, hs[d]], mt01[:, hs[d]], g_col[d], None, ALU.mult, None, [consts, gtok], [GM[i]])
                    vts(cl[:, d * 8 + 6:d * 8 + 8], ones[:, 0:2], g_col[d], None, ALU.mult, None, [consts, gtok], [cl])
                yield
                pD, qD = nextq2()
                for d in range(2):
                    mm(pD[:, hs[d]], [qD], GM[i][:, hs[d]], ones, d == 0, False, [GM[i], consts])
                mm(pD, [qD], negones, GM[i][:], False, False, [GM[i], consts])
                mm(pD, [qD], ident, negs01, False, True, [consts])
                act(decS[i][:], pD, AF.Exp, [qD], [decS[i]])
                yield
                pDT, qDT = nextq2()
                mm(pDT, [qDT], ones, GM[i][:], True, False, [GM[i], consts])
                for d in range(2):
                    mm(pDT[:, hs[d]], [qDT], GM[i][:, hs[d]], negones, False, False, [GM[i], consts])
                mm(pDT, [qDT], ident, negst01, False, True, [consts])
                act(decI[i][:], pDT, AF.Exp, [qDT], [decI[i]])
                yield
                pc, qc = nextq2()
                for d in range(2):
                    mm(pc[:, d * 4:d * 4 + 2], [qc], GM[i][:, hs[d]], ones[:, 0:2], True, True, [GM[i], consts])
                    mm(pc[:, d * 4 + 2:d * 4 + 4], [qc], ones, cl[:, d * 8 + 6:d * 8 + 8], True, True, [cl, consts])
                cl3 = cl[:, 0:16].rearrange("p (d c) -> p d c", d=2)
                pc3 = pc[:, 0:8].rearrange("p (d c) -> p d c", d=2)
                vcopy(cl3[:, :, 0:2], pc3[:, :, 1:3], [qc], [cl])
                act(cl3[:, :, 2:4], cl3[:, :, 0:2], AF.Exp, [cl], [cl])
                for d in range(2):
                    act(cl[:, d * 8 + 4:d * 8 + 5], cl[:, d * 8:d * 8 + 1], AF.Exp, [cl], [cl], bias=cl[:, d * 8 + 1:d * 8 + 2], scale=-1.0)
                    vtt(cl[:, d * 8 + 5:d * 8 + 6], cl[:, d * 8 + 2:d * 8 + 3], b_col[d], ALU.mult, [cl, btok], [cl])
                yield
                pK, qK = nextq2()
                for d in range(2):
                    mm(pK[:, hs[d]], [qK], kT[:, tsl[d]], kT[:, tsl[d]], True, True, [kT])
                for d in range(2):
                    vstt(Af[i][:, hs[d]], pK[:, hs[d]], b_col[d], decS[i][:, hs[d]], ALU.mult, ALU.mult, [qK, btok, decS[i]], [Af[i]])
                vtt(Ad[i][:], Af[i][:], bd2, ALU.mult, [Af[i], consts], [Ad[i]])
                yield
                pQ, qQ = nextq2()
                for d in range(2):
                    mm(pQ[:, hs[d]], [qQ], kT[:, tsl[d]], qT[:, tsl[d]], True, True, [kT, qT])
                vtt(decI[i][:], decI[i][:], ident2, ALU.add, [decI[i], consts], [decI[i]])
                vtt(attnT[hi][:], pQ, decI[i][:], ALU.mult, [qQ, decI[i]], [attnT[hi]])
                yield
                pB, qB = nextq2()
                for d in range(2):
                    mm(pB[:, hs[d]], [qB], Ad[i][:, hs[d]], identb2[:, 0:128], True, True, [Ad[i], identb])
                act(Bm[i][:], pB, AF.Copy, [qB], [Bm[i]])
                vtt(Pm[i][:], ident2, Bm[i][:], ALU.subtract, [consts, Bm[i]], [Pm[i]])
                yield
                yprev, ytprev = Bm[i], Ad[i]
                for lev in range(3):
                    ycur, ytcur = Y[lev % 2][i], YT[lev % 2][i]
                    pyt, qyt = nextq2()
                    for d in range(2):
                        mm(pyt[:, hs[d]], [qyt], yprev[:, hs[d]], ytprev[:, hs[d]], True, True, [yprev, ytprev])
                    act(ytcur[:], pyt, AF.Copy, [qyt], [ytcur])
                    if lev < 2:
                        py, qy = nextq2()
                        for d in range(2):
                            mm(py[:, hs[d]], [qy], ytprev[:, hs[d]], yprev[:, hs[d]], True, True, [yprev, ytprev])
                        act(ycur[:], py, AF.Copy, [qy], [ycur])
                    yield
                    pp, qp = nextq2()
                    for d in range(2):
                        mm(pp[:, hs[d]], [qp], ytcur[:, hs[d]], Pm[i][:, hs[d]], True, True, [ytcur, Pm[i]])
                    vtt(Pm[i][:], Pm[i][:], pp, ALU.add, [Pm[i], qp], [Pm[i]])
                    yield
                    yprev, ytprev = ycur, ytcur
                for mi, msk in enumerate(offmasks2):
                    ppt, qpt = nextq2()
                    for d in range(2):
                        mm(ppt[:, hs[d]], [qpt], Pm[i][:, hs[d]], identb2[:, 0:128], True, True, [Pm[i], identb])
                    act(PTm[i][:], ppt, AF.Copy, [qpt], [PTm[i]])
                    vtt(Aoff[i][:], Af[i][:], msk, ALU.mult, [Af[i], consts], [Aoff[i]])
                    pz, qz = nextq2()
                    for d in range(2):
                        mm(pz[:, hs[d]], [qz], Aoff[i][:, hs[d]], Pm[i][:, hs[d]], True, True, [Aoff[i], Pm[i]])
                    act(Zm[i][:], pz, AF.Copy, [qz], [Zm[i]])
                    yield
                    pt2, qt2 = nextq2()
                    for d in range(2):
                        mm(pt2[:, hs[d]], [qt2], PTm[i][:, hs[d]], Zm[i][:, hs[d]], True, True, [PTm[i], Zm[i]])
                    if mi < 2:
                        vtt(Pm[i][:], Pm[i][:], pt2, ALU.subtract, [Pm[i], qt2], [Pm[i]])
                    else:
                        vtt(TTb[i][:], Pm[i][:], pt2, ALU.subtract, [Pm[i], qt2], [TTb[i]])
                    yield
                for d in range(2):
                    vts(vb[i][:, hs[d]], vtok[:, ns[d], :], b_col[d], None, ALU.mult, None, [vtok, btok], [vb[i]])
                    vts(kbg[i][:, hs[d]], ktok[:, ns[d], :], cl[:, d * 8 + 5:d * 8 + 6], None, ALU.mult, None, [ktok, cl], [kbg[i]])
                    vts(kdec[hi][:, hs[d]], ktok[:, ns[d], :], cl[:, d * 8 + 4:d * 8 + 5], None, ALU.mult, None, [ktok, cl], [kdec[hi]])
                pu, qu = nextq2()
                for d in range(2):
                    mm(pu[:, hs[d]], [qu], TTb[i][:, hs[d]], vb[i][:, hs[d]], True, True, [TTb[i], vb[i]])
                act(um[hi][:], pu, AF.Copy, [qu], [um[hi]])
                yield
                pw, qw = nextq2()
                for d in range(2):
                    mm(pw[:, hs[d]], [qw], kbg[i][:, hs[d]], TTb[i][:, hs[d]], True, True, [TTb[i], kbg[i]])
                act(wTm[hi][:], pw, AF.Copy, [qw], [wTm[hi]])

            def step_gen(it, i):
                ns = [tn(0, it), tn(1, it)]
                tsl = [slice(n * 128, (n + 1) * 128) for n in ns]
                cl = cols[i]
                p1, q1 = nextq2()
                for d in range(2):
                    mm(p1[:, hs[d]], [q1], wTm[i][:, hs[d]], Sb2[:, hs[d]], True, True, [wTm[i], Sb2])
                vstt(vnew[i][:], p1, -1.0, um[i][:], ALU.mult, ALU.add, [q1, um[i]], [vnew[i]])
                p2, q2 = nextq2()
                for d in range(2):
                    mm(p2[:, hs[d]], [q2], qT[:, tsl[d]], Sb2[:, hs[d]], True, True, [qT, Sb2])
                for d in range(2):
                    vts(tmpo[i][:, hs[d]], p2[:, hs[d]], cl[:, d * 8 + 2:d * 8 + 3], None, ALU.mult, None, [q2, cl], [tmpo[i]])
                yield
                p4, q4 = nextq2()
                for d in range(2):
                    mm(p4[:, hs[d]], [q4], kdec[i][:, hs[d]], vnew[i][:, hs[d]], True, True, [kdec[i], vnew[i]])
                for d in range(2):
                    vstt(S2[:, hs[d]], S2[:, hs[d]], cl[:, d * 8 + 3:d * 8 + 4], p4[:, hs[d]], ALU.mult, ALU.add, [S2, cl, q4], [S2])
                act(Sb2[:], S2[:], AF.Copy, [S2], [Sb2])
                yield
                p3, q3 = nextq2()
                for d in range(2):
                    mm(p3[:, hs[d]], [q3], attnT[i][:, hs[d]], vnew[i][:, hs[d]], True, True, [attnT[i], vnew[i]])
                vtt(tmpo[i][:], tmpo[i][:], p3, ALU.add, [q3, tmpo[i]], [tmpo[i]])
                for d in range(2):
                    vtt(otok[:, ns[d], :], otok[:, ns[d], :], tmpo[i][:, hs[d]], ALU.add, [otok, tmpo[i]], [otok])

            nxt = 0
            free_s = list(range(G))
            free_h = list(range(NH))
            hmap = {}
            prep_done = set()
            step_done = -1
            step_started = -1
            active = []
            while step_done < NTL - 1:
                while nxt < NTL and free_s and free_h:
                    it = nxt
                    nxt += 1
                    si = free_s.pop(0)
                    hi = free_h.pop(0)
                    hmap[it] = hi
                    active.append(("P", it, prep_gen(it, si, hi), si))
                it = step_started + 1
                if it < NTL and it in prep_done and step_done == it - 1:
                    active.append(("S", it, step_gen(it, hmap[it]), None))
                    step_started = it
                for task in list(active):
                    kind, it, gen, si = task
                    try:
                        next(gen)
                    except StopIteration:
                        active.remove(task)
                        if kind == "P":
                            prep_done.add(it)
                            free_s.append(si)
                        else:
                            step_done = it
                            free_h.append(hmap[it])
            P.barrier()
        sl = wslot()
        wz = sl[:, 0:1024].rearrange("p (k n) -> p k n", k=8)
        wload(sl, wz, wA, wA.t[l, :, :, C_GZ + hh * 128:C_GZ + (hh + 1) * 128])
        def outA(tl):
            sz, on, ssq = szs[tl % 3], ons[tl % 3], ssqs[tl % 3]
            pt, pq = nextbank()
            for k in range(8):
                mm(pt[:, 0:128], pq, hT[:, k, tl * 128:(tl + 1) * 128], wz[:, k, :], k == 0, k == 7, [hT, sl])
            act(sz[:], pt[:, 0:128], AF.Silu, pq, [sz])
            P.op("dve", lambda: nc.vector.tensor_tensor(out=on[:], in0=otok[:, tl, :], in1=otok[:, tl, :], op=ALU.mult), [otok], [on])
            P.op("dve", lambda: nc.vector.tensor_reduce(out=ssq[:, 0:1], in_=on[:], axis=AX.X, op=ALU.add), [on], [ssq])
            act(ssq[:, 1:2], ssq[:, 0:1], AF.Ln, [ssq], [ssq], bias=epsc[:, 0:1], scale=1.0 / 128)
            act(ssq[:, 1:2], ssq[:, 1:2], AF.Exp, [ssq], [ssq], scale=-0.5)
            vstt(on[:], otok[:, tl, :], ssq[:, 1:2], vec(l, "gng", 0, 128), ALU.mult, ALU.mult, [otok, ssq, vecs], [on])
            vtt(on[:], on[:], sz[:], ALU.mult, [on, sz], [on])

        def outB(tl):
            on = ons[tl % 3]
            qa, qk = nextq()
            mm(qa, [qk], on[:], ident, True, True, [on, consts])
            act(ob[:, hh, tl * 128:(tl + 1) * 128], qa, AF.Copy, [qk], [ob])

        for tl in range(NTL + 2):
            if tl < NTL:
                outA(tl)
            if tl >= 2:
                outB(tl - 2)


def emit_merge(P, nc, env, sc, l, s):
    e = _h(env)
    hT, obT, consts = e.hT, e.obT, e.consts
    mm, act, vtt, vts, vstt, vcopy = e.mm, e.act, e.vtt, e.vts, e.vstt, e.vcopy
    nextbank, cst, vec, wslot, wload, hrhs = e.nextbank, e.cst, e.vec, e.wslot, e.wload, e.hrhs
    wA, wBR, wO, mod = e.wA, e.wBR, e.wO, e.mod
    mT = P.sbuf("mT", [128, 8, T], BF16, sc)
    Gs = [P.sbuf("Gs", [128, 512], F32, sc) for _ in range(2)]
    macc = P.sbuf("macc", [128, 512], F32, sc)
    mtmp = P.sbuf("mtmp", [128, 512], F32, sc)
    xb = P.sbuf("xb", [128, 8, 512], F32, sc)
    sq = P.sbuf("sq", [128, 8, 512], F32, sc)
    rstd = P.sbuf("rstd", [128, 512], F32, sc)
    tmp = P.sbuf("tmp", [128, 512], F32, sc)
    gi = 0
    for j in range(8):
        sl = wslot()
        wg = sl[:, 0:3072].rearrange("p (b k n) -> p b k n", b=3, k=8)
        for b in range(3):
            wload(sl, wg[:, b], wA, wA.t[l, :, :, C_GATE + b * 1024 + j * 128:C_GATE + b * 1024 + (j + 1) * 128])
        sl2 = wslot()
        wb = sl2[:, 0:1536].rearrange("p (k n) -> p k n", k=12)
        wload(sl2, wb, wBR, wBR.t[l, :, :, j * 128:(j + 1) * 128])
        for tb in range(NTB):
            blk = slice(tb * 512, (tb + 1) * 512)
            for b in range(3):
                pg, qg = nextbank()
                for k in range(8):
                    mm(pg[:, :], qg, wg[:, b, k, :], hrhs(k, tb), k == 0, k == 7, [sl, hT])
                Gt = Gs[gi % 2]
                gi += 1
                act(Gt[:], pg[:, :], AF.Sigmoid, qg, [Gt])
                py, qy = nextbank()
                for k in range(4):
                    mm(py[:, :], qy, wb[:, b * 4 + k, :], obT[b][:, k, blk], k == 0, k == 3, [sl2, obT[b]])
                if b == 0:
                    vtt(macc[:], py[:, :], Gt[:], ALU.mult, qy + [Gt], [macc])
                elif b == 1:
                    vtt(mtmp[:], py[:, :], Gt[:], ALU.mult, qy + [Gt], [mtmp])
                    vtt(macc[:], macc[:], mtmp[:], ALU.add, [macc, mtmp], [macc])
                else:
                    vtt(mtmp[:], py[:, :], Gt[:], ALU.mult, qy + [Gt], [mtmp])
                    vtt(mT[:, j, blk], macc[:], mtmp[:], ALU.add, [macc, mtmp], [mT])
    for tb in range(NTB):
        blk = slice(tb * 512, (tb + 1) * 512)
        e.load_xblock(xb, tb)
        for half in range(2):
            sl = wslot()
            wv = sl[:, 0:4096].rearrange("p (k n) -> p k n", k=8)
            wload(sl, wv, wO, wO.t[l, :, :, half * 512:(half + 1) * 512])
            for ii in range(4):
                i = half * 4 + ii
                pt, pq = nextbank()
                for k in range(8):
                    mm(pt[:, :], pq, wv[:, k, ii * 128:(ii + 1) * 128], mT[:, k, blk], k == 0, k == 7, [sl, mT])
                vstt(xb[:, i, :], pt[:, :], mod[:, l, 16 + i, s:s + 1], xb[:, i, :], ALU.mult, ALU.add, pq + [mod, xb], [xb])
        e.store_xblock(xb, tb)
        e.norm_block(xb, tb, l, s, 2, sq, rstd, tmp)


def emit_ffn(P, nc, env, sc, l, s):
    e = _h(env)
    hT, consts = e.hT, e.consts
    mm, act, vtt, vts, vstt, vcopy, vmemset = e.mm, e.act, e.vtt, e.vts, e.vstt, e.vcopy, e.vmemset
    nextbank, cst, vec, wslot, wload, hrhs = e.nextbank, e.cst, e.vec, e.wslot, e.wload, e.hrhs
    wUP, wDN, mod, vecs, xT_d, xT_tk = e.wUP, e.wDN, e.mod, e.vecs, e.xT_d, e.xT_tk
    actT = P.sbuf("actT", [128, NJ, T], BF16, sc)
    upad = [P.sbuf("upad", [128, T + 2], F32, sc) for _ in range(2)]
    cv = [P.sbuf("cv", [128, T], F32, sc) for _ in range(2)]
    xs = [P.sbuf("xs", [128, 512], F32, sc) for _ in range(2)]
    for u in upad:
        vmemset(u[:], 0.0, [u])
    for jp in range(NJ // 2):
        sl = wslot()
        wv = sl[:, 0:4096].rearrange("p (k n) -> p k n", k=8)
        for jj in range(2):
            j = jp * 2 + jj
            wload(sl, wv[:, :, jj * 128:(jj + 1) * 128], wUP, wUP.t[l, :, :, j * 128:(j + 1) * 128])
            wload(sl, wv[:, :, 256 + jj * 128:256 + (jj + 1) * 128], wUP, wUP.t[l, :, :, DFF + j * 128:DFF + (j + 1) * 128])
        for jj in range(2):
            j = jp * 2 + jj
            for lg in range(2):
                ch = lg * NJ + j
                wc = lg * 256 + jj * 128
                for tb in range(NTB):
                    pt, pq = nextbank()
                    for k in range(8):
                        mm(pt[:, :], pq, wv[:, k, wc:wc + 128], hrhs(k, tb), k == 0, k == 7, [sl, hT])
                    act(upad[lg][:, 1 + tb * 512:1 + (tb + 1) * 512], pt[:, :], AF.Copy, pq, [upad[lg]])
                act(cv[lg][:], upad[lg][:, 0:T], AF.Identity, [upad[lg], vecs], [cv[lg]],
                    bias=vec(l, "fcb", ch), scale=vec(l, "fcw", ch * 3 + 0))
                for k in range(1, 3):
                    vstt(cv[lg][:], upad[lg][:, k:k + T], vec(l, "fcw", ch * 3 + k), cv[lg][:], ALU.mult, ALU.add,
                         [upad[lg], cv[lg], vecs], [cv[lg]])
            act(cv[1][:], cv[1][:], AF.Silu, [cv[1]], [cv[1]])
            vtt(actT[:, j, :], cv[1][:], cv[0][:], ALU.mult, [cv[0], cv[1]], [actT])
    xi = 0
    for i in range(8):
        sl = wslot()
        wv = sl[:, 0:NJ * 128].rearrange("p (k n) -> p k n", k=NJ)
        wload(sl, wv, wDN, wDN.t[l, :, :, i * 128:(i + 1) * 128])
        for tb in range(NTB):
            blk = slice(tb * 512, (tb + 1) * 512)
            pt, pq = nextbank()
            for k in range(NJ):
                mm(pt[:, :], pq, wv[:, k, :], actT[:, k, blk], k == 0, k == NJ - 1, [sl, actT])
            x1 = xs[xi % 2]
            xi += 1
            P.dma([x1], x1[:], [xT_tk[i][tb]], xT_d[:, i, blk])
            vstt(x1[:], pt[:, :], mod[:, l, 40 + i, s:s + 1], x1[:], ALU.mult, ALU.add, pq + [mod, x1], [x1])
            P.dma([xT_tk[i][tb]], xT_d[:, i, blk], [x1], x1[:], sem_t=x1)


def _fm(w, kc):
    K, N = w.shape
    return np.ascontiguousarray(w.reshape(kc, 128, N).transpose(1, 0, 2))


def _colv(v, nch):
    return np.ascontiguousarray(np.asarray(v).reshape(nch, 128).T)


def _consts():
    c = np.zeros((128, NC), np.float32)
    i = np.arange(128)
    ident = np.eye(128, dtype=np.float32)
    c[:, COFF["ident"]:COFF["ident"] + 128] = ident
    c[:, COFF["ident_b"]:COFF["ident_b"] + 128] = ident
    c[:, COFF["ones"]:COFF["ones"] + 256] = 1.0
    c[:, COFF["negones"]:COFF["negones"] + 256] = -1.0
    m, ii = np.meshgrid(i, i, indexing="ij")
    c[:, COFF["mt0"]:COFF["mt0"] + 128] = (m <= ii)
    c[:, COFF["mt1"]:COFF["mt1"] + 128] = (m >= ii)
    a, b = np.meshgrid(i, i, indexing="ij")
    v0 = (b < a)
    v1 = (b > a)
    c[:, COFF["negs0"]:COFF["negs0"] + 128] = np.where(v0, 0.0, -BIG)
    c[:, COFF["negs1"]:COFF["negs1"] + 128] = np.where(v1, 0.0, -BIG)
    c[:, COFF["negst0"]:COFF["negst0"] + 128] = np.where(v0.T, 0.0, -BIG)
    c[:, COFF["negst1"]:COFF["negst1"] + 128] = np.where(v1.T, 0.0, -BIG)
    bdm = lambda n: ((a // n) == (b // n)).astype(np.float32)
    for sfx in ("", "_b"):
        c[:, COFF["bd16" + sfx]:COFF["bd16" + sfx] + 128] = bdm(16)
        c[:, COFF["m32" + sfx]:COFF["m32" + sfx] + 128] = bdm(32) - bdm(16)
        c[:, COFF["m64" + sfx]:COFF["m64" + sfx] + 128] = bdm(64) - bdm(32)
        c[:, COFF["m128" + sfx]:COFF["m128" + sfx] + 128] = 1.0 - bdm(64)
    return c


def _rope_tables():
    inv = 1.0 / (10000.0 ** (np.arange(0, 32, 2, dtype=np.float32) / 32.0))
    ang = np.arange(T, dtype=np.float32)[None, :] * inv[:, None].astype(np.float32)
    cos, sin = np.cos(ang).astype(np.float32), np.sin(ang).astype(np.float32)
    r = np.zeros((2, 96, T), np.float32)
    r[0, 0:64] = 1.0
    r[0, 64:80] = cos
    r[0, 80:96] = cos
    r[1, 64:80] = -sin
    r[1, 80:96] = sin
    return r


def prep_weights(inp):
    L = L_ALL
    f = lambda k: np.asarray(inp[k], np.float32)
    w_in = f("w_in")
    wA = np.zeros((L, 128, 8, NA), np.float32)
    sw = np.concatenate([np.arange(16, 32), np.arange(0, 16)])
    wUQ = np.zeros((L, 128, 3, 1536), np.float32)
    wKN = np.zeros((L, 128, 2, 768), np.float32)
    wV = np.zeros((L, 128, 2, 512), np.float32)
    wRG = np.zeros((L, 128, 16, 128), np.float32)
    vecs = np.zeros((128, L, NV), np.float32)
    for l in range(L):
        ext = np.zeros((D, 192), np.float32)
        kr = w_in[l][:, 640:672]
        ext[:, 64:96] = kr
        ext[:, 96 + 64:96 + 96] = kr[:, sw]
        wA[l] = _fm(np.concatenate([w_in[l], ext], axis=1), 8)
        uq = f("mla_w_uq")[l]
        uqs = uq.reshape(384, 8, 96).copy()
        uqs[:, :, 64:96] = uqs[:, :, 64:96][:, :, sw]
        wUQ[l] = _fm(np.concatenate([uq, uqs.reshape(384, 768)], axis=1), 3)
        ukv = f("mla_w_ukv")[l].reshape(256, 8, 128)
        kn = np.zeros((256, 8, 96), np.float32)
        kn[:, :, 0:64] = ukv[:, :, 0:64]
        wKN[l] = _fm(kn.reshape(256, 768), 2)
        wV[l] = _fm(np.ascontiguousarray(ukv[:, :, 64:128]).reshape(256, 512), 2)
        for d in range(2):
            for ai, nm in enumerate(("rg_w_a", "rg_w_i")):
                w = f(nm)[l, d]
                for c in range(4):
                    m = np.zeros((128, 128), np.float32)
                    m[0:64, 0:64] = w[2 * c]
                    m[64:128, 64:128] = w[2 * c + 1]
                    wRG[l, :, (d * 2 + ai) * 4 + c, :] = m

        def put(name, arr):
            o, w = VOFF[name]
            vecs[:, l, o:o + w] = arr
        put("ln1", _colv(f("ln1_g")[l], 8))
        put("ln2", _colv(f("ln2_g")[l], 8))
        put("bmod", _colv(f("b_mod")[l], 48))
        put("qg", _colv(f("mla_q_norm_g")[l], 3))
        put("kvg", _colv(f("mla_kv_norm_g")[l], 2))
        put("rgcw", np.ascontiguousarray(f("rg_conv_w")[l].reshape(4, 4, 128).transpose(2, 1, 0)).reshape(128, 16))
        put("rgcb", _colv(f("rg_conv_b")[l], 4))
        put("rgba", np.ascontiguousarray(f("rg_b_a")[l].reshape(2, 4, 128).transpose(2, 0, 1)).reshape(128, 8))
        put("rgbi", np.ascontiguousarray(f("rg_b_i")[l].reshape(2, 4, 128).transpose(2, 0, 1)).reshape(128, 8))
        put("rglam", np.ascontiguousarray(f("rg_lam")[l].reshape(2, 4, 128).transpose(2, 0, 1)).reshape(128, 8))
        put("gcw", np.ascontiguousarray(f("gdn_conv_w")[l].reshape(4, 12, 128).transpose(2, 1, 0)).reshape(128, 48))
        put("gng", np.broadcast_to(f("gdn_norm_g")[l][None, :], (128, 128)))
        put("galog", np.broadcast_to(np.tile(f("gdn_a_log")[l].reshape(8), 16)[None, :], (128, 128)))
        put("gdtb", np.broadcast_to(np.tile(f("gdn_dt_bias")[l].reshape(8), 16)[None, :], (128, 128)))
        put("fcw", np.ascontiguousarray(f("ffn_conv_w")[l].reshape(3, 44, 128).transpose(2, 1, 0)).reshape(128, 132))
        put("fcb", _colv(f("ffn_conv_b")[l], 44))
        put("fng", _colv(f("final_norm_g"), 8))
    wd = {
        "wA": wA, "wUQ": wUQ, "wKN": wKN, "wV": wV, "wRG": wRG,
        "wBR": np.stack([_fm(f("w_branch")[l].reshape(1536, 1024), 12) for l in range(L)]),
        "wO": np.stack([_fm(f("w_out")[l], 8) for l in range(L)]),
        "wUP": np.stack([_fm(f("ffn_w_up")[l], 8) for l in range(L)]),
        "wDN": np.stack([_fm(f("ffn_w_down")[l], NJ) for l in range(L)]),
        "wMOD": np.stack([_fm(f("w_mod")[l], 8) for l in range(L)]),
        "vecs": vecs, "consts": _consts(), "ropet": _rope_tables(),
    }
    return wd


def core_inputs(wd, xs, cs):
    m = dict(wd)
    m["xin"] = np.ascontiguousarray(xs, dtype=np.float32)
    nseq = xs.shape[0]
    m["cT"] = np.ascontiguousarray(np.asarray(cs, np.float32).reshape(nseq, 8, 128).transpose(2, 1, 0))
    return m


_CACHE = {}


def kernel(**inputs):
    wd = prep_weights(inputs)
    xp = np.asarray(inputs["x_prompt"], np.float32)
    xsm = np.asarray(inputs["x_sample"], np.float32)
    cp = np.asarray(inputs["c_prompt"], np.float32)
    csm = np.asarray(inputs["c_sample"], np.float32)
    xall = np.concatenate([xp, xsm], axis=0)
    call = np.concatenate([cp, csm], axis=0)
    nseq = xall.shape[0] // NCORES
    if "nc" not in _CACHE:
        _CACHE["nc"] = build(nseq)[0]
    nc = _CACHE["nc"]
    in_maps = [core_inputs(wd, xall[i * nseq:(i + 1) * nseq], call[i * nseq:(i + 1) * nseq]) for i in range(NCORES)]
    res = run_bass_kernel_spmd(nc, in_maps, core_ids=list(range(NCORES)))
    y = np.concatenate([np.asarray(r["yout"], np.float32) for r in res.results], axis=0)
    nb = xp.shape[0]
    return (np.ascontiguousarray(y[:nb]), np.ascontiguousarray(y[nb:]))
```

```python
import contextlib
import math
import numpy as np
import concourse.bass as bass
import concourse.mybir as mybir
from concourse.bass_utils import run_bass_kernel_spmd

F32 = mybir.dt.float32
BF16 = mybir.dt.bfloat16
AF = mybir.ActivationFunctionType
ALU = mybir.AluOpType
AX = mybir.AxisListType

D = 1024
T = 2048
NTB = 4
NTL = 16
L_ALL = 2
NCORES = 8
EPS = 1e-6
N_IN = 6832
KR0 = N_IN
NA = N_IN + 192
C_RGX, C_RGG, C_GQKV, C_GZ, C_AB, C_GATE = 672, 1184, 1696, 3232, 3744, 3760
DFF = 2816
NJ = 22
BIG = 30000.0
SEM_LIMIT = 30000

VOFF = {}
_o = 0
for _n, _w in [("ln1", 8), ("ln2", 8), ("bmod", 48), ("qg", 3), ("kvg", 2), ("rgcw", 16), ("rgcb", 4),
               ("rgba", 8), ("rgbi", 8), ("rglam", 8), ("gcw", 48), ("gng", 128), ("galog", 128), ("gdtb", 128),
               ("fcw", 132), ("fcb", 44), ("fng", 8)]:
    VOFF[_n] = (_o, _w)
    _o += _w
NV = _o
COFF = {}
_o = 0
for _n in ["ident", "ident_b", "ones", "ones_b", "negones", "negones_b", "mt0", "mt1", "negs0", "negs1", "negst0", "negst1",
           "bd16", "bd16_b", "m32", "m32_b", "m64", "m64_b", "m128", "m128_b"]:
    COFF[_n] = _o
    _o += 128
NC = _o


class Tk:
    __slots__ = ("name", "w", "r", "t", "dk")

    def __init__(self, name, t=None):
        self.name = name
        self.w = None
        self.r = {}
        self.t = t
        self.dk = name

    def __getitem__(self, idx):
        return self.t[idx]


class Prog:
    def __init__(self, nc):
        self.nc = nc
        self.es = contextlib.ExitStack()
        self.eng = {"pe": nc.tensor, "act": nc.scalar, "dve": nc.vector, "pool": nc.gpsimd, "sp": nc.sync}
        self.sems = {}
        self.cnt = {}
        self.cur = {}
        self.epoch = {e: 0 for e in self.eng}
        self.seen = {e: {} for e in self.eng}
        self.ninst = 0
        self.uid = 0
        self.namectr = {}
        for e in self.eng:
            self._new_epoch(e)

    def _mk_sem(self, key):
        s = self.es.enter_context(self.nc.semaphore("s_%s" % (key,)))
        self.sems[key] = s
        self.cnt[key] = 0
        return s

    def _new_epoch(self, e):
        key = "%s%d" % (e, self.epoch[e])
        self.epoch[e] += 1
        self._mk_sem(key)
        self.cur[e] = key

    def sbuf(self, name, shape, dt, es=None):
        self.uid += 1
        t = (es or self.es).enter_context(self.nc.sbuf_tensor("%s_%d" % (name, self.uid), list(shape), dt))
        tk = Tk(name + str(self.uid), t)
        c = self.namectr.get(name, 0)
        self.namectr[name] = c + 1
        tk.dk = "%s_%d" % (name, c % 4)
        return tk

    def dram(self, name, shape, dt, kind="Internal"):
        t = self.nc.dram_tensor(name, list(shape), dt, kind=kind).ap()
        return Tk(name, t)

    def _deps(self, reads, writes):
        deps = {}
        for t in reads:
            if t.w is not None:
                k, v = t.w
                if deps.get(k, 0) < v:
                    deps[k] = v
        for t in writes:
            if t.w is not None:
                k, v = t.w
                if deps.get(k, 0) < v:
                    deps[k] = v
            for k, v in t.r.items():
                if deps.get(k, 0) < v:
                    deps[k] = v
        return deps

    def _wait(self, e, deps):
        eng = self.eng[e]
        seen = self.seen[e]
        for k, v in deps.items():
            isdma = k.startswith("dma")
            if (not isdma) and k.startswith(e) and e in ("pe", "sp"):
                continue
            if isdma:
                v = self.cnt[k]
            if seen.get(k, 0) < v:
                eng.wait_ge(self.sems[k], v)
                seen[k] = v
                self.ninst += 1

    def op(self, e, fn, reads=(), writes=()):
        self._wait(e, self._deps(reads, writes))
        if self.cnt[self.cur[e]] >= SEM_LIMIT:
            self._new_epoch(e)
        key = self.cur[e]
        inst = fn()
        self.cnt[key] += 1
        v = self.cnt[key]
        inst.then_inc(self.sems[key], 1)
        self.ninst += 1
        for t in reads:
            t.r[key] = v
        for t in writes:
            t.w = (key, v)
            t.r = {}
        return inst

    def dma(self, out_ts, out_ap, in_ts, in_ap, q="sp", sem_t=None):
        st = sem_t if sem_t is not None else out_ts[0]
        key = "dma_" + st.dk
        if key not in self.sems:
            self._mk_sem(key)
        self._wait(q, self._deps(in_ts, out_ts))
        inst = self.eng[q].dma_start(out=out_ap, in_=in_ap)
        self.cnt[key] += 16
        v = self.cnt[key]
        inst.then_inc(self.sems[key], 16)
        self.ninst += 1
        for t in in_ts:
            t.r[key] = v
        for t in out_ts:
            t.w = (key, v)
            t.r = {}
        return inst

    def barrier(self):
        allk = {k: v for k, v in self.cnt.items() if v > 0}
        for e in self.eng:
            self._wait(e, allk)

    def finish(self, outs, q="sp"):
        deps = {}
        for t in outs:
            if t.w is not None:
                k, v = t.w
                deps[k] = max(deps.get(k, 0), v)
        self._wait(q, deps)


OUTSTAGE = 3
GDNSTAGE = 9
GDN_G = 2
GDN_POOL = False
GDN_BF16INV = True
GSUB = 9
RGDBG = 0
PHASES = set(['N1', 'MLA', 'RG', 'GDN', 'MERGE', 'FFN'])


def build(nseq, nlayer=L_ALL, dbg=False):
    nc = bass.Bass("TRN2", target_bir_lowering=False)
    P = Prog(nc)
    L = L_ALL
    xin = P.dram("xin", [nseq, T, D], F32, "ExternalInput")
    cT = P.dram("cT", [128, 8, nseq], F32, "ExternalInput")
    wA = P.dram("wA", [L, 128, 8, NA], F32, "ExternalInput")
    wUQ = P.dram("wUQ", [L, 128, 3, 1536], F32, "ExternalInput")
    wKN = P.dram("wKN", [L, 128, 2, 768], F32, "ExternalInput")
    wV = P.dram("wV", [L, 128, 2, 512], F32, "ExternalInput")
    wRG = P.dram("wRG", [L, 128, 16, 128], F32, "ExternalInput")
    wBR = P.dram("wBR", [L, 128, 12, 1024], F32, "ExternalInput")
    wO = P.dram("wO", [L, 128, 8, 1024], F32, "ExternalInput")
    wUP = P.dram("wUP", [L, 128, 8, 2 * DFF], F32, "ExternalInput")
    wDN = P.dram("wDN", [L, 128, NJ, 1024], F32, "ExternalInput")
    wMOD = P.dram("wMOD", [L, 128, 8, 6144], F32, "ExternalInput")
    vecs_d = P.dram("vecs", [128, L, NV], F32, "ExternalInput")
    consts_d = P.dram("consts", [128, NC], F32, "ExternalInput")
    ropet = P.dram("ropet", [2, 96, T], F32, "ExternalInput")
    yout = P.dram("yout", [nseq, T, D], F32, "ExternalOutput")
    xT_d = nc.dram_tensor("xT_scr", [128, 8, T], F32, kind="Internal").ap()
    xT_tk = [[Tk("xT_%d_%d" % (c, tb)) for tb in range(NTB)] for c in range(8)]
    dbg_d = None
    if dbg:
        dbg_d = P.dram("dbg", [3, 128, 4, T], F32, "ExternalOutput")

    V = nc.vector
    A = nc.scalar
    PE = nc.tensor
    G = nc.gpsimd

    with P.es:
        base = P.es
        consts = P.sbuf("consts", [128, NC], F32)
        vecs = P.sbuf("vecs", [128, L, NV], F32)
        identb = P.sbuf("identb", [128, 256], BF16)
        hT = P.sbuf("hT", [128, 8, T], BF16)
        mod = P.sbuf("mod", [128, L, 48, nseq], F32)
        gs1 = P.sbuf("gs1", [128, L, 8, nseq], F32)
        gs2 = P.sbuf("gs2", [128, L, 8, nseq], F32)
        cneg = P.sbuf("cneg", [128, L, 8], F32)
        cneg2 = P.sbuf("cneg2", [128, L, 8], F32)
        negA = P.sbuf("negA", [128, L, 128], F32)
        NSLOT = 3
        wslots = [P.sbuf("wslot%d" % i, [128, 4096], BF16) for i in range(NSLOT)]
        wctr = [0]
        pbank = []
        for i in range(8):
            t = P.es.enter_context(nc.psum_tensor("psb%d" % i, [128, 512], F32))
            pbank.append((t, [Tk("psq%d_%d" % (i, q)) for q in range(4)]))
        pctr = [0]
        qctr = [0]

        def nextbank(pool=None):
            if pool is None:
                b = pbank[pctr[0] % 8]
            else:
                b = pbank[pool[pctr[0] % len(pool)]]
            pctr[0] += 1
            return b

        def nextq():
            t, qs = pbank[pctr[0] % 8]
            pctr[0] += 1
            return t[:, 0:128], qs[0]

        def nextq2():
            t, qs = pbank[pctr[0] % 8]
            pctr[0] += 1
            return t[:, 0:256], qs[0]

        def cst(name, w=128):
            o = COFF[name]
            return consts[:, o:o + w]

        def vec(l, name, j=0, w=1):
            o, _ = VOFF[name]
            return vecs[:, l, o + j:o + j + w]

        def wslot():
            s = wslots[wctr[0] % NSLOT]
            wctr[0] += 1
            return s

        def wload(slot, dst_ap, src_tk, src_ap):
            P.dma([slot], dst_ap, [src_tk], src_ap, q="pool", sem_t=slot)

        def mm(ps_ap, ps_tks, lhsT, rhs, start, stop, reads):
            P.op("pe", lambda: PE.matmul(ps_ap, lhsT=lhsT, rhs=rhs, start=start, stop=stop), reads, ps_tks)

        def act(out_ap, in_ap, func, reads, writes, bias=None, scale=None):
            kw = {}
            if bias is not None:
                kw["bias"] = bias
            if scale is not None:
                kw["scale"] = scale
            P.op("act", lambda: A.activation(out=out_ap, in_=in_ap, func=func, **kw), reads, writes)

        def vtt(out_ap, in0, in1, op, reads, writes, e="dve"):
            eng = V if e == "dve" else G
            P.op(e, lambda: eng.tensor_tensor(out=out_ap, in0=in0, in1=in1, op=op), reads, writes)

        def vts(out_ap, in0, s1, s2, op0, op1, reads, writes, e="dve"):
            eng = V if e == "dve" else G
            if op1 is None:
                P.op(e, lambda: eng.tensor_scalar(out=out_ap, in0=in0, scalar1=s1, scalar2=None, op0=op0), reads, writes)
            else:
                P.op(e, lambda: eng.tensor_scalar(out=out_ap, in0=in0, scalar1=s1, scalar2=s2, op0=op0, op1=op1), reads, writes)

        def vstt(out_ap, in0, scalar, in1, op0, op1, reads, writes, e="dve"):
            eng = V if e == "dve" else G
            P.op(e, lambda: eng.scalar_tensor_tensor(out=out_ap, in0=in0, scalar=scalar, in1=in1, op0=op0, op1=op1), reads, writes)

        def vcopy(out_ap, in_ap, reads, writes, e="dve"):
            eng = V if e == "dve" else G
            P.op(e, lambda: eng.tensor_copy(out=out_ap, in_=in_ap), reads, writes)

        def vmemset(ap, val, writes, e="dve"):
            eng = V if e == "dve" else G
            P.op(e, lambda: eng.memset(ap, val), [], writes)

        def rsqrt_from(out_ap, in_ap, scale, reads, writes):
            act(out_ap, in_ap, AF.Ln, reads, writes, bias=epsc[:, 0:1], scale=scale)
            act(out_ap, out_ap, AF.Exp, writes, writes, scale=-0.5)

        def softplus_acc(out_ap, x_ap, t1, t2, t3, rd, wr):
            vts(t1, x_ap, -1.0, None, ALU.mult, None, rd, wr)
            vtt(t1, t1, x_ap, ALU.max, rd + wr, wr)
            act(t1, t1, AF.Exp, wr, wr, scale=-1.0)
            vts(t2, t1, 2.0, None, ALU.add, None, wr, wr)
            P.op("dve", lambda: V.reciprocal(out=t2, in_=t2), wr, wr)
            vtt(t1, t1, t2, ALU.mult, wr, wr)
            vtt(t2, t1, t1, ALU.mult, wr, wr)
            vts(t3, t2, 1.0 / 11, 1.0 / 9, ALU.mult, ALU.add, wr, wr)
            for cf in (1.0 / 7, 1.0 / 5, 1.0 / 3, 1.0):
                vtt(t3, t3, t2, ALU.mult, wr, wr)
                vts(t3, t3, cf, None, ALU.add, None, wr, wr)
            vtt(t3, t3, t1, ALU.mult, wr, wr)
            vts(t1, x_ap, 0.0, None, ALU.max, None, rd + wr, wr)
            vstt(out_ap, t3, 2.0, t1, ALU.mult, ALU.add, wr, wr)

        epsc = P.sbuf("epsc", [128, 1], F32)
        vmemset(epsc[:], EPS, [epsc])
        P.dma([consts], consts[:], [consts_d], consts_d.t)
        P.dma([vecs], vecs[:], [vecs_d], vecs_d.t)
        vcopy(identb[:], cst("ident", 256), [consts], [identb])
        with contextlib.ExitStack() as sc:
            craw = P.sbuf("craw", [128, 8, nseq], F32, sc)
            csil = P.sbuf("csil", [128, 8, nseq], BF16, sc)
            tmpm = P.sbuf("tmpm", [128, 8, nseq], F32, sc)
            spx = P.sbuf("spx", [128, 8], F32, sc)
            spt = P.sbuf("spt", [128, 3, 8], F32, sc)
            P.dma([craw], craw[:], [cT], cT.t)
            act(csil[:], craw[:], AF.Silu, [craw], [csil])
            for l in range(nlayer):
                for pn in range(12):
                    sl = wslot()
                    wv = sl[:, 0:4096].rearrange("p (k n) -> p k n", k=8)
                    wload(sl, wv, wMOD, wMOD.t[l, :, :, pn * 512:(pn + 1) * 512])
                    for jj in range(4):
                        j = pn * 4 + jj
                        pt, pq = nextbank()
                        for k in range(8):
                            mm(pt[:, 0:nseq], pq, wv[:, k, jj * 128:(jj + 1) * 128], csil[:, k, :], k == 0, k == 7, [sl, csil])
                        act(mod[:, l, j, :], pt[:, 0:nseq], AF.Identity, pq, [mod], bias=vec(l, "bmod", j))
                for (gs, lnn, j0) in ((gs1, "ln1", 8), (gs2, "ln2", 32)):
                    vts(tmpm[:], mod[:, l, j0:j0 + 8, :], 1.0, None, ALU.add, None, [mod], [tmpm])
                    for s in range(nseq):
                        vtt(gs[:, l, :, s], tmpm[:, :, s], vec(l, lnn, 0, 8), ALU.mult, [tmpm, vecs], [gs])
                vts(spx[:], vec(l, "rglam", 0, 8), -1.0, None, ALU.mult, None, [vecs], [spx])
                softplus_acc(cneg[:, l, :], spx[:], spt[:, 0, :], spt[:, 1, :], spt[:, 2, :], [spx], [spt, cneg])
                vts(cneg2[:, l, :], cneg[:, l, :], -16.0, None, ALU.mult, None, [cneg], [cneg2])
                vts(cneg[:, l, :], cneg[:, l, :], -8.0, None, ALU.mult, None, [cneg], [cneg])
                act(negA[:, l, :], vec(l, "galog", 0, 128), AF.Exp, [vecs], [negA])
                vts(negA[:, l, :], negA[:, l, :], -1.0, None, ALU.mult, None, [negA], [negA])
            P.barrier()

        def proj_fm(wv, kc, M, rhs_fn, rhs_tks, slot, consume):
            for tb in range(NTB):
                pt, pq = nextbank()
                for k in range(kc):
                    mm(pt[0:M, :], pq, wv[:, k, 0:M], rhs_fn(k, tb), k == 0, k == kc - 1, [slot] + rhs_tks)
                consume(tb, pt, pq)

        def hrhs(k, tb):
            return hT[:, k, tb * 512:(tb + 1) * 512]

        def load_xblock(xb, tb):
            for c in range(8):
                pass
            P.dma([xb], xb[:], [xT_tk[c][tb] for c in range(8)], xT_d[:, :, tb * 512:(tb + 1) * 512])

        def store_xblock(xb, tb):
            P.dma([xT_tk[c][tb] for c in range(8)], xT_d[:, :, tb * 512:(tb + 1) * 512], [xb], xb[:], sem_t=xb)

        def norm_block(xb, tb, l, s, which, sq, rstd, tmp):
            gs = gs1 if which == 1 else gs2
            shj = 0 if which == 1 else 24
            vtt(sq[:], xb[:], xb[:], ALU.mult, [xb], [sq])
            pt, pq = nextbank()
            for c in range(8):
                mm(pt[:, :], pq, cst("ones"), sq[:, c, :], c == 0, c == 7, [consts, sq])
            rsqrt_from(rstd[:], pt[:, :], 1.0 / D, pq, [rstd])
            for c in range(8):
                vtt(tmp[:], xb[:, c, :], rstd[:], ALU.mult, [xb, rstd], [tmp])
                act(hT[:, c, tb * 512:(tb + 1) * 512], tmp[:], AF.Identity, [tmp, gs, mod], [hT],
                    bias=mod[:, l, shj + c, s:s + 1], scale=gs[:, l, c, s:s + 1])

        for s in range(nseq if 'NOSEQ' not in PHASES else 0):
            with contextlib.ExitStack() as sc:
                xtok = [P.sbuf("xtok", [128, D], F32, sc) for _ in range(2)]
                xblk = [P.sbuf("xblk", [128, 8, 128], F32, sc) for _ in range(2)]
                for tl in range(NTL if 'NOL' not in PHASES else 0):
                    xt = xtok[tl % 2]
                    xk = xblk[tl % 2]
                    P.dma([xt], xt[:], [xin], xin.t[s, tl * 128:(tl + 1) * 128, :])
                    for half in range(2):
                        pt, pq = nextbank()
                        for q in range(4):
                            c = half * 4 + q
                            P.op("pe", lambda: PE.transpose(out=pt[:, q * 128:(q + 1) * 128], in_=xt[:, c * 128:(c + 1) * 128],
                                                            identity=cst("ident")), [xt, consts], pq)
                        act(xk[:, half * 4:half * 4 + 4, :], pt[:, :].rearrange("p (q n) -> p q n", q=4), AF.Copy, pq, [xk])
                    tb = tl // 4
                    P.dma([xT_tk[c][tb] for c in range(8)], xT_d[:, :, tl * 128:(tl + 1) * 128], [xk], xk[:], sem_t=xk)
                P.barrier()

            for l in range(nlayer):
                with contextlib.ExitStack() as sc:
                    xb = P.sbuf("xb", [128, 8, 512], F32, sc)
                    sq = P.sbuf("sq", [128, 8, 512], F32, sc)
                    rstd = P.sbuf("rstd", [128, 512], F32, sc)
                    tmp = P.sbuf("tmp", [128, 512], F32, sc)
                    for tb in range(NTB if 'N1' in PHASES else 0):
                        load_xblock(xb, tb)
                        norm_block(xb, tb, l, s, 1, sq, rstd, tmp)
                    P.barrier()

                mixsc = contextlib.ExitStack()
                obT = []
                obT.append(P.sbuf("obT_mla", [128, 4, T], BF16, mixsc))
                with contextlib.ExitStack() as sc:
                    if 'MLA' in PHASES:
                        emit_mla(P, nc, locals(), sc, l, s)
                    P.barrier()
                obT.append(P.sbuf("obT_rg", [128, 4, T], BF16, mixsc))
                with contextlib.ExitStack() as sc:
                    if 'RG' in PHASES:
                        emit_rg(P, nc, locals(), sc, l, s)
                    P.barrier()
                obT.append(P.sbuf("obT_gdn", [128, 4, T], BF16, mixsc))
                with contextlib.ExitStack() as sc:
                    if 'GDN' in PHASES:
                        emit_gdn(P, nc, locals(), sc, l, s)
                    P.barrier()
                if dbg and l == 0 and s == 0:
                    with contextlib.ExitStack() as sc:
                        dtmp = P.sbuf("dtmp", [128, 4, T], F32, sc)
                        for b in range(3):
                            vcopy(dtmp[:], obT[b][:], [obT[b]], [dtmp])
                            P.dma([dbg_d], dbg_d.t[b], [dtmp], dtmp[:], sem_t=dtmp)
                        P.barrier()
                with contextlib.ExitStack() as sc:
                    if 'MERGE' in PHASES:
                        emit_merge(P, nc, locals(), sc, l, s)
                    P.barrier()
                mixsc.close()
                with contextlib.ExitStack() as sc:
                    if 'FFN' in PHASES:
                        emit_ffn(P, nc, locals(), sc, l, s)
                    P.barrier()

            with contextlib.ExitStack() as sc:
                xb = P.sbuf("xb", [128, 8, 512], F32, sc)
                sq = P.sbuf("sq", [128, 8, 512], F32, sc)
                rstd = P.sbuf("rstd", [128, 512], F32, sc)
                ytok = [P.sbuf("ytok", [128, D], F32, sc) for _ in range(2)]
                yc = 0
                for tb in range(NTB if 'NOOUT' not in PHASES else 0):
                    load_xblock(xb, tb)
                    vtt(sq[:], xb[:], xb[:], ALU.mult, [xb], [sq])
                    if OUTSTAGE >= 1:
                        pt, pq = nextbank()
                        for c in range(8):
                            mm(pt[:, :], pq, cst("ones"), sq[:, c, :], c == 0, c == 7, [consts, sq])
                        if OUTSTAGE >= 2:
                            rsqrt_from(rstd[:], pt[:, :], 1.0 / D, pq, [rstd])
                        else:
                            act(rstd[:], pt[:, :], AF.Copy, pq, [rstd])
                        if OUTSTAGE >= 3:
                            for c in range(8):
                                vstt(sq[:, c, :], xb[:, c, :], vec(0, "fng", c), rstd[:], ALU.mult, ALU.mult, [xb, rstd, vecs], [sq])
                    for t4 in range(4):
                        yt = ytok[yc % 2]
                        yc += 1
                        for half in range(2):
                            pt, pq = nextbank()
                            for q in range(4):
                                c = half * 4 + q
                                P.op("pe", lambda: PE.transpose(out=pt[:, q * 128:(q + 1) * 128], in_=sq[:, c, t4 * 128:(t4 + 1) * 128],
                                                                identity=cst("ident")), [sq, consts], pq)
                            act(yt[:, half * 512:(half + 1) * 512], pt[:, :], AF.Copy, pq, [yt])
                        tl = tb * 4 + t4
                        P.dma([yout], yout.t[s, tl * 128:(tl + 1) * 128, :], [yt], yt[:], sem_t=yt)
                P.barrier()
        outs = [yout] + ([dbg_d] if dbg else [])
        P.finish(outs)
        P.barrier()
    return nc, P


def _h(env):
    class E:
        pass
    e = E()
    e.__dict__.update(env)
    return e


def emit_mla(P, nc, env, sc, l, s):
    e = _h(env)
    hT, obT, consts, identb = e.hT, e.obT, e.consts, e.identb
    mm, act, vtt, vts, vstt, vcopy, vmemset = e.mm, e.act, e.vtt, e.vts, e.vstt, e.vcopy, e.vmemset
    nextbank, cst, vec, wslot, wload, rsqrt_from, hrhs = e.nextbank, e.cst, e.vec, e.wslot, e.wload, e.rsqrt_from, e.hrhs
    wA, wUQ, wKN, wV, ropet, vecs = e.wA, e.wUQ, e.wKN, e.wV, e.ropet, e.vecs
    ob = obT[0]
    cosT = P.sbuf("cosT", [96, T], BF16, sc)
    sinT = P.sbuf("sinT", [96, T], BF16, sc)
    cq = P.sbuf("cq", [128, 3, T], BF16, sc)
    ckv = P.sbuf("ckv", [128, 2, T], BF16, sc)
    krT = P.sbuf("krT", [96, T], BF16, sc)
    Vall = P.sbuf("Vall", [128, NTL, 8, 128], BF16, sc)
    qhs = [P.sbuf("qh", [96, T], BF16, sc) for _ in range(2)]
    khs = [P.sbuf("kh", [96, T], BF16, sc) for _ in range(2)]
    PTs = [P.sbuf("PT", [128, 512], BF16, sc) for _ in range(4)]
    sqs = [P.sbuf("sqm", [128, 512], F32, sc) for _ in range(3)]
    rstd = P.sbuf("rstdm", [128, 512], F32, sc)
    t1 = P.sbuf("t1m", [128, 512], F32, sc)
    t2 = P.sbuf("t2m", [128, 512], F32, sc)
    dsh = P.sbuf("dsh", [128, 512], F32, sc)
    P.dma([cosT], cosT[:], [ropet], ropet.t[0], q="pool")
    P.dma([sinT], sinT[:], [ropet], ropet.t[1], q="pool")
    vmemset(Vall[:], 1.0, [Vall])

    for (dst, col0, nch, gname) in ((cq, 0, 3, "qg"), (ckv, 384, 2, "kvg")):
        sl = wslot()
        wv = sl[:, 0:8 * nch * 128].rearrange("p (k n) -> p k n", k=8)
        wload(sl, wv, wA, wA.t[l, :, :, col0:col0 + nch * 128])
        for tb in range(NTB):
            pcs = []
            for c in range(nch):
                pt, pq = nextbank()
                for k in range(8):
                    mm(pt[:, :], pq, wv[:, k, c * 128:(c + 1) * 128], hrhs(k, tb), k == 0, k == 7, [sl, hT])
                act(sqs[c][:], pt[:, :], AF.Square, pq, [sqs[c]])
                pcs.append((pt, pq))
            pt2, pq2 = nextbank()
            for c in range(nch):
                mm(pt2[:, :], pq2, cst("ones"), sqs[c][:], c == 0, c == nch - 1, [consts, sqs[c]])
            rsqrt_from(rstd[:], pt2[:, :], 1.0 / (nch * 128), pq2, [rstd])
            for c in range(nch):
                pt, pq = pcs[c]
                vtt(t1[:], pt[:, :], rstd[:], ALU.mult, pq + [rstd], [t1])
                act(dst[:, c, tb * 512:(tb + 1) * 512], t1[:], AF.Identity, [t1, vecs], [dst], scale=vec(l, gname, c))
    sl = wslot()
    wv = sl[:, 0:8 * 192].rearrange("p (k n) -> p k n", k=8)
    wload(sl, wv, wA, wA.t[l, :, :, KR0:KR0 + 192])
    for tb in range(NTB):
        blk = slice(tb * 512, (tb + 1) * 512)
        pa, qa = nextbank()
        pb, qb_ = nextbank()
        for k in range(8):
            mm(pa[0:96, :], qa, wv[:, k, 0:96], hrhs(k, tb), k == 0, k == 7, [sl, hT])
        for k in range(8):
            mm(pb[0:96, :], qb_, wv[:, k, 96:192], hrhs(k, tb), k == 0, k == 7, [sl, hT])
        vtt(t1[0:96, :], pa[0:96, :], cosT[:, blk], ALU.mult, qa + [cosT], [t1])
        vtt(t2[0:96, :], pb[0:96, :], sinT[:, blk], ALU.mult, qb_ + [sinT], [t2])
        vtt(krT[:, blk], t1[0:96, :], t2[0:96, :], ALU.add, [t1, t2], [krT])
    sl = wslot()
    wv = sl[:, 0:1024].rearrange("p (k n) -> p k n", k=2)
    wload(sl, wv, wV, wV.t[l])
    for tl in range(NTL):
        pt, pq = nextbank()
        for c in range(2):
            mm(pt[:, :], pq, ckv[:, c, tl * 128:(tl + 1) * 128], wv[:, c, :], c == 0, c == 1, [ckv, sl])
        pv = pt[:, :].rearrange("p (h d) -> p h d", h=8)
        for par in range(2):
            act(Vall[:, tl, par::2, par * 64:par * 64 + 64], pv[:, par::2, :], AF.Copy, pq, [Vall])
    scale = 96.0 ** -0.5
    for h in range(8):
        qh, kh = qhs[h % 2], khs[h % 2]
        sl = wslot()
        wq = sl[:, 0:3 * 192].rearrange("p (k n) -> p k n", k=3)
        wload(sl, wq[:, :, 0:96], wUQ, wUQ.t[l, :, :, h * 96:(h + 1) * 96])
        wload(sl, wq[:, :, 96:192], wUQ, wUQ.t[l, :, :, 768 + h * 96:768 + (h + 1) * 96])
        wk = sl[:, 1024:1024 + 2 * 96].rearrange("p (k n) -> p k n", k=2)
        wload(sl, wk, wKN, wKN.t[l, :, :, h * 96:(h + 1) * 96])
        for tb in range(NTB):
            blk = slice(tb * 512, (tb + 1) * 512)
            pa, qa = nextbank()
            pb, qb_ = nextbank()
            for c in range(3):
                mm(pa[0:96, :], qa, wq[:, c, 0:96], cq[:, c, blk], c == 0, c == 2, [sl, cq])
            for c in range(3):
                mm(pb[0:96, :], qb_, wq[:, c, 96:192], cq[:, c, blk], c == 0, c == 2, [sl, cq])
            vtt(t1[0:96, :], pa[0:96, :], cosT[:, blk], ALU.mult, qa + [cosT], [t1])
            vtt(t2[0:96, :], pb[0:96, :], sinT[:, blk], ALU.mult, qb_ + [sinT], [t2])
            vtt(qh[:, blk], t1[0:96, :], t2[0:96, :], ALU.add, [t1, t2], [qh])
            pk, qk = nextbank()
            for c in range(2):
                mm(pk[0:96, :], qk, wk[:, c, 0:96], ckv[:, c, blk], c == 0, False, [sl, ckv])
            mm(pk[0:96, :], qk, identb[0:96, 0:96], krT[:, blk], False, True, [identb, krT])
            act(kh[:, blk], pk[0:96, :], AF.Copy, qk, [kh])
        par = h % 2
        num = slice(par * 64, par * 64 + 64)
        den = slice((1 - par) * 64, (1 - par) * 64 + 64)
        pti = 0
        for qb in range(NTB):
            qblk = slice(qb * 512, (qb + 1) * 512)
            po, qo = nextbank([0, 1])
            LOOK = 2
            pend = []
            for kt in range(NTL + LOOK):
                if kt < NTL:
                    ps_, qs_ = nextbank([2, 3, 4, 5, 6, 7])
                    mm(ps_[:, :], qs_, kh[:, kt * 128:(kt + 1) * 128], qh[:, qblk], True, True, [kh, qh])
                    PT = PTs[pti % 4]
                    pti += 1
                    act(PT[:], ps_[:, :], AF.Exp, qs_, [PT], scale=scale)
                    pend.append(PT)
                if kt >= LOOK:
                    k2 = kt - LOOK
                    PT2 = pend[k2]
                    mm(po[:, :], qo, Vall[:, k2, h, :], PT2[:], k2 == 0, k2 == NTL - 1, [Vall, PT2])
            act(dsh[num, :], po[den, :], AF.Copy, qo, [dsh])
            P.op("dve", lambda: nc.vector.reciprocal(out=dsh[num, :], in_=dsh[num, :]), [dsh], [dsh])
            vtt(ob[num, h // 2, qblk], po[num, :], dsh[num, :], ALU.mult, qo + [dsh], [ob])


def emit_rg(P, nc, env, sc, l, s):
    e = _h(env)
    hT, obT, consts = e.hT, e.obT, e.consts
    mm, act, vtt, vts, vstt, vcopy, vmemset = e.mm, e.act, e.vtt, e.vts, e.vstt, e.vcopy, e.vmemset
    nextbank, cst, vec, wslot, wload, hrhs = e.nextbank, e.cst, e.vec, e.wslot, e.wload, e.hrhs
    wA, wRG, vecs, cneg, cneg2 = e.wA, e.wRG, e.vecs, e.cneg, e.cneg2
    ob = obT[1]
    xpad = P.sbuf("xpad", [128, T + 4], F32, sc)
    xc = P.sbuf("xc", [128, T], F32, sc)
    xcb = P.sbuf("xcb", [128, T], BF16, sc)
    R = P.sbuf("R", [128, T], F32, sc)
    I = P.sbuf("I", [128, T], F32, sc)
    Ab = P.sbuf("Ab", [128, T], F32, sc)
    HF = P.sbuf("HF", [128, T], F32, sc)
    HB = P.sbuf("HB", [128, T], F32, sc)
    gt = P.sbuf("gt", [128, T], F32, sc)
    wrg = P.sbuf("wrg", [128, 16, 128], BF16, sc)
    P.dma([wrg], wrg[:], [wRG], wRG.t[l], q="pool")
    vmemset(xpad[:], 0.0, [xpad])
    for c in range(4):
        sl = wslot()
        wv = sl[:, 0:2048].rearrange("p (k n) -> p k n", k=8)
        wload(sl, wv[:, :, 0:128], wA, wA.t[l, :, :, C_RGX + c * 128:C_RGX + (c + 1) * 128])
        wload(sl, wv[:, :, 128:256], wA, wA.t[l, :, :, C_RGG + c * 128:C_RGG + (c + 1) * 128])
        for tb in range(NTB):
            pt, pq = nextbank()
            for k in range(8):
                mm(pt[:, :], pq, wv[:, k, 0:128], hrhs(k, tb), k == 0, k == 7, [sl, hT])
            act(xpad[:, 1 + tb * 512:1 + (tb + 1) * 512], pt[:, :], AF.Copy, pq, [xpad])
        for tb in range(NTB):
            pt, pq = nextbank()
            for k in range(8):
                mm(pt[:, :], pq, wv[:, k, 128:256], hrhs(k, tb), k == 0, k == 7, [sl, hT])
            act(gt[:, tb * 512:(tb + 1) * 512], pt[:, :], AF.Copy, pq, [gt])
        vtt(HB[:], gt[:], gt[:], ALU.mult, [gt], [HB])
        vts(HB[:], HB[:], 0.044715, 1.0, ALU.mult, ALU.add, [HB], [HB])
        vtt(HB[:], HB[:], gt[:], ALU.mult, [HB, gt], [HB])
        act(HB[:], HB[:], AF.Sigmoid, [HB], [HB], scale=2.0 * math.sqrt(2.0 / math.pi))
        vtt(gt[:], gt[:], HB[:], ALU.mult, [gt, HB], [gt])
        act(xc[:], xpad[:, 0:T], AF.Identity, [xpad, vecs], [xc], bias=vec(l, "rgcb", c), scale=vec(l, "rgcw", c * 4 + 0))
        for k in range(1, 4):
            vstt(xc[:], xpad[:, k:k + T], vec(l, "rgcw", c * 4 + k), xc[:], ALU.mult, ALU.add, [xpad, xc, vecs], [xc])
        vcopy(xcb[:], xc[:], [xc], [xcb])
        if RGDBG == 1:
            vcopy(ob[:, c, :], xc[:], [xc], [ob])
            continue
        if RGDBG == 2:
            vcopy(ob[:, c, :], gt[:], [gt], [ob])
            continue
        for d in range(2):
            for (ai, dst, bname) in ((0, R, "rgba"), (1, I, "rgbi")):
                wi = (d * 2 + ai) * 4 + c
                for tb in range(NTB):
                    blk = slice(tb * 512, (tb + 1) * 512)
                    pt, pq = nextbank()
                    mm(pt[:, :], pq, wrg[:, wi, :], xcb[:, blk], True, True, [wrg, xcb])
                    act(dst[:, blk], pt[:, :], AF.Sigmoid, pq + [vecs], [dst], bias=vec(l, bname, d * 4 + c))
            if RGDBG == 3 and d == 0:
                vcopy(ob[:, c, :], R[:], [R], [ob])
            if RGDBG == 4 and d == 0:
                vcopy(ob[:, c, :], I[:], [I], [ob])
            act(Ab[:], R[:], AF.Exp, [R, cneg], [Ab], scale=cneg[:, l, d * 4 + c:d * 4 + c + 1])
            if RGDBG == 5 and d == 0:
                vcopy(ob[:, c, :], Ab[:], [Ab], [ob])
            act(R[:], R[:], AF.Exp, [R, cneg2], [R], scale=cneg2[:, l, d * 4 + c:d * 4 + c + 1])
            vts(R[:], R[:], -1.0, 1.0, ALU.mult, ALU.add, [R], [R])
            vts(R[:], R[:], 0.0, None, ALU.max, None, [R], [R])
            act(R[:], R[:], AF.Sqrt, [R], [R])
            edge = 0 if d == 0 else T - 1
            vmemset(R[:, edge:edge + 1], 1.0, [R])
            vtt(I[:], I[:], xc[:], ALU.mult, [I, xc], [I])
            vtt(I[:], I[:], R[:], ALU.mult, [I, R], [I])
            if d == 0:
                P.op("dve", lambda: nc.vector.tensor_tensor_scan(out=HF[:], data0=Ab[:], data1=I[:], initial=0.0,
                                                                 op0=ALU.mult, op1=ALU.add), [Ab, I], [HF])
            else:
                P.op("dve", lambda: nc.vector.tensor_tensor_scan(out=HB[:, ::-1], data0=Ab[:, ::-1], data1=I[:, ::-1], initial=0.0,
                                                                 op0=ALU.mult, op1=ALU.add), [Ab, I], [HB])
        if RGDBG in (3, 4, 5):
            continue
        if RGDBG == 6:
            vcopy(ob[:, c, :], HF[:], [HF], [ob])
            continue
        if RGDBG == 7:
            vcopy(ob[:, c, :], HB[:], [HB], [ob])
            continue
        vtt(HF[:], HF[:], HB[:], ALU.add, [HF, HB], [HF])
        vtt(ob[:, c, :], HF[:], gt[:], ALU.mult, [HF, gt], [ob])


def emit_gdn(P, nc, env, sc, l, s):
    e = _h(env)
    hT, obT, consts, identb = e.hT, e.obT, e.consts, e.identb
    mm, act, vtt, vts, vstt, vcopy, vmemset = e.mm, e.act, e.vtt, e.vts, e.vstt, e.vcopy, e.vmemset
    nextbank, nextq, cst, vec, wslot, wload, rsqrt_from, hrhs = e.nextbank, e.nextq, e.cst, e.vec, e.wslot, e.wload, e.rsqrt_from, e.hrhs
    nextq2 = e.nextq2
    wA, vecs, negA, epsc = e.wA, e.vecs, e.negA, e.epsc
    ob = obT[2]
    qT = P.sbuf("gqT", [128, T], BF16, sc)
    kT = P.sbuf("gkT", [128, T], BF16, sc)
    ktok = P.sbuf("gktok", [128, NTL, 128], BF16, sc)
    vtok = P.sbuf("gvtok", [128, NTL, 128], BF16, sc)
    otok = P.sbuf("gotok", [128, NTL, 128], F32, sc)
    gtok = P.sbuf("ggtok", [128, NTL, 8], F32, sc)
    btok = P.sbuf("gbtok", [128, NTL, 8], F32, sc)
    rtmp = P.sbuf("grtmp", [128, 512], F32, sc)
    S = [P.sbuf("gS", [128, 128], F32, sc) for _ in range(2)]
    Sb = [P.sbuf("gSb", [128, 128], BF16, sc) for _ in range(2)]
    szs = [P.sbuf("gsz", [128, 128], F32, sc) for _ in range(3)]
    ons = [P.sbuf("gon", [128, 128], F32, sc) for _ in range(3)]
    ssqs = [P.sbuf("gssq", [128, 4], F32, sc) for _ in range(3)]

    sl = wslot()
    wab = sl[:, 0:128].rearrange("p (k n) -> p k n", k=8)
    wload(sl, wab, wA, wA.t[l, :, :, C_AB:C_AB + 16])
    for tl in range(NTL):
        pt, pq = nextbank()
        for k in range(8):
            mm(pt[:, 0:16], pq, hT[:, k, tl * 128:(tl + 1) * 128], wab[:, k, :], k == 0, k == 7, [hT, sl])
        vtt(gtok[:, tl, :], pt[:, 0:8], vec(l, "gdtb", 0, 8), ALU.add, pq + [vecs], [gtok])
        act(btok[:, tl, :], pt[:, 8:16], AF.Sigmoid, pq, [btok])
    gfl = gtok[:].rearrange("p a b -> p (a b)")
    e.softplus_acc(rtmp[:, 0:128], gfl, rtmp[:, 128:256], rtmp[:, 256:384], rtmp[:, 384:512], [gtok], [rtmp])
    vtt(gfl, rtmp[:, 0:128], negA[:, l, :], ALU.mult, [rtmp, negA], [gtok])

    ident, ones, negones, bd = cst("ident"), cst("ones"), cst("negones"), cst("bd16")
    offmasks = [cst("m32"), cst("m64"), cst("m128")]
    G = GDN_G
    NH = G + 4
    PL = "pool" if GDN_POOL else "dve"

    for hh in range(4):
        with contextlib.ExitStack() as sc2:
            xpads = [P.sbuf("gxpad", [128, T + 4], F32, sc2) for _ in range(2)]
            xc = P.sbuf("gxc", [128, T], F32, sc2)
            rts = [rtmp] + [P.sbuf("grt", [128, 512], F32, sc2) for _ in range(3)]
            for xp in xpads:
                vmemset(xp[:], 0.0, [xp])

            def qkv_proj(role):
                ch = role * 4 + hh
                col0 = C_GQKV + ch * 128
                xpad = xpads[role % 2]
                sl = wslot()
                wv = sl[:, 0:1024].rearrange("p (k n) -> p k n", k=8)
                wload(sl, wv, wA, wA.t[l, :, :, col0:col0 + 128])
                for tb in range(NTB):
                    pt, pq = nextbank()
                    for k in range(8):
                        mm(pt[:, :], pq, wv[:, k, :], hrhs(k, tb), k == 0, k == 7, [sl, hT])
                    act(xpad[:, 1 + tb * 512:1 + (tb + 1) * 512], pt[:, :], AF.Copy, pq, [xpad])

            def qkv_post(role):
                ch = role * 4 + hh
                xpad = xpads[role % 2]
                act(xc[:], xpad[:, 0:T], AF.Identity, [xpad, vecs], [xc], scale=vec(l, "gcw", ch * 4 + 0))
                for k in range(1, 4):
                    vstt(xc[:], xpad[:, k:k + T], vec(l, "gcw", ch * 4 + k), xc[:], ALU.mult, ALU.add, [xpad, xc, vecs], [xc])
                act(xc[:], xc[:], AF.Silu, [xc], [xc])
                if role < 2:
                    blks = [slice(tb * 512, (tb + 1) * 512) for tb in range(NTB)]
                    for tb in range(NTB):
                        vtt(rts[tb][:], xc[:, blks[tb]], xc[:, blks[tb]], ALU.mult, [xc], [rts[tb]])
                    pts = []
                    for tb in range(NTB):
                        pt, pq = nextbank()
                        mm(pt[:, :], pq, ones, rts[tb][:], True, True, [consts, rts[tb]])
                        pts.append((pt, pq))
                    for tb in range(NTB):
                        pt, pq = pts[tb]
                        act(rts[tb][:], pt[:, :], AF.Ln, pq, [rts[tb]], bias=epsc[:, 0:1], scale=1.0)
                    for tb in range(NTB):
                        act(rts[tb][:], rts[tb][:], AF.Exp, [rts[tb]], [rts[tb]], scale=-0.5)
                    for tb in range(NTB):
                        if role == 0:
                            vstt(xc[:, blks[tb]], xc[:, blks[tb]], 128.0 ** -0.5, rts[tb][:], ALU.mult, ALU.mult, [xc, rts[tb]], [xc])
                        else:
                            vtt(xc[:, blks[tb]], xc[:, blks[tb]], rts[tb][:], ALU.mult, [xc, rts[tb]], [xc])
                    vcopy((qT if role == 0 else kT)[:], xc[:], [xc], [qT if role == 0 else kT])
                if role >= 1:
                    dst = ktok if role == 1 else vtok
                    for tl in range(NTL):
                        qa, qk = nextq()
                        mm(qa, [qk], xc[:, tl * 128:(tl + 1) * 128], ident, True, True, [xc, consts])
                        act(dst[:, tl, :], qa, AF.Copy, [qk], [dst])

            qkv_proj(0)
            for role in range(3):
                if role + 1 < 3:
                    qkv_proj(role + 1)
                qkv_post(role)
            P.barrier()
        vmemset(otok[:], 0.0, [otok])

        with contextlib.ExitStack() as sc3:
            W2 = 256
            def mk(name, n, dt=F32, w=W2):
                return [P.sbuf(name, [128, w], dt, sc3) for _ in range(n)]
            GM, decS, decI, Af = (mk(nm, G) for nm in ("GM", "decS", "decI", "Af"))
            IDT = BF16
            Ad, Aoff, Bm = (mk(nm, G, IDT) for nm in ("Ad", "Aoff", "Bm"))
            Y = [mk("Ya", G, IDT), mk("Yb", G, IDT)]
            YT = [mk("YTa", G, IDT), mk("YTb", G, IDT)]
            Pm, PTm, Zm, TTb = mk("Pm", G, IDT), mk("PTm", G, IDT), mk("Zm", G, IDT), mk("TTb", G, BF16)
            vb, kbg = mk("vb", G, BF16), mk("kbg", G, BF16)
            um, wTm, kdec, attnT = mk("um", NH), mk("wTm", NH, BF16), mk("kdec", NH, BF16), mk("attnT", NH, BF16)
            cols = mk("cols", NH, F32, 16)
            vnew = mk("vnew", NH, BF16)
            tmpo = mk("tmpo", NH)
            S2 = P.sbuf("gS2", [128, W2], F32, sc3)
            Sb2 = P.sbuf("gSb2", [128, W2], BF16, sc3)
            vmemset(S2[:], 0.0, [S2])
            vmemset(Sb2[:], 0.0, [Sb2])
            ident2, identb2 = cst("ident", 256), identb[:, 0:256]
            ones2, negones2 = cst("ones", 256), cst("negones", 256)
            mt01, negs01, negst01 = cst("mt0", 256), cst("negs0", 256), cst("negst0", 256)
            bd2 = cst("bd16", 256)
            offmasks2 = [cst("m32", 256), cst("m64", 256), cst("m128", 256)]
            hs = [slice(0, 128), slice(128, 256)]
            tn = lambda d, it: it if d == 0 else NTL - 1 - it

            def prep_gen(it, i, hi):
                ns = [tn(0, it), tn(1, it)]
                colj = [0 * 4 + hh, 1 * 4 + hh]
                tsl = [slice(n * 128, (n + 1) * 128) for n in ns]
                g_col = [gtok[:, ns[d], colj[d]:colj[d] + 1] for d in range(2)]
                b_col = [btok[:, ns[d], colj[d]:colj[d] + 1] for d in range(2)]
                cl = cols[hi]
                for d in range(2):
                    vts(GM[i][:, hs[d]], mt01[:, hs[d]], g_col[d], None, ALU.mult, None, [consts, gtok], [GM[i]])
                    vts(cl[:, d * 8 + 6:d * 8 + 8], ones[:, 0:2], g_col[d], None, ALU.mult, None, [consts, gtok], [cl])
                yield
                pD, qD = nextq2()
                for d in range(2):
                    mm(pD[:, hs[d]], [qD], GM[i][:, hs[d]], ones, d == 0, False, [GM[i], consts])
                mm(pD, [qD], negones, GM[i][:], False, False, [GM[i], consts])
                mm(pD, [qD], ident, negs01, False, True, [consts])
                act(decS[i][:], pD, AF.Exp, [qD], [decS[i]])
                yield
                pDT, qDT = nextq2()
                mm(pDT, [qDT], ones, GM[i][:], True, False, [GM[i], consts])
                for d in range(2):
                    mm(pDT[:, hs[d]], [qDT], GM[i][:, hs[d]], negones, False, False, [GM[i], consts])
                mm(pDT, [qDT], ident, negst01, False, True, [consts])
                act(decI[i][:], pDT, AF.Exp, [qDT], [decI[i]])
                yield
                pc, qc = nextq2()
                for d in range(2):
                    mm(pc[:, d * 4:d * 4 + 2], [qc], GM[i][:, hs[d]], ones[:, 0:2], True, True, [GM[i], consts])
                    mm(pc[:, d * 4 + 2:d * 4 + 4], [qc], ones, cl[:, d * 8 + 6:d * 8 + 8], True, True, [cl, consts])
                cl3 = cl[:, 0:16].rearrange("p (d c) -> p d c", d=2)
                pc3 = pc[:, 0:8].rearrange("p (d c) -> p d c", d=2)
                vcopy(cl3[:, :, 0:2], pc3[:, :, 1:3], [qc], [cl])
                act(cl3[:, :, 2:4], cl3[:, :, 0:2], AF.Exp, [cl], [cl])
                for d in range(2):
                    act(cl[:, d * 8 + 4:d * 8 + 5], cl[:, d * 8:d * 8 + 1], AF.Exp, [cl], [cl], bias=cl[:, d * 8 + 1:d * 8 + 2], scale=-1.0)
                    vtt(cl[:, d * 8 + 5:d * 8 + 6], cl[:, d * 8 + 2:d * 8 + 3], b_col[d], ALU.mult, [cl, btok], [cl])
                yield
                pK, qK = nextq2()
                for d in range(2):
                    mm(pK[:, hs[d]], [qK], kT[:, tsl[d]], kT[:, tsl[d]], True, True, [kT])
                for d in range(2):
                    vstt(Af[i][:, hs[d]], pK[:, hs[d]], b_col[d], decS[i][:, hs[d]], ALU.mult, ALU.mult, [qK, btok, decS[i]], [Af[i]])
                vtt(Ad[i][:], Af[i][:], bd2, ALU.mult, [Af[i], consts], [Ad[i]])
                yield
                pQ, qQ = nextq2()
                for d in range(2):
                    mm(pQ[:, hs[d]], [qQ], kT[:, tsl[d]], qT[:, tsl[d]], True, True, [kT, qT])
                vtt(decI[i][:], decI[i][:], ident2, ALU.add, [decI[i], consts], [decI[i]])
                vtt(attnT[hi][:], pQ, decI[i][:], ALU.mult, [qQ, decI[i]], [attnT[hi]])
                yield
                pB, qB = nextq2()
                for d in range(2):
                    mm(pB[:, hs[d]], [qB], Ad[i][:, hs[d]], identb2[:, 0:128], True, True, [Ad[i], identb])
                act(Bm[i][:], pB, AF.Copy, [qB], [Bm[i]])
                vtt(Pm[i][:], ident2, Bm[i][:], ALU.subtract, [consts, Bm[i]], [Pm[i]])
                yield
                yprev, ytprev = Bm[i], Ad[i]
                for lev in range(3):
                    ycur, ytcur = Y[lev % 2][i], YT[lev % 2][i]
                    pyt, qyt = nextq2()
                    for d in range(2):
                        mm(pyt[:, hs[d]], [qyt], yprev[:, hs[d]], ytprev[:, hs[d]], True, True, [yprev, ytprev])
                    act(ytcur[:], pyt, AF.Copy, [qyt], [ytcur])
                    if lev < 2:
                        py, qy = nextq2()
                        for d in range(2):
                            mm(py[:, hs[d]], [qy], ytprev[:, hs[d]], yprev[:, hs[d]], True, True, [yprev, ytprev])
                        act(ycur[:], py, AF.Copy, [qy], [ycur])
                    yield
                    pp, qp = nextq2()
                    for d in range(2):
                        mm(pp[:, hs[d]], [qp], ytcur[:, hs[d]], Pm[i][:, hs[d]], True, True, [ytcur, Pm[i]])
                    vtt(Pm[i][:], Pm[i][:], pp, ALU.add, [Pm[i], qp], [Pm[i]])
                    yield
                    yprev, ytprev = ycur, ytcur
                for mi, msk in enumerate(offmasks2):
                    ppt, qpt = nextq2()
                    for d in range(2):
                        mm(ppt[:, hs[d]], [qpt], Pm[i][:, hs[d]], identb2[:, 0:128], True, True, [Pm[i], identb])
                    act(PTm[i][:], ppt, AF.Copy, [qpt], [PTm[i]])
                    vtt(Aoff[i][:], Af[i][:], msk, ALU.mult, [Af[i], consts], [Aoff[i]])
                    pz, qz = nextq2()
                    for d in range(2):
                        mm(pz[:, hs[d]], [qz], Aoff[i][:, hs[d]], Pm[i][:, hs[d]], True, True, [Aoff[i], Pm[i]])
                    act(Zm[i][:], pz, AF.Copy, [qz], [Zm[i]])
                    yield
                    pt2, qt2 = nextq2()
                    for d in range(2):
                        mm(pt2[:, hs[d]], [qt2], PTm[i][:, hs[d]], Zm[i][:, hs[d]], True, True, [PTm[i], Zm[i]])
                    if mi < 2:
                        vtt(Pm[i][:], Pm[i][:], pt2, ALU.subtract, [Pm[i], qt2], [Pm[i]])
                    else:
                        vtt(TTb[i][:], Pm[i][:], pt2, ALU.subtract, [Pm[i], qt2], [TTb[i]])
                    yield
                for d in range(2):
                    vts(vb[i][:, hs[d]], vtok[:, ns[d], :], b_col[d], None, ALU.mult, None, [vtok, btok], [vb[i]])
                    vts(kbg[i][:, hs[d]], ktok[:, ns[d], :], cl[:, d * 8 + 5:d * 8 + 6], None, ALU.mult, None, [ktok, cl], [kbg[i]])
                    vts(kdec[hi][:, hs[d]], ktok[:, ns[d], :], cl[:, d * 8 + 4:d * 8 + 5], None, ALU.mult, None, [ktok, cl], [kdec[hi]])
                pu, qu = nextq2()
                for d in range(2):
                    mm(pu[:, hs[d]], [qu], TTb[i][:, hs[d]], vb[i][:, hs[d]], True, True, [TTb[i], vb[i]])
                act(um[hi][:], pu, AF.Copy, [qu], [um[hi]])
                yield
                pw, qw = nextq2()
                for d in range(2):
                    mm(pw[:, hs[d]], [qw], kbg[i][:, hs[d]], TTb[i][:, hs[d]], True, True, [TTb[i], kbg[i]])
                act(wTm[hi][:], pw, AF.Copy, [qw], [wTm[hi]])

            def step_gen(it, i):
                ns = [tn(0, it), tn(1, it)]
                tsl = [slice(n * 128, (n + 1) * 128) for n in ns]
                cl = cols[i]
                p1, q1 = nextq2()
                for d in range(2):
                    mm(p1[:, hs[d]], [q1], wTm[i][:, hs[d]], Sb2[:, hs[d]], True, True, [wTm[i], Sb2])
                vstt(vnew[i][:], p1, -1.0, um[i][:], ALU.mult, ALU.add, [q1, um[i]], [vnew[i]])
                p2, q2 = nextq2()
                for d in range(2):
                    mm(p2[:, hs[d]], [q2], qT[:, tsl[d]], Sb2[:, hs[d]], True, True, [qT, Sb2])
                for d in range(2):
                    vts(tmpo[i][:, hs[d]], p2[:, hs[d]], cl[:, d * 8 + 2:d * 8 + 3], None, ALU.mult, None, [q2, cl], [tmpo[i]])
                yield
                p4, q4 = nextq2()
                for d in range(2):
                    mm(p4[:, hs[d]], [q4], kdec[i][:, hs[d]], vnew[i][:, hs[d]], True, True, [kdec[i], vnew[i]])
                for d in range(2):
                    vstt(S2[:, hs[d]], S2[:, hs[d]], cl[:, d * 8 + 3:d * 8 + 4], p4[:, hs[d]], ALU.mult, ALU.add, [S2, cl, q4], [S2])
                act(Sb2[:], S2[:], AF.Copy, [S2], [Sb2])
                yield
                p3, q3 = nextq2()
                for d in range(2):
                    mm(p3[:, hs[d]], [q3], attnT[i][:, hs[d]], vnew[i][:, hs[d]], True, True, [attnT[i], vnew[i]])
                vtt(tmpo[i][:], tmpo[i][:], p3, ALU.add, [q3, tmpo[i]], [tmpo[i]])
                for d in range(2):
                    vtt(otok[:, ns[d], :], otok[:, ns[d], :], tmpo[i][:, hs[d]], ALU.add, [otok, tmpo[i]], [otok])

            nxt = 0
            free_s = list(range(G))
            free_h = list(range(NH))
            hmap = {}
            prep_done = set()
            step_done = -1
            step_started = -1
            active = []
            while step_done < NTL - 1:
                while nxt < NTL and free_s and free_h:
                    it = nxt
                    nxt += 1
                    si = free_s.pop(0)
                    hi = free_h.pop(0)
                    hmap[it] = hi
                    active.append(("P", it, prep_gen(it, si, hi), si))
                it = step_started + 1
                if it < NTL and it in prep_done and step_done == it - 1:
                    active.append(("S", it, step_gen(it, hmap[it]), None))
                    step_started = it
                for task in list(active):
                    kind, it, gen, si = task
                    try:
                        next(gen)
                    except StopIteration:
                        active.remove(task)
                        if kind == "P":
                            prep_done.add(it)
                            free_s.append(si)
                        else:
                            step_done = it
                            free_h.append(hmap[it])
            P.barrier()
        sl = wslot()
        wz = sl[:, 0:1024].rearrange("p (k n) -> p k n", k=8)
        wload(sl, wz, wA, wA.t[l, :, :, C_GZ + hh * 128:C_GZ + (hh + 1) * 128])
        def outA(tl):
            sz, on, ssq = szs[tl % 3], ons[tl % 3], ssqs[tl % 3]
            pt, pq = nextbank()
            for k in range(8):
                mm(pt[:, 0:128], pq, hT[:, k, tl * 128:(tl + 1) * 128], wz[:, k, :], k == 0, k == 7, [hT, sl])
            act(sz[:], pt[:, 0:128], AF.Silu, pq, [sz])
            P.op("dve", lambda: nc.vector.tensor_tensor(out=on[:], in0=otok[:, tl, :], in1=otok[:, tl, :], op=ALU.mult), [otok], [on])
            P.op("dve", lambda: nc.vector.tensor_reduce(out=ssq[:, 0:1], in_=on[:], axis=AX.X, op=ALU.add), [on], [ssq])
            act(ssq[:, 1:2], ssq[:, 0:1], AF.Ln, [ssq], [ssq], bias=epsc[:, 0:1], scale=1.0 / 128)
            act(ssq[:, 1:2], ssq[:, 1:2], AF.Exp, [ssq], [ssq], scale=-0.5)
            vstt(on[:], otok[:, tl, :], ssq[:, 1:2], vec(l, "gng", 0, 128), ALU.mult, ALU.mult, [otok, ssq, vecs], [on])
            vtt(on[:], on[:], sz[:], ALU.mult, [on, sz], [on])

        def outB(tl):
            on = ons[tl % 3]
            qa, qk = nextq()
            mm(qa, [qk], on[:], ident, True, True, [on, consts])
            act(ob[:, hh, tl * 128:(tl + 1) * 128], qa, AF.Copy, [qk], [ob])

        for tl in range(NTL + 2):
            if tl < NTL:
                outA(tl)
            if tl >= 2:
                outB(tl - 2)


def emit_merge(P, nc, env, sc, l, s):
    e = _h(env)
    hT, obT, consts = e.hT, e.obT, e.consts
    mm, act, vtt, vts, vstt, vcopy = e.mm, e.act, e.vtt, e.vts, e.vstt, e.vcopy
    nextbank, cst, vec, wslot, wload, hrhs = e.nextbank, e.cst, e.vec, e.wslot, e.wload, e.hrhs
    wA, wBR, wO, mod = e.wA, e.wBR, e.wO, e.mod
    mT = P.sbuf("mT", [128, 8, T], BF16, sc)
    Gs = [P.sbuf("Gs", [128, 512], F32, sc) for _ in range(2)]
    macc = P.sbuf("macc", [128, 512], F32, sc)
    mtmp = P.sbuf("mtmp", [128, 512], F32, sc)
    xb = P.sbuf("xb", [128, 8, 512], F32, sc)
    sq = P.sbuf("sq", [128, 8, 512], F32, sc)
    rstd = P.sbuf("rstd", [128, 512], F32, sc)
    tmp = P.sbuf("tmp", [128, 512], F32, sc)
    gi = 0
    for j in range(8):
        sl = wslot()
        wg = sl[:, 0:3072].rearrange("p (b k n) -> p b k n", b=3, k=8)
        for b in range(3):
            wload(sl, wg[:, b], wA, wA.t[l, :, :, C_GATE + b * 1024 + j * 128:C_GATE + b * 1024 + (j + 1) * 128])
        sl2 = wslot()
        wb = sl2[:, 0:1536].rearrange("p (k n) -> p k n", k=12)
        wload(sl2, wb, wBR, wBR.t[l, :, :, j * 128:(j + 1) * 128])
        for tb in range(NTB):
            blk = slice(tb * 512, (tb + 1) * 512)
            for b in range(3):
                pg, qg = nextbank()
                for k in range(8):
                    mm(pg[:, :], qg, wg[:, b, k, :], hrhs(k, tb), k == 0, k == 7, [sl, hT])
                Gt = Gs[gi % 2]
                gi += 1
                act(Gt[:], pg[:, :], AF.Sigmoid, qg, [Gt])
                py, qy = nextbank()
                for k in range(4):
                    mm(py[:, :], qy, wb[:, b * 4 + k, :], obT[b][:, k, blk], k == 0, k == 3, [sl2, obT[b]])
                if b == 0:
                    vtt(macc[:], py[:, :], Gt[:], ALU.mult, qy + [Gt], [macc])
                elif b == 1:
                    vtt(mtmp[:], py[:, :], Gt[:], ALU.mult, qy + [Gt], [mtmp])
                    vtt(macc[:], macc[:], mtmp[:], ALU.add, [macc, mtmp], [macc])
                else:
                    vtt(mtmp[:], py[:, :], Gt[:], ALU.mult, qy + [Gt], [mtmp])
                    vtt(mT[:, j, blk], macc[:], mtmp[:], ALU.add, [macc, mtmp], [mT])
    for tb in range(NTB):
        blk = slice(tb * 512, (tb + 1) * 512)
        e.load_xblock(xb, tb)
        for half in range(2):
            sl = wslot()
            wv = sl[:, 0:4096].rearrange("p (k n) -> p k n", k=8)
            wload(sl, wv, wO, wO.t[l, :, :, half * 512:(half + 1) * 512])
            for ii in range(4):
                i = half * 4 + ii
                pt, pq = nextbank()
                for k in range(8):
                    mm(pt[:, :], pq, wv[:, k, ii * 128:(ii + 1) * 128], mT[:, k, blk], k == 0, k == 7, [sl, mT])
                vstt(xb[:, i, :], pt[:, :], mod[:, l, 16 + i, s:s + 1], xb[:, i, :], ALU.mult, ALU.add, pq + [mod, xb], [xb])
        e.store_xblock(xb, tb)
        e.norm_block(xb, tb, l, s, 2, sq, rstd, tmp)


def emit_ffn(P, nc, env, sc, l, s):
    e = _h(env)
    hT, consts = e.hT, e.consts
    mm, act, vtt, vts, vstt, vcopy, vmemset = e.mm, e.act, e.vtt, e.vts, e.vstt, e.vcopy, e.vmemset
    nextbank, cst, vec, wslot, wload, hrhs = e.nextbank, e.cst, e.vec, e.wslot, e.wload, e.hrhs
    wUP, wDN, mod, vecs, xT_d, xT_tk = e.wUP, e.wDN, e.mod, e.vecs, e.xT_d, e.xT_tk
    actT = P.sbuf("actT", [128, NJ, T], BF16, sc)
    upad = [P.sbuf("upad", [128, T + 2], F32, sc) for _ in range(2)]
    cv = [P.sbuf("cv", [128, T], F32, sc) for _ in range(2)]
    xs = [P.sbuf("xs", [128, 512], F32, sc) for _ in range(2)]
    for u in upad:
        vmemset(u[:], 0.0, [u])
    for jp in range(NJ // 2):
        sl = wslot()
        wv = sl[:, 0:4096].rearrange("p (k n) -> p k n", k=8)
        for jj in range(2):
            j = jp * 2 + jj
            wload(sl, wv[:, :, jj * 128:(jj + 1) * 128], wUP, wUP.t[l, :, :, j * 128:(j + 1) * 128])
            wload(sl, wv[:, :, 256 + jj * 128:256 + (jj + 1) * 128], wUP, wUP.t[l, :, :, DFF + j * 128:DFF + (j + 1) * 128])
        for jj in range(2):
            j = jp * 2 + jj
            for lg in range(2):
                ch = lg * NJ + j
                wc = lg * 256 + jj * 128
                for tb in range(NTB):
                    pt, pq = nextbank()
                    for k in range(8):
                        mm(pt[:, :], pq, wv[:, k, wc:wc + 128], hrhs(k, tb), k == 0, k == 7, [sl, hT])
                    act(upad[lg][:, 1 + tb * 512:1 + (tb + 1) * 512], pt[:, :], AF.Copy, pq, [upad[lg]])
                act(cv[lg][:], upad[lg][:, 0:T], AF.Identity, [upad[lg], vecs], [cv[lg]],
                    bias=vec(l, "fcb", ch), scale=vec(l, "fcw", ch * 3 + 0))
                for k in range(1, 3):
                    vstt(cv[lg][:], upad[lg][:, k:k + T], vec(l, "fcw", ch * 3 + k), cv[lg][:], ALU.mult, ALU.add,
                         [upad[lg], cv[lg], vecs], [cv[lg]])
            act(cv[1][:], cv[1][:], AF.Silu, [cv[1]], [cv[1]])
            vtt(actT[:, j, :], cv[1][:], cv[0][:], ALU.mult, [cv[0], cv[1]], [actT])
    xi = 0
    for i in range(8):
        sl = wslot()
        wv = sl[:, 0:NJ * 128].rearrange("p (k n) -> p k n", k=NJ)
        wload(sl, wv, wDN, wDN.t[l, :, :, i * 128:(i + 1) * 128])
        for tb in range(NTB):
            blk = slice(tb * 512, (tb + 1) * 512)
            pt, pq = nextbank()
            for k in range(NJ):
                mm(pt[:, :], pq, wv[:, k, :], actT[:, k, blk], k == 0, k == NJ - 1, [sl, actT])
            x1 = xs[xi % 2]
            xi += 1
            P.dma([x1], x1[:], [xT_tk[i][tb]], xT_d[:, i, blk])
            vstt(x1[:], pt[:, :], mod[:, l, 40 + i, s:s + 1], x1[:], ALU.mult, ALU.add, pq + [mod, x1], [x1])
            P.dma([xT_tk[i][tb]], xT_d[:, i, blk], [x1], x1[:], sem_t=x1)


def _fm(w, kc):
    K, N = w.shape
    return np.ascontiguousarray(w.reshape(kc, 128, N).transpose(1, 0, 2))


def _colv(v, nch):
    return np.ascontiguousarray(np.asarray(v).reshape(nch, 128).T)


def _consts():
    c = np.zeros((128, NC), np.float32)
    i = np.arange(128)
    ident = np.eye(128, dtype=np.float32)
    c[:, COFF["ident"]:COFF["ident"] + 128] = ident
    c[:, COFF["ident_b"]:COFF["ident_b"] + 128] = ident
    c[:, COFF["ones"]:COFF["ones"] + 256] = 1.0
    c[:, COFF["negones"]:COFF["negones"] + 256] = -1.0
    m, ii = np.meshgrid(i, i, indexing="ij")
    c[:, COFF["mt0"]:COFF["mt0"] + 128] = (m <= ii)
    c[:, COFF["mt1"]:COFF["mt1"] + 128] = (m >= ii)
    a, b = np.meshgrid(i, i, indexing="ij")
    v0 = (b < a)
    v1 = (b > a)
    c[:, COFF["negs0"]:COFF["negs0"] + 128] = np.where(v0, 0.0, -BIG)
    c[:, COFF["negs1"]:COFF["negs1"] + 128] = np.where(v1, 0.0, -BIG)
    c[:, COFF["negst0"]:COFF["negst0"] + 128] = np.where(v0.T, 0.0, -BIG)
    c[:, COFF["negst1"]:COFF["negst1"] + 128] = np.where(v1.T, 0.0, -BIG)
    bdm = lambda n: ((a // n) == (b // n)).astype(np.float32)
    for sfx in ("", "_b"):
        c[:, COFF["bd16" + sfx]:COFF["bd16" + sfx] + 128] = bdm(16)
        c[:, COFF["m32" + sfx]:COFF["m32" + sfx] + 128] = bdm(32) - bdm(16)
        c[:, COFF["m64" + sfx]:COFF["m64" + sfx] + 128] = bdm(64) - bdm(32)
        c[:, COFF["m128" + sfx]:COFF["m128" + sfx] + 128] = 1.0 - bdm(64)
    return c


def _rope_tables():
    inv = 1.0 / (10000.0 ** (np.arange(0, 32, 2, dtype=np.float32) / 32.0))
    ang = np.arange(T, dtype=np.float32)[None, :] * inv[:, None].astype(np.float32)
    cos, sin = np.cos(ang).astype(np.float32), np.sin(ang).astype(np.float32)
    r = np.zeros((2, 96, T), np.float32)
    r[0, 0:64] = 1.0
    r[0, 64:80] = cos
    r[0, 80:96] = cos
    r[1, 64:80] = -sin
    r[1, 80:96] = sin
    return r


def prep_weights(inp):
    L = L_ALL
    f = lambda k: np.asarray(inp[k], np.float32)
    w_in = f("w_in")
    wA = np.zeros((L, 128, 8, NA), np.float32)
    sw = np.concatenate([np.arange(16, 32), np.arange(0, 16)])
    wUQ = np.zeros((L, 128, 3, 1536), np.float32)
    wKN = np.zeros((L, 128, 2, 768), np.float32)
    wV = np.zeros((L, 128, 2, 512), np.float32)
    wRG = np.zeros((L, 128, 16, 128), np.float32)
    vecs = np.zeros((128, L, NV), np.float32)
    for l in range(L):
        ext = np.zeros((D, 192), np.float32)
        kr = w_in[l][:, 640:672]
        ext[:, 64:96] = kr
        ext[:, 96 + 64:96 + 96] = kr[:, sw]
        wA[l] = _fm(np.concatenate([w_in[l], ext], axis=1), 8)
        uq = f("mla_w_uq")[l]
        uqs = uq.reshape(384, 8, 96).copy()
        uqs[:, :, 64:96] = uqs[:, :, 64:96][:, :, sw]
        wUQ[l] = _fm(np.concatenate([uq, uqs.reshape(384, 768)], axis=1), 3)
        ukv = f("mla_w_ukv")[l].reshape(256, 8, 128)
        kn = np.zeros((256, 8, 96), np.float32)
        kn[:, :, 0:64] = ukv[:, :, 0:64]
        wKN[l] = _fm(kn.reshape(256, 768), 2)
        wV[l] = _fm(np.ascontiguousarray(ukv[:, :, 64:128]).reshape(256, 512), 2)
        for d in range(2):
            for ai, nm in enumerate(("rg_w_a", "rg_w_i")):
                w = f(nm)[l, d]
                for c in range(4):
                    m = np.zeros((128, 128), np.float32)
                    m[0:64, 0:64] = w[2 * c]
                    m[64:128, 64:128] = w[2 * c + 1]
                    wRG[l, :, (d * 2 + ai) * 4 + c, :] = m

        def put(name, arr):
            o, w = VOFF[name]
            vecs[:, l, o:o + w] = arr
        put("ln1", _colv(f("ln1_g")[l], 8))
        put("ln2", _colv(f("ln2_g")[l], 8))
        put("bmod", _colv(f("b_mod")[l], 48))
        put("qg", _colv(f("mla_q_norm_g")[l], 3))
        put("kvg", _colv(f("mla_kv_norm_g")[l], 2))
        put("rgcw", np.ascontiguousarray(f("rg_conv_w")[l].reshape(4, 4, 128).transpose(2, 1, 0)).reshape(128, 16))
        put("rgcb", _colv(f("rg_conv_b")[l], 4))
        put("rgba", np.ascontiguousarray(f("rg_b_a")[l].reshape(2, 4, 128).transpose(2, 0, 1)).reshape(128, 8))
        put("rgbi", np.ascontiguousarray(f("rg_b_i")[l].reshape(2, 4, 128).transpose(2, 0, 1)).reshape(128, 8))
        put("rglam", np.ascontiguousarray(f("rg_lam")[l].reshape(2, 4, 128).transpose(2, 0, 1)).reshape(128, 8))
        put("gcw", np.ascontiguousarray(f("gdn_conv_w")[l].reshape(4, 12, 128).transpose(2, 1, 0)).reshape(128, 48))
        put("gng", np.broadcast_to(f("gdn_norm_g")[l][None, :], (128, 128)))
        put("galog", np.broadcast_to(np.tile(f("gdn_a_log")[l].reshape(8), 16)[None, :], (128, 128)))
        put("gdtb", np.broadcast_to(np.tile(f("gdn_dt_bias")[l].reshape(8), 16)[None, :], (128, 128)))
        put("fcw", np.ascontiguousarray(f("ffn_conv_w")[l].reshape(3, 44, 128).transpose(2, 1, 0)).reshape(128, 132))
        put("fcb", _colv(f("ffn_conv_b")[l], 44))
        put("fng", _colv(f("final_norm_g"), 8))
    wd = {
        "wA": wA, "wUQ": wUQ, "wKN": wKN, "wV": wV, "wRG": wRG,
        "wBR": np.stack([_fm(f("w_branch")[l].reshape(1536, 1024), 12) for l in range(L)]),
        "wO": np.stack([_fm(f("w_out")[l], 8) for l in range(L)]),
        "wUP": np.stack([_fm(f("ffn_w_up")[l], 8) for l in range(L)]),
        "wDN": np.stack([_fm(f("ffn_w_down")[l], NJ) for l in range(L)]),
        "wMOD": np.stack([_fm(f("w_mod")[l], 8) for l in range(L)]),
        "vecs": vecs, "consts": _consts(), "ropet": _rope_tables(),
    }
    return wd


def core_inputs(wd, xs, cs):
    m = dict(wd)
    m["xin"] = np.ascontiguousarray(xs, dtype=np.float32)
    nseq = xs.shape[0]
    m["cT"] = np.ascontiguousarray(np.asarray(cs, np.float32).reshape(nseq, 8, 128).transpose(2, 1, 0))
    return m


_CACHE = {}


def kernel(**inputs):
    wd = prep_weights(inputs)
    xp = np.asarray(inputs["x_prompt"], np.float32)
    xsm = np.asarray(inputs["x_sample"], np.float32)
    cp = np.asarray(inputs["c_prompt"], np.float32)
    csm = np.asarray(inputs["c_sample"], np.float32)
    xall = np.concatenate([xp, xsm], axis=0)
    call = np.concatenate([cp, csm], axis=0)
    nseq = xall.shape[0] // NCORES
    if "nc" not in _CACHE:
        _CACHE["nc"] = build(nseq)[0]
    nc = _CACHE["nc"]
    in_maps = [core_inputs(wd, xall[i * nseq:(i + 1) * nseq], call[i * nseq:(i + 1) * nseq]) for i in range(NCORES)]
    res = run_bass_kernel_spmd(nc, in_maps, core_ids=list(range(NCORES)))
    y = np.concatenate([np.asarray(r["yout"], np.float32) for r in res.results], axis=0)
    nb = xp.shape[0]
    return (np.ascontiguousarray(y[:nb]), np.ascontiguousarray(y[nb:]))
```

```python
import contextlib
import math
import numpy as np
import concourse.bass as bass
import concourse.mybir as mybir
from concourse.bass_utils import run_bass_kernel_spmd

F32 = mybir.dt.float32
BF16 = mybir.dt.bfloat16
AF = mybir.ActivationFunctionType
ALU = mybir.AluOpType
AX = mybir.AxisListType

D = 1024
T = 2048
NTB = 4
NTL = 16
L_ALL = 2
NCORES = 8
EPS = 1e-6
N_IN = 6832
KR0 = N_IN
NA = N_IN + 192
C_RGX, C_RGG, C_GQKV, C_GZ, C_AB, C_GATE = 672, 1184, 1696, 3232, 3744, 3760
DFF = 2816
NJ = 22
BIG = 30000.0
SEM_LIMIT = 30000

VOFF = {}
_o = 0
for _n, _w in [("ln1", 8), ("ln2", 8), ("bmod", 48), ("qg", 3), ("kvg", 2), ("rgcw", 16), ("rgcb", 4),
               ("rgba", 8), ("rgbi", 8), ("rglam", 8), ("gcw", 48), ("gng", 128), ("galog", 128), ("gdtb", 128),
               ("fcw", 132), ("fcb", 44), ("fng", 8)]:
    VOFF[_n] = (_o, _w)
    _o += _w
NV = _o
COFF = {}
_o = 0
for _n in ["ident", "ident_b", "ones", "ones_b", "negones", "negones_b", "mt0", "mt1", "negs0", "negs1", "negst0", "negst1",
           "bd16", "bd16_b", "m32", "m32_b", "m64", "m64_b", "m128", "m128_b"]:
    COFF[_n] = _o
    _o += 128
NC = _o


class Tk:
    __slots__ = ("name", "w", "r", "t", "dk")

    def __init__(self, name, t=None):
        self.name = name
        self.w = None
        self.r = {}
        self.t = t
        self.dk = name

    def __getitem__(self, idx):
        return self.t[idx]


class Prog:
    def __init__(self, nc):
        self.nc = nc
        self.es = contextlib.ExitStack()
        self.eng = {"pe": nc.tensor, "act": nc.scalar, "dve": nc.vector, "pool": nc.gpsimd, "sp": nc.sync}
        self.sems = {}
        self.cnt = {}
        self.cur = {}
        self.epoch = {e: 0 for e in self.eng}
        self.seen = {e: {} for e in self.eng}
        self.ninst = 0
        self.uid = 0
        self.namectr = {}
        for e in self.eng:
            self._new_epoch(e)

    def _mk_sem(self, key):
        s = self.es.enter_context(self.nc.semaphore("s_%s" % (key,)))
        self.sems[key] = s
        self.cnt[key] = 0
        return s

    def _new_epoch(self, e):
        key = "%s%d" % (e, self.epoch[e])
        self.epoch[e] += 1
        self._mk_sem(key)
        self.cur[e] = key

    def sbuf(self, name, shape, dt, es=None):
        self.uid += 1
        t = (es or self.es).enter_context(self.nc.sbuf_tensor("%s_%d" % (name, self.uid), list(shape), dt))
        tk = Tk(name + str(self.uid), t)
        c = self.namectr.get(name, 0)
        self.namectr[name] = c + 1
        tk.dk = "%s_%d" % (name, c % 4)
        return tk

    def dram(self, name, shape, dt, kind="Internal"):
        t = self.nc.dram_tensor(name, list(shape), dt, kind=kind).ap()
        return Tk(name, t)

    def _deps(self, reads, writes):
        deps = {}
        for t in reads:
            if t.w is not None:
                k, v = t.w
                if deps.get(k, 0) < v:
                    deps[k] = v
        for t in writes:
            if t.w is not None:
                k, v = t.w
                if deps.get(k, 0) < v:
                    deps[k] = v
            for k, v in t.r.items():
                if deps.get(k, 0) < v:
                    deps[k] = v
        return deps

    def _wait(self, e, deps):
        eng = self.eng[e]
        seen = self.seen[e]
        for k, v in deps.items():
            isdma = k.startswith("dma")
            if (not isdma) and k.startswith(e) and e in ("pe", "sp"):
                continue
            if isdma:
                v = self.cnt[k]
            if seen.get(k, 0) < v:
                eng.wait_ge(self.sems[k], v)
                seen[k] = v
                self.ninst += 1

    def op(self, e, fn, reads=(), writes=()):
        self._wait(e, self._deps(reads, writes))
        if self.cnt[self.cur[e]] >= SEM_LIMIT:
            self._new_epoch(e)
        key = self.cur[e]
        inst = fn()
        self.cnt[key] += 1
        v = self.cnt[key]
        inst.then_inc(self.sems[key], 1)
        self.ninst += 1
        for t in reads:
            t.r[key] = v
        for t in writes:
            t.w = (key, v)
            t.r = {}
        return inst

    def dma(self, out_ts, out_ap, in_ts, in_ap, q="sp", sem_t=None):
        st = sem_t if sem_t is not None else out_ts[0]
        key = "dma_" + st.dk
        if key not in self.sems:
            self._mk_sem(key)
        self._wait(q, self._deps(in_ts, out_ts))
        inst = self.eng[q].dma_start(out=out_ap, in_=in_ap)
        self.cnt[key] += 16
        v = self.cnt[key]
        inst.then_inc(self.sems[key], 16)
        self.ninst += 1
        for t in in_ts:
            t.r[key] = v
        for t in out_ts:
            t.w = (key, v)
            t.r = {}
        return inst

    def barrier(self):
        allk = {k: v for k, v in self.cnt.items() if v > 0}
        for e in self.eng:
            self._wait(e, allk)

    def finish(self, outs, q="sp"):
        deps = {}
        for t in outs:
            if t.w is not None:
                k, v = t.w
                deps[k] = max(deps.get(k, 0), v)
        self._wait(q, deps)


OUTSTAGE = 3
GDNSTAGE = 9
GDN_G = 2
GDN_POOL = False
GDN_BF16INV = True
GSUB = 9
RGDBG = 0
PHASES = set(['N1', 'MLA', 'RG', 'GDN', 'MERGE', 'FFN'])


def build(nseq, nlayer=L_ALL, dbg=False):
    nc = bass.Bass("TRN2", target_bir_lowering=False)
    P = Prog(nc)
    L = L_ALL
    xin = P.dram("xin", [nseq, T, D], F32, "ExternalInput")
    cT = P.dram("cT", [128, 8, nseq], F32, "ExternalInput")
    wA = P.dram("wA", [L, 128, 8, NA], F32, "ExternalInput")
    wUQ = P.dram("wUQ", [L, 128, 3, 1536], F32, "ExternalInput")
    wKN = P.dram("wKN", [L, 128, 2, 768], F32, "ExternalInput")
    wV = P.dram("wV", [L, 128, 2, 512], F32, "ExternalInput")
    wRG = P.dram("wRG", [L, 128, 16, 128], F32, "ExternalInput")
    wBR = P.dram("wBR", [L, 128, 12, 1024], F32, "ExternalInput")
    wO = P.dram("wO", [L, 128, 8, 1024], F32, "ExternalInput")
    wUP = P.dram("wUP", [L, 128, 8, 2 * DFF], F32, "ExternalInput")
    wDN = P.dram("wDN", [L, 128, NJ, 1024], F32, "ExternalInput")
    wMOD = P.dram("wMOD", [L, 128, 8, 6144], F32, "ExternalInput")
    vecs_d = P.dram("vecs", [128, L, NV], F32, "ExternalInput")
    consts_d = P.dram("consts", [128, NC], F32, "ExternalInput")
    ropet = P.dram("ropet", [2, 96, T], F32, "ExternalInput")
    yout = P.dram("yout", [nseq, T, D], F32, "ExternalOutput")
    xT_d = nc.dram_tensor("xT_scr", [128, 8, T], F32, kind="Internal").ap()
    xT_tk = [[Tk("xT_%d_%d" % (c, tb)) for tb in range(NTB)] for c in range(8)]
    dbg_d = None
    if dbg:
        dbg_d = P.dram("dbg", [3, 128, 4, T], F32, "ExternalOutput")

    V = nc.vector
    A = nc.scalar
    PE = nc.tensor
    G = nc.gpsimd

    with P.es:
        base = P.es
        consts = P.sbuf("consts", [128, NC], F32)
        vecs = P.sbuf("vecs", [128, L, NV], F32)
        identb = P.sbuf("identb", [128, 256], BF16)
        negidentb = P.sbuf("negidentb", [128, 128], BF16)
        hT = P.sbuf("hT", [128, 8, T], BF16)
        mod = P.sbuf("mod", [128, L, 48, nseq], F32)
        gs1 = P.sbuf("gs1", [128, L, 8, nseq], F32)
        gs2 = P.sbuf("gs2", [128, L, 8, nseq], F32)
        cneg = P.sbuf("cneg", [128, L, 8], F32)
        cneg2 = P.sbuf("cneg2", [128, L, 8], F32)
        negA = P.sbuf("negA", [128, L, 128], F32)
        NSLOT = 3
        wslots = [P.sbuf("wslot%d" % i, [128, 4096], BF16) for i in range(NSLOT)]
        wctr = [0]
        pbank = []
        for i in range(8):
            t = P.es.enter_context(nc.psum_tensor("psb%d" % i, [128, 512], F32))
            pbank.append((t, [Tk("psq%d_%d" % (i, q)) for q in range(4)]))
        pctr = [0]
        qctr = [0]

        def nextbank(pool=None):
            if pool is None:
                b = pbank[pctr[0] % 8]
            else:
                b = pbank[pool[pctr[0] % len(pool)]]
            pctr[0] += 1
            return b

        def nextq():
            t, qs = pbank[pctr[0] % 8]
            pctr[0] += 1
            return t[:, 0:128], qs[0]

        def nextq2():
            t, qs = pbank[pctr[0] % 8]
            pctr[0] += 1
            return t[:, 0:256], qs[0]

        def cst(name, w=128):
            o = COFF[name]
            return consts[:, o:o + w]

        def vec(l, name, j=0, w=1):
            o, _ = VOFF[name]
            return vecs[:, l, o + j:o + j + w]

        def wslot():
            s = wslots[wctr[0] % NSLOT]
            wctr[0] += 1
            return s

        def wload(slot, dst_ap, src_tk, src_ap):
            P.dma([slot], dst_ap, [src_tk], src_ap, q="pool", sem_t=slot)

        def mm(ps_ap, ps_tks, lhsT, rhs, start, stop, reads):
            P.op("pe", lambda: PE.matmul(ps_ap, lhsT=lhsT, rhs=rhs, start=start, stop=stop), reads, ps_tks)

        def act(out_ap, in_ap, func, reads, writes, bias=None, scale=None):
            kw = {}
            if bias is not None:
                kw["bias"] = bias
            if scale is not None:
                kw["scale"] = scale
            P.op("act", lambda: A.activation(out=out_ap, in_=in_ap, func=func, **kw), reads, writes)

        def vtt(out_ap, in0, in1, op, reads, writes, e="dve"):
            eng = V if e == "dve" else G
            P.op(e, lambda: eng.tensor_tensor(out=out_ap, in0=in0, in1=in1, op=op), reads, writes)

        def vts(out_ap, in0, s1, s2, op0, op1, reads, writes, e="dve"):
            eng = V if e == "dve" else G
            if op1 is None:
                P.op(e, lambda: eng.tensor_scalar(out=out_ap, in0=in0, scalar1=s1, scalar2=None, op0=op0), reads, writes)
            else:
                P.op(e, lambda: eng.tensor_scalar(out=out_ap, in0=in0, scalar1=s1, scalar2=s2, op0=op0, op1=op1), reads, writes)

        def vstt(out_ap, in0, scalar, in1, op0, op1, reads, writes, e="dve"):
            eng = V if e == "dve" else G
            P.op(e, lambda: eng.scalar_tensor_tensor(out=out_ap, in0=in0, scalar=scalar, in1=in1, op0=op0, op1=op1), reads, writes)

        def vcopy(out_ap, in_ap, reads, writes, e="dve"):
            eng = V if e == "dve" else G
            P.op(e, lambda: eng.tensor_copy(out=out_ap, in_=in_ap), reads, writes)

        def vmemset(ap, val, writes, e="dve"):
            eng = V if e == "dve" else G
            P.op(e, lambda: eng.memset(ap, val), [], writes)

        def rsqrt_from(out_ap, in_ap, scale, reads, writes):
            act(out_ap, in_ap, AF.Ln, reads, writes, bias=epsc[:, 0:1], scale=scale)
            act(out_ap, out_ap, AF.Exp, writes, writes, scale=-0.5)

        def softplus_acc(out_ap, x_ap, t1, t2, t3, rd, wr):
            vts(t1, x_ap, -1.0, None, ALU.mult, None, rd, wr)
            vtt(t1, t1, x_ap, ALU.max, rd + wr, wr)
            act(t1, t1, AF.Exp, wr, wr, scale=-1.0)
            vts(t2, t1, 2.0, None, ALU.add, None, wr, wr)
            P.op("dve", lambda: V.reciprocal(out=t2, in_=t2), wr, wr)
            vtt(t1, t1, t2, ALU.mult, wr, wr)
            vtt(t2, t1, t1, ALU.mult, wr, wr)
            vts(t3, t2, 1.0 / 11, 1.0 / 9, ALU.mult, ALU.add, wr, wr)
            for cf in (1.0 / 7, 1.0 / 5, 1.0 / 3, 1.0):
                vtt(t3, t3, t2, ALU.mult, wr, wr)
                vts(t3, t3, cf, None, ALU.add, None, wr, wr)
            vtt(t3, t3, t1, ALU.mult, wr, wr)
            vts(t1, x_ap, 0.0, None, ALU.max, None, rd + wr, wr)
            vstt(out_ap, t3, 2.0, t1, ALU.mult, ALU.add, wr, wr)

        epsc = P.sbuf("epsc", [128, 1], F32)
        vmemset(epsc[:], EPS, [epsc])
        P.dma([consts], consts[:], [consts_d], consts_d.t)
        P.dma([vecs], vecs[:], [vecs_d], vecs_d.t)
        vcopy(identb[:], cst("ident", 256), [consts], [identb])
        vts(negidentb[:], cst("ident"), -1.0, None, ALU.mult, None, [consts], [negidentb])
        with contextlib.ExitStack() as sc:
            craw = P.sbuf("craw", [128, 8, nseq], F32, sc)
            csil = P.sbuf("csil", [128, 8, nseq], BF16, sc)
            tmpm = P.sbuf("tmpm", [128, 8, nseq], F32, sc)
            spx = P.sbuf("spx", [128, 8], F32, sc)
            spt = P.sbuf("spt", [128, 3, 8], F32, sc)
            P.dma([craw], craw[:], [cT], cT.t)
            act(csil[:], craw[:], AF.Silu, [craw], [csil])
            for l in range(nlayer):
                for pn in range(12):
                    sl = wslot()
                    wv = sl[:, 0:4096].rearrange("p (k n) -> p k n", k=8)
                    wload(sl, wv, wMOD, wMOD.t[l, :, :, pn * 512:(pn + 1) * 512])
                    for jj in range(4):
                        j = pn * 4 + jj
                        pt, pq = nextbank()
                        for k in range(8):
                            mm(pt[:, 0:nseq], pq, wv[:, k, jj * 128:(jj + 1) * 128], csil[:, k, :], k == 0, k == 7, [sl, csil])
                        act(mod[:, l, j, :], pt[:, 0:nseq], AF.Identity, pq, [mod], bias=vec(l, "bmod", j))
                for (gs, lnn, j0) in ((gs1, "ln1", 8), (gs2, "ln2", 32)):
                    vts(tmpm[:], mod[:, l, j0:j0 + 8, :], 1.0, None, ALU.add, None, [mod], [tmpm])
                    for s in range(nseq):
                        vtt(gs[:, l, :, s], tmpm[:, :, s], vec(l, lnn, 0, 8), ALU.mult, [tmpm, vecs], [gs])
                vts(spx[:], vec(l, "rglam", 0, 8), -1.0, None, ALU.mult, None, [vecs], [spx])
                softplus_acc(cneg[:, l, :], spx[:], spt[:, 0, :], spt[:, 1, :], spt[:, 2, :], [spx], [spt, cneg])
                vts(cneg2[:, l, :], cneg[:, l, :], -16.0, None, ALU.mult, None, [cneg], [cneg2])
                vts(cneg[:, l, :], cneg[:, l, :], -8.0, None, ALU.mult, None, [cneg], [cneg])
                act(negA[:, l, :], vec(l, "galog", 0, 128), AF.Exp, [vecs], [negA])
                vts(negA[:, l, :], negA[:, l, :], -1.0, None, ALU.mult, None, [negA], [negA])
            P.barrier()

        def proj_fm(wv, kc, M, rhs_fn, rhs_tks, slot, consume):
            for tb in range(NTB):
                pt, pq = nextbank()
                for k in range(kc):
                    mm(pt[0:M, :], pq, wv[:, k, 0:M], rhs_fn(k, tb), k == 0, k == kc - 1, [slot] + rhs_tks)
                consume(tb, pt, pq)

        def hrhs(k, tb):
            return hT[:, k, tb * 512:(tb + 1) * 512]

        def load_xblock(xb, tb):
            for c in range(8):
                pass
            P.dma([xb], xb[:], [xT_tk[c][tb] for c in range(8)], xT_d[:, :, tb * 512:(tb + 1) * 512])

        def store_xblock(xb, tb):
            P.dma([xT_tk[c][tb] for c in range(8)], xT_d[:, :, tb * 512:(tb + 1) * 512], [xb], xb[:], sem_t=xb)

        def norm_block(xb, tb, l, s, which, sq, rstd, tmp):
            gs = gs1 if which == 1 else gs2
            shj = 0 if which == 1 else 24
            vtt(sq[:], xb[:], xb[:], ALU.mult, [xb], [sq])
            pt, pq = nextbank()
            for c in range(8):
                mm(pt[:, :], pq, cst("ones"), sq[:, c, :], c == 0, c == 7, [consts, sq])
            rsqrt_from(rstd[:], pt[:, :], 1.0 / D, pq, [rstd])
            for c in range(8):
                vtt(tmp[:], xb[:, c, :], rstd[:], ALU.mult, [xb, rstd], [tmp])
                act(hT[:, c, tb * 512:(tb + 1) * 512], tmp[:], AF.Identity, [tmp, gs, mod], [hT],
                    bias=mod[:, l, shj + c, s:s + 1], scale=gs[:, l, c, s:s + 1])

        for s in range(nseq if 'NOSEQ' not in PHASES else 0):
            with contextlib.ExitStack() as sc:
                xtok = [P.sbuf("xtok", [128, D], F32, sc) for _ in range(2)]
                xblk = [P.sbuf("xblk", [128, 8, 128], F32, sc) for _ in range(2)]
                for tl in range(NTL if 'NOL' not in PHASES else 0):
                    xt = xtok[tl % 2]
                    xk = xblk[tl % 2]
                    P.dma([xt], xt[:], [xin], xin.t[s, tl * 128:(tl + 1) * 128, :])
                    for half in range(2):
                        pt, pq = nextbank()
                        for q in range(4):
                            c = half * 4 + q
                            P.op("pe", lambda: PE.transpose(out=pt[:, q * 128:(q + 1) * 128], in_=xt[:, c * 128:(c + 1) * 128],
                                                            identity=cst("ident")), [xt, consts], pq)
                        act(xk[:, half * 4:half * 4 + 4, :], pt[:, :].rearrange("p (q n) -> p q n", q=4), AF.Copy, pq, [xk])
                    tb = tl // 4
                    P.dma([xT_tk[c][tb] for c in range(8)], xT_d[:, :, tl * 128:(tl + 1) * 128], [xk], xk[:], sem_t=xk)
                P.barrier()

            for l in range(nlayer):
                with contextlib.ExitStack() as sc:
                    xb = P.sbuf("xb", [128, 8, 512], F32, sc)
                    sq = P.sbuf("sq", [128, 8, 512], F32, sc)
                    rstd = P.sbuf("rstd", [128, 512], F32, sc)
                    tmp = P.sbuf("tmp", [128, 512], F32, sc)
                    for tb in range(NTB if 'N1' in PHASES else 0):
                        load_xblock(xb, tb)
                        norm_block(xb, tb, l, s, 1, sq, rstd, tmp)
                    P.barrier()

                mixsc = contextlib.ExitStack()
                obT = []
                obT.append(P.sbuf("obT_mla", [128, 4, T], BF16, mixsc))
                with contextlib.ExitStack() as sc:
                    if 'MLA' in PHASES:
                        emit_mla(P, nc, locals(), sc, l, s)
                    P.barrier()
                obT.append(P.sbuf("obT_rg", [128, 4, T], BF16, mixsc))
                with contextlib.ExitStack() as sc:
                    if 'RG' in PHASES:
                        emit_rg(P, nc, locals(), sc, l, s)
                    P.barrier()
                obT.append(P.sbuf("obT_gdn", [128, 4, T], BF16, mixsc))
                with contextlib.ExitStack() as sc:
                    if 'GDN' in PHASES:
                        emit_gdn(P, nc, locals(), sc, l, s)
                    P.barrier()
                if dbg and l == 0 and s == 0:
                    with contextlib.ExitStack() as sc:
                        dtmp = P.sbuf("dtmp", [128, 4, T], F32, sc)
                        for b in range(3):
                            vcopy(dtmp[:], obT[b][:], [obT[b]], [dtmp])
                            P.dma([dbg_d], dbg_d.t[b], [dtmp], dtmp[:], sem_t=dtmp)
                        P.barrier()
                with contextlib.ExitStack() as sc:
                    if 'MERGE' in PHASES:
                        emit_merge(P, nc, locals(), sc, l, s)
                    P.barrier()
                mixsc.close()
                with contextlib.ExitStack() as sc:
                    if 'FFN' in PHASES:
                        emit_ffn(P, nc, locals(), sc, l, s)
                    P.barrier()

            with contextlib.ExitStack() as sc:
                xb = P.sbuf("xb", [128, 8, 512], F32, sc)
                sq = P.sbuf("sq", [128, 8, 512], F32, sc)
                rstd = P.sbuf("rstd", [128, 512], F32, sc)
                ytok = [P.sbuf("ytok", [128, D], F32, sc) for _ in range(2)]
                yc = 0
                for tb in range(NTB if 'NOOUT' not in PHASES else 0):
                    load_xblock(xb, tb)
                    vtt(sq[:], xb[:], xb[:], ALU.mult, [xb], [sq])
                    if OUTSTAGE >= 1:
                        pt, pq = nextbank()
                        for c in range(8):
                            mm(pt[:, :], pq, cst("ones"), sq[:, c, :], c == 0, c == 7, [consts, sq])
                        if OUTSTAGE >= 2:
                            rsqrt_from(rstd[:], pt[:, :], 1.0 / D, pq, [rstd])
                        else:
                            act(rstd[:], pt[:, :], AF.Copy, pq, [rstd])
                        if OUTSTAGE >= 3:
                            for c in range(8):
                                vstt(sq[:, c, :], xb[:, c, :], vec(0, "fng", c), rstd[:], ALU.mult, ALU.mult, [xb, rstd, vecs], [sq])
                    for t4 in range(4):
                        yt = ytok[yc % 2]
                        yc += 1
                        for half in range(2):
                            pt, pq = nextbank()
                            for q in range(4):
                                c = half * 4 + q
                                P.op("pe", lambda: PE.transpose(out=pt[:, q * 128:(q + 1) * 128], in_=sq[:, c, t4 * 128:(t4 + 1) * 128],
                                                                identity=cst("ident")), [sq, consts], pq)
                            act(yt[:, half * 512:(half + 1) * 512], pt[:, :], AF.Copy, pq, [yt])
                        tl = tb * 4 + t4
                        P.dma([yout], yout.t[s, tl * 128:(tl + 1) * 128, :], [yt], yt[:], sem_t=yt)
                P.barrier()
        outs = [yout] + ([dbg_d] if dbg else [])
        P.finish(outs)
        P.barrier()
    return nc, P


def _h(env):
    class E:
        pass
    e = E()
    e.__dict__.update(env)
    return e


def emit_mla(P, nc, env, sc, l, s):
    e = _h(env)
    hT, obT, consts, identb = e.hT, e.obT, e.consts, e.identb
    mm, act, vtt, vts, vstt, vcopy, vmemset = e.mm, e.act, e.vtt, e.vts, e.vstt, e.vcopy, e.vmemset
    nextbank, cst, vec, wslot, wload, rsqrt_from, hrhs = e.nextbank, e.cst, e.vec, e.wslot, e.wload, e.rsqrt_from, e.hrhs
    wA, wUQ, wKN, wV, ropet, vecs = e.wA, e.wUQ, e.wKN, e.wV, e.ropet, e.vecs
    ob = obT[0]
    cosT = P.sbuf("cosT", [96, T], BF16, sc)
    sinT = P.sbuf("sinT", [96, T], BF16, sc)
    cq = P.sbuf("cq", [128, 3, T], BF16, sc)
    ckv = P.sbuf("ckv", [128, 2, T], BF16, sc)
    krT = P.sbuf("krT", [96, T], BF16, sc)
    Vall = P.sbuf("Vall", [128, NTL, 8, 128], BF16, sc)
    qhs = [P.sbuf("qh", [96, T], BF16, sc) for _ in range(2)]
    khs = [P.sbuf("kh", [96, T], BF16, sc) for _ in range(2)]
    PTs = [P.sbuf("PT", [128, 512], BF16, sc) for _ in range(4)]
    sqs = [P.sbuf("sqm", [128, 512], F32, sc) for _ in range(3)]
    rstd = P.sbuf("rstdm", [128, 512], F32, sc)
    t1 = P.sbuf("t1m", [128, 512], F32, sc)
    t2 = P.sbuf("t2m", [128, 512], F32, sc)
    dsh = P.sbuf("dsh", [128, 512], F32, sc)
    P.dma([cosT], cosT[:], [ropet], ropet.t[0], q="pool")
    P.dma([sinT], sinT[:], [ropet], ropet.t[1], q="pool")
    vmemset(Vall[:], 1.0, [Vall])

    for (dst, col0, nch, gname) in ((cq, 0, 3, "qg"), (ckv, 384, 2, "kvg")):
        sl = wslot()
        wv = sl[:, 0:8 * nch * 128].rearrange("p (k n) -> p k n", k=8)
        wload(sl, wv, wA, wA.t[l, :, :, col0:col0 + nch * 128])
        for tb in range(NTB):
            pcs = []
            for c in range(nch):
                pt, pq = nextbank()
                for k in range(8):
                    mm(pt[:, :], pq, wv[:, k, c * 128:(c + 1) * 128], hrhs(k, tb), k == 0, k == 7, [sl, hT])
                act(sqs[c][:], pt[:, :], AF.Square, pq, [sqs[c]])
                pcs.append((pt, pq))
            pt2, pq2 = nextbank()
            for c in range(nch):
                mm(pt2[:, :], pq2, cst("ones"), sqs[c][:], c == 0, c == nch - 1, [consts, sqs[c]])
            rsqrt_from(rstd[:], pt2[:, :], 1.0 / (nch * 128), pq2, [rstd])
            for c in range(nch):
                pt, pq = pcs[c]
                vtt(t1[:], pt[:, :], rstd[:], ALU.mult, pq + [rstd], [t1])
                act(dst[:, c, tb * 512:(tb + 1) * 512], t1[:], AF.Identity, [t1, vecs], [dst], scale=vec(l, gname, c))
    sl = wslot()
    wv = sl[:, 0:8 * 192].rearrange("p (k n) -> p k n", k=8)
    wload(sl, wv, wA, wA.t[l, :, :, KR0:KR0 + 192])
    for tb in range(NTB):
        blk = slice(tb * 512, (tb + 1) * 512)
        pa, qa = nextbank()
        pb, qb_ = nextbank()
        for k in range(8):
            mm(pa[0:96, :], qa, wv[:, k, 0:96], hrhs(k, tb), k == 0, k == 7, [sl, hT])
        for k in range(8):
            mm(pb[0:96, :], qb_, wv[:, k, 96:192], hrhs(k, tb), k == 0, k == 7, [sl, hT])
        vtt(t1[0:96, :], pa[0:96, :], cosT[:, blk], ALU.mult, qa + [cosT], [t1])
        vtt(t2[0:96, :], pb[0:96, :], sinT[:, blk], ALU.mult, qb_ + [sinT], [t2])
        vtt(krT[:, blk], t1[0:96, :], t2[0:96, :], ALU.add, [t1, t2], [krT])
    sl = wslot()
    wv = sl[:, 0:1024].rearrange("p (k n) -> p k n", k=2)
    wload(sl, wv, wV, wV.t[l])
    for tl in range(NTL):
        pt, pq = nextbank()
        for c in range(2):
            mm(pt[:, :], pq, ckv[:, c, tl * 128:(tl + 1) * 128], wv[:, c, :], c == 0, c == 1, [ckv, sl])
        pv = pt[:, :].rearrange("p (h d) -> p h d", h=8)
        for par in range(2):
            act(Vall[:, tl, par::2, par * 64:par * 64 + 64], pv[:, par::2, :], AF.Copy, pq, [Vall])
    scale = 96.0 ** -0.5
    for h in range(8):
        qh, kh = qhs[h % 2], khs[h % 2]
        sl = wslot()
        wq = sl[:, 0:3 * 192].rearrange("p (k n) -> p k n", k=3)
        wload(sl, wq[:, :, 0:96], wUQ, wUQ.t[l, :, :, h * 96:(h + 1) * 96])
        wload(sl, wq[:, :, 96:192], wUQ, wUQ.t[l, :, :, 768 + h * 96:768 + (h + 1) * 96])
        wk = sl[:, 1024:1024 + 2 * 96].rearrange("p (k n) -> p k n", k=2)
        wload(sl, wk, wKN, wKN.t[l, :, :, h * 96:(h + 1) * 96])
        for tb in range(NTB):
            blk = slice(tb * 512, (tb + 1) * 512)
            pa, qa = nextbank()
            pb, qb_ = nextbank()
            for c in range(3):
                mm(pa[0:96, :], qa, wq[:, c, 0:96], cq[:, c, blk], c == 0, c == 2, [sl, cq])
            for c in range(3):
                mm(pb[0:96, :], qb_, wq[:, c, 96:192], cq[:, c, blk], c == 0, c == 2, [sl, cq])
            vtt(t1[0:96, :], pa[0:96, :], cosT[:, blk], ALU.mult, qa + [cosT], [t1])
            vtt(t2[0:96, :], pb[0:96, :], sinT[:, blk], ALU.mult, qb_ + [sinT], [t2])
            vtt(qh[:, blk], t1[0:96, :], t2[0:96, :], ALU.add, [t1, t2], [qh])
            pk, qk = nextbank()
            for c in range(2):
                mm(pk[0:96, :], qk, wk[:, c, 0:96], ckv[:, c, blk], c == 0, False, [sl, ckv])
            mm(pk[0:96, :], qk, identb[0:96, 0:96], krT[:, blk], False, True, [identb, krT])
            act(kh[:, blk], pk[0:96, :], AF.Copy, qk, [kh])
        par = h % 2
        num = slice(par * 64, par * 64 + 64)
        den = slice((1 - par) * 64, (1 - par) * 64 + 64)
        pti = 0
        for qb in range(NTB):
            qblk = slice(qb * 512, (qb + 1) * 512)
            po, qo = nextbank([0, 1])
            LOOK = 2
            pend = []
            for kt in range(NTL + LOOK):
                if kt < NTL:
                    ps_, qs_ = nextbank([2, 3, 4, 5, 6, 7])
                    mm(ps_[:, :], qs_, kh[:, kt * 128:(kt + 1) * 128], qh[:, qblk], True, True, [kh, qh])
                    PT = PTs[pti % 4]
                    pti += 1
                    act(PT[:], ps_[:, :], AF.Exp, qs_, [PT], scale=scale)
                    pend.append(PT)
                if kt >= LOOK:
                    k2 = kt - LOOK
                    PT2 = pend[k2]
                    mm(po[:, :], qo, Vall[:, k2, h, :], PT2[:], k2 == 0, k2 == NTL - 1, [Vall, PT2])
            act(dsh[num, :], po[den, :], AF.Copy, qo, [dsh])
            P.op("dve", lambda: nc.vector.reciprocal(out=dsh[num, :], in_=dsh[num, :]), [dsh], [dsh])
            vtt(ob[num, h // 2, qblk], po[num, :], dsh[num, :], ALU.mult, qo + [dsh], [ob])


def emit_rg(P, nc, env, sc, l, s):
    e = _h(env)
    hT, obT, consts = e.hT, e.obT, e.consts
    mm, act, vtt, vts, vstt, vcopy, vmemset = e.mm, e.act, e.vtt, e.vts, e.vstt, e.vcopy, e.vmemset
    nextbank, cst, vec, wslot, wload, hrhs = e.nextbank, e.cst, e.vec, e.wslot, e.wload, e.hrhs
    wA, wRG, vecs, cneg, cneg2 = e.wA, e.wRG, e.vecs, e.cneg, e.cneg2
    ob = obT[1]
    xpads = [P.sbuf("xpad", [128, T + 4], F32, sc) for _ in range(2)]
    xc = P.sbuf("xc", [128, T], F32, sc)
    xcb = P.sbuf("xcb", [128, T], BF16, sc)
    R = P.sbuf("R", [128, T], F32, sc)
    I = P.sbuf("I", [128, T], F32, sc)
    Ab = P.sbuf("Ab", [128, T], F32, sc)
    HF = P.sbuf("HF", [128, T], F32, sc)
    HB = P.sbuf("HB", [128, T], F32, sc)
    gts = [P.sbuf("gt", [128, T], F32, sc) for _ in range(2)]
    wrg = P.sbuf("wrg", [128, 16, 128], BF16, sc)
    P.dma([wrg], wrg[:], [wRG], wRG.t[l], q="pool")
    for xp in xpads:
        vmemset(xp[:], 0.0, [xp])

    def rg_proj(c):
        xpad, gt = xpads[c % 2], gts[c % 2]
        sl = wslot()
        wv = sl[:, 0:2048].rearrange("p (k n) -> p k n", k=8)
        wload(sl, wv[:, :, 0:128], wA, wA.t[l, :, :, C_RGX + c * 128:C_RGX + (c + 1) * 128])
        wload(sl, wv[:, :, 128:256], wA, wA.t[l, :, :, C_RGG + c * 128:C_RGG + (c + 1) * 128])
        for tb in range(NTB):
            pt, pq = nextbank()
            for k in range(8):
                mm(pt[:, :], pq, wv[:, k, 0:128], hrhs(k, tb), k == 0, k == 7, [sl, hT])
            act(xpad[:, 1 + tb * 512:1 + (tb + 1) * 512], pt[:, :], AF.Copy, pq, [xpad])
        for tb in range(NTB):
            pt, pq = nextbank()
            for k in range(8):
                mm(pt[:, :], pq, wv[:, k, 128:256], hrhs(k, tb), k == 0, k == 7, [sl, hT])
            act(gt[:, tb * 512:(tb + 1) * 512], pt[:, :], AF.Copy, pq, [gt])

    rg_proj(0)
    for c in range(4):
        if c + 1 < 4:
            rg_proj(c + 1)
        xpad, gt = xpads[c % 2], gts[c % 2]
        vtt(HB[:], gt[:], gt[:], ALU.mult, [gt], [HB])
        vts(HB[:], HB[:], 0.044715, 1.0, ALU.mult, ALU.add, [HB], [HB])
        vtt(HB[:], HB[:], gt[:], ALU.mult, [HB, gt], [HB])
        act(HB[:], HB[:], AF.Sigmoid, [HB], [HB], scale=2.0 * math.sqrt(2.0 / math.pi))
        vtt(gt[:], gt[:], HB[:], ALU.mult, [gt, HB], [gt])
        act(xc[:], xpad[:, 0:T], AF.Identity, [xpad, vecs], [xc], bias=vec(l, "rgcb", c), scale=vec(l, "rgcw", c * 4 + 0))
        for k in range(1, 4):
            vstt(xc[:], xpad[:, k:k + T], vec(l, "rgcw", c * 4 + k), xc[:], ALU.mult, ALU.add, [xpad, xc, vecs], [xc])
        vcopy(xcb[:], xc[:], [xc], [xcb])
        if RGDBG == 1:
            vcopy(ob[:, c, :], xc[:], [xc], [ob])
            continue
        if RGDBG == 2:
            vcopy(ob[:, c, :], gt[:], [gt], [ob])
            continue
        for d in range(2):
            for (ai, dst, bname) in ((0, R, "rgba"), (1, I, "rgbi")):
                wi = (d * 2 + ai) * 4 + c
                for tb in range(NTB):
                    blk = slice(tb * 512, (tb + 1) * 512)
                    pt, pq = nextbank()
                    mm(pt[:, :], pq, wrg[:, wi, :], xcb[:, blk], True, True, [wrg, xcb])
                    act(dst[:, blk], pt[:, :], AF.Sigmoid, pq + [vecs], [dst], bias=vec(l, bname, d * 4 + c))
            if RGDBG == 3 and d == 0:
                vcopy(ob[:, c, :], R[:], [R], [ob])
            if RGDBG == 4 and d == 0:
                vcopy(ob[:, c, :], I[:], [I], [ob])
            act(Ab[:], R[:], AF.Exp, [R, cneg], [Ab], scale=cneg[:, l, d * 4 + c:d * 4 + c + 1])
            if RGDBG == 5 and d == 0:
                vcopy(ob[:, c, :], Ab[:], [Ab], [ob])
            act(R[:], R[:], AF.Exp, [R, cneg2], [R], scale=cneg2[:, l, d * 4 + c:d * 4 + c + 1])
            vts(R[:], R[:], -1.0, 1.0, ALU.mult, ALU.add, [R], [R])
            vts(R[:], R[:], 0.0, None, ALU.max, None, [R], [R])
            act(R[:], R[:], AF.Sqrt, [R], [R])
            edge = 0 if d == 0 else T - 1
            vmemset(R[:, edge:edge + 1], 1.0, [R])
            vtt(I[:], I[:], xc[:], ALU.mult, [I, xc], [I])
            vtt(I[:], I[:], R[:], ALU.mult, [I, R], [I])
            if d == 0:
                P.op("dve", lambda: nc.vector.tensor_tensor_scan(out=HF[:], data0=Ab[:], data1=I[:], initial=0.0,
                                                                 op0=ALU.mult, op1=ALU.add), [Ab, I], [HF])
            else:
                P.op("dve", lambda: nc.vector.tensor_tensor_scan(out=HB[:, ::-1], data0=Ab[:, ::-1], data1=I[:, ::-1], initial=0.0,
                                                                 op0=ALU.mult, op1=ALU.add), [Ab, I], [HB])
        if RGDBG in (3, 4, 5):
            continue
        if RGDBG == 6:
            vcopy(ob[:, c, :], HF[:], [HF], [ob])
            continue
        if RGDBG == 7:
            vcopy(ob[:, c, :], HB[:], [HB], [ob])
            continue
        vtt(HF[:], HF[:], HB[:], ALU.add, [HF, HB], [HF])
        vtt(ob[:, c, :], HF[:], gt[:], ALU.mult, [HF, gt], [ob])


def emit_gdn(P, nc, env, sc, l, s):
    e = _h(env)
    hT, obT, consts, identb = e.hT, e.obT, e.consts, e.identb
    mm, act, vtt, vts, vstt, vcopy, vmemset = e.mm, e.act, e.vtt, e.vts, e.vstt, e.vcopy, e.vmemset
    nextbank, nextq, cst, vec, wslot, wload, rsqrt_from, hrhs = e.nextbank, e.nextq, e.cst, e.vec, e.wslot, e.wload, e.rsqrt_from, e.hrhs
    nextq2 = e.nextq2
    negidentb = e.negidentb
    wA, vecs, negA, epsc = e.wA, e.vecs, e.negA, e.epsc
    ob = obT[2]
    qT = P.sbuf("gqT", [128, T], BF16, sc)
    kT = P.sbuf("gkT", [128, T], BF16, sc)
    ktok = P.sbuf("gktok", [128, NTL, 128], BF16, sc)
    vtok = P.sbuf("gvtok", [128, NTL, 128], BF16, sc)
    otok = P.sbuf("gotok", [128, NTL, 128], F32, sc)
    gtok = P.sbuf("ggtok", [128, NTL, 8], F32, sc)
    btok = P.sbuf("gbtok", [128, NTL, 8], F32, sc)
    rtmp = P.sbuf("grtmp", [128, 512], F32, sc)
    S = [P.sbuf("gS", [128, 128], F32, sc) for _ in range(2)]
    Sb = [P.sbuf("gSb", [128, 128], BF16, sc) for _ in range(2)]
    szs = [P.sbuf("gsz", [128, 128], F32, sc) for _ in range(3)]
    ons = [P.sbuf("gon", [128, 128], F32, sc) for _ in range(3)]
    ssqs = [P.sbuf("gssq", [128, 4], F32, sc) for _ in range(3)]

    sl = wslot()
    wab = sl[:, 0:128].rearrange("p (k n) -> p k n", k=8)
    wload(sl, wab, wA, wA.t[l, :, :, C_AB:C_AB + 16])
    for tl in range(NTL):
        pt, pq = nextbank()
        for k in range(8):
            mm(pt[:, 0:16], pq, hT[:, k, tl * 128:(tl + 1) * 128], wab[:, k, :], k == 0, k == 7, [hT, sl])
        vtt(gtok[:, tl, :], pt[:, 0:8], vec(l, "gdtb", 0, 8), ALU.add, pq + [vecs], [gtok])
        act(btok[:, tl, :], pt[:, 8:16], AF.Sigmoid, pq, [btok])
    gfl = gtok[:].rearrange("p a b -> p (a b)")
    e.softplus_acc(rtmp[:, 0:128], gfl, rtmp[:, 128:256], rtmp[:, 256:384], rtmp[:, 384:512], [gtok], [rtmp])
    vtt(gfl, rtmp[:, 0:128], negA[:, l, :], ALU.mult, [rtmp, negA], [gtok])

    ident, ones, negones, bd = cst("ident"), cst("ones"), cst("negones"), cst("bd16")
    offmasks = [cst("m32"), cst("m64"), cst("m128")]
    G = GDN_G
    NH = G + 4
    PL = "pool" if GDN_POOL else "dve"

    for hh in range(4):
        with contextlib.ExitStack() as sc2:
            xpads = [P.sbuf("gxpad", [128, T + 4], F32, sc2) for _ in range(2)]
            xc = P.sbuf("gxc", [128, T], F32, sc2)
            rts = [rtmp] + [P.sbuf("grt", [128, 512], F32, sc2) for _ in range(3)]
            for xp in xpads:
                vmemset(xp[:], 0.0, [xp])

            def qkv_proj(role):
                ch = role * 4 + hh
                col0 = C_GQKV + ch * 128
                xpad = xpads[role % 2]
                sl = wslot()
                wv = sl[:, 0:1024].rearrange("p (k n) -> p k n", k=8)
                wload(sl, wv, wA, wA.t[l, :, :, col0:col0 + 128])
                for tb in range(NTB):
                    pt, pq = nextbank()
                    for k in range(8):
                        mm(pt[:, :], pq, wv[:, k, :], hrhs(k, tb), k == 0, k == 7, [sl, hT])
                    act(xpad[:, 1 + tb * 512:1 + (tb + 1) * 512], pt[:, :], AF.Copy, pq, [xpad])

            def qkv_post(role):
                ch = role * 4 + hh
                xpad = xpads[role % 2]
                act(xc[:], xpad[:, 0:T], AF.Identity, [xpad, vecs], [xc], scale=vec(l, "gcw", ch * 4 + 0))
                for k in range(1, 4):
                    vstt(xc[:], xpad[:, k:k + T], vec(l, "gcw", ch * 4 + k), xc[:], ALU.mult, ALU.add, [xpad, xc, vecs], [xc])
                act(xc[:], xc[:], AF.Silu, [xc], [xc])
                if role < 2:
                    blks = [slice(tb * 512, (tb + 1) * 512) for tb in range(NTB)]
                    for tb in range(NTB):
                        vtt(rts[tb][:], xc[:, blks[tb]], xc[:, blks[tb]], ALU.mult, [xc], [rts[tb]])
                    pts = []
                    for tb in range(NTB):
                        pt, pq = nextbank()
                        mm(pt[:, :], pq, ones, rts[tb][:], True, True, [consts, rts[tb]])
                        pts.append((pt, pq))
                    for tb in range(NTB):
                        pt, pq = pts[tb]
                        act(rts[tb][:], pt[:, :], AF.Ln, pq, [rts[tb]], bias=epsc[:, 0:1], scale=1.0)
                    for tb in range(NTB):
                        act(rts[tb][:], rts[tb][:], AF.Exp, [rts[tb]], [rts[tb]], scale=-0.5)
                    for tb in range(NTB):
                        if role == 0:
                            vstt(xc[:, blks[tb]], xc[:, blks[tb]], 128.0 ** -0.5, rts[tb][:], ALU.mult, ALU.mult, [xc, rts[tb]], [xc])
                        else:
                            vtt(xc[:, blks[tb]], xc[:, blks[tb]], rts[tb][:], ALU.mult, [xc, rts[tb]], [xc])
                    vcopy((qT if role == 0 else kT)[:], xc[:], [xc], [qT if role == 0 else kT])
                if role >= 1:
                    dst = ktok if role == 1 else vtok
                    for tl in range(NTL):
                        qa, qk = nextq()
                        mm(qa, [qk], xc[:, tl * 128:(tl + 1) * 128], ident, True, True, [xc, consts])
                        act(dst[:, tl, :], qa, AF.Copy, [qk], [dst])

            qkv_proj(0)
            for role in range(3):
                if role + 1 < 3:
                    qkv_proj(role + 1)
                qkv_post(role)
            P.barrier()
        vmemset(otok[:], 0.0, [otok])

        with contextlib.ExitStack() as sc3:
            W2 = 256
            def mk(name, n, dt=F32, w=W2):
                return [P.sbuf(name, [128, w], dt, sc3) for _ in range(n)]
            GM, decS, decI, Af = (mk(nm, G) for nm in ("GM", "decS", "decI", "Af"))
            IDT = BF16
            Ad, Aoff, Bm = (mk(nm, G, IDT) for nm in ("Ad", "Aoff", "Bm"))
            Y = [mk("Ya", G, IDT), mk("Yb", G, IDT)]
            YT = [mk("YTa", G, IDT), mk("YTb", G, IDT)]
            Pm, PTm, Zm, TTb = mk("Pm", G, IDT), mk("PTm", G, IDT), mk("Zm", G, IDT), mk("TTb", G, BF16)
            vb, kbg = mk("vb", G, BF16), mk("kbg", G, BF16)
            um, wTm, kdec, attnT = mk("um", NH), mk("wTm", NH, BF16), mk("kdec", NH, BF16), mk("attnT", NH, BF16)
            cols = mk("cols", NH, F32, 16)
            vnew = mk("vnew", NH, BF16)
            tmpo = mk("tmpo", NH)
            S2 = P.sbuf("gS2", [128, W2], F32, sc3)
            Sb2 = P.sbuf("gSb2", [128, W2], BF16, sc3)
            vmemset(S2[:], 0.0, [S2])
            vmemset(Sb2[:], 0.0, [Sb2])
            ident2, identb2 = cst("ident", 256), identb[:, 0:256]
            ones2, negones2 = cst("ones", 256), cst("negones", 256)
            mt01, negs01, negst01 = cst("mt0", 256), cst("negs0", 256), cst("negst0", 256)
            bd2 = cst("bd16", 256)
            offmasks2 = [cst("m32", 256), cst("m64", 256), cst("m128", 256)]
            hs = [slice(0, 128), slice(128, 256)]
            tn = lambda d, it: it if d == 0 else NTL - 1 - it

            def prep_gen(it, i, hi):
                ns = [tn(0, it), tn(1, it)]
                colj = [0 * 4 + hh, 1 * 4 + hh]
                tsl = [slice(n * 128, (n + 1) * 128) for n in ns]
                g_col = [gtok[:, ns[d], colj[d]:colj[d] + 1] for d in range(2)]
                b_col = [btok[:, ns[d], colj[d]:colj[d] + 1] for d in range(2)]
                cl = cols[hi]
                for d in range(2):
                    vts(GM[i][:, hs[d]], mt01[:, hs[d]], g_col[d], None, ALU.mult, None, [consts, gtok], [GM[i]])
                    vts(cl[:, d * 8 + 6:d * 8 + 8], ones[:, 0:2], g_col[d], None, ALU.mult, None, [consts, gtok], [cl])
                yield
                pD, qD = nextq2()
                for d in range(2):
                    mm(pD[:, hs[d]], [qD], GM[i][:, hs[d]], ones, d == 0, False, [GM[i], consts])
                mm(pD, [qD], negones, GM[i][:], False, False, [GM[i], consts])
                mm(pD, [qD], ident, negs01, False, True, [consts])
                act(decS[i][:], pD, AF.Exp, [qD], [decS[i]])
                yield
                pDT, qDT = nextq2()
                mm(pDT, [qDT], ones, GM[i][:], True, False, [GM[i], consts])
                for d in range(2):
                    mm(pDT[:, hs[d]], [qDT], GM[i][:, hs[d]], negones, False, False, [GM[i], consts])
                mm(pDT, [qDT], ident, negst01, False, True, [consts])
                act(decI[i][:], pDT, AF.Exp, [qDT], [decI[i]])
                yield
                pc, qc = nextq2()
                for d in range(2):
                    mm(pc[:, d * 4:d * 4 + 2], [qc], GM[i][:, hs[d]], ones[:, 0:2], True, True, [GM[i], consts])
                    mm(pc[:, d * 4 + 2:d * 4 + 4], [qc], ones, cl[:, d * 8 + 6:d * 8 + 8], True, True, [cl, consts])
                cl3 = cl[:, 0:16].rearrange("p (d c) -> p d c", d=2)
                pc3 = pc[:, 0:8].rearrange("p (d c) -> p d c", d=2)
                vcopy(cl3[:, :, 0:2], pc3[:, :, 1:3], [qc], [cl])
                act(cl3[:, :, 2:4], cl3[:, :, 0:2], AF.Exp, [cl], [cl])
                for d in range(2):
                    act(cl[:, d * 8 + 4:d * 8 + 5], cl[:, d * 8:d * 8 + 1], AF.Exp, [cl], [cl], bias=cl[:, d * 8 + 1:d * 8 + 2], scale=-1.0)
                    vtt(cl[:, d * 8 + 5:d * 8 + 6], cl[:, d * 8 + 2:d * 8 + 3], b_col[d], ALU.mult, [cl, btok], [cl])
                yield
                pK, qK = nextq2()
                for d in range(2):
                    mm(pK[:, hs[d]], [qK], kT[:, tsl[d]], kT[:, tsl[d]], True, True, [kT])
                for d in range(2):
                    vstt(Af[i][:, hs[d]], pK[:, hs[d]], b_col[d], decS[i][:, hs[d]], ALU.mult, ALU.mult, [qK, btok, decS[i]], [Af[i]])
                vtt(Ad[i][:], Af[i][:], bd2, ALU.mult, [Af[i], consts], [Ad[i]])
                yield
                pQ, qQ = nextq2()
                for d in range(2):
                    mm(pQ[:, hs[d]], [qQ], kT[:, tsl[d]], qT[:, tsl[d]], True, True, [kT, qT])
                vtt(attnT[hi][:], pQ, decI[i][:], ALU.mult, [qQ, decI[i]], [attnT[hi]])
                yield
                pB, qB = nextq2()
                for d in range(2):
                    mm(pB[:, hs[d]], [qB], Ad[i][:, hs[d]], identb2[:, 0:128], True, True, [Ad[i], identb])
                act(Bm[i][:], pB, AF.Copy, [qB], [Bm[i]])
                pP, qP = nextq2()
                mm(pP, [qP], identb2[:, 0:128], identb2, True, False, [identb])
                for d in range(2):
                    mm(pP[:, hs[d]], [qP], Ad[i][:, hs[d]], negidentb[:], False, d == 1, [Ad[i], negidentb])
                act(Pm[i][:], pP, AF.Copy, [qP], [Pm[i]])
                yield
                yprev, ytprev = Bm[i], Ad[i]
                for lev in range(3):
                    ycur, ytcur = Y[lev % 2][i], YT[lev % 2][i]
                    pyt, qyt = nextq2()
                    for d in range(2):
                        mm(pyt[:, hs[d]], [qyt], yprev[:, hs[d]], ytprev[:, hs[d]], True, True, [yprev, ytprev])
                    act(ytcur[:], pyt, AF.Copy, [qyt], [ytcur])
                    if lev < 2:
                        py, qy = nextq2()
                        for d in range(2):
                            mm(py[:, hs[d]], [qy], ytprev[:, hs[d]], yprev[:, hs[d]], True, True, [yprev, ytprev])
                        act(ycur[:], py, AF.Copy, [qy], [ycur])
                    yield
                    pp, qp = nextq2()
                    mm(pp, [qp], identb2[:, 0:128], Pm[i][:], True, False, [identb, Pm[i]])
                    for d in range(2):
                        mm(pp[:, hs[d]], [qp], ytcur[:, hs[d]], Pm[i][:, hs[d]], False, d == 1, [ytcur, Pm[i]])
                    act(Pm[i][:], pp, AF.Copy, [qp], [Pm[i]])
                    yield
                    yprev, ytprev = ycur, ytcur
                for mi, msk in enumerate(offmasks2):
                    ppt, qpt = nextq2()
                    for d in range(2):
                        mm(ppt[:, hs[d]], [qpt], Pm[i][:, hs[d]], identb2[:, 0:128], True, True, [Pm[i], identb])
                    act(PTm[i][:], ppt, AF.Copy, [qpt], [PTm[i]])
                    vtt(Aoff[i][:], Af[i][:], msk, ALU.mult, [Af[i], consts], [Aoff[i]])
                    pz, qz = nextq2()
                    for d in range(2):
                        mm(pz[:, hs[d]], [qz], Aoff[i][:, hs[d]], Pm[i][:, hs[d]], True, True, [Aoff[i], Pm[i]])
                    act(Zm[i][:], pz, AF.Copy, [qz], [Zm[i]], scale=-1.0)
                    yield
                    pt2, qt2 = nextq2()
                    mm(pt2, [qt2], identb2[:, 0:128], Pm[i][:], True, False, [identb, Pm[i]])
                    for d in range(2):
                        mm(pt2[:, hs[d]], [qt2], PTm[i][:, hs[d]], Zm[i][:, hs[d]], False, d == 1, [PTm[i], Zm[i]])
                    if mi < 2:
                        act(Pm[i][:], pt2, AF.Copy, [qt2], [Pm[i]])
                    else:
                        act(TTb[i][:], pt2, AF.Copy, [qt2], [TTb[i]])
                    yield
                for d in range(2):
                    vts(vb[i][:, hs[d]], vtok[:, ns[d], :], b_col[d], None, ALU.mult, None, [vtok, btok], [vb[i]])
                    vts(kbg[i][:, hs[d]], ktok[:, ns[d], :], cl[:, d * 8 + 5:d * 8 + 6], None, ALU.mult, None, [ktok, cl], [kbg[i]])
                    vts(kdec[hi][:, hs[d]], ktok[:, ns[d], :], cl[:, d * 8 + 4:d * 8 + 5], None, ALU.mult, None, [ktok, cl], [kdec[hi]])
                pu, qu = nextq2()
                for d in range(2):
                    mm(pu[:, hs[d]], [qu], TTb[i][:, hs[d]], vb[i][:, hs[d]], True, True, [TTb[i], vb[i]])
                act(um[hi][:], pu, AF.Copy, [qu], [um[hi]])
                yield
                pw, qw = nextq2()
                for d in range(2):
                    mm(pw[:, hs[d]], [qw], kbg[i][:, hs[d]], TTb[i][:, hs[d]], True, True, [TTb[i], kbg[i]])
                act(wTm[hi][:], pw, AF.Copy, [qw], [wTm[hi]])

            def step_gen(it, i):
                ns = [tn(0, it), tn(1, it)]
                tsl = [slice(n * 128, (n + 1) * 128) for n in ns]
                cl = cols[i]
                p1, q1 = nextq2()
                for d in range(2):
                    mm(p1[:, hs[d]], [q1], wTm[i][:, hs[d]], Sb2[:, hs[d]], True, True, [wTm[i], Sb2])
                vstt(vnew[i][:], p1, -1.0, um[i][:], ALU.mult, ALU.add, [q1, um[i]], [vnew[i]])
                p2, q2 = nextq2()
                for d in range(2):
                    mm(p2[:, hs[d]], [q2], qT[:, tsl[d]], Sb2[:, hs[d]], True, True, [qT, Sb2])
                for d in range(2):
                    vts(tmpo[i][:, hs[d]], p2[:, hs[d]], cl[:, d * 8 + 2:d * 8 + 3], None, ALU.mult, None, [q2, cl], [tmpo[i]])
                yield
                p4, q4 = nextq2()
                for d in range(2):
                    mm(p4[:, hs[d]], [q4], kdec[i][:, hs[d]], vnew[i][:, hs[d]], True, True, [kdec[i], vnew[i]])
                for d in range(2):
                    vstt(S2[:, hs[d]], S2[:, hs[d]], cl[:, d * 8 + 3:d * 8 + 4], p4[:, hs[d]], ALU.mult, ALU.add, [S2, cl, q4], [S2])
                act(Sb2[:], S2[:], AF.Copy, [S2], [Sb2])
                yield
                p3, q3 = nextq2()
                for d in range(2):
                    mm(p3[:, hs[d]], [q3], attnT[i][:, hs[d]], vnew[i][:, hs[d]], True, True, [attnT[i], vnew[i]])
                vtt(tmpo[i][:], tmpo[i][:], p3, ALU.add, [q3, tmpo[i]], [tmpo[i]])
                for d in range(2):
                    vtt(otok[:, ns[d], :], otok[:, ns[d], :], tmpo[i][:, hs[d]], ALU.add, [otok, tmpo[i]], [otok])

            nxt = 0
            free_s = list(range(G))
            free_h = list(range(NH))
            hmap = {}
            prep_done = set()
            step_done = -1
            step_started = -1
            active = []
            while step_done < NTL - 1:
                while nxt < NTL and free_s and free_h:
                    it = nxt
                    nxt += 1
                    si = free_s.pop(0)
                    hi = free_h.pop(0)
                    hmap[it] = hi
                    active.append(("P", it, prep_gen(it, si, hi), si))
                it = step_started + 1
                if it < NTL and it in prep_done and step_done == it - 1:
                    active.append(("S", it, step_gen(it, hmap[it]), None))
                    step_started = it
                for task in list(active):
                    kind, it, gen, si = task
                    try:
                        next(gen)
                    except StopIteration:
                        active.remove(task)
                        if kind == "P":
                            prep_done.add(it)
                            free_s.append(si)
                        else:
                            step_done = it
                            free_h.append(hmap[it])
            P.barrier()
        sl = wslot()
        wz = sl[:, 0:1024].rearrange("p (k n) -> p k n", k=8)
        wload(sl, wz, wA, wA.t[l, :, :, C_GZ + hh * 128:C_GZ + (hh + 1) * 128])
        def outA(tl):
            sz, on, ssq = szs[tl % 3], ons[tl % 3], ssqs[tl % 3]
            pt, pq = nextbank()
            for k in range(8):
                mm(pt[:, 0:128], pq, hT[:, k, tl * 128:(tl + 1) * 128], wz[:, k, :], k == 0, k == 7, [hT, sl])
            act(sz[:], pt[:, 0:128], AF.Silu, pq, [sz])
            P.op("dve", lambda: nc.vector.tensor_tensor(out=on[:], in0=otok[:, tl, :], in1=otok[:, tl, :], op=ALU.mult), [otok], [on])
            P.op("dve", lambda: nc.vector.tensor_reduce(out=ssq[:, 0:1], in_=on[:], axis=AX.X, op=ALU.add), [on], [ssq])
            act(ssq[:, 1:2], ssq[:, 0:1], AF.Ln, [ssq], [ssq], bias=epsc[:, 0:1], scale=1.0 / 128)
            act(ssq[:, 1:2], ssq[:, 1:2], AF.Exp, [ssq], [ssq], scale=-0.5)
            vstt(on[:], otok[:, tl, :], ssq[:, 1:2], vec(l, "gng", 0, 128), ALU.mult, ALU.mult, [otok, ssq, vecs], [on])
            vtt(on[:], on[:], sz[:], ALU.mult, [on, sz], [on])

        def outB(tl):
            on = ons[tl % 3]
            qa, qk = nextq()
            mm(qa, [qk], on[:], ident, True, True, [on, consts])
            act(ob[:, hh, tl * 128:(tl + 1) * 128], qa, AF.Copy, [qk], [ob])

        for tl in range(NTL + 2):
            if tl < NTL:
                outA(tl)
            if tl >= 2:
                outB(tl - 2)


def emit_merge(P, nc, env, sc, l, s):
    e = _h(env)
    hT, obT, consts = e.hT, e.obT, e.consts
    mm, act, vtt, vts, vstt, vcopy = e.mm, e.act, e.vtt, e.vts, e.vstt, e.vcopy
    nextbank, cst, vec, wslot, wload, hrhs = e.nextbank, e.cst, e.vec, e.wslot, e.wload, e.hrhs
    wA, wBR, wO, mod = e.wA, e.wBR, e.wO, e.mod
    mT = P.sbuf("mT", [128, 8, T], BF16, sc)
    Gs = [P.sbuf("Gs", [128, 512], F32, sc) for _ in range(2)]
    macc = P.sbuf("macc", [128, 512], F32, sc)
    mtmp = P.sbuf("mtmp", [128, 512], F32, sc)
    xb = P.sbuf("xb", [128, 8, 512], F32, sc)
    sq = P.sbuf("sq", [128, 8, 512], F32, sc)
    rstd = P.sbuf("rstd", [128, 512], F32, sc)
    tmp = P.sbuf("tmp", [128, 512], F32, sc)
    gi = 0
    for j in range(8):
        sl = wslot()
        wg = sl[:, 0:3072].rearrange("p (b k n) -> p b k n", b=3, k=8)
        for b in range(3):
            wload(sl, wg[:, b], wA, wA.t[l, :, :, C_GATE + b * 1024 + j * 128:C_GATE + b * 1024 + (j + 1) * 128])
        sl2 = wslot()
        wb = sl2[:, 0:1536].rearrange("p (k n) -> p k n", k=12)
        wload(sl2, wb, wBR, wBR.t[l, :, :, j * 128:(j + 1) * 128])
        for tb in range(NTB):
            blk = slice(tb * 512, (tb + 1) * 512)
            for b in range(3):
                pg, qg = nextbank()
                for k in range(8):
                    mm(pg[:, :], qg, wg[:, b, k, :], hrhs(k, tb), k == 0, k == 7, [sl, hT])
                Gt = Gs[gi % 2]
                gi += 1
                act(Gt[:], pg[:, :], AF.Sigmoid, qg, [Gt])
                py, qy = nextbank()
                for k in range(4):
                    mm(py[:, :], qy, wb[:, b * 4 + k, :], obT[b][:, k, blk], k == 0, k == 3, [sl2, obT[b]])
                if b == 0:
                    vtt(macc[:], py[:, :], Gt[:], ALU.mult, qy + [Gt], [macc])
                elif b == 1:
                    vtt(mtmp[:], py[:, :], Gt[:], ALU.mult, qy + [Gt], [mtmp])
                    vtt(macc[:], macc[:], mtmp[:], ALU.add, [macc, mtmp], [macc])
                else:
                    vtt(mtmp[:], py[:, :], Gt[:], ALU.mult, qy + [Gt], [mtmp])
                    vtt(mT[:, j, blk], macc[:], mtmp[:], ALU.add, [macc, mtmp], [mT])
    for tb in range(NTB):
        blk = slice(tb * 512, (tb + 1) * 512)
        e.load_xblock(xb, tb)
        for half in range(2):
            sl = wslot()
            wv = sl[:, 0:4096].rearrange("p (k n) -> p k n", k=8)
            wload(sl, wv, wO, wO.t[l, :, :, half * 512:(half + 1) * 512])
            for ii in range(4):
                i = half * 4 + ii
                pt, pq = nextbank()
                for k in range(8):
                    mm(pt[:, :], pq, wv[:, k, ii * 128:(ii + 1) * 128], mT[:, k, blk], k == 0, k == 7, [sl, mT])
                vstt(xb[:, i, :], pt[:, :], mod[:, l, 16 + i, s:s + 1], xb[:, i, :], ALU.mult, ALU.add, pq + [mod, xb], [xb])
        e.store_xblock(xb, tb)
        e.norm_block(xb, tb, l, s, 2, sq, rstd, tmp)


def emit_ffn(P, nc, env, sc, l, s):
    e = _h(env)
    hT, consts = e.hT, e.consts
    mm, act, vtt, vts, vstt, vcopy, vmemset = e.mm, e.act, e.vtt, e.vts, e.vstt, e.vcopy, e.vmemset
    nextbank, cst, vec, wslot, wload, hrhs = e.nextbank, e.cst, e.vec, e.wslot, e.wload, e.hrhs
    wUP, wDN, mod, vecs, xT_d, xT_tk = e.wUP, e.wDN, e.mod, e.vecs, e.xT_d, e.xT_tk
    actT = P.sbuf("actT", [128, NJ, T], BF16, sc)
    upad = [P.sbuf("upad", [128, T + 2], F32, sc) for _ in range(2)]
    cv = [P.sbuf("cv", [128, T], F32, sc) for _ in range(2)]
    xs = [P.sbuf("xs", [128, 512], F32, sc) for _ in range(2)]
    for u in upad:
        vmemset(u[:], 0.0, [u])
    for jp in range(NJ // 2):
        sl = wslot()
        wv = sl[:, 0:4096].rearrange("p (k n) -> p k n", k=8)
        for jj in range(2):
            j = jp * 2 + jj
            wload(sl, wv[:, :, jj * 128:(jj + 1) * 128], wUP, wUP.t[l, :, :, j * 128:(j + 1) * 128])
            wload(sl, wv[:, :, 256 + jj * 128:256 + (jj + 1) * 128], wUP, wUP.t[l, :, :, DFF + j * 128:DFF + (j + 1) * 128])
        for jj in range(2):
            j = jp * 2 + jj
            for lg in range(2):
                ch = lg * NJ + j
                wc = lg * 256 + jj * 128
                for tb in range(NTB):
                    pt, pq = nextbank()
                    for k in range(8):
                        mm(pt[:, :], pq, wv[:, k, wc:wc + 128], hrhs(k, tb), k == 0, k == 7, [sl, hT])
                    act(upad[lg][:, 1 + tb * 512:1 + (tb + 1) * 512], pt[:, :], AF.Copy, pq, [upad[lg]])
                act(cv[lg][:], upad[lg][:, 0:T], AF.Identity, [upad[lg], vecs], [cv[lg]],
                    bias=vec(l, "fcb", ch), scale=vec(l, "fcw", ch * 3 + 0))
                for k in range(1, 3):
                    vstt(cv[lg][:], upad[lg][:, k:k + T], vec(l, "fcw", ch * 3 + k), cv[lg][:], ALU.mult, ALU.add,
                         [upad[lg], cv[lg], vecs], [cv[lg]])
            act(cv[1][:], cv[1][:], AF.Silu, [cv[1]], [cv[1]])
            vtt(actT[:, j, :], cv[1][:], cv[0][:], ALU.mult, [cv[0], cv[1]], [actT])
    xi = 0
    for i in range(8):
        sl = wslot()
        wv = sl[:, 0:NJ * 128].rearrange("p (k n) -> p k n", k=NJ)
        wload(sl, wv, wDN, wDN.t[l, :, :, i * 128:(i + 1) * 128])
        for tb in range(NTB):
            blk = slice(tb * 512, (tb + 1) * 512)
            pt, pq = nextbank()
            for k in range(NJ):
                mm(pt[:, :], pq, wv[:, k, :], actT[:, k, blk], k == 0, k == NJ - 1, [sl, actT])
            x1 = xs[xi % 2]
            xi += 1
            P.dma([x1], x1[:], [xT_tk[i][tb]], xT_d[:, i, blk])
            vstt(x1[:], pt[:, :], mod[:, l, 40 + i, s:s + 1], x1[:], ALU.mult, ALU.add, pq + [mod, x1], [x1])
            P.dma([xT_tk[i][tb]], xT_d[:, i, blk], [x1], x1[:], sem_t=x1)


def _fm(w, kc):
    K, N = w.shape
    return np.ascontiguousarray(w.reshape(kc, 128, N).transpose(1, 0, 2))


def _colv(v, nch):
    return np.ascontiguousarray(np.asarray(v).reshape(nch, 128).T)


def _consts():
    c = np.zeros((128, NC), np.float32)
    i = np.arange(128)
    ident = np.eye(128, dtype=np.float32)
    c[:, COFF["ident"]:COFF["ident"] + 128] = ident
    c[:, COFF["ident_b"]:COFF["ident_b"] + 128] = ident
    c[:, COFF["ones"]:COFF["ones"] + 256] = 1.0
    c[:, COFF["negones"]:COFF["negones"] + 256] = -1.0
    m, ii = np.meshgrid(i, i, indexing="ij")
    c[:, COFF["mt0"]:COFF["mt0"] + 128] = (m <= ii)
    c[:, COFF["mt1"]:COFF["mt1"] + 128] = (m >= ii)
    a, b = np.meshgrid(i, i, indexing="ij")
    v0 = (b < a)
    v1 = (b > a)
    c[:, COFF["negs0"]:COFF["negs0"] + 128] = np.where(v0, 0.0, -BIG)
    c[:, COFF["negs1"]:COFF["negs1"] + 128] = np.where(v1, 0.0, -BIG)
    eye = np.eye(128, dtype=bool)
    c[:, COFF["negst0"]:COFF["negst0"] + 128] = np.where(v0.T | eye, 0.0, -BIG)
    c[:, COFF["negst1"]:COFF["negst1"] + 128] = np.where(v1.T | eye, 0.0, -BIG)
    bdm = lambda n: ((a // n) == (b // n)).astype(np.float32)
    for sfx in ("", "_b"):
        c[:, COFF["bd16" + sfx]:COFF["bd16" + sfx] + 128] = bdm(16)
        c[:, COFF["m32" + sfx]:COFF["m32" + sfx] + 128] = bdm(32) - bdm(16)
        c[:, COFF["m64" + sfx]:COFF["m64" + sfx] + 128] = bdm(64) - bdm(32)
        c[:, COFF["m128" + sfx]:COFF["m128" + sfx] + 128] = 1.0 - bdm(64)
    return c


def _rope_tables():
    inv = 1.0 / (10000.0 ** (np.arange(0, 32, 2, dtype=np.float32) / 32.0))
    ang = np.arange(T, dtype=np.float32)[None, :] * inv[:, None].astype(np.float32)
    cos, sin = np.cos(ang).astype(np.float32), np.sin(ang).astype(np.float32)
    r = np.zeros((2, 96, T), np.float32)
    r[0, 0:64] = 1.0
    r[0, 64:80] = cos
    r[0, 80:96] = cos
    r[1, 64:80] = -sin
    r[1, 80:96] = sin
    return r


def prep_weights(inp):
    L = L_ALL
    f = lambda k: np.asarray(inp[k], np.float32)
    w_in = f("w_in")
    wA = np.zeros((L, 128, 8, NA), np.float32)
    sw = np.concatenate([np.arange(16, 32), np.arange(0, 16)])
    wUQ = np.zeros((L, 128, 3, 1536), np.float32)
    wKN = np.zeros((L, 128, 2, 768), np.float32)
    wV = np.zeros((L, 128, 2, 512), np.float32)
    wRG = np.zeros((L, 128, 16, 128), np.float32)
    vecs = np.zeros((128, L, NV), np.float32)
    for l in range(L):
        ext = np.zeros((D, 192), np.float32)
        kr = w_in[l][:, 640:672]
        ext[:, 64:96] = kr
        ext[:, 96 + 64:96 + 96] = kr[:, sw]
        wA[l] = _fm(np.concatenate([w_in[l], ext], axis=1), 8)
        uq = f("mla_w_uq")[l]
        uqs = uq.reshape(384, 8, 96).copy()
        uqs[:, :, 64:96] = uqs[:, :, 64:96][:, :, sw]
        wUQ[l] = _fm(np.concatenate([uq, uqs.reshape(384, 768)], axis=1), 3)
        ukv = f("mla_w_ukv")[l].reshape(256, 8, 128)
        kn = np.zeros((256, 8, 96), np.float32)
        kn[:, :, 0:64] = ukv[:, :, 0:64]
        wKN[l] = _fm(kn.reshape(256, 768), 2)
        wV[l] = _fm(np.ascontiguousarray(ukv[:, :, 64:128]).reshape(256, 512), 2)
        for d in range(2):
            for ai, nm in enumerate(("rg_w_a", "rg_w_i")):
                w = f(nm)[l, d]
                for c in range(4):
                    m = np.zeros((128, 128), np.float32)
                    m[0:64, 0:64] = w[2 * c]
                    m[64:128, 64:128] = w[2 * c + 1]
                    wRG[l, :, (d * 2 + ai) * 4 + c, :] = m

        def put(name, arr):
            o, w = VOFF[name]
            vecs[:, l, o:o + w] = arr
        put("ln1", _colv(f("ln1_g")[l], 8))
        put("ln2", _colv(f("ln2_g")[l], 8))
        put("bmod", _colv(f("b_mod")[l], 48))
        put("qg", _colv(f("mla_q_norm_g")[l], 3))
        put("kvg", _colv(f("mla_kv_norm_g")[l], 2))
        put("rgcw", np.ascontiguousarray(f("rg_conv_w")[l].reshape(4, 4, 128).transpose(2, 1, 0)).reshape(128, 16))
        put("rgcb", _colv(f("rg_conv_b")[l], 4))
        put("rgba", np.ascontiguousarray(f("rg_b_a")[l].reshape(2, 4, 128).transpose(2, 0, 1)).reshape(128, 8))
        put("rgbi", np.ascontiguousarray(f("rg_b_i")[l].reshape(2, 4, 128).transpose(2, 0, 1)).reshape(128, 8))
        put("rglam", np.ascontiguousarray(f("rg_lam")[l].reshape(2, 4, 128).transpose(2, 0, 1)).reshape(128, 8))
        put("gcw", np.ascontiguousarray(f("gdn_conv_w")[l].reshape(4, 12, 128).transpose(2, 1, 0)).reshape(128, 48))
        put("gng", np.broadcast_to(f("gdn_norm_g")[l][None, :], (128, 128)))
        put("galog", np.broadcast_to(np.tile(f("gdn_a_log")[l].reshape(8), 16)[None, :], (128, 128)))
        put("gdtb", np.broadcast_to(np.tile(f("gdn_dt_bias")[l].reshape(8), 16)[None, :], (128, 128)))
        put("fcw", np.ascontiguousarray(f("ffn_conv_w")[l].reshape(3, 44, 128).transpose(2, 1, 0)).reshape(128, 132))
        put("fcb", _colv(f("ffn_conv_b")[l], 44))
        put("fng", _colv(f("final_norm_g"), 8))
    wd = {
        "wA": wA, "wUQ": wUQ, "wKN": wKN, "wV": wV, "wRG": wRG,
        "wBR": np.stack([_fm(f("w_branch")[l].reshape(1536, 1024), 12) for l in range(L)]),
        "wO": np.stack([_fm(f("w_out")[l], 8) for l in range(L)]),
        "wUP": np.stack([_fm(f("ffn_w_up")[l], 8) for l in range(L)]),
        "wDN": np.stack([_fm(f("ffn_w_down")[l], NJ) for l in range(L)]),
        "wMOD": np.stack([_fm(f("w_mod")[l], 8) for l in range(L)]),
        "vecs": vecs, "consts": _consts(), "ropet": _rope_tables(),
    }
    return wd


def core_inputs(wd, xs, cs):
    m = dict(wd)
    m["xin"] = np.ascontiguousarray(xs, dtype=np.float32)
    nseq = xs.shape[0]
    m["cT"] = np.ascontiguousarray(np.asarray(cs, np.float32).reshape(nseq, 8, 128).transpose(2, 1, 0))
    return m


_CACHE = {}


def kernel(**inputs):
    wd = prep_weights(inputs)
    xp = np.asarray(inputs["x_prompt"], np.float32)
    xsm = np.asarray(inputs["x_sample"], np.float32)
    cp = np.asarray(inputs["c_prompt"], np.float32)
    csm = np.asarray(inputs["c_sample"], np.float32)
    xall = np.concatenate([xp, xsm], axis=0)
    call = np.concatenate([cp, csm], axis=0)
    nseq = xall.shape[0] // NCORES
    if "nc" not in _CACHE:
        _CACHE["nc"] = build(nseq)[0]
    nc = _CACHE["nc"]
    in_maps = [core_inputs(wd, xall[i * nseq:(i + 1) * nseq], call[i * nseq:(i + 1) * nseq]) for i in range(NCORES)]
    res = run_bass_kernel_spmd(nc, in_maps, core_ids=list(range(NCORES)))
    y = np.concatenate([np.asarray(r["yout"], np.float32) for r in res.results], axis=0)
    nb = xp.shape[0]
    return (np.ascontiguousarray(y[:nb]), np.ascontiguousarray(y[nb:]))
```

```python
import contextlib
import math
import numpy as np
import concourse.bass as bass
import concourse.mybir as mybir
from concourse.bass_utils import run_bass_kernel_spmd

F32 = mybir.dt.float32
BF16 = mybir.dt.bfloat16
AF = mybir.ActivationFunctionType
ALU = mybir.AluOpType
AX = mybir.AxisListType

D = 1024
T = 2048
NTB = 4
NTL = 16
L_ALL = 2
NCORES = 8
EPS = 1e-6
N_IN = 6832
KR0 = N_IN
NA = N_IN + 192
C_RGX, C_RGG, C_GQKV, C_GZ, C_AB, C_GATE = 672, 1184, 1696, 3232, 3744, 3760
DFF = 2816
NJ = 22
BIG = 30000.0
SEM_LIMIT = 30000

VOFF = {}
_o = 0
for _n, _w in [("ln1", 8), ("ln2", 8), ("bmod", 48), ("qg", 3), ("kvg", 2), ("rgcw", 16), ("rgcb", 4),
               ("rgba", 8), ("rgbi", 8), ("rglam", 8), ("gcw", 48), ("gng", 128), ("galog", 128), ("gdtb", 128),
               ("fcw", 132), ("fcb", 44), ("fng", 8)]:
    VOFF[_n] = (_o, _w)
    _o += _w
NV = _o
COFF = {}
_o = 0
for _n in ["ident", "ident_b", "ones", "ones_b", "negones", "negones_b", "mt0", "mt1", "negs0", "negs1", "negst0", "negst1",
           "bd16", "bd16_b", "m32", "m32_b", "m64", "m64_b", "m128", "m128_b"]:
    COFF[_n] = _o
    _o += 128
NC = _o


class Tk:
    __slots__ = ("name", "w", "r", "t", "dk")

    def __init__(self, name, t=None):
        self.name = name
        self.w = None
        self.r = {}
        self.t = t
        self.dk = name

    def __getitem__(self, idx):
        return self.t[idx]


class Prog:
    def __init__(self, nc):
        self.nc = nc
        self.es = contextlib.ExitStack()
        self.eng = {"pe": nc.tensor, "act": nc.scalar, "dve": nc.vector, "pool": nc.gpsimd, "sp": nc.sync}
        self.sems = {}
        self.cnt = {}
        self.cur = {}
        self.epoch = {e: 0 for e in self.eng}
        self.seen = {e: {} for e in self.eng}
        self.ninst = 0
        self.uid = 0
        self.namectr = {}
        for e in self.eng:
            self._new_epoch(e)

    def _mk_sem(self, key):
        s = self.es.enter_context(self.nc.semaphore("s_%s" % (key,)))
        self.sems[key] = s
        self.cnt[key] = 0
        return s

    def _new_epoch(self, e):
        key = "%s%d" % (e, self.epoch[e])
        self.epoch[e] += 1
        self._mk_sem(key)
        self.cur[e] = key

    def sbuf(self, name, shape, dt, es=None):
        self.uid += 1
        t = (es or self.es).enter_context(self.nc.sbuf_tensor("%s_%d" % (name, self.uid), list(shape), dt))
        tk = Tk(name + str(self.uid), t)
        c = self.namectr.get(name, 0)
        self.namectr[name] = c + 1
        tk.dk = "%s_%d" % (name, c % 4)
        return tk

    def dram(self, name, shape, dt, kind="Internal"):
        t = self.nc.dram_tensor(name, list(shape), dt, kind=kind).ap()
        return Tk(name, t)

    def _deps(self, reads, writes):
        deps = {}
        for t in reads:
            if t.w is not None:
                k, v = t.w
                if deps.get(k, 0) < v:
                    deps[k] = v
        for t in writes:
            if t.w is not None:
                k, v = t.w
                if deps.get(k, 0) < v:
                    deps[k] = v
            for k, v in t.r.items():
                if deps.get(k, 0) < v:
                    deps[k] = v
        return deps

    def _wait(self, e, deps):
        eng = self.eng[e]
        seen = self.seen[e]
        for k, v in deps.items():
            isdma = k.startswith("dma")
            if (not isdma) and k.startswith(e) and e in ("pe", "sp"):
                continue
            if isdma:
                v = self.cnt[k]
            if seen.get(k, 0) < v:
                eng.wait_ge(self.sems[k], v)
                seen[k] = v
                self.ninst += 1

    def op(self, e, fn, reads=(), writes=()):
        self._wait(e, self._deps(reads, writes))
        if self.cnt[self.cur[e]] >= SEM_LIMIT:
            self._new_epoch(e)
        key = self.cur[e]
        inst = fn()
        self.cnt[key] += 1
        v = self.cnt[key]
        inst.then_inc(self.sems[key], 1)
        self.ninst += 1
        for t in reads:
            t.r[key] = v
        for t in writes:
            t.w = (key, v)
            t.r = {}
        return inst

    def dma(self, out_ts, out_ap, in_ts, in_ap, q="sp", sem_t=None):
        st = sem_t if sem_t is not None else out_ts[0]
        key = "dma_" + st.dk
        if key not in self.sems:
            self._mk_sem(key)
        self._wait(q, self._deps(in_ts, out_ts))
        inst = self.eng[q].dma_start(out=out_ap, in_=in_ap)
        self.cnt[key] += 16
        v = self.cnt[key]
        inst.then_inc(self.sems[key], 16)
        self.ninst += 1
        for t in in_ts:
            t.r[key] = v
        for t in out_ts:
            t.w = (key, v)
            t.r = {}
        return inst

    def barrier(self):
        allk = {k: v for k, v in self.cnt.items() if v > 0}
        for e in self.eng:
            self._wait(e, allk)

    def finish(self, outs, q="sp"):
        deps = {}
        for t in outs:
            if t.w is not None:
                k, v = t.w
                deps[k] = max(deps.get(k, 0), v)
        self._wait(q, deps)


OUTSTAGE = 3
GDNSTAGE = 9
GDN_G = 3
GDN_POOL = False
GDN_BF16INV = True
GSUB = 9
RGDBG = 0
PHASES = set(['N1', 'MLA', 'RG', 'GDN', 'MERGE', 'FFN'])


def build(nseq, nlayer=L_ALL, dbg=False):
    nc = bass.Bass("TRN2", target_bir_lowering=False)
    P = Prog(nc)
    L = L_ALL
    xin = P.dram("xin", [nseq, T, D], F32, "ExternalInput")
    cT = P.dram("cT", [128, 8, nseq], F32, "ExternalInput")
    wA = P.dram("wA", [L, 128, 8, NA], F32, "ExternalInput")
    wUQ = P.dram("wUQ", [L, 128, 3, 1536], F32, "ExternalInput")
    wKN = P.dram("wKN", [L, 128, 2, 768], F32, "ExternalInput")
    wV = P.dram("wV", [L, 128, 2, 512], F32, "ExternalInput")
    wRG = P.dram("wRG", [L, 128, 16, 128], F32, "ExternalInput")
    wBR = P.dram("wBR", [L, 128, 12, 1024], F32, "ExternalInput")
    wO = P.dram("wO", [L, 128, 8, 1024], F32, "ExternalInput")
    wUP = P.dram("wUP", [L, 128, 8, 2 * DFF], F32, "ExternalInput")
    wDN = P.dram("wDN", [L, 128, NJ, 1024], F32, "ExternalInput")
    wMOD = P.dram("wMOD", [L, 128, 8, 6144], F32, "ExternalInput")
    vecs_d = P.dram("vecs", [128, L, NV], F32, "ExternalInput")
    consts_d = P.dram("consts", [128, NC], F32, "ExternalInput")
    ropet = P.dram("ropet", [2, 96, T], F32, "ExternalInput")
    yout = P.dram("yout", [nseq, T, D], F32, "ExternalOutput")
    xT_d = nc.dram_tensor("xT_scr", [128, 8, T], F32, kind="Internal").ap()
    xT_tk = [[Tk("xT_%d_%d" % (c, tb)) for tb in range(NTB)] for c in range(8)]
    dbg_d = None
    if dbg:
        dbg_d = P.dram("dbg", [3, 128, 4, T], F32, "ExternalOutput")

    V = nc.vector
    A = nc.scalar
    PE = nc.tensor
    G = nc.gpsimd

    with P.es:
        base = P.es
        consts = P.sbuf("consts", [128, NC], F32)
        vecs = P.sbuf("vecs", [128, L, NV], F32)
        identb = P.sbuf("identb", [128, 256], BF16)
        negidentb = P.sbuf("negidentb", [128, 128], BF16)
        hT = P.sbuf("hT", [128, 8, T], BF16)
        mod = P.sbuf("mod", [128, L, 48, nseq], F32)
        gs1 = P.sbuf("gs1", [128, L, 8, nseq], F32)
        gs2 = P.sbuf("gs2", [128, L, 8, nseq], F32)
        cneg = P.sbuf("cneg", [128, L, 8], F32)
        cneg2 = P.sbuf("cneg2", [128, L, 8], F32)
        negA = P.sbuf("negA", [128, L, 128], F32)
        NSLOT = 3
        wslots = [P.sbuf("wslot%d" % i, [128, 4096], BF16) for i in range(NSLOT)]
        wctr = [0]
        pbank = []
        for i in range(8):
            t = P.es.enter_context(nc.psum_tensor("psb%d" % i, [128, 512], F32))
            pbank.append((t, [Tk("psq%d_%d" % (i, q)) for q in range(4)]))
        pctr = [0]
        qctr = [0]

        def nextbank(pool=None):
            if pool is None:
                b = pbank[pctr[0] % 8]
            else:
                b = pbank[pool[pctr[0] % len(pool)]]
            pctr[0] += 1
            return b

        def nextq():
            t, qs = pbank[pctr[0] % 8]
            pctr[0] += 1
            return t[:, 0:128], qs[0]

        def nextq2():
            t, qs = pbank[pctr[0] % 8]
            pctr[0] += 1
            return t[:, 0:256], qs[0]

        def cst(name, w=128):
            o = COFF[name]
            return consts[:, o:o + w]

        def vec(l, name, j=0, w=1):
            o, _ = VOFF[name]
            return vecs[:, l, o + j:o + j + w]

        def wslot():
            s = wslots[wctr[0] % NSLOT]
            wctr[0] += 1
            return s

        def wload(slot, dst_ap, src_tk, src_ap):
            P.dma([slot], dst_ap, [src_tk], src_ap, q="pool", sem_t=slot)

        def mm(ps_ap, ps_tks, lhsT, rhs, start, stop, reads):
            P.op("pe", lambda: PE.matmul(ps_ap, lhsT=lhsT, rhs=rhs, start=start, stop=stop), reads, ps_tks)

        def act(out_ap, in_ap, func, reads, writes, bias=None, scale=None):
            kw = {}
            if bias is not None:
                kw["bias"] = bias
            if scale is not None:
                kw["scale"] = scale
            P.op("act", lambda: A.activation(out=out_ap, in_=in_ap, func=func, **kw), reads, writes)

        def vtt(out_ap, in0, in1, op, reads, writes, e="dve"):
            eng = V if e == "dve" else G
            P.op(e, lambda: eng.tensor_tensor(out=out_ap, in0=in0, in1=in1, op=op), reads, writes)

        def vts(out_ap, in0, s1, s2, op0, op1, reads, writes, e="dve"):
            eng = V if e == "dve" else G
            if op1 is None:
                P.op(e, lambda: eng.tensor_scalar(out=out_ap, in0=in0, scalar1=s1, scalar2=None, op0=op0), reads, writes)
            else:
                P.op(e, lambda: eng.tensor_scalar(out=out_ap, in0=in0, scalar1=s1, scalar2=s2, op0=op0, op1=op1), reads, writes)

        def vstt(out_ap, in0, scalar, in1, op0, op1, reads, writes, e="dve"):
            eng = V if e == "dve" else G
            P.op(e, lambda: eng.scalar_tensor_tensor(out=out_ap, in0=in0, scalar=scalar, in1=in1, op0=op0, op1=op1), reads, writes)

        def vcopy(out_ap, in_ap, reads, writes, e="dve"):
            eng = V if e == "dve" else G
            P.op(e, lambda: eng.tensor_copy(out=out_ap, in_=in_ap), reads, writes)

        def vmemset(ap, val, writes, e="dve"):
            eng = V if e == "dve" else G
            P.op(e, lambda: eng.memset(ap, val), [], writes)

        def rsqrt_from(out_ap, in_ap, scale, reads, writes):
            act(out_ap, in_ap, AF.Ln, reads, writes, bias=epsc[:, 0:1], scale=scale)
            act(out_ap, out_ap, AF.Exp, writes, writes, scale=-0.5)

        def softplus_acc(out_ap, x_ap, t1, t2, t3, rd, wr):
            vts(t1, x_ap, -1.0, None, ALU.mult, None, rd, wr)
            vtt(t1, t1, x_ap, ALU.max, rd + wr, wr)
            act(t1, t1, AF.Exp, wr, wr, scale=-1.0)
            vts(t2, t1, 2.0, None, ALU.add, None, wr, wr)
            P.op("dve", lambda: V.reciprocal(out=t2, in_=t2), wr, wr)
            vtt(t1, t1, t2, ALU.mult, wr, wr)
            vtt(t2, t1, t1, ALU.mult, wr, wr)
            vts(t3, t2, 1.0 / 11, 1.0 / 9, ALU.mult, ALU.add, wr, wr)
            for cf in (1.0 / 7, 1.0 / 5, 1.0 / 3, 1.0):
                vtt(t3, t3, t2, ALU.mult, wr, wr)
                vts(t3, t3, cf, None, ALU.add, None, wr, wr)
            vtt(t3, t3, t1, ALU.mult, wr, wr)
            vts(t1, x_ap, 0.0, None, ALU.max, None, rd + wr, wr)
            vstt(out_ap, t3, 2.0, t1, ALU.mult, ALU.add, wr, wr)

        epsc = P.sbuf("epsc", [128, 1], F32)
        vmemset(epsc[:], EPS, [epsc])
        P.dma([consts], consts[:], [consts_d], consts_d.t)
        P.dma([vecs], vecs[:], [vecs_d], vecs_d.t)
        vcopy(identb[:], cst("ident", 256), [consts], [identb])
        vts(negidentb[:], cst("ident"), -1.0, None, ALU.mult, None, [consts], [negidentb])
        with contextlib.ExitStack() as sc:
            craw = P.sbuf("craw", [128, 8, nseq], F32, sc)
            csil = P.sbuf("csil", [128, 8, nseq], BF16, sc)
            tmpm = P.sbuf("tmpm", [128, 8, nseq], F32, sc)
            spx = P.sbuf("spx", [128, 8], F32, sc)
            spt = P.sbuf("spt", [128, 3, 8], F32, sc)
            P.dma([craw], craw[:], [cT], cT.t)
            act(csil[:], craw[:], AF.Silu, [craw], [csil])
            for l in range(nlayer):
                for pn in range(12):
                    sl = wslot()
                    wv = sl[:, 0:4096].rearrange("p (k n) -> p k n", k=8)
                    wload(sl, wv, wMOD, wMOD.t[l, :, :, pn * 512:(pn + 1) * 512])
                    for jj in range(4):
                        j = pn * 4 + jj
                        pt, pq = nextbank()
                        for k in range(8):
                            mm(pt[:, 0:nseq], pq, wv[:, k, jj * 128:(jj + 1) * 128], csil[:, k, :], k == 0, k == 7, [sl, csil])
                        act(mod[:, l, j, :], pt[:, 0:nseq], AF.Identity, pq, [mod], bias=vec(l, "bmod", j))
                for (gs, lnn, j0) in ((gs1, "ln1", 8), (gs2, "ln2", 32)):
                    vts(tmpm[:], mod[:, l, j0:j0 + 8, :], 1.0, None, ALU.add, None, [mod], [tmpm])
                    for s in range(nseq):
                        vtt(gs[:, l, :, s], tmpm[:, :, s], vec(l, lnn, 0, 8), ALU.mult, [tmpm, vecs], [gs])
                vts(spx[:], vec(l, "rglam", 0, 8), -1.0, None, ALU.mult, None, [vecs], [spx])
                softplus_acc(cneg[:, l, :], spx[:], spt[:, 0, :], spt[:, 1, :], spt[:, 2, :], [spx], [spt, cneg])
                vts(cneg2[:, l, :], cneg[:, l, :], -16.0, None, ALU.mult, None, [cneg], [cneg2])
                vts(cneg[:, l, :], cneg[:, l, :], -8.0, None, ALU.mult, None, [cneg], [cneg])
                act(negA[:, l, :], vec(l, "galog", 0, 128), AF.Exp, [vecs], [negA])
                vts(negA[:, l, :], negA[:, l, :], -1.0, None, ALU.mult, None, [negA], [negA])
            P.barrier()

        def proj_fm(wv, kc, M, rhs_fn, rhs_tks, slot, consume):
            for tb in range(NTB):
                pt, pq = nextbank()
                for k in range(kc):
                    mm(pt[0:M, :], pq, wv[:, k, 0:M], rhs_fn(k, tb), k == 0, k == kc - 1, [slot] + rhs_tks)
                consume(tb, pt, pq)

        def hrhs(k, tb):
            return hT[:, k, tb * 512:(tb + 1) * 512]

        def load_xblock(xb, tb):
            for c in range(8):
                pass
            P.dma([xb], xb[:], [xT_tk[c][tb] for c in range(8)], xT_d[:, :, tb * 512:(tb + 1) * 512])

        def store_xblock(xb, tb):
            P.dma([xT_tk[c][tb] for c in range(8)], xT_d[:, :, tb * 512:(tb + 1) * 512], [xb], xb[:], sem_t=xb)

        def norm_block(xb, tb, l, s, which, sq, rstd, tmp):
            gs = gs1 if which == 1 else gs2
            shj = 0 if which == 1 else 24
            vtt(sq[:], xb[:], xb[:], ALU.mult, [xb], [sq])
            pt, pq = nextbank()
            for c in range(8):
                mm(pt[:, :], pq, cst("ones"), sq[:, c, :], c == 0, c == 7, [consts, sq])
            rsqrt_from(rstd[:], pt[:, :], 1.0 / D, pq, [rstd])
            for c in range(8):
                vtt(tmp[:], xb[:, c, :], rstd[:], ALU.mult, [xb, rstd], [tmp])
                act(hT[:, c, tb * 512:(tb + 1) * 512], tmp[:], AF.Identity, [tmp, gs, mod], [hT],
                    bias=mod[:, l, shj + c, s:s + 1], scale=gs[:, l, c, s:s + 1])

        for s in range(nseq if 'NOSEQ' not in PHASES else 0):
            with contextlib.ExitStack() as sc:
                xtok = [P.sbuf("xtok", [128, D], F32, sc) for _ in range(2)]
                xblk = [P.sbuf("xblk", [128, 8, 128], F32, sc) for _ in range(2)]
                for tl in range(NTL if 'NOL' not in PHASES else 0):
                    xt = xtok[tl % 2]
                    xk = xblk[tl % 2]
                    P.dma([xt], xt[:], [xin], xin.t[s, tl * 128:(tl + 1) * 128, :])
                    for half in range(2):
                        pt, pq = nextbank()
                        for q in range(4):
                            c = half * 4 + q
                            P.op("pe", lambda: PE.transpose(out=pt[:, q * 128:(q + 1) * 128], in_=xt[:, c * 128:(c + 1) * 128],
                                                            identity=cst("ident")), [xt, consts], pq)
                        act(xk[:, half * 4:half * 4 + 4, :], pt[:, :].rearrange("p (q n) -> p q n", q=4), AF.Copy, pq, [xk])
                    tb = tl // 4
                    P.dma([xT_tk[c][tb] for c in range(8)], xT_d[:, :, tl * 128:(tl + 1) * 128], [xk], xk[:], sem_t=xk)
                P.barrier()

            for l in range(nlayer):
                with contextlib.ExitStack() as sc:
                    xb = P.sbuf("xb", [128, 8, 512], F32, sc)
                    sq = P.sbuf("sq", [128, 8, 512], F32, sc)
                    rstd = P.sbuf("rstd", [128, 512], F32, sc)
                    tmp = P.sbuf("tmp", [128, 512], F32, sc)
                    for tb in range(NTB if 'N1' in PHASES else 0):
                        load_xblock(xb, tb)
                        norm_block(xb, tb, l, s, 1, sq, rstd, tmp)
                    P.barrier()

                mixsc = contextlib.ExitStack()
                obT = []
                obT.append(P.sbuf("obT_mla", [128, 4, T], BF16, mixsc))
                with contextlib.ExitStack() as sc:
                    if 'MLA' in PHASES:
                        emit_mla(P, nc, locals(), sc, l, s)
                    P.barrier()
                obT.append(P.sbuf("obT_rg", [128, 4, T], BF16, mixsc))
                with contextlib.ExitStack() as sc:
                    if 'RG' in PHASES:
                        emit_rg(P, nc, locals(), sc, l, s)
                    P.barrier()
                obT.append(P.sbuf("obT_gdn", [128, 4, T], BF16, mixsc))
                with contextlib.ExitStack() as sc:
                    if 'GDN' in PHASES:
                        emit_gdn(P, nc, locals(), sc, l, s)
                    P.barrier()
                if dbg and l == 0 and s == 0:
                    with contextlib.ExitStack() as sc:
                        dtmp = P.sbuf("dtmp", [128, 4, T], F32, sc)
                        for b in range(3):
                            vcopy(dtmp[:], obT[b][:], [obT[b]], [dtmp])
                            P.dma([dbg_d], dbg_d.t[b], [dtmp], dtmp[:], sem_t=dtmp)
                        P.barrier()
                with contextlib.ExitStack() as sc:
                    if 'MERGE' in PHASES:
                        emit_merge(P, nc, locals(), sc, l, s)
                    P.barrier()
                mixsc.close()
                with contextlib.ExitStack() as sc:
                    if 'FFN' in PHASES:
                        emit_ffn(P, nc, locals(), sc, l, s)
                    P.barrier()

            with contextlib.ExitStack() as sc:
                xb = P.sbuf("xb", [128, 8, 512], F32, sc)
                sq = P.sbuf("sq", [128, 8, 512], F32, sc)
                rstd = P.sbuf("rstd", [128, 512], F32, sc)
                ytok = [P.sbuf("ytok", [128, D], F32, sc) for _ in range(2)]
                yc = 0
                for tb in range(NTB if 'NOOUT' not in PHASES else 0):
                    load_xblock(xb, tb)
                    vtt(sq[:], xb[:], xb[:], ALU.mult, [xb], [sq])
                    if OUTSTAGE >= 1:
                        pt, pq = nextbank()
                        for c in range(8):
                            mm(pt[:, :], pq, cst("ones"), sq[:, c, :], c == 0, c == 7, [consts, sq])
                        if OUTSTAGE >= 2:
                            rsqrt_from(rstd[:], pt[:, :], 1.0 / D, pq, [rstd])
                        else:
                            act(rstd[:], pt[:, :], AF.Copy, pq, [rstd])
                        if OUTSTAGE >= 3:
                            for c in range(8):
                                vstt(sq[:, c, :], xb[:, c, :], vec(0, "fng", c), rstd[:], ALU.mult, ALU.mult, [xb, rstd, vecs], [sq])
                    for t4 in range(4):
                        yt = ytok[yc % 2]
                        yc += 1
                        for half in range(2):
                            pt, pq = nextbank()
                            for q in range(4):
                                c = half * 4 + q
                                P.op("pe", lambda: PE.transpose(out=pt[:, q * 128:(q + 1) * 128], in_=sq[:, c, t4 * 128:(t4 + 1) * 128],
                                                                identity=cst("ident")), [sq, consts], pq)
                            act(yt[:, half * 512:(half + 1) * 512], pt[:, :], AF.Copy, pq, [yt])
                        tl = tb * 4 + t4
                        P.dma([yout], yout.t[s, tl * 128:(tl + 1) * 128, :], [yt], yt[:], sem_t=yt)
                P.barrier()
        outs = [yout] + ([dbg_d] if dbg else [])
        P.finish(outs)
        P.barrier()
    return nc, P


def _h(env):
    class E:
        pass
    e = E()
    e.__dict__.update(env)
    return e


def emit_mla(P, nc, env, sc, l, s):
    e = _h(env)
    hT, obT, consts, identb = e.hT, e.obT, e.consts, e.identb
    mm, act, vtt, vts, vstt, vcopy, vmemset = e.mm, e.act, e.vtt, e.vts, e.vstt, e.vcopy, e.vmemset
    nextbank, cst, vec, wslot, wload, rsqrt_from, hrhs = e.nextbank, e.cst, e.vec, e.wslot, e.wload, e.rsqrt_from, e.hrhs
    wA, wUQ, wKN, wV, ropet, vecs = e.wA, e.wUQ, e.wKN, e.wV, e.ropet, e.vecs
    ob = obT[0]
    cosT = P.sbuf("cosT", [96, T], BF16, sc)
    sinT = P.sbuf("sinT", [96, T], BF16, sc)
    cq = P.sbuf("cq", [128, 3, T], BF16, sc)
    ckv = P.sbuf("ckv", [128, 2, T], BF16, sc)
    krT = P.sbuf("krT", [96, T], BF16, sc)
    Vall = P.sbuf("Vall", [128, NTL, 8, 128], BF16, sc)
    qhs = [P.sbuf("qh", [96, T], BF16, sc) for _ in range(2)]
    khs = [P.sbuf("kh", [96, T], BF16, sc) for _ in range(2)]
    PTs = [P.sbuf("PT", [128, 512], BF16, sc) for _ in range(4)]
    sqs = [P.sbuf("sqm", [128, 512], F32, sc) for _ in range(3)]
    rstd = P.sbuf("rstdm", [128, 512], F32, sc)
    t1 = P.sbuf("t1m", [128, 512], F32, sc)
    t2 = P.sbuf("t2m", [128, 512], F32, sc)
    dsh = P.sbuf("dsh", [128, 512], F32, sc)
    P.dma([cosT], cosT[:], [ropet], ropet.t[0], q="pool")
    P.dma([sinT], sinT[:], [ropet], ropet.t[1], q="pool")
    vmemset(Vall[:], 1.0, [Vall])

    for (dst, col0, nch, gname) in ((cq, 0, 3, "qg"), (ckv, 384, 2, "kvg")):
        sl = wslot()
        wv = sl[:, 0:8 * nch * 128].rearrange("p (k n) -> p k n", k=8)
        wload(sl, wv, wA, wA.t[l, :, :, col0:col0 + nch * 128])
        for tb in range(NTB):
            pcs = []
            for c in range(nch):
                pt, pq = nextbank()
                for k in range(8):
                    mm(pt[:, :], pq, wv[:, k, c * 128:(c + 1) * 128], hrhs(k, tb), k == 0, k == 7, [sl, hT])
                act(sqs[c][:], pt[:, :], AF.Square, pq, [sqs[c]])
                pcs.append((pt, pq))
            pt2, pq2 = nextbank()
            for c in range(nch):
                mm(pt2[:, :], pq2, cst("ones"), sqs[c][:], c == 0, c == nch - 1, [consts, sqs[c]])
            rsqrt_from(rstd[:], pt2[:, :], 1.0 / (nch * 128), pq2, [rstd])
            for c in range(nch):
                pt, pq = pcs[c]
                vtt(t1[:], pt[:, :], rstd[:], ALU.mult, pq + [rstd], [t1])
                act(dst[:, c, tb * 512:(tb + 1) * 512], t1[:], AF.Identity, [t1, vecs], [dst], scale=vec(l, gname, c))
    sl = wslot()
    wv = sl[:, 0:8 * 192].rearrange("p (k n) -> p k n", k=8)
    wload(sl, wv, wA, wA.t[l, :, :, KR0:KR0 + 192])
    for tb in range(NTB):
        blk = slice(tb * 512, (tb + 1) * 512)
        pa, qa = nextbank()
        pb, qb_ = nextbank()
        for k in range(8):
            mm(pa[0:96, :], qa, wv[:, k, 0:96], hrhs(k, tb), k == 0, k == 7, [sl, hT])
        for k in range(8):
            mm(pb[0:96, :], qb_, wv[:, k, 96:192], hrhs(k, tb), k == 0, k == 7, [sl, hT])
        vtt(t1[0:96, :], pa[0:96, :], cosT[:, blk], ALU.mult, qa + [cosT], [t1])
        vtt(t2[0:96, :], pb[0:96, :], sinT[:, blk], ALU.mult, qb_ + [sinT], [t2])
        vtt(krT[:, blk], t1[0:96, :], t2[0:96, :], ALU.add, [t1, t2], [krT])
    sl = wslot()
    wv = sl[:, 0:1024].rearrange("p (k n) -> p k n", k=2)
    wload(sl, wv, wV, wV.t[l])
    for tl in range(NTL):
        pt, pq = nextbank()
        for c in range(2):
            mm(pt[:, :], pq, ckv[:, c, tl * 128:(tl + 1) * 128], wv[:, c, :], c == 0, c == 1, [ckv, sl])
        pv = pt[:, :].rearrange("p (h d) -> p h d", h=8)
        for par in range(2):
            act(Vall[:, tl, par::2, par * 64:par * 64 + 64], pv[:, par::2, :], AF.Copy, pq, [Vall])
    scale = 96.0 ** -0.5
    for h in range(8):
        qh, kh = qhs[h % 2], khs[h % 2]
        sl = wslot()
        wq = sl[:, 0:3 * 192].rearrange("p (k n) -> p k n", k=3)
        wload(sl, wq[:, :, 0:96], wUQ, wUQ.t[l, :, :, h * 96:(h + 1) * 96])
        wload(sl, wq[:, :, 96:192], wUQ, wUQ.t[l, :, :, 768 + h * 96:768 + (h + 1) * 96])
        wk = sl[:, 1024:1024 + 2 * 96].rearrange("p (k n) -> p k n", k=2)
        wload(sl, wk, wKN, wKN.t[l, :, :, h * 96:(h + 1) * 96])
        for tb in range(NTB):
            blk = slice(tb * 512, (tb + 1) * 512)
            pa, qa = nextbank()
            pb, qb_ = nextbank()
            for c in range(3):
                mm(pa[0:96, :], qa, wq[:, c, 0:96], cq[:, c, blk], c == 0, c == 2, [sl, cq])
            for c in range(3):
                mm(pb[0:96, :], qb_, wq[:, c, 96:192], cq[:, c, blk], c == 0, c == 2, [sl, cq])
            vtt(t1[0:96, :], pa[0:96, :], cosT[:, blk], ALU.mult, qa + [cosT], [t1])
            vtt(t2[0:96, :], pb[0:96, :], sinT[:, blk], ALU.mult, qb_ + [sinT], [t2])
            vtt(qh[:, blk], t1[0:96, :], t2[0:96, :], ALU.add, [t1, t2], [qh])
            pk, qk = nextbank()
            for c in range(2):
                mm(pk[0:96, :], qk, wk[:, c, 0:96], ckv[:, c, blk], c == 0, False, [sl, ckv])
            mm(pk[0:96, :], qk, identb[0:96, 0:96], krT[:, blk], False, True, [identb, krT])
            act(kh[:, blk], pk[0:96, :], AF.Copy, qk, [kh])
        par = h % 2
        num = slice(par * 64, par * 64 + 64)
        den = slice((1 - par) * 64, (1 - par) * 64 + 64)
        pti = 0
        for qb in range(NTB):
            qblk = slice(qb * 512, (qb + 1) * 512)
            po, qo = nextbank([0, 1])
            LOOK = 2
            pend = []
            for kt in range(NTL + LOOK):
                if kt < NTL:
                    ps_, qs_ = nextbank([2, 3, 4, 5, 6, 7])
                    mm(ps_[:, :], qs_, kh[:, kt * 128:(kt + 1) * 128], qh[:, qblk], True, True, [kh, qh])
                    PT = PTs[pti % 4]
                    pti += 1
                    act(PT[:], ps_[:, :], AF.Exp, qs_, [PT], scale=scale)
                    pend.append(PT)
                if kt >= LOOK:
                    k2 = kt - LOOK
                    PT2 = pend[k2]
                    mm(po[:, :], qo, Vall[:, k2, h, :], PT2[:], k2 == 0, k2 == NTL - 1, [Vall, PT2])
            act(dsh[num, :], po[den, :], AF.Copy, qo, [dsh])
            P.op("dve", lambda: nc.vector.reciprocal(out=dsh[num, :], in_=dsh[num, :]), [dsh], [dsh])
            vtt(ob[num, h // 2, qblk], po[num, :], dsh[num, :], ALU.mult, qo + [dsh], [ob])


def emit_rg(P, nc, env, sc, l, s):
    e = _h(env)
    hT, obT, consts = e.hT, e.obT, e.consts
    mm, act, vtt, vts, vstt, vcopy, vmemset = e.mm, e.act, e.vtt, e.vts, e.vstt, e.vcopy, e.vmemset
    nextbank, cst, vec, wslot, wload, hrhs = e.nextbank, e.cst, e.vec, e.wslot, e.wload, e.hrhs
    wA, wRG, vecs, cneg, cneg2 = e.wA, e.wRG, e.vecs, e.cneg, e.cneg2
    ob = obT[1]
    xpads = [P.sbuf("xpad", [128, T + 4], F32, sc) for _ in range(2)]
    xc = P.sbuf("xc", [128, T], F32, sc)
    xcb = P.sbuf("xcb", [128, T], BF16, sc)
    R = P.sbuf("R", [128, T], F32, sc)
    I = P.sbuf("I", [128, T], F32, sc)
    Ab = P.sbuf("Ab", [128, T], F32, sc)
    HF = P.sbuf("HF", [128, T], F32, sc)
    HB = P.sbuf("HB", [128, T], F32, sc)
    gts = [P.sbuf("gt", [128, T], F32, sc) for _ in range(2)]
    wrg = P.sbuf("wrg", [128, 16, 128], BF16, sc)
    P.dma([wrg], wrg[:], [wRG], wRG.t[l], q="pool")
    for xp in xpads:
        vmemset(xp[:], 0.0, [xp])

    def rg_proj(c):
        xpad, gt = xpads[c % 2], gts[c % 2]
        sl = wslot()
        wv = sl[:, 0:2048].rearrange("p (k n) -> p k n", k=8)
        wload(sl, wv[:, :, 0:128], wA, wA.t[l, :, :, C_RGX + c * 128:C_RGX + (c + 1) * 128])
        wload(sl, wv[:, :, 128:256], wA, wA.t[l, :, :, C_RGG + c * 128:C_RGG + (c + 1) * 128])
        for tb in range(NTB):
            pt, pq = nextbank()
            for k in range(8):
                mm(pt[:, :], pq, wv[:, k, 0:128], hrhs(k, tb), k == 0, k == 7, [sl, hT])
            act(xpad[:, 1 + tb * 512:1 + (tb + 1) * 512], pt[:, :], AF.Copy, pq, [xpad])
        for tb in range(NTB):
            pt, pq = nextbank()
            for k in range(8):
                mm(pt[:, :], pq, wv[:, k, 128:256], hrhs(k, tb), k == 0, k == 7, [sl, hT])
            act(gt[:, tb * 512:(tb + 1) * 512], pt[:, :], AF.Copy, pq, [gt])

    rg_proj(0)
    for c in range(4):
        if c + 1 < 4:
            rg_proj(c + 1)
        xpad, gt = xpads[c % 2], gts[c % 2]
        vtt(HB[:], gt[:], gt[:], ALU.mult, [gt], [HB])
        vts(HB[:], HB[:], 0.044715, 1.0, ALU.mult, ALU.add, [HB], [HB])
        vtt(HB[:], HB[:], gt[:], ALU.mult, [HB, gt], [HB])
        act(HB[:], HB[:], AF.Sigmoid, [HB], [HB], scale=2.0 * math.sqrt(2.0 / math.pi))
        vtt(gt[:], gt[:], HB[:], ALU.mult, [gt, HB], [gt])
        act(xc[:], xpad[:, 0:T], AF.Identity, [xpad, vecs], [xc], bias=vec(l, "rgcb", c), scale=vec(l, "rgcw", c * 4 + 0))
        for k in range(1, 4):
            vstt(xc[:], xpad[:, k:k + T], vec(l, "rgcw", c * 4 + k), xc[:], ALU.mult, ALU.add, [xpad, xc, vecs], [xc])
        vcopy(xcb[:], xc[:], [xc], [xcb])
        if RGDBG == 1:
            vcopy(ob[:, c, :], xc[:], [xc], [ob])
            continue
        if RGDBG == 2:
            vcopy(ob[:, c, :], gt[:], [gt], [ob])
            continue
        for d in range(2):
            for (ai, dst, bname) in ((0, R, "rgba"), (1, I, "rgbi")):
                wi = (d * 2 + ai) * 4 + c
                for tb in range(NTB):
                    blk = slice(tb * 512, (tb + 1) * 512)
                    pt, pq = nextbank()
                    mm(pt[:, :], pq, wrg[:, wi, :], xcb[:, blk], True, True, [wrg, xcb])
                    act(dst[:, blk], pt[:, :], AF.Sigmoid, pq + [vecs], [dst], bias=vec(l, bname, d * 4 + c))
            if RGDBG == 3 and d == 0:
                vcopy(ob[:, c, :], R[:], [R], [ob])
            if RGDBG == 4 and d == 0:
                vcopy(ob[:, c, :], I[:], [I], [ob])
            act(Ab[:], R[:], AF.Exp, [R, cneg], [Ab], scale=cneg[:, l, d * 4 + c:d * 4 + c + 1])
            if RGDBG == 5 and d == 0:
                vcopy(ob[:, c, :], Ab[:], [Ab], [ob])
            act(R[:], R[:], AF.Exp, [R, cneg2], [R], scale=cneg2[:, l, d * 4 + c:d * 4 + c + 1])
            vts(R[:], R[:], -1.0, 1.0, ALU.mult, ALU.add, [R], [R])
            vts(R[:], R[:], 0.0, None, ALU.max, None, [R], [R])
            act(R[:], R[:], AF.Sqrt, [R], [R])
            edge = 0 if d == 0 else T - 1
            vmemset(R[:, edge:edge + 1], 1.0, [R])
            vtt(I[:], I[:], xc[:], ALU.mult, [I, xc], [I])
            vtt(I[:], I[:], R[:], ALU.mult, [I, R], [I])
            if d == 0:
                P.op("dve", lambda: nc.vector.tensor_tensor_scan(out=HF[:], data0=Ab[:], data1=I[:], initial=0.0,
                                                                 op0=ALU.mult, op1=ALU.add), [Ab, I], [HF])
            else:
                P.op("dve", lambda: nc.vector.tensor_tensor_scan(out=HB[:, ::-1], data0=Ab[:, ::-1], data1=I[:, ::-1], initial=0.0,
                                                                 op0=ALU.mult, op1=ALU.add), [Ab, I], [HB])
        if RGDBG in (3, 4, 5):
            continue
        if RGDBG == 6:
            vcopy(ob[:, c, :], HF[:], [HF], [ob])
            continue
        if RGDBG == 7:
            vcopy(ob[:, c, :], HB[:], [HB], [ob])
            continue
        vtt(HF[:], HF[:], HB[:], ALU.add, [HF, HB], [HF])
        vtt(ob[:, c, :], HF[:], gt[:], ALU.mult, [HF, gt], [ob])


def emit_gdn(P, nc, env, sc, l, s):
    e = _h(env)
    hT, obT, consts, identb = e.hT, e.obT, e.consts, e.identb
    mm, act, vtt, vts, vstt, vcopy, vmemset = e.mm, e.act, e.vtt, e.vts, e.vstt, e.vcopy, e.vmemset
    nextbank, nextq, cst, vec, wslot, wload, rsqrt_from, hrhs = e.nextbank, e.nextq, e.cst, e.vec, e.wslot, e.wload, e.rsqrt_from, e.hrhs
    nextq2 = e.nextq2
    negidentb = e.negidentb
    wA, vecs, negA, epsc = e.wA, e.vecs, e.negA, e.epsc
    ob = obT[2]
    qT = P.sbuf("gqT", [128, T], BF16, sc)
    kT = P.sbuf("gkT", [128, T], BF16, sc)
    ktok = P.sbuf("gktok", [128, NTL, 128], BF16, sc)
    vtok = P.sbuf("gvtok", [128, NTL, 128], BF16, sc)
    otok = P.sbuf("gotok", [128, NTL, 128], F32, sc)
    gtok = P.sbuf("ggtok", [128, NTL, 8], F32, sc)
    btok = P.sbuf("gbtok", [128, NTL, 8], F32, sc)
    rtmp = P.sbuf("grtmp", [128, 512], F32, sc)
    szs = [P.sbuf("gsz", [128, 128], F32, sc) for _ in range(3)]
    ons = [P.sbuf("gon", [128, 128], F32, sc) for _ in range(3)]
    ssqs = [P.sbuf("gssq", [128, 4], F32, sc) for _ in range(3)]

    sl = wslot()
    wab = sl[:, 0:128].rearrange("p (k n) -> p k n", k=8)
    wload(sl, wab, wA, wA.t[l, :, :, C_AB:C_AB + 16])
    for tl in range(NTL):
        pt, pq = nextbank()
        for k in range(8):
            mm(pt[:, 0:16], pq, hT[:, k, tl * 128:(tl + 1) * 128], wab[:, k, :], k == 0, k == 7, [hT, sl])
        vtt(gtok[:, tl, :], pt[:, 0:8], vec(l, "gdtb", 0, 8), ALU.add, pq + [vecs], [gtok])
        act(btok[:, tl, :], pt[:, 8:16], AF.Sigmoid, pq, [btok])
    gfl = gtok[:].rearrange("p a b -> p (a b)")
    e.softplus_acc(rtmp[:, 0:128], gfl, rtmp[:, 128:256], rtmp[:, 256:384], rtmp[:, 384:512], [gtok], [rtmp])
    vtt(gfl, rtmp[:, 0:128], negA[:, l, :], ALU.mult, [rtmp, negA], [gtok])

    ident, ones, negones, bd = cst("ident"), cst("ones"), cst("negones"), cst("bd16")
    offmasks = [cst("m32"), cst("m64"), cst("m128")]
    G = GDN_G
    NH = G + 1
    PL = "pool" if GDN_POOL else "dve"

    for hh in range(4):
        with contextlib.ExitStack() as sc2:
            xpads = [P.sbuf("gxpad", [128, T + 4], F32, sc2) for _ in range(2)]
            xc = P.sbuf("gxc", [128, T], F32, sc2)
            rts = [rtmp] + [P.sbuf("grt", [128, 512], F32, sc2) for _ in range(3)]
            for xp in xpads:
                vmemset(xp[:], 0.0, [xp])

            def qkv_proj(role):
                ch = role * 4 + hh
                col0 = C_GQKV + ch * 128
                xpad = xpads[role % 2]
                sl = wslot()
                wv = sl[:, 0:1024].rearrange("p (k n) -> p k n", k=8)
                wload(sl, wv, wA, wA.t[l, :, :, col0:col0 + 128])
                for tb in range(NTB):
                    pt, pq = nextbank()
                    for k in range(8):
                        mm(pt[:, :], pq, wv[:, k, :], hrhs(k, tb), k == 0, k == 7, [sl, hT])
                    act(xpad[:, 1 + tb * 512:1 + (tb + 1) * 512], pt[:, :], AF.Copy, pq, [xpad])

            def qkv_post(role):
                ch = role * 4 + hh
                xpad = xpads[role % 2]
                act(xc[:], xpad[:, 0:T], AF.Identity, [xpad, vecs], [xc], scale=vec(l, "gcw", ch * 4 + 0))
                for k in range(1, 4):
                    vstt(xc[:], xpad[:, k:k + T], vec(l, "gcw", ch * 4 + k), xc[:], ALU.mult, ALU.add, [xpad, xc, vecs], [xc])
                act(xc[:], xc[:], AF.Silu, [xc], [xc])
                if role < 2:
                    blks = [slice(tb * 512, (tb + 1) * 512) for tb in range(NTB)]
                    for tb in range(NTB):
                        vtt(rts[tb][:], xc[:, blks[tb]], xc[:, blks[tb]], ALU.mult, [xc], [rts[tb]])
                    pts = []
                    for tb in range(NTB):
                        pt, pq = nextbank()
                        mm(pt[:, :], pq, ones, rts[tb][:], True, True, [consts, rts[tb]])
                        pts.append((pt, pq))
                    for tb in range(NTB):
                        pt, pq = pts[tb]
                        act(rts[tb][:], pt[:, :], AF.Ln, pq, [rts[tb]], bias=epsc[:, 0:1], scale=1.0)
                    for tb in range(NTB):
                        act(rts[tb][:], rts[tb][:], AF.Exp, [rts[tb]], [rts[tb]], scale=-0.5)
                    for tb in range(NTB):
                        if role == 0:
                            vstt(xc[:, blks[tb]], xc[:, blks[tb]], 128.0 ** -0.5, rts[tb][:], ALU.mult, ALU.mult, [xc, rts[tb]], [xc])
                        else:
                            vtt(xc[:, blks[tb]], xc[:, blks[tb]], rts[tb][:], ALU.mult, [xc, rts[tb]], [xc])
                    vcopy((qT if role == 0 else kT)[:], xc[:], [xc], [qT if role == 0 else kT])
                if role >= 1:
                    dst = ktok if role == 1 else vtok
                    for tl in range(NTL):
                        qa, qk = nextq()
                        mm(qa, [qk], xc[:, tl * 128:(tl + 1) * 128], ident, True, True, [xc, consts])
                        act(dst[:, tl, :], qa, AF.Copy, [qk], [dst])

            qkv_proj(0)
            for role in range(3):
                if role + 1 < 3:
                    qkv_proj(role + 1)
                qkv_post(role)
            P.barrier()
        vmemset(otok[:], 0.0, [otok])

        with contextlib.ExitStack() as sc3:
            W2 = 256
            def mk(name, n, dt=F32, w=W2):
                return [P.sbuf(name, [128, w], dt, sc3) for _ in range(n)]
            GM, decS, decI, Af = (mk(nm, G) for nm in ("GM", "decS", "decI", "Af"))
            IDT = BF16
            Ad, Aoff, Bm = (mk(nm, G, IDT) for nm in ("Ad", "Aoff", "Bm"))
            Y = [mk("Ya", G, IDT), mk("Yb", G, IDT)]
            YT = [mk("YTa", G, IDT), mk("YTb", G, IDT)]
            Pm, PTm, Zm, TTb = mk("Pm", G, IDT), mk("PTm", G, IDT), mk("Zm", G, IDT), mk("TTb", G, BF16)
            vb, kbg = mk("vb", G, BF16), mk("kbg", G, BF16)
            um, wTm, kdec, attnT = mk("um", NH), mk("wTm", NH, BF16), mk("kdec", NH, BF16), mk("attnT", NH, BF16)
            cols = mk("cols", NH, F32, 16)
            vnew = mk("vnew", NH, BF16)
            tmpo = mk("tmpo", NH)
            S2 = P.sbuf("gS2", [128, W2], F32, sc3)
            Sb2 = P.sbuf("gSb2", [128, W2], BF16, sc3)
            vmemset(S2[:], 0.0, [S2])
            vmemset(Sb2[:], 0.0, [Sb2])
            ident2, identb2 = cst("ident", 256), identb[:, 0:256]
            ones2, negones2 = cst("ones", 256), cst("negones", 256)
            mt01, negs01, negst01 = cst("mt0", 256), cst("negs0", 256), cst("negst0", 256)
            bd2 = cst("bd16", 256)
            offmasks2 = [cst("m32", 256), cst("m64", 256), cst("m128", 256)]
            hs = [slice(0, 128), slice(128, 256)]
            tn = lambda d, it: it if d == 0 else NTL - 1 - it

            def prep_gen(it, i, hi):
                ns = [tn(0, it), tn(1, it)]
                colj = [0 * 4 + hh, 1 * 4 + hh]
                tsl = [slice(n * 128, (n + 1) * 128) for n in ns]
                g_col = [gtok[:, ns[d], colj[d]:colj[d] + 1] for d in range(2)]
                b_col = [btok[:, ns[d], colj[d]:colj[d] + 1] for d in range(2)]
                cl = cols[hi]
                for d in range(2):
                    vts(GM[i][:, hs[d]], mt01[:, hs[d]], g_col[d], None, ALU.mult, None, [consts, gtok], [GM[i]])
                    vts(cl[:, d * 8 + 6:d * 8 + 8], ones[:, 0:2], g_col[d], None, ALU.mult, None, [consts, gtok], [cl])
                yield
                pD, qD = nextq2()
                for d in range(2):
                    mm(pD[:, hs[d]], [qD], GM[i][:, hs[d]], ones, d == 0, False, [GM[i], consts])
                mm(pD, [qD], negones, GM[i][:], False, False, [GM[i], consts])
                mm(pD, [qD], ident, negs01, False, True, [consts])
                act(decS[i][:], pD, AF.Exp, [qD], [decS[i]])
                yield
                pDT, qDT = nextq2()
                mm(pDT, [qDT], ones, GM[i][:], True, False, [GM[i], consts])
                for d in range(2):
                    mm(pDT[:, hs[d]], [qDT], GM[i][:, hs[d]], negones, False, False, [GM[i], consts])
                mm(pDT, [qDT], ident, negst01, False, True, [consts])
                act(decI[i][:], pDT, AF.Exp, [qDT], [decI[i]])
                yield
                pc, qc = nextq2()
                for d in range(2):
                    mm(pc[:, d * 4:d * 4 + 2], [qc], GM[i][:, hs[d]], ones[:, 0:2], True, True, [GM[i], consts])
                    mm(pc[:, d * 4 + 2:d * 4 + 4], [qc], ones, cl[:, d * 8 + 6:d * 8 + 8], True, True, [cl, consts])
                cl3 = cl[:, 0:16].rearrange("p (d c) -> p d c", d=2)
                pc3 = pc[:, 0:8].rearrange("p (d c) -> p d c", d=2)
                vcopy(cl3[:, :, 0:2], pc3[:, :, 1:3], [qc], [cl])
                act(cl3[:, :, 2:4], cl3[:, :, 0:2], AF.Exp, [cl], [cl])
                for d in range(2):
                    act(cl[:, d * 8 + 4:d * 8 + 5], cl[:, d * 8:d * 8 + 1], AF.Exp, [cl], [cl], bias=cl[:, d * 8 + 1:d * 8 + 2], scale=-1.0)
                    vtt(cl[:, d * 8 + 5:d * 8 + 6], cl[:, d * 8 + 2:d * 8 + 3], b_col[d], ALU.mult, [cl, btok], [cl])
                yield
                pK, qK = nextq2()
                for d in range(2):
                    mm(pK[:, hs[d]], [qK], kT[:, tsl[d]], kT[:, tsl[d]], True, True, [kT])
                for d in range(2):
                    vstt(Af[i][:, hs[d]], pK[:, hs[d]], b_col[d], decS[i][:, hs[d]], ALU.mult, ALU.mult, [qK, btok, decS[i]], [Af[i]])
                vtt(Ad[i][:], Af[i][:], bd2, ALU.mult, [Af[i], consts], [Ad[i]])
                yield
                pQ, qQ = nextq2()
                for d in range(2):
                    mm(pQ[:, hs[d]], [qQ], kT[:, tsl[d]], qT[:, tsl[d]], True, True, [kT, qT])
                vtt(attnT[hi][:], pQ, decI[i][:], ALU.mult, [qQ, decI[i]], [attnT[hi]])
                yield
                pB, qB = nextq2()
                for d in range(2):
                    mm(pB[:, hs[d]], [qB], Ad[i][:, hs[d]], identb2[:, 0:128], True, True, [Ad[i], identb])
                act(Bm[i][:], pB, AF.Copy, [qB], [Bm[i]])
                pP, qP = nextq2()
                mm(pP, [qP], identb2[:, 0:128], identb2, True, False, [identb])
                for d in range(2):
                    mm(pP[:, hs[d]], [qP], Ad[i][:, hs[d]], negidentb[:], False, d == 1, [Ad[i], negidentb])
                act(Pm[i][:], pP, AF.Copy, [qP], [Pm[i]])
                yield
                yprev, ytprev = Bm[i], Ad[i]
                for lev in range(3):
                    ycur, ytcur = Y[lev % 2][i], YT[lev % 2][i]
                    pyt, qyt = nextq2()
                    for d in range(2):
                        mm(pyt[:, hs[d]], [qyt], yprev[:, hs[d]], ytprev[:, hs[d]], True, True, [yprev, ytprev])
                    act(ytcur[:], pyt, AF.Copy, [qyt], [ytcur])
                    if lev < 2:
                        py, qy = nextq2()
                        for d in range(2):
                            mm(py[:, hs[d]], [qy], ytprev[:, hs[d]], yprev[:, hs[d]], True, True, [yprev, ytprev])
                        act(ycur[:], py, AF.Copy, [qy], [ycur])
                    yield
                    pp, qp = nextq2()
                    mm(pp, [qp], identb2[:, 0:128], Pm[i][:], True, False, [identb, Pm[i]])
                    for d in range(2):
                        mm(pp[:, hs[d]], [qp], ytcur[:, hs[d]], Pm[i][:, hs[d]], False, d == 1, [ytcur, Pm[i]])
                    act(Pm[i][:], pp, AF.Copy, [qp], [Pm[i]])
                    yield
                    yprev, ytprev = ycur, ytcur
                for mi, msk in enumerate(offmasks2):
                    ppt, qpt = nextq2()
                    for d in range(2):
                        mm(ppt[:, hs[d]], [qpt], Pm[i][:, hs[d]], identb2[:, 0:128], True, True, [Pm[i], identb])
                    act(PTm[i][:], ppt, AF.Copy, [qpt], [PTm[i]])
                    vtt(Aoff[i][:], Af[i][:], msk, ALU.mult, [Af[i], consts], [Aoff[i]])
                    pz, qz = nextq2()
                    for d in range(2):
                        mm(pz[:, hs[d]], [qz], Aoff[i][:, hs[d]], Pm[i][:, hs[d]], True, True, [Aoff[i], Pm[i]])
                    act(Zm[i][:], pz, AF.Copy, [qz], [Zm[i]], scale=-1.0)
                    yield
                    pt2, qt2 = nextq2()
                    mm(pt2, [qt2], identb2[:, 0:128], Pm[i][:], True, False, [identb, Pm[i]])
                    for d in range(2):
                        mm(pt2[:, hs[d]], [qt2], PTm[i][:, hs[d]], Zm[i][:, hs[d]], False, d == 1, [PTm[i], Zm[i]])
                    if mi < 2:
                        act(Pm[i][:], pt2, AF.Copy, [qt2], [Pm[i]])
                    else:
                        act(TTb[i][:], pt2, AF.Copy, [qt2], [TTb[i]])
                    yield
                for d in range(2):
                    vts(vb[i][:, hs[d]], vtok[:, ns[d], :], b_col[d], None, ALU.mult, None, [vtok, btok], [vb[i]])
                    vts(kbg[i][:, hs[d]], ktok[:, ns[d], :], cl[:, d * 8 + 5:d * 8 + 6], None, ALU.mult, None, [ktok, cl], [kbg[i]])
                    vts(kdec[hi][:, hs[d]], ktok[:, ns[d], :], cl[:, d * 8 + 4:d * 8 + 5], None, ALU.mult, None, [ktok, cl], [kdec[hi]])
                pu, qu = nextq2()
                for d in range(2):
                    mm(pu[:, hs[d]], [qu], TTb[i][:, hs[d]], vb[i][:, hs[d]], True, True, [TTb[i], vb[i]])
                act(um[hi][:], pu, AF.Copy, [qu], [um[hi]])
                yield
                pw, qw = nextq2()
                for d in range(2):
                    mm(pw[:, hs[d]], [qw], kbg[i][:, hs[d]], TTb[i][:, hs[d]], True, True, [TTb[i], kbg[i]])
                act(wTm[hi][:], pw, AF.Copy, [qw], [wTm[hi]])

            def step_gen(it, i):
                ns = [tn(0, it), tn(1, it)]
                tsl = [slice(n * 128, (n + 1) * 128) for n in ns]
                cl = cols[i]
                p1, q1 = nextq2()
                for d in range(2):
                    mm(p1[:, hs[d]], [q1], wTm[i][:, hs[d]], Sb2[:, hs[d]], True, True, [wTm[i], Sb2])
                vstt(vnew[i][:], p1, -1.0, um[i][:], ALU.mult, ALU.add, [q1, um[i]], [vnew[i]])
                p2, q2 = nextq2()
                for d in range(2):
                    mm(p2[:, hs[d]], [q2], qT[:, tsl[d]], Sb2[:, hs[d]], True, True, [qT, Sb2])
                for d in range(2):
                    vts(tmpo[i][:, hs[d]], p2[:, hs[d]], cl[:, d * 8 + 2:d * 8 + 3], None, ALU.mult, None, [q2, cl], [tmpo[i]])
                yield
                p4, q4 = nextq2()
                for d in range(2):
                    mm(p4[:, hs[d]], [q4], kdec[i][:, hs[d]], vnew[i][:, hs[d]], True, True, [kdec[i], vnew[i]])
                for d in range(2):
                    vstt(S2[:, hs[d]], S2[:, hs[d]], cl[:, d * 8 + 3:d * 8 + 4], p4[:, hs[d]], ALU.mult, ALU.add, [S2, cl, q4], [S2])
                act(Sb2[:], S2[:], AF.Copy, [S2], [Sb2])
                yield
                p3, q3 = nextq2()
                for d in range(2):
                    mm(p3[:, hs[d]], [q3], attnT[i][:, hs[d]], vnew[i][:, hs[d]], True, True, [attnT[i], vnew[i]])
                vtt(tmpo[i][:], tmpo[i][:], p3, ALU.add, [q3, tmpo[i]], [tmpo[i]])
                for d in range(2):
                    vtt(otok[:, ns[d], :], otok[:, ns[d], :], tmpo[i][:, hs[d]], ALU.add, [otok, tmpo[i]], [otok])

            nxt = 0
            free_s = list(range(G))
            free_h = list(range(NH))
            hmap = {}
            prep_done = set()
            step_done = -1
            step_started = -1
            active = []
            while step_done < NTL - 1:
                while nxt < NTL and free_s and free_h:
                    it = nxt
                    nxt += 1
                    si = free_s.pop(0)
                    hi = free_h.pop(0)
                    hmap[it] = hi
                    active.append(("P", it, prep_gen(it, si, hi), si))
                it = step_started + 1
                if it < NTL and it in prep_done and step_done == it - 1:
                    active.append(("S", it, step_gen(it, hmap[it]), None))
                    step_started = it
                for task in list(active):
                    kind, it, gen, si = task
                    try:
                        next(gen)
                    except StopIteration:
                        active.remove(task)
                        if kind == "P":
                            prep_done.add(it)
                            free_s.append(si)
                        else:
                            step_done = it
                            free_h.append(hmap[it])
            P.barrier()
        sl = wslot()
        wz = sl[:, 0:1024].rearrange("p (k n) -> p k n", k=8)
        wload(sl, wz, wA, wA.t[l, :, :, C_GZ + hh * 128:C_GZ + (hh + 1) * 128])
        def outA(tl):
            sz, on, ssq = szs[tl % 3], ons[tl % 3], ssqs[tl % 3]
            pt, pq = nextbank()
            for k in range(8):
                mm(pt[:, 0:128], pq, hT[:, k, tl * 128:(tl + 1) * 128], wz[:, k, :], k == 0, k == 7, [hT, sl])
            act(sz[:], pt[:, 0:128], AF.Silu, pq, [sz])
            P.op("dve", lambda: nc.vector.tensor_tensor(out=on[:], in0=otok[:, tl, :], in1=otok[:, tl, :], op=ALU.mult), [otok], [on])
            P.op("dve", lambda: nc.vector.tensor_reduce(out=ssq[:, 0:1], in_=on[:], axis=AX.X, op=ALU.add), [on], [ssq])
            act(ssq[:, 1:2], ssq[:, 0:1], AF.Ln, [ssq], [ssq], bias=epsc[:, 0:1], scale=1.0 / 128)
            act(ssq[:, 1:2], ssq[:, 1:2], AF.Exp, [ssq], [ssq], scale=-0.5)
            vstt(on[:], otok[:, tl, :], ssq[:, 1:2], vec(l, "gng", 0, 128), ALU.mult, ALU.mult, [otok, ssq, vecs], [on])
            vtt(on[:], on[:], sz[:], ALU.mult, [on, sz], [on])

        def outB(tl):
            on = ons[tl % 3]
            qa, qk = nextq()
            mm(qa, [qk], on[:], ident, True, True, [on, consts])
            act(ob[:, hh, tl * 128:(tl + 1) * 128], qa, AF.Copy, [qk], [ob])

        for tl in range(NTL + 2):
            if tl < NTL:
                outA(tl)
            if tl >= 2:
                outB(tl - 2)


def emit_merge(P, nc, env, sc, l, s):
    e = _h(env)
    hT, obT, consts = e.hT, e.obT, e.consts
    mm, act, vtt, vts, vstt, vcopy = e.mm, e.act, e.vtt, e.vts, e.vstt, e.vcopy
    nextbank, cst, vec, wslot, wload, hrhs = e.nextbank, e.cst, e.vec, e.wslot, e.wload, e.hrhs
    wA, wBR, wO, mod = e.wA, e.wBR, e.wO, e.mod
    mT = P.sbuf("mT", [128, 8, T], BF16, sc)
    Gs = [P.sbuf("Gs", [128, 512], F32, sc) for _ in range(2)]
    macc = P.sbuf("macc", [128, 512], F32, sc)
    mtmp = P.sbuf("mtmp", [128, 512], F32, sc)
    xb = P.sbuf("xb", [128, 8, 512], F32, sc)
    sq = P.sbuf("sq", [128, 8, 512], F32, sc)
    rstd = P.sbuf("rstd", [128, 512], F32, sc)
    tmp = P.sbuf("tmp", [128, 512], F32, sc)
    gi = 0
    for j in range(8):
        sl = wslot()
        wg = sl[:, 0:3072].rearrange("p (b k n) -> p b k n", b=3, k=8)
        for b in range(3):
            wload(sl, wg[:, b], wA, wA.t[l, :, :, C_GATE + b * 1024 + j * 128:C_GATE + b * 1024 + (j + 1) * 128])
        sl2 = wslot()
        wb = sl2[:, 0:1536].rearrange("p (k n) -> p k n", k=12)
        wload(sl2, wb, wBR, wBR.t[l, :, :, j * 128:(j + 1) * 128])
        for tb in range(NTB):
            blk = slice(tb * 512, (tb + 1) * 512)
            for b in range(3):
                pg, qg = nextbank()
                for k in range(8):
                    mm(pg[:, :], qg, wg[:, b, k, :], hrhs(k, tb), k == 0, k == 7, [sl, hT])
                Gt = Gs[gi % 2]
                gi += 1
                act(Gt[:], pg[:, :], AF.Sigmoid, qg, [Gt])
                py, qy = nextbank()
                for k in range(4):
                    mm(py[:, :], qy, wb[:, b * 4 + k, :], obT[b][:, k, blk], k == 0, k == 3, [sl2, obT[b]])
                if b == 0:
                    vtt(macc[:], py[:, :], Gt[:], ALU.mult, qy + [Gt], [macc])
                elif b == 1:
                    vtt(mtmp[:], py[:, :], Gt[:], ALU.mult, qy + [Gt], [mtmp])
                    vtt(macc[:], macc[:], mtmp[:], ALU.add, [macc, mtmp], [macc])
                else:
                    vtt(mtmp[:], py[:, :], Gt[:], ALU.mult, qy + [Gt], [mtmp])
                    vtt(mT[:, j, blk], macc[:], mtmp[:], ALU.add, [macc, mtmp], [mT])
    for tb in range(NTB):
        blk = slice(tb * 512, (tb + 1) * 512)
        e.load_xblock(xb, tb)
        for half in range(2):
            sl = wslot()
            wv = sl[:, 0:4096].rearrange("p (k n) -> p k n", k=8)
            wload(sl, wv, wO, wO.t[l, :, :, half * 512:(half + 1) * 512])
            for ii in range(4):
                i = half * 4 + ii
                pt, pq = nextbank()
                for k in range(8):
                    mm(pt[:, :], pq, wv[:, k, ii * 128:(ii + 1) * 128], mT[:, k, blk], k == 0, k == 7, [sl, mT])
                vstt(xb[:, i, :], pt[:, :], mod[:, l, 16 + i, s:s + 1], xb[:, i, :], ALU.mult, ALU.add, pq + [mod, xb], [xb])
        e.store_xblock(xb, tb)
        e.norm_block(xb, tb, l, s, 2, sq, rstd, tmp)


def emit_ffn(P, nc, env, sc, l, s):
    e = _h(env)
    hT, consts = e.hT, e.consts
    mm, act, vtt, vts, vstt, vcopy, vmemset = e.mm, e.act, e.vtt, e.vts, e.vstt, e.vcopy, e.vmemset
    nextbank, cst, vec, wslot, wload, hrhs = e.nextbank, e.cst, e.vec, e.wslot, e.wload, e.hrhs
    wUP, wDN, mod, vecs, xT_d, xT_tk = e.wUP, e.wDN, e.mod, e.vecs, e.xT_d, e.xT_tk
    actT = P.sbuf("actT", [128, NJ, T], BF16, sc)
    upad = [P.sbuf("upad", [128, T + 2], F32, sc) for _ in range(2)]
    cv = [P.sbuf("cv", [128, T], F32, sc) for _ in range(2)]
    xs = [P.sbuf("xs", [128, 512], F32, sc) for _ in range(2)]
    for u in upad:
        vmemset(u[:], 0.0, [u])
    for jp in range(NJ // 2):
        sl = wslot()
        wv = sl[:, 0:4096].rearrange("p (k n) -> p k n", k=8)
        for jj in range(2):
            j = jp * 2 + jj
            wload(sl, wv[:, :, jj * 128:(jj + 1) * 128], wUP, wUP.t[l, :, :, j * 128:(j + 1) * 128])
            wload(sl, wv[:, :, 256 + jj * 128:256 + (jj + 1) * 128], wUP, wUP.t[l, :, :, DFF + j * 128:DFF + (j + 1) * 128])
        for jj in range(2):
            j = jp * 2 + jj
            for lg in range(2):
                ch = lg * NJ + j
                wc = lg * 256 + jj * 128
                for tb in range(NTB):
                    pt, pq = nextbank()
                    for k in range(8):
                        mm(pt[:, :], pq, wv[:, k, wc:wc + 128], hrhs(k, tb), k == 0, k == 7, [sl, hT])
                    act(upad[lg][:, 1 + tb * 512:1 + (tb + 1) * 512], pt[:, :], AF.Copy, pq, [upad[lg]])
                act(cv[lg][:], upad[lg][:, 0:T], AF.Identity, [upad[lg], vecs], [cv[lg]],
                    bias=vec(l, "fcb", ch), scale=vec(l, "fcw", ch * 3 + 0))
                for k in range(1, 3):
                    vstt(cv[lg][:], upad[lg][:, k:k + T], vec(l, "fcw", ch * 3 + k), cv[lg][:], ALU.mult, ALU.add,
                         [upad[lg], cv[lg], vecs], [cv[lg]])
            act(cv[1][:], cv[1][:], AF.Silu, [cv[1]], [cv[1]])
            vtt(actT[:, j, :], cv[1][:], cv[0][:], ALU.mult, [cv[0], cv[1]], [actT])
    xi = 0
    for i in range(8):
        sl = wslot()
        wv = sl[:, 0:NJ * 128].rearrange("p (k n) -> p k n", k=NJ)
        wload(sl, wv, wDN, wDN.t[l, :, :, i * 128:(i + 1) * 128])
        for tb in range(NTB):
            blk = slice(tb * 512, (tb + 1) * 512)
            pt, pq = nextbank()
            for k in range(NJ):
                mm(pt[:, :], pq, wv[:, k, :], actT[:, k, blk], k == 0, k == NJ - 1, [sl, actT])
            x1 = xs[xi % 2]
            xi += 1
            P.dma([x1], x1[:], [xT_tk[i][tb]], xT_d[:, i, blk])
            vstt(x1[:], pt[:, :], mod[:, l, 40 + i, s:s + 1], x1[:], ALU.mult, ALU.add, pq + [mod, x1], [x1])
            P.dma([xT_tk[i][tb]], xT_d[:, i, blk], [x1], x1[:], sem_t=x1)


def _fm(w, kc):
    K, N = w.shape
    return np.ascontiguousarray(w.reshape(kc, 128, N).transpose(1, 0, 2))


def _colv(v, nch):
    return np.ascontiguousarray(np.asarray(v).reshape(nch, 128).T)


def _consts():
    c = np.zeros((128, NC), np.float32)
    i = np.arange(128)
    ident = np.eye(128, dtype=np.float32)
    c[:, COFF["ident"]:COFF["ident"] + 128] = ident
    c[:, COFF["ident_b"]:COFF["ident_b"] + 128] = ident
    c[:, COFF["ones"]:COFF["ones"] + 256] = 1.0
    c[:, COFF["negones"]:COFF["negones"] + 256] = -1.0
    m, ii = np.meshgrid(i, i, indexing="ij")
    c[:, COFF["mt0"]:COFF["mt0"] + 128] = (m <= ii)
    c[:, COFF["mt1"]:COFF["mt1"] + 128] = (m >= ii)
    a, b = np.meshgrid(i, i, indexing="ij")
    v0 = (b < a)
    v1 = (b > a)
    c[:, COFF["negs0"]:COFF["negs0"] + 128] = np.where(v0, 0.0, -BIG)
    c[:, COFF["negs1"]:COFF["negs1"] + 128] = np.where(v1, 0.0, -BIG)
    eye = np.eye(128, dtype=bool)
    c[:, COFF["negst0"]:COFF["negst0"] + 128] = np.where(v0.T | eye, 0.0, -BIG)
    c[:, COFF["negst1"]:COFF["negst1"] + 128] = np.where(v1.T | eye, 0.0, -BIG)
    bdm = lambda n: ((a // n) == (b // n)).astype(np.float32)
    for sfx in ("", "_b"):
        c[:, COFF["bd16" + sfx]:COFF["bd16" + sfx] + 128] = bdm(16)
        c[:, COFF["m32" + sfx]:COFF["m32" + sfx] + 128] = bdm(32) - bdm(16)
        c[:, COFF["m64" + sfx]:COFF["m64" + sfx] + 128] = bdm(64) - bdm(32)
        c[:, COFF["m128" + sfx]:COFF["m128" + sfx] + 128] = 1.0 - bdm(64)
    return c


def _rope_tables():
    inv = 1.0 / (10000.0 ** (np.arange(0, 32, 2, dtype=np.float32) / 32.0))
    ang = np.arange(T, dtype=np.float32)[None, :] * inv[:, None].astype(np.float32)
    cos, sin = np.cos(ang).astype(np.float32), np.sin(ang).astype(np.float32)
    r = np.zeros((2, 96, T), np.float32)
    r[0, 0:64] = 1.0
    r[0, 64:80] = cos
    r[0, 80:96] = cos
    r[1, 64:80] = -sin
    r[1, 80:96] = sin
    return r


def prep_weights(inp):
    L = L_ALL
    f = lambda k: np.asarray(inp[k], np.float32)
    w_in = f("w_in")
    wA = np.zeros((L, 128, 8, NA), np.float32)
    sw = np.concatenate([np.arange(16, 32), np.arange(0, 16)])
    wUQ = np.zeros((L, 128, 3, 1536), np.float32)
    wKN = np.zeros((L, 128, 2, 768), np.float32)
    wV = np.zeros((L, 128, 2, 512), np.float32)
    wRG = np.zeros((L, 128, 16, 128), np.float32)
    vecs = np.zeros((128, L, NV), np.float32)
    for l in range(L):
        ext = np.zeros((D, 192), np.float32)
        kr = w_in[l][:, 640:672]
        ext[:, 64:96] = kr
        ext[:, 96 + 64:96 + 96] = kr[:, sw]
        wA[l] = _fm(np.concatenate([w_in[l], ext], axis=1), 8)
        uq = f("mla_w_uq")[l]
        uqs = uq.reshape(384, 8, 96).copy()
        uqs[:, :, 64:96] = uqs[:, :, 64:96][:, :, sw]
        wUQ[l] = _fm(np.concatenate([uq, uqs.reshape(384, 768)], axis=1), 3)
        ukv = f("mla_w_ukv")[l].reshape(256, 8, 128)
        kn = np.zeros((256, 8, 96), np.float32)
        kn[:, :, 0:64] = ukv[:, :, 0:64]
        wKN[l] = _fm(kn.reshape(256, 768), 2)
        wV[l] = _fm(np.ascontiguousarray(ukv[:, :, 64:128]).reshape(256, 512), 2)
        for d in range(2):
            for ai, nm in enumerate(("rg_w_a", "rg_w_i")):
                w = f(nm)[l, d]
                for c in range(4):
                    m = np.zeros((128, 128), np.float32)
                    m[0:64, 0:64] = w[2 * c]
                    m[64:128, 64:128] = w[2 * c + 1]
                    wRG[l, :, (d * 2 + ai) * 4 + c, :] = m

        def put(name, arr):
            o, w = VOFF[name]
            vecs[:, l, o:o + w] = arr
        put("ln1", _colv(f("ln1_g")[l], 8))
        put("ln2", _colv(f("ln2_g")[l], 8))
        put("bmod", _colv(f("b_mod")[l], 48))
        put("qg", _colv(f("mla_q_norm_g")[l], 3))
        put("kvg", _colv(f("mla_kv_norm_g")[l], 2))
        put("rgcw", np.ascontiguousarray(f("rg_conv_w")[l].reshape(4, 4, 128).transpose(2, 1, 0)).reshape(128, 16))
        put("rgcb", _colv(f("rg_conv_b")[l], 4))
        put("rgba", np.ascontiguousarray(f("rg_b_a")[l].reshape(2, 4, 128).transpose(2, 0, 1)).reshape(128, 8))
        put("rgbi", np.ascontiguousarray(f("rg_b_i")[l].reshape(2, 4, 128).transpose(2, 0, 1)).reshape(128, 8))
        put("rglam", np.ascontiguousarray(f("rg_lam")[l].reshape(2, 4, 128).transpose(2, 0, 1)).reshape(128, 8))
        put("gcw", np.ascontiguousarray(f("gdn_conv_w")[l].reshape(4, 12, 128).transpose(2, 1, 0)).reshape(128, 48))
        put("gng", np.broadcast_to(f("gdn_norm_g")[l][None, :], (128, 128)))
        put("galog", np.broadcast_to(np.tile(f("gdn_a_log")[l].reshape(8), 16)[None, :], (128, 128)))
        put("gdtb", np.broadcast_to(np.tile(f("gdn_dt_bias")[l].reshape(8), 16)[None, :], (128, 128)))
        put("fcw", np.ascontiguousarray(f("ffn_conv_w")[l].reshape(3, 44, 128).transpose(2, 1, 0)).reshape(128, 132))
        put("fcb", _colv(f("ffn_conv_b")[l], 44))
        put("fng", _colv(f("final_norm_g"), 8))
    wd = {
        "wA": wA, "wUQ": wUQ, "wKN": wKN, "wV": wV, "wRG": wRG,
        "wBR": np.stack([_fm(f("w_branch")[l].reshape(1536, 1024), 12) for l in range(L)]),
        "wO": np.stack([_fm(f("w_out")[l], 8) for l in range(L)]),
        "wUP": np.stack([_fm(f("ffn_w_up")[l], 8) for l in range(L)]),
        "wDN": np.stack([_fm(f("ffn_w_down")[l], NJ) for l in range(L)]),
        "wMOD": np.stack([_fm(f("w_mod")[l], 8) for l in range(L)]),
        "vecs": vecs, "consts": _consts(), "ropet": _rope_tables(),
    }
    return wd


def core_inputs(wd, xs, cs):
    m = dict(wd)
    m["xin"] = np.ascontiguousarray(xs, dtype=np.float32)
    nseq = xs.shape[0]
    m["cT"] = np.ascontiguousarray(np.asarray(cs, np.float32).reshape(nseq, 8, 128).transpose(2, 1, 0))
    return m


_CACHE = {}


def kernel(**inputs):
    wd = prep_weights(inputs)
    xp = np.asarray(inputs["x_prompt"], np.float32)
    xsm = np.asarray(inputs["x_sample"], np.float32)
    cp = np.asarray(inputs["c_prompt"], np.float32)
    csm = np.asarray(inputs["c_sample"], np.float32)
    xall = np.concatenate([xp, xsm], axis=0)
    call = np.concatenate([cp, csm], axis=0)
    nseq = xall.shape[0] // NCORES
    if "nc" not in _CACHE:
        _CACHE["nc"] = build(nseq)[0]
    nc = _CACHE["nc"]
    in_maps = [core_inputs(wd, xall[i * nseq:(i + 1) * nseq], call[i * nseq:(i + 1) * nseq]) for i in range(NCORES)]
    res = run_bass_kernel_spmd(nc, in_maps, core_ids=list(range(NCORES)))
    y = np.concatenate([np.asarray(r["yout"], np.float32) for r in res.results], axis=0)
    nb = xp.shape[0]
    return (np.ascontiguousarray(y[:nb]), np.ascontiguousarray(y[nb:]))
```

```python
import contextlib
import math
import numpy as np
import concourse.bass as bass
import concourse.mybir as mybir
from concourse.bass_utils import run_bass_kernel_spmd

F32 = mybir.dt.float32
BF16 = mybir.dt.bfloat16
AF = mybir.ActivationFunctionType
ALU = mybir.AluOpType
AX = mybir.AxisListType

D = 1024
T = 2048
NTB = 4
NTL = 16
L_ALL = 2
NCORES = 8
EPS = 1e-6
N_IN = 6832
KR0 = N_IN
NA = N_IN + 192
C_RGX, C_RGG, C_GQKV, C_GZ, C_AB, C_GATE = 672, 1184, 1696, 3232, 3744, 3760
DFF = 2816
NJ = 22
BIG = 30000.0
SEM_LIMIT = 30000

VOFF = {}
_o = 0
for _n, _w in [("ln1", 8), ("ln2", 8), ("bmod", 48), ("qg", 3), ("kvg", 2), ("rgcw", 16), ("rgcb", 4),
               ("rgba", 8), ("rgbi", 8), ("rglam", 8), ("gcw", 48), ("gng", 128), ("galog", 128), ("gdtb", 128),
               ("fcw", 132), ("fcb", 44), ("fng", 8)]:
    VOFF[_n] = (_o, _w)
    _o += _w
NV = _o
COFF = {}
_o = 0
for _n in ["ident", "ident_b", "ones", "ones_b", "negones", "negones_b", "mt0", "mt1", "negs0", "negs1", "negst0", "negst1",
           "bd16", "bd16_b", "m32", "m32_b", "m64", "m64_b", "m128", "m128_b"]:
    COFF[_n] = _o
    _o += 128
NC = _o


class Tk:
    __slots__ = ("name", "w", "r", "t", "dk")

    def __init__(self, name, t=None):
        self.name = name
        self.w = None
        self.r = {}
        self.t = t
        self.dk = name

    def __getitem__(self, idx):
        return self.t[idx]


class Prog:
    def __init__(self, nc):
        self.nc = nc
        self.es = contextlib.ExitStack()
        self.eng = {"pe": nc.tensor, "act": nc.scalar, "dve": nc.vector, "pool": nc.gpsimd, "sp": nc.sync}
        self.sems = {}
        self.cnt = {}
        self.cur = {}
        self.epoch = {e: 0 for e in self.eng}
        self.seen = {e: {} for e in self.eng}
        self.ninst = 0
        self.uid = 0
        self.namectr = {}
        for e in self.eng:
            self._new_epoch(e)

    def _mk_sem(self, key):
        s = self.es.enter_context(self.nc.semaphore("s_%s" % (key,)))
        self.sems[key] = s
        self.cnt[key] = 0
        return s

    def _new_epoch(self, e):
        key = "%s%d" % (e, self.epoch[e])
        self.epoch[e] += 1
        self._mk_sem(key)
        self.cur[e] = key

    def sbuf(self, name, shape, dt, es=None):
        self.uid += 1
        t = (es or self.es).enter_context(self.nc.sbuf_tensor("%s_%d" % (name, self.uid), list(shape), dt))
        tk = Tk(name + str(self.uid), t)
        c = self.namectr.get(name, 0)
        self.namectr[name] = c + 1
        tk.dk = "%s_%d" % (name, c % 4)
        return tk

    def dram(self, name, shape, dt, kind="Internal"):
        t = self.nc.dram_tensor(name, list(shape), dt, kind=kind).ap()
        return Tk(name, t)

    def _deps(self, reads, writes):
        deps = {}
        for t in reads:
            if t.w is not None:
                k, v = t.w
                if deps.get(k, 0) < v:
                    deps[k] = v
        for t in writes:
            if t.w is not None:
                k, v = t.w
                if deps.get(k, 0) < v:
                    deps[k] = v
            for k, v in t.r.items():
                if deps.get(k, 0) < v:
                    deps[k] = v
        return deps

    def _wait(self, e, deps):
        eng = self.eng[e]
        seen = self.seen[e]
        for k, v in deps.items():
            isdma = k.startswith("dma")
            if (not isdma) and k.startswith(e) and e in ("pe", "sp"):
                continue
            if isdma:
                v = self.cnt[k]
            if seen.get(k, 0) < v:
                eng.wait_ge(self.sems[k], v)
                seen[k] = v
                self.ninst += 1

    def op(self, e, fn, reads=(), writes=()):
        self._wait(e, self._deps(reads, writes))
        if self.cnt[self.cur[e]] >= SEM_LIMIT:
            self._new_epoch(e)
        key = self.cur[e]
        inst = fn()
        self.cnt[key] += 1
        v = self.cnt[key]
        inst.then_inc(self.sems[key], 1)
        self.ninst += 1
        for t in reads:
            t.r[key] = v
        for t in writes:
            t.w = (key, v)
            t.r = {}
        return inst

    def dma(self, out_ts, out_ap, in_ts, in_ap, q="sp", sem_t=None):
        st = sem_t if sem_t is not None else out_ts[0]
        key = "dma_" + st.dk
        if key not in self.sems:
            self._mk_sem(key)
        self._wait(q, self._deps(in_ts, out_ts))
        inst = self.eng[q].dma_start(out=out_ap, in_=in_ap)
        self.cnt[key] += 16
        v = self.cnt[key]
        inst.then_inc(self.sems[key], 16)
        self.ninst += 1
        for t in in_ts:
            t.r[key] = v
        for t in out_ts:
            t.w = (key, v)
            t.r = {}
        return inst

    def barrier(self):
        allk = {k: v for k, v in self.cnt.items() if v > 0}
        for e in self.eng:
            self._wait(e, allk)

    def finish(self, outs, q="sp"):
        deps = {}
        for t in outs:
            if t.w is not None:
                k, v = t.w
                deps[k] = max(deps.get(k, 0), v)
        self._wait(q, deps)


OUTSTAGE = 3
GDNSTAGE = 9
GDN_G = 3
GDN_POOL = False
GDN_BF16INV = True
GSUB = 9
RGDBG = 0
PHASES = set(['N1', 'MLA', 'RG', 'GDN', 'MERGE', 'FFN'])


def build(nseq, nlayer=L_ALL, dbg=False):
    nc = bass.Bass("TRN2", target_bir_lowering=False)
    P = Prog(nc)
    L = L_ALL
    xin = P.dram("xin", [nseq, T, D], F32, "ExternalInput")
    cT = P.dram("cT", [128, 8, nseq], F32, "ExternalInput")
    wA = P.dram("wA", [L, 128, 8, NA], F32, "ExternalInput")
    wUQ = P.dram("wUQ", [L, 128, 3, 1536], F32, "ExternalInput")
    wKN = P.dram("wKN", [L, 128, 2, 768], F32, "ExternalInput")
    wV = P.dram("wV", [L, 128, 2, 512], F32, "ExternalInput")
    wRG = P.dram("wRG", [L, 128, 16, 128], F32, "ExternalInput")
    wBR = P.dram("wBR", [L, 128, 12, 1024], F32, "ExternalInput")
    wO = P.dram("wO", [L, 128, 8, 1024], F32, "ExternalInput")
    wUP = P.dram("wUP", [L, 128, 8, 2 * DFF], F32, "ExternalInput")
    wDN = P.dram("wDN", [L, 128, NJ, 1024], F32, "ExternalInput")
    wMOD = P.dram("wMOD", [L, 128, 8, 6144], F32, "ExternalInput")
    vecs_d = P.dram("vecs", [128, L, NV], F32, "ExternalInput")
    consts_d = P.dram("consts", [128, NC], F32, "ExternalInput")
    ropet = P.dram("ropet", [2, 96, T], F32, "ExternalInput")
    yout = P.dram("yout", [nseq, T, D], F32, "ExternalOutput")
    xT_d = nc.dram_tensor("xT_scr", [128, 8, T], F32, kind="Internal").ap()
    xT_tk = [[Tk("xT_%d_%d" % (c, tb)) for tb in range(NTB)] for c in range(8)]
    dbg_d = None
    if dbg:
        dbg_d = P.dram("dbg", [3, 128, 4, T], F32, "ExternalOutput")

    V = nc.vector
    A = nc.scalar
    PE = nc.tensor
    G = nc.gpsimd

    with P.es:
        base = P.es
        consts = P.sbuf("consts", [128, NC], F32)
        vecs = P.sbuf("vecs", [128, L, NV], F32)
        identb = P.sbuf("identb", [128, 256], BF16)
        negidentb = P.sbuf("negidentb", [128, 128], BF16)
        onesb = P.sbuf("onesb", [128, 128], BF16)
        hT = P.sbuf("hT", [128, 8, T], BF16)
        mod = P.sbuf("mod", [128, L, 48, nseq], F32)
        gs1 = P.sbuf("gs1", [128, L, 8, nseq], F32)
        gs2 = P.sbuf("gs2", [128, L, 8, nseq], F32)
        cneg = P.sbuf("cneg", [128, L, 8], F32)
        cneg2 = P.sbuf("cneg2", [128, L, 8], F32)
        negA = P.sbuf("negA", [128, L, 128], F32)
        NSLOT = 3
        wslots = [P.sbuf("wslot%d" % i, [128, 4096], BF16) for i in range(NSLOT)]
        wctr = [0]
        pbank = []
        for i in range(8):
            t = P.es.enter_context(nc.psum_tensor("psb%d" % i, [128, 512], F32))
            pbank.append((t, [Tk("psq%d_%d" % (i, q)) for q in range(4)]))
        pctr = [0]
        qctr = [0]

        def nextbank(pool=None):
            if pool is None:
                b = pbank[pctr[0] % 8]
            else:
                b = pbank[pool[pctr[0] % len(pool)]]
            pctr[0] += 1
            return b

        def nextq():
            t, qs = pbank[pctr[0] % 8]
            pctr[0] += 1
            return t[:, 0:128], qs[0]

        def nextq2():
            t, qs = pbank[pctr[0] % 8]
            pctr[0] += 1
            return t[:, 0:256], qs[0]

        def cst(name, w=128):
            o = COFF[name]
            return consts[:, o:o + w]

        def vec(l, name, j=0, w=1):
            o, _ = VOFF[name]
            return vecs[:, l, o + j:o + j + w]

        def wslot():
            s = wslots[wctr[0] % NSLOT]
            wctr[0] += 1
            return s

        def wload(slot, dst_ap, src_tk, src_ap):
            P.dma([slot], dst_ap, [src_tk], src_ap, q="pool", sem_t=slot)

        def mm(ps_ap, ps_tks, lhsT, rhs, start, stop, reads):
            P.op("pe", lambda: PE.matmul(ps_ap, lhsT=lhsT, rhs=rhs, start=start, stop=stop), reads, ps_tks)

        def act(out_ap, in_ap, func, reads, writes, bias=None, scale=None):
            kw = {}
            if bias is not None:
                kw["bias"] = bias
            if scale is not None:
                kw["scale"] = scale
            P.op("act", lambda: A.activation(out=out_ap, in_=in_ap, func=func, **kw), reads, writes)

        def vtt(out_ap, in0, in1, op, reads, writes, e="dve"):
            eng = V if e == "dve" else G
            P.op(e, lambda: eng.tensor_tensor(out=out_ap, in0=in0, in1=in1, op=op), reads, writes)

        def vts(out_ap, in0, s1, s2, op0, op1, reads, writes, e="dve"):
            eng = V if e == "dve" else G
            if op1 is None:
                P.op(e, lambda: eng.tensor_scalar(out=out_ap, in0=in0, scalar1=s1, scalar2=None, op0=op0), reads, writes)
            else:
                P.op(e, lambda: eng.tensor_scalar(out=out_ap, in0=in0, scalar1=s1, scalar2=s2, op0=op0, op1=op1), reads, writes)

        def vstt(out_ap, in0, scalar, in1, op0, op1, reads, writes, e="dve"):
            eng = V if e == "dve" else G
            P.op(e, lambda: eng.scalar_tensor_tensor(out=out_ap, in0=in0, scalar=scalar, in1=in1, op0=op0, op1=op1), reads, writes)

        def vcopy(out_ap, in_ap, reads, writes, e="dve"):
            eng = V if e == "dve" else G
            P.op(e, lambda: eng.tensor_copy(out=out_ap, in_=in_ap), reads, writes)

        def vmemset(ap, val, writes, e="dve"):
            eng = V if e == "dve" else G
            P.op(e, lambda: eng.memset(ap, val), [], writes)

        def rsqrt_from(out_ap, in_ap, scale, reads, writes):
            act(out_ap, in_ap, AF.Ln, reads, writes, bias=epsc[:, 0:1], scale=scale)
            act(out_ap, out_ap, AF.Exp, writes, writes, scale=-0.5)

        def softplus_acc(out_ap, x_ap, t1, t2, t3, rd, wr):
            vts(t1, x_ap, -1.0, None, ALU.mult, None, rd, wr)
            vtt(t1, t1, x_ap, ALU.max, rd + wr, wr)
            act(t1, t1, AF.Exp, wr, wr, scale=-1.0)
            vts(t2, t1, 2.0, None, ALU.add, None, wr, wr)
            P.op("dve", lambda: V.reciprocal(out=t2, in_=t2), wr, wr)
            vtt(t1, t1, t2, ALU.mult, wr, wr)
            vtt(t2, t1, t1, ALU.mult, wr, wr)
            vts(t3, t2, 1.0 / 11, 1.0 / 9, ALU.mult, ALU.add, wr, wr)
            for cf in (1.0 / 7, 1.0 / 5, 1.0 / 3, 1.0):
                vtt(t3, t3, t2, ALU.mult, wr, wr)
                vts(t3, t3, cf, None, ALU.add, None, wr, wr)
            vtt(t3, t3, t1, ALU.mult, wr, wr)
            vts(t1, x_ap, 0.0, None, ALU.max, None, rd + wr, wr)
            vstt(out_ap, t3, 2.0, t1, ALU.mult, ALU.add, wr, wr)

        epsc = P.sbuf("epsc", [128, 1], F32)
        vmemset(epsc[:], EPS, [epsc])
        P.dma([consts], consts[:], [consts_d], consts_d.t)
        P.dma([vecs], vecs[:], [vecs_d], vecs_d.t)
        vcopy(identb[:], cst("ident", 256), [consts], [identb])
        vts(negidentb[:], cst("ident"), -1.0, None, ALU.mult, None, [consts], [negidentb])
        vcopy(onesb[:], cst("ones"), [consts], [onesb])
        with contextlib.ExitStack() as sc:
            craw = P.sbuf("craw", [128, 8, nseq], F32, sc)
            csil = P.sbuf("csil", [128, 8, nseq], BF16, sc)
            tmpm = P.sbuf("tmpm", [128, 8, nseq], F32, sc)
            spx = P.sbuf("spx", [128, 8], F32, sc)
            spt = P.sbuf("spt", [128, 3, 8], F32, sc)
            P.dma([craw], craw[:], [cT], cT.t)
            act(csil[:], craw[:], AF.Silu, [craw], [csil])
            for l in range(nlayer):
                for pn in range(12):
                    sl = wslot()
                    wv = sl[:, 0:4096].rearrange("p (k n) -> p k n", k=8)
                    wload(sl, wv, wMOD, wMOD.t[l, :, :, pn * 512:(pn + 1) * 512])
                    for jj in range(4):
                        j = pn * 4 + jj
                        pt, pq = nextbank()
                        for k in range(8):
                            mm(pt[:, 0:nseq], pq, wv[:, k, jj * 128:(jj + 1) * 128], csil[:, k, :], k == 0, k == 7, [sl, csil])
                        act(mod[:, l, j, :], pt[:, 0:nseq], AF.Identity, pq, [mod], bias=vec(l, "bmod", j))
                for (gs, lnn, j0) in ((gs1, "ln1", 8), (gs2, "ln2", 32)):
                    vts(tmpm[:], mod[:, l, j0:j0 + 8, :], 1.0, None, ALU.add, None, [mod], [tmpm])
                    for s in range(nseq):
                        vtt(gs[:, l, :, s], tmpm[:, :, s], vec(l, lnn, 0, 8), ALU.mult, [tmpm, vecs], [gs])
                vts(spx[:], vec(l, "rglam", 0, 8), -1.0, None, ALU.mult, None, [vecs], [spx])
                softplus_acc(cneg[:, l, :], spx[:], spt[:, 0, :], spt[:, 1, :], spt[:, 2, :], [spx], [spt, cneg])
                vts(cneg2[:, l, :], cneg[:, l, :], -16.0, None, ALU.mult, None, [cneg], [cneg2])
                vts(cneg[:, l, :], cneg[:, l, :], -8.0, None, ALU.mult, None, [cneg], [cneg])
                act(negA[:, l, :], vec(l, "galog", 0, 128), AF.Exp, [vecs], [negA])
                vts(negA[:, l, :], negA[:, l, :], -1.0, None, ALU.mult, None, [negA], [negA])
            P.barrier()

        def proj_fm(wv, kc, M, rhs_fn, rhs_tks, slot, consume):
            for tb in range(NTB):
                pt, pq = nextbank()
                for k in range(kc):
                    mm(pt[0:M, :], pq, wv[:, k, 0:M], rhs_fn(k, tb), k == 0, k == kc - 1, [slot] + rhs_tks)
                consume(tb, pt, pq)

        def hrhs(k, tb):
            return hT[:, k, tb * 512:(tb + 1) * 512]

        def load_xblock(xb, tb):
            for c in range(8):
                pass
            P.dma([xb], xb[:], [xT_tk[c][tb] for c in range(8)], xT_d[:, :, tb * 512:(tb + 1) * 512])

        def store_xblock(xb, tb):
            P.dma([xT_tk[c][tb] for c in range(8)], xT_d[:, :, tb * 512:(tb + 1) * 512], [xb], xb[:], sem_t=xb)

        def norm_block(xb, tb, l, s, which, sq, rstd, tmp):
            gs = gs1 if which == 1 else gs2
            shj = 0 if which == 1 else 24
            vtt(sq[:], xb[:], xb[:], ALU.mult, [xb], [sq])
            pt, pq = nextbank()
            for c in range(8):
                mm(pt[:, :], pq, onesb[:], sq[:, c, :], c == 0, c == 7, [onesb, sq])
            rsqrt_from(rstd[:], pt[:, :], 1.0 / D, pq, [rstd])
            for c in range(8):
                vtt(tmp[:], xb[:, c, :], rstd[:], ALU.mult, [xb, rstd], [tmp])
                act(hT[:, c, tb * 512:(tb + 1) * 512], tmp[:], AF.Identity, [tmp, gs, mod], [hT],
                    bias=mod[:, l, shj + c, s:s + 1], scale=gs[:, l, c, s:s + 1])

        for s in range(nseq if 'NOSEQ' not in PHASES else 0):
            with contextlib.ExitStack() as sc:
                xtok = [P.sbuf("xtok", [128, D], F32, sc) for _ in range(2)]
                xblk = [P.sbuf("xblk", [128, 8, 128], F32, sc) for _ in range(2)]
                for tl in range(NTL if 'NOL' not in PHASES else 0):
                    xt = xtok[tl % 2]
                    xk = xblk[tl % 2]
                    P.dma([xt], xt[:], [xin], xin.t[s, tl * 128:(tl + 1) * 128, :])
                    for half in range(2):
                        pt, pq = nextbank()
                        for q in range(4):
                            c = half * 4 + q
                            P.op("pe", lambda: PE.transpose(out=pt[:, q * 128:(q + 1) * 128], in_=xt[:, c * 128:(c + 1) * 128],
                                                            identity=cst("ident")), [xt, consts], pq)
                        act(xk[:, half * 4:half * 4 + 4, :], pt[:, :].rearrange("p (q n) -> p q n", q=4), AF.Copy, pq, [xk])
                    tb = tl // 4
                    P.dma([xT_tk[c][tb] for c in range(8)], xT_d[:, :, tl * 128:(tl + 1) * 128], [xk], xk[:], sem_t=xk)
                P.barrier()

            for l in range(nlayer):
                with contextlib.ExitStack() as sc:
                    xb = P.sbuf("xb", [128, 8, 512], F32, sc)
                    sq = P.sbuf("sq", [128, 8, 512], BF16, sc)
                    rstd = P.sbuf("rstd", [128, 512], F32, sc)
                    tmp = P.sbuf("tmp", [128, 512], F32, sc)
                    for tb in range(NTB if 'N1' in PHASES else 0):
                        load_xblock(xb, tb)
                        norm_block(xb, tb, l, s, 1, sq, rstd, tmp)
                    P.barrier()

                mixsc = contextlib.ExitStack()
                obT = []
                obT.append(P.sbuf("obT_mla", [128, 4, T], BF16, mixsc))
                with contextlib.ExitStack() as sc:
                    if 'MLA' in PHASES:
                        emit_mla(P, nc, locals(), sc, l, s)
                    P.barrier()
                obT.append(P.sbuf("obT_rg", [128, 4, T], BF16, mixsc))
                with contextlib.ExitStack() as sc:
                    if 'RG' in PHASES:
                        emit_rg(P, nc, locals(), sc, l, s)
                    P.barrier()
                obT.append(P.sbuf("obT_gdn", [128, 4, T], BF16, mixsc))
                with contextlib.ExitStack() as sc:
                    if 'GDN' in PHASES:
                        emit_gdn(P, nc, locals(), sc, l, s)
                    P.barrier()
                if dbg and l == 0 and s == 0:
                    with contextlib.ExitStack() as sc:
                        dtmp = P.sbuf("dtmp", [128, 4, T], F32, sc)
                        for b in range(3):
                            vcopy(dtmp[:], obT[b][:], [obT[b]], [dtmp])
                            P.dma([dbg_d], dbg_d.t[b], [dtmp], dtmp[:], sem_t=dtmp)
                        P.barrier()
                with contextlib.ExitStack() as sc:
                    if 'MERGE' in PHASES:
                        emit_merge(P, nc, locals(), sc, l, s)
                    P.barrier()
                mixsc.close()
                with contextlib.ExitStack() as sc:
                    if 'FFN' in PHASES:
                        emit_ffn(P, nc, locals(), sc, l, s)
                    P.barrier()

            with contextlib.ExitStack() as sc:
                xb = P.sbuf("xb", [128, 8, 512], F32, sc)
                sq = P.sbuf("sq", [128, 8, 512], F32, sc)
                rstd = P.sbuf("rstd", [128, 512], F32, sc)
                ytok = [P.sbuf("ytok", [128, D], F32, sc) for _ in range(2)]
                yc = 0
                for tb in range(NTB if 'NOOUT' not in PHASES else 0):
                    load_xblock(xb, tb)
                    vtt(sq[:], xb[:], xb[:], ALU.mult, [xb], [sq])
                    if OUTSTAGE >= 1:
                        pt, pq = nextbank()
                        for c in range(8):
                            mm(pt[:, :], pq, cst("ones"), sq[:, c, :], c == 0, c == 7, [consts, sq])
                        if OUTSTAGE >= 2:
                            rsqrt_from(rstd[:], pt[:, :], 1.0 / D, pq, [rstd])
                        else:
                            act(rstd[:], pt[:, :], AF.Copy, pq, [rstd])
                        if OUTSTAGE >= 3:
                            for c in range(8):
                                vstt(sq[:, c, :], xb[:, c, :], vec(0, "fng", c), rstd[:], ALU.mult, ALU.mult, [xb, rstd, vecs], [sq])
                    for t4 in range(4):
                        yt = ytok[yc % 2]
                        yc += 1
                        for half in range(2):
                            pt, pq = nextbank()
                            for q in range(4):
                                c = half * 4 + q
                                P.op("pe", lambda: PE.transpose(out=pt[:, q * 128:(q + 1) * 128], in_=sq[:, c, t4 * 128:(t4 + 1) * 128],
                                                                identity=cst("ident")), [sq, consts], pq)
                            act(yt[:, half * 512:(half + 1) * 512], pt[:, :], AF.Copy, pq, [yt])
                        tl = tb * 4 + t4
                        P.dma([yout], yout.t[s, tl * 128:(tl + 1) * 128, :], [yt], yt[:], sem_t=yt)
                P.barrier()
        outs = [yout] + ([dbg_d] if dbg else [])
        P.finish(outs)
        P.barrier()
    return nc, P


def _h(env):
    class E:
        pass
    e = E()
    e.__dict__.update(env)
    return e


def emit_mla(P, nc, env, sc, l, s):
    e = _h(env)
    hT, obT, consts, identb = e.hT, e.obT, e.consts, e.identb
    mm, act, vtt, vts, vstt, vcopy, vmemset = e.mm, e.act, e.vtt, e.vts, e.vstt, e.vcopy, e.vmemset
    nextbank, cst, vec, wslot, wload, rsqrt_from, hrhs = e.nextbank, e.cst, e.vec, e.wslot, e.wload, e.rsqrt_from, e.hrhs
    wA, wUQ, wKN, wV, ropet, vecs = e.wA, e.wUQ, e.wKN, e.wV, e.ropet, e.vecs
    ob = obT[0]
    cosT = P.sbuf("cosT", [96, T], BF16, sc)
    sinT = P.sbuf("sinT", [96, T], BF16, sc)
    cq = P.sbuf("cq", [128, 3, T], BF16, sc)
    ckv = P.sbuf("ckv", [128, 2, T], BF16, sc)
    krT = P.sbuf("krT", [96, T], BF16, sc)
    Vall = P.sbuf("Vall", [128, NTL, 8, 128], BF16, sc)
    qhs = [P.sbuf("qh", [96, T], BF16, sc) for _ in range(2)]
    khs = [P.sbuf("kh", [96, T], BF16, sc) for _ in range(2)]
    PTs = [P.sbuf("PT", [128, 512], BF16, sc) for _ in range(4)]
    sqs = [P.sbuf("sqm", [128, 512], BF16, sc) for _ in range(3)]
    onesb = e.onesb
    rstd = P.sbuf("rstdm", [128, 512], F32, sc)
    t1 = P.sbuf("t1m", [128, 512], F32, sc)
    t2 = P.sbuf("t2m", [128, 512], F32, sc)
    dsh = P.sbuf("dsh", [128, 512], F32, sc)
    P.dma([cosT], cosT[:], [ropet], ropet.t[0], q="pool")
    P.dma([sinT], sinT[:], [ropet], ropet.t[1], q="pool")
    vmemset(Vall[:], 1.0, [Vall])

    for (dst, col0, nch, gname) in ((cq, 0, 3, "qg"), (ckv, 384, 2, "kvg")):
        sl = wslot()
        wv = sl[:, 0:8 * nch * 128].rearrange("p (k n) -> p k n", k=8)
        wload(sl, wv, wA, wA.t[l, :, :, col0:col0 + nch * 128])
        for tb in range(NTB):
            pcs = []
            for c in range(nch):
                pt, pq = nextbank()
                for k in range(8):
                    mm(pt[:, :], pq, wv[:, k, c * 128:(c + 1) * 128], hrhs(k, tb), k == 0, k == 7, [sl, hT])
                act(sqs[c][:], pt[:, :], AF.Square, pq, [sqs[c]])
                pcs.append((pt, pq))
            pt2, pq2 = nextbank()
            for c in range(nch):
                mm(pt2[:, :], pq2, onesb[:], sqs[c][:], c == 0, c == nch - 1, [onesb, sqs[c]])
            rsqrt_from(rstd[:], pt2[:, :], 1.0 / (nch * 128), pq2, [rstd])
            for c in range(nch):
                pt, pq = pcs[c]
                vtt(t1[:], pt[:, :], rstd[:], ALU.mult, pq + [rstd], [t1])
                act(dst[:, c, tb * 512:(tb + 1) * 512], t1[:], AF.Identity, [t1, vecs], [dst], scale=vec(l, gname, c))
    sl = wslot()
    wv = sl[:, 0:8 * 192].rearrange("p (k n) -> p k n", k=8)
    wload(sl, wv, wA, wA.t[l, :, :, KR0:KR0 + 192])
    for tb in range(NTB):
        blk = slice(tb * 512, (tb + 1) * 512)
        pa, qa = nextbank()
        pb, qb_ = nextbank()
        for k in range(8):
            mm(pa[0:96, :], qa, wv[:, k, 0:96], hrhs(k, tb), k == 0, k == 7, [sl, hT])
        for k in range(8):
            mm(pb[0:96, :], qb_, wv[:, k, 96:192], hrhs(k, tb), k == 0, k == 7, [sl, hT])
        vtt(t1[0:96, :], pa[0:96, :], cosT[:, blk], ALU.mult, qa + [cosT], [t1])
        vtt(t2[0:96, :], pb[0:96, :], sinT[:, blk], ALU.mult, qb_ + [sinT], [t2])
        vtt(krT[:, blk], t1[0:96, :], t2[0:96, :], ALU.add, [t1, t2], [krT])
    sl = wslot()
    wv = sl[:, 0:1024].rearrange("p (k n) -> p k n", k=2)
    wload(sl, wv, wV, wV.t[l])
    for tl in range(NTL):
        pt, pq = nextbank()
        for c in range(2):
            mm(pt[:, :], pq, ckv[:, c, tl * 128:(tl + 1) * 128], wv[:, c, :], c == 0, c == 1, [ckv, sl])
        pv = pt[:, :].rearrange("p (h d) -> p h d", h=8)
        for par in range(2):
            act(Vall[:, tl, par::2, par * 64:par * 64 + 64], pv[:, par::2, :], AF.Copy, pq, [Vall])
    scale = 96.0 ** -0.5
    for h in range(8):
        qh, kh = qhs[h % 2], khs[h % 2]
        sl = wslot()
        wq = sl[:, 0:3 * 192].rearrange("p (k n) -> p k n", k=3)
        wload(sl, wq[:, :, 0:96], wUQ, wUQ.t[l, :, :, h * 96:(h + 1) * 96])
        wload(sl, wq[:, :, 96:192], wUQ, wUQ.t[l, :, :, 768 + h * 96:768 + (h + 1) * 96])
        wk = sl[:, 1024:1024 + 2 * 96].rearrange("p (k n) -> p k n", k=2)
        wload(sl, wk, wKN, wKN.t[l, :, :, h * 96:(h + 1) * 96])
        for tb in range(NTB):
            blk = slice(tb * 512, (tb + 1) * 512)
            pa, qa = nextbank()
            pb, qb_ = nextbank()
            for c in range(3):
                mm(pa[0:96, :], qa, wq[:, c, 0:96], cq[:, c, blk], c == 0, c == 2, [sl, cq])
            for c in range(3):
                mm(pb[0:96, :], qb_, wq[:, c, 96:192], cq[:, c, blk], c == 0, c == 2, [sl, cq])
            vtt(t1[0:96, :], pa[0:96, :], cosT[:, blk], ALU.mult, qa + [cosT], [t1])
            vtt(t2[0:96, :], pb[0:96, :], sinT[:, blk], ALU.mult, qb_ + [sinT], [t2])
            vtt(qh[:, blk], t1[0:96, :], t2[0:96, :], ALU.add, [t1, t2], [qh])
            pk, qk = nextbank()
            for c in range(2):
                mm(pk[0:96, :], qk, wk[:, c, 0:96], ckv[:, c, blk], c == 0, False, [sl, ckv])
            mm(pk[0:96, :], qk, identb[0:96, 0:96], krT[:, blk], False, True, [identb, krT])
            act(kh[:, blk], pk[0:96, :], AF.Copy, qk, [kh])
        par = h % 2
        num = slice(par * 64, par * 64 + 64)
        den = slice((1 - par) * 64, (1 - par) * 64 + 64)
        pti = 0
        for qb in range(NTB):
            qblk = slice(qb * 512, (qb + 1) * 512)
            po, qo = nextbank([0, 1])
            LOOK = 2
            pend = []
            for kt in range(NTL + LOOK):
                if kt < NTL:
                    ps_, qs_ = nextbank([2, 3, 4, 5, 6, 7])
                    mm(ps_[:, :], qs_, kh[:, kt * 128:(kt + 1) * 128], qh[:, qblk], True, True, [kh, qh])
                    PT = PTs[pti % 4]
                    pti += 1
                    act(PT[:], ps_[:, :], AF.Exp, qs_, [PT], scale=scale)
                    pend.append(PT)
                if kt >= LOOK:
                    k2 = kt - LOOK
                    PT2 = pend[k2]
                    mm(po[:, :], qo, Vall[:, k2, h, :], PT2[:], k2 == 0, k2 == NTL - 1, [Vall, PT2])
            act(dsh[num, :], po[den, :], AF.Copy, qo, [dsh])
            P.op("dve", lambda: nc.vector.reciprocal(out=dsh[num, :], in_=dsh[num, :]), [dsh], [dsh])
            vtt(ob[num, h // 2, qblk], po[num, :], dsh[num, :], ALU.mult, qo + [dsh], [ob])


def emit_rg(P, nc, env, sc, l, s):
    e = _h(env)
    hT, obT, consts = e.hT, e.obT, e.consts
    mm, act, vtt, vts, vstt, vcopy, vmemset = e.mm, e.act, e.vtt, e.vts, e.vstt, e.vcopy, e.vmemset
    nextbank, cst, vec, wslot, wload, hrhs = e.nextbank, e.cst, e.vec, e.wslot, e.wload, e.hrhs
    wA, wRG, vecs, cneg, cneg2 = e.wA, e.wRG, e.vecs, e.cneg, e.cneg2
    ob = obT[1]
    xpads = [P.sbuf("xpad", [128, T + 4], F32, sc) for _ in range(2)]
    xc = P.sbuf("xc", [128, T], F32, sc)
    xcb = P.sbuf("xcb", [128, T], BF16, sc)
    R = P.sbuf("R", [128, T], F32, sc)
    I = P.sbuf("I", [128, T], F32, sc)
    Ab = P.sbuf("Ab", [128, T], F32, sc)
    HF = P.sbuf("HF", [128, T], F32, sc)
    HB = P.sbuf("HB", [128, T], F32, sc)
    gts = [P.sbuf("gt", [128, T], F32, sc) for _ in range(2)]
    wrg = P.sbuf("wrg", [128, 16, 128], BF16, sc)
    P.dma([wrg], wrg[:], [wRG], wRG.t[l], q="pool")
    for xp in xpads:
        vmemset(xp[:], 0.0, [xp])

    def rg_proj(c):
        xpad, gt = xpads[c % 2], gts[c % 2]
        sl = wslot()
        wv = sl[:, 0:2048].rearrange("p (k n) -> p k n", k=8)
        wload(sl, wv[:, :, 0:128], wA, wA.t[l, :, :, C_RGX + c * 128:C_RGX + (c + 1) * 128])
        wload(sl, wv[:, :, 128:256], wA, wA.t[l, :, :, C_RGG + c * 128:C_RGG + (c + 1) * 128])
        for tb in range(NTB):
            pt, pq = nextbank()
            for k in range(8):
                mm(pt[:, :], pq, wv[:, k, 0:128], hrhs(k, tb), k == 0, k == 7, [sl, hT])
            act(xpad[:, 1 + tb * 512:1 + (tb + 1) * 512], pt[:, :], AF.Copy, pq, [xpad])
        for tb in range(NTB):
            pt, pq = nextbank()
            for k in range(8):
                mm(pt[:, :], pq, wv[:, k, 128:256], hrhs(k, tb), k == 0, k == 7, [sl, hT])
            act(gt[:, tb * 512:(tb + 1) * 512], pt[:, :], AF.Copy, pq, [gt])

    rg_proj(0)
    for c in range(4):
        if c + 1 < 4:
            rg_proj(c + 1)
        xpad, gt = xpads[c % 2], gts[c % 2]
        vtt(HB[:], gt[:], gt[:], ALU.mult, [gt], [HB])
        vts(HB[:], HB[:], 0.044715, 1.0, ALU.mult, ALU.add, [HB], [HB])
        vtt(HB[:], HB[:], gt[:], ALU.mult, [HB, gt], [HB])
        act(HB[:], HB[:], AF.Sigmoid, [HB], [HB], scale=2.0 * math.sqrt(2.0 / math.pi))
        vtt(gt[:], gt[:], HB[:], ALU.mult, [gt, HB], [gt])
        act(xc[:], xpad[:, 0:T], AF.Identity, [xpad, vecs], [xc], bias=vec(l, "rgcb", c), scale=vec(l, "rgcw", c * 4 + 0))
        for k in range(1, 4):
            vstt(xc[:], xpad[:, k:k + T], vec(l, "rgcw", c * 4 + k), xc[:], ALU.mult, ALU.add, [xpad, xc, vecs], [xc])
        vcopy(xcb[:], xc[:], [xc], [xcb])
        if RGDBG == 1:
            vcopy(ob[:, c, :], xc[:], [xc], [ob])
            continue
        if RGDBG == 2:
            vcopy(ob[:, c, :], gt[:], [gt], [ob])
            continue
        for d in range(2):
            for (ai, dst, bname) in ((0, R, "rgba"), (1, I, "rgbi")):
                wi = (d * 2 + ai) * 4 + c
                for tb in range(NTB):
                    blk = slice(tb * 512, (tb + 1) * 512)
                    pt, pq = nextbank()
                    mm(pt[:, :], pq, wrg[:, wi, :], xcb[:, blk], True, True, [wrg, xcb])
                    act(dst[:, blk], pt[:, :], AF.Sigmoid, pq + [vecs], [dst], bias=vec(l, bname, d * 4 + c))
            if RGDBG == 3 and d == 0:
                vcopy(ob[:, c, :], R[:], [R], [ob])
            if RGDBG == 4 and d == 0:
                vcopy(ob[:, c, :], I[:], [I], [ob])
            act(Ab[:], R[:], AF.Exp, [R, cneg], [Ab], scale=cneg[:, l, d * 4 + c:d * 4 + c + 1])
            if RGDBG == 5 and d == 0:
                vcopy(ob[:, c, :], Ab[:], [Ab], [ob])
            act(R[:], R[:], AF.Exp, [R, cneg2], [R], scale=cneg2[:, l, d * 4 + c:d * 4 + c + 1])
            vts(R[:], R[:], -1.0, 1.0, ALU.mult, ALU.add, [R], [R])
            vts(R[:], R[:], 0.0, None, ALU.max, None, [R], [R])
            act(R[:], R[:], AF.Sqrt, [R], [R])
            edge = 0 if d == 0 else T - 1
            vmemset(R[:, edge:edge + 1], 1.0, [R])
            vtt(I[:], I[:], xc[:], ALU.mult, [I, xc], [I])
            vtt(I[:], I[:], R[:], ALU.mult, [I, R], [I])
            if d == 0:
                P.op("dve", lambda: nc.vector.tensor_tensor_scan(out=HF[:], data0=Ab[:], data1=I[:], initial=0.0,
                                                                 op0=ALU.mult, op1=ALU.add), [Ab, I], [HF])
            else:
                P.op("dve", lambda: nc.vector.tensor_tensor_scan(out=HB[:, ::-1], data0=Ab[:, ::-1], data1=I[:, ::-1], initial=0.0,
                                                                 op0=ALU.mult, op1=ALU.add), [Ab, I], [HB])
        if RGDBG in (3, 4, 5):
            continue
        if RGDBG == 6:
            vcopy(ob[:, c, :], HF[:], [HF], [ob])
            continue
        if RGDBG == 7:
            vcopy(ob[:, c, :], HB[:], [HB], [ob])
            continue
        vtt(HF[:], HF[:], HB[:], ALU.add, [HF, HB], [HF])
        vtt(ob[:, c, :], HF[:], gt[:], ALU.mult, [HF, gt], [ob])


def emit_gdn(P, nc, env, sc, l, s):
    e = _h(env)
    hT, obT, consts, identb = e.hT, e.obT, e.consts, e.identb
    mm, act, vtt, vts, vstt, vcopy, vmemset = e.mm, e.act, e.vtt, e.vts, e.vstt, e.vcopy, e.vmemset
    nextbank, nextq, cst, vec, wslot, wload, rsqrt_from, hrhs = e.nextbank, e.nextq, e.cst, e.vec, e.wslot, e.wload, e.rsqrt_from, e.hrhs
    nextq2 = e.nextq2
    negidentb = e.negidentb
    wA, vecs, negA, epsc = e.wA, e.vecs, e.negA, e.epsc
    ob = obT[2]
    qT = P.sbuf("gqT", [128, T], BF16, sc)
    kT = P.sbuf("gkT", [128, T], BF16, sc)
    ktok = P.sbuf("gktok", [128, NTL, 128], BF16, sc)
    vtok = P.sbuf("gvtok", [128, NTL, 128], BF16, sc)
    otok = P.sbuf("gotok", [128, NTL, 128], F32, sc)
    gtok = P.sbuf("ggtok", [128, NTL, 8], F32, sc)
    btok = P.sbuf("gbtok", [128, NTL, 8], F32, sc)
    rtmp = P.sbuf("grtmp", [128, 512], F32, sc)
    szs = [P.sbuf("gsz", [128, 128], F32, sc) for _ in range(3)]
    ons = [P.sbuf("gon", [128, 128], F32, sc) for _ in range(3)]
    ssqs = [P.sbuf("gssq", [128, 4], F32, sc) for _ in range(3)]

    sl = wslot()
    wab = sl[:, 0:128].rearrange("p (k n) -> p k n", k=8)
    wload(sl, wab, wA, wA.t[l, :, :, C_AB:C_AB + 16])
    for tl in range(NTL):
        pt, pq = nextbank()
        for k in range(8):
            mm(pt[:, 0:16], pq, hT[:, k, tl * 128:(tl + 1) * 128], wab[:, k, :], k == 0, k == 7, [hT, sl])
        vtt(gtok[:, tl, :], pt[:, 0:8], vec(l, "gdtb", 0, 8), ALU.add, pq + [vecs], [gtok])
        act(btok[:, tl, :], pt[:, 8:16], AF.Sigmoid, pq, [btok])
    gfl = gtok[:].rearrange("p a b -> p (a b)")
    e.softplus_acc(rtmp[:, 0:128], gfl, rtmp[:, 128:256], rtmp[:, 256:384], rtmp[:, 384:512], [gtok], [rtmp])
    vtt(gfl, rtmp[:, 0:128], negA[:, l, :], ALU.mult, [rtmp, negA], [gtok])

    ident, ones, negones, bd = cst("ident"), cst("ones"), cst("negones"), cst("bd16")
    offmasks = [cst("m32"), cst("m64"), cst("m128")]
    G = GDN_G
    NH = G + 1
    PL = "pool" if GDN_POOL else "dve"

    for hh in range(4):
        with contextlib.ExitStack() as sc2:
            xpads = [P.sbuf("gxpad", [128, T + 4], F32, sc2) for _ in range(2)]
            xc = P.sbuf("gxc", [128, T], F32, sc2)
            rts = [rtmp] + [P.sbuf("grt", [128, 512], F32, sc2) for _ in range(3)]
            rsq = [P.sbuf("grsq", [128, 512], BF16, sc2) for _ in range(4)]
            for xp in xpads:
                vmemset(xp[:], 0.0, [xp])

            def qkv_proj(role):
                ch = role * 4 + hh
                col0 = C_GQKV + ch * 128
                xpad = xpads[role % 2]
                sl = wslot()
                wv = sl[:, 0:1024].rearrange("p (k n) -> p k n", k=8)
                wload(sl, wv, wA, wA.t[l, :, :, col0:col0 + 128])
                for tb in range(NTB):
                    pt, pq = nextbank()
                    for k in range(8):
                        mm(pt[:, :], pq, wv[:, k, :], hrhs(k, tb), k == 0, k == 7, [sl, hT])
                    act(xpad[:, 1 + tb * 512:1 + (tb + 1) * 512], pt[:, :], AF.Copy, pq, [xpad])

            def qkv_post(role):
                ch = role * 4 + hh
                xpad = xpads[role % 2]
                act(xc[:], xpad[:, 0:T], AF.Identity, [xpad, vecs], [xc], scale=vec(l, "gcw", ch * 4 + 0))
                for k in range(1, 4):
                    vstt(xc[:], xpad[:, k:k + T], vec(l, "gcw", ch * 4 + k), xc[:], ALU.mult, ALU.add, [xpad, xc, vecs], [xc])
                act(xc[:], xc[:], AF.Silu, [xc], [xc])
                if role < 2:
                    blks = [slice(tb * 512, (tb + 1) * 512) for tb in range(NTB)]
                    for tb in range(NTB):
                        vtt(rsq[tb][:], xc[:, blks[tb]], xc[:, blks[tb]], ALU.mult, [xc], [rsq[tb]])
                    pts = []
                    for tb in range(NTB):
                        pt, pq = nextbank()
                        mm(pt[:, :], pq, e.onesb[:], rsq[tb][:], True, True, [e.onesb, rsq[tb]])
                        pts.append((pt, pq))
                    for tb in range(NTB):
                        pt, pq = pts[tb]
                        act(rts[tb][:], pt[:, :], AF.Ln, pq, [rts[tb]], bias=epsc[:, 0:1], scale=1.0)
                    for tb in range(NTB):
                        act(rts[tb][:], rts[tb][:], AF.Exp, [rts[tb]], [rts[tb]], scale=-0.5)
                    for tb in range(NTB):
                        if role == 0:
                            vstt(xc[:, blks[tb]], xc[:, blks[tb]], 128.0 ** -0.5, rts[tb][:], ALU.mult, ALU.mult, [xc, rts[tb]], [xc])
                        else:
                            vtt(xc[:, blks[tb]], xc[:, blks[tb]], rts[tb][:], ALU.mult, [xc, rts[tb]], [xc])
                    vcopy((qT if role == 0 else kT)[:], xc[:], [xc], [qT if role == 0 else kT])
                if role >= 1:
                    dst = ktok if role == 1 else vtok
                    for tl in range(NTL):
                        qa, qk = nextq()
                        mm(qa, [qk], xc[:, tl * 128:(tl + 1) * 128], ident, True, True, [xc, consts])
                        act(dst[:, tl, :], qa, AF.Copy, [qk], [dst])

            qkv_proj(0)
            for role in range(3):
                if role + 1 < 3:
                    qkv_proj(role + 1)
                qkv_post(role)
            P.barrier()
        vmemset(otok[:], 0.0, [otok])

        with contextlib.ExitStack() as sc3:
            W2 = 256
            def mk(name, n, dt=F32, w=W2):
                return [P.sbuf(name, [128, w], dt, sc3) for _ in range(n)]
            GM, decS, decI, Af = (mk(nm, G) for nm in ("GM", "decS", "decI", "Af"))
            IDT = BF16
            Ad, Aoff, Bm = (mk(nm, G, IDT) for nm in ("Ad", "Aoff", "Bm"))
            Y = [mk("Ya", G, IDT), mk("Yb", G, IDT)]
            YT = [mk("YTa", G, IDT), mk("YTb", G, IDT)]
            Pm, PTm, Zm, TTb = mk("Pm", G, IDT), mk("PTm", G, IDT), mk("Zm", G, IDT), mk("TTb", G, BF16)
            vb, kbg = mk("vb", G, BF16), mk("kbg", G, BF16)
            um, wTm, kdec, attnT = mk("um", NH), mk("wTm", NH, BF16), mk("kdec", NH, BF16), mk("attnT", NH, BF16)
            cols = mk("cols", NH, F32, 16)
            vnew = mk("vnew", NH, BF16)
            tmpo = mk("tmpo", NH)
            S2 = P.sbuf("gS2", [128, W2], F32, sc3)
            Sb2 = P.sbuf("gSb2", [128, W2], BF16, sc3)
            vmemset(S2[:], 0.0, [S2])
            vmemset(Sb2[:], 0.0, [Sb2])
            ident2, identb2 = cst("ident", 256), identb[:, 0:256]
            ones2, negones2 = cst("ones", 256), cst("negones", 256)
            mt01, negs01, negst01 = cst("mt0", 256), cst("negs0", 256), cst("negst0", 256)
            bd2 = cst("bd16", 256)
            offmasks2 = [cst("m32", 256), cst("m64", 256), cst("m128", 256)]
            hs = [slice(0, 128), slice(128, 256)]
            tn = lambda d, it: it if d == 0 else NTL - 1 - it

            def prep_gen(it, i, hi):
                ns = [tn(0, it), tn(1, it)]
                colj = [0 * 4 + hh, 1 * 4 + hh]
                tsl = [slice(n * 128, (n + 1) * 128) for n in ns]
                g_col = [gtok[:, ns[d], colj[d]:colj[d] + 1] for d in range(2)]
                b_col = [btok[:, ns[d], colj[d]:colj[d] + 1] for d in range(2)]
                cl = cols[hi]
                for d in range(2):
                    vts(GM[i][:, hs[d]], mt01[:, hs[d]], g_col[d], None, ALU.mult, None, [consts, gtok], [GM[i]])
                    vts(cl[:, d * 8 + 6:d * 8 + 8], ones[:, 0:2], g_col[d], None, ALU.mult, None, [consts, gtok], [cl])
                yield
                pD, qD = nextq2()
                for d in range(2):
                    mm(pD[:, hs[d]], [qD], GM[i][:, hs[d]], ones, d == 0, False, [GM[i], consts])
                mm(pD, [qD], negones, GM[i][:], False, False, [GM[i], consts])
                mm(pD, [qD], ident, negs01, False, True, [consts])
                act(decS[i][:], pD, AF.Exp, [qD], [decS[i]])
                yield
                pDT, qDT = nextq2()
                mm(pDT, [qDT], ones, GM[i][:], True, False, [GM[i], consts])
                for d in range(2):
                    mm(pDT[:, hs[d]], [qDT], GM[i][:, hs[d]], negones, False, False, [GM[i], consts])
                mm(pDT, [qDT], ident, negst01, False, True, [consts])
                act(decI[i][:], pDT, AF.Exp, [qDT], [decI[i]])
                yield
                pc, qc = nextq2()
                for d in range(2):
                    mm(pc[:, d * 4:d * 4 + 2], [qc], GM[i][:, hs[d]], ones[:, 0:2], True, True, [GM[i], consts])
                    mm(pc[:, d * 4 + 2:d * 4 + 4], [qc], ones, cl[:, d * 8 + 6:d * 8 + 8], True, True, [cl, consts])
                cl3 = cl[:, 0:16].rearrange("p (d c) -> p d c", d=2)
                pc3 = pc[:, 0:8].rearrange("p (d c) -> p d c", d=2)
                vcopy(cl3[:, :, 0:2], pc3[:, :, 1:3], [qc], [cl])
                act(cl3[:, :, 2:4], cl3[:, :, 0:2], AF.Exp, [cl], [cl])
                for d in range(2):
                    act(cl[:, d * 8 + 4:d * 8 + 5], cl[:, d * 8:d * 8 + 1], AF.Exp, [cl], [cl], bias=cl[:, d * 8 + 1:d * 8 + 2], scale=-1.0)
                    vtt(cl[:, d * 8 + 5:d * 8 + 6], cl[:, d * 8 + 2:d * 8 + 3], b_col[d], ALU.mult, [cl, btok], [cl])
                yield
                pK, qK = nextq2()
                for d in range(2):
                    mm(pK[:, hs[d]], [qK], kT[:, tsl[d]], kT[:, tsl[d]], True, True, [kT])
                for d in range(2):
                    vstt(Af[i][:, hs[d]], pK[:, hs[d]], b_col[d], decS[i][:, hs[d]], ALU.mult, ALU.mult, [qK, btok, decS[i]], [Af[i]])
                vtt(Ad[i][:], Af[i][:], bd2, ALU.mult, [Af[i], consts], [Ad[i]])
                yield
                pQ, qQ = nextq2()
                for d in range(2):
                    mm(pQ[:, hs[d]], [qQ], kT[:, tsl[d]], qT[:, tsl[d]], True, True, [kT, qT])
                vtt(attnT[hi][:], pQ, decI[i][:], ALU.mult, [qQ, decI[i]], [attnT[hi]])
                yield
                pB, qB = nextq2()
                for d in range(2):
                    mm(pB[:, hs[d]], [qB], Ad[i][:, hs[d]], identb2[:, 0:128], True, True, [Ad[i], identb])
                act(Bm[i][:], pB, AF.Copy, [qB], [Bm[i]])
                pP, qP = nextq2()
                mm(pP, [qP], identb2[:, 0:128], identb2, True, False, [identb])
                for d in range(2):
                    mm(pP[:, hs[d]], [qP], Ad[i][:, hs[d]], negidentb[:], False, d == 1, [Ad[i], negidentb])
                act(Pm[i][:], pP, AF.Copy, [qP], [Pm[i]])
                yield
                yprev, ytprev = Bm[i], Ad[i]
                for lev in range(3):
                    ycur, ytcur = Y[lev % 2][i], YT[lev % 2][i]
                    pyt, qyt = nextq2()
                    for d in range(2):
                        mm(pyt[:, hs[d]], [qyt], yprev[:, hs[d]], ytprev[:, hs[d]], True, True, [yprev, ytprev])
                    act(ytcur[:], pyt, AF.Copy, [qyt], [ytcur])
                    if lev < 2:
                        py, qy = nextq2()
                        for d in range(2):
                            mm(py[:, hs[d]], [qy], ytprev[:, hs[d]], yprev[:, hs[d]], True, True, [yprev, ytprev])
                        act(ycur[:], py, AF.Copy, [qy], [ycur])
                    yield
                    pp, qp = nextq2()
                    mm(pp, [qp], identb2[:, 0:128], Pm[i][:], True, False, [identb, Pm[i]])
                    for d in range(2):
                        mm(pp[:, hs[d]], [qp], ytcur[:, hs[d]], Pm[i][:, hs[d]], False, d == 1, [ytcur, Pm[i]])
                    act(Pm[i][:], pp, AF.Copy, [qp], [Pm[i]])
                    yield
                    yprev, ytprev = ycur, ytcur
                for mi, msk in enumerate(offmasks2):
                    ppt, qpt = nextq2()
                    for d in range(2):
                        mm(ppt[:, hs[d]], [qpt], Pm[i][:, hs[d]], identb2[:, 0:128], True, True, [Pm[i], identb])
                    act(PTm[i][:], ppt, AF.Copy, [qpt], [PTm[i]])
                    vtt(Aoff[i][:], Af[i][:], msk, ALU.mult, [Af[i], consts], [Aoff[i]])
                    pz, qz = nextq2()
                    for d in range(2):
                        mm(pz[:, hs[d]], [qz], Aoff[i][:, hs[d]], Pm[i][:, hs[d]], True, True, [Aoff[i], Pm[i]])
                    act(Zm[i][:], pz, AF.Copy, [qz], [Zm[i]], scale=-1.0)
                    yield
                    pt2, qt2 = nextq2()
                    mm(pt2, [qt2], identb2[:, 0:128], Pm[i][:], True, False, [identb, Pm[i]])
                    for d in range(2):
                        mm(pt2[:, hs[d]], [qt2], PTm[i][:, hs[d]], Zm[i][:, hs[d]], False, d == 1, [PTm[i], Zm[i]])
                    if mi < 2:
                        act(Pm[i][:], pt2, AF.Copy, [qt2], [Pm[i]])
                    else:
                        act(TTb[i][:], pt2, AF.Copy, [qt2], [TTb[i]])
                    yield
                for d in range(2):
                    vts(vb[i][:, hs[d]], vtok[:, ns[d], :], b_col[d], None, ALU.mult, None, [vtok, btok], [vb[i]])
                    vts(kbg[i][:, hs[d]], ktok[:, ns[d], :], cl[:, d * 8 + 5:d * 8 + 6], None, ALU.mult, None, [ktok, cl], [kbg[i]])
                    vts(kdec[hi][:, hs[d]], ktok[:, ns[d], :], cl[:, d * 8 + 4:d * 8 + 5], None, ALU.mult, None, [ktok, cl], [kdec[hi]])
                pu, qu = nextq2()
                for d in range(2):
                    mm(pu[:, hs[d]], [qu], TTb[i][:, hs[d]], vb[i][:, hs[d]], True, True, [TTb[i], vb[i]])
                act(um[hi][:], pu, AF.Copy, [qu], [um[hi]])
                yield
                pw, qw = nextq2()
                for d in range(2):
                    mm(pw[:, hs[d]], [qw], kbg[i][:, hs[d]], TTb[i][:, hs[d]], True, True, [TTb[i], kbg[i]])
                act(wTm[hi][:], pw, AF.Copy, [qw], [wTm[hi]])

            def step_gen(it, i):
                ns = [tn(0, it), tn(1, it)]
                tsl = [slice(n * 128, (n + 1) * 128) for n in ns]
                cl = cols[i]
                p1, q1 = nextq2()
                for d in range(2):
                    mm(p1[:, hs[d]], [q1], wTm[i][:, hs[d]], Sb2[:, hs[d]], True, True, [wTm[i], Sb2])
                vstt(vnew[i][:], p1, -1.0, um[i][:], ALU.mult, ALU.add, [q1, um[i]], [vnew[i]])
                p2, q2 = nextq2()
                for d in range(2):
                    mm(p2[:, hs[d]], [q2], qT[:, tsl[d]], Sb2[:, hs[d]], True, True, [qT, Sb2])
                for d in range(2):
                    vts(tmpo[i][:, hs[d]], p2[:, hs[d]], cl[:, d * 8 + 2:d * 8 + 3], None, ALU.mult, None, [q2, cl], [tmpo[i]])
                yield
                p4, q4 = nextq2()
                for d in range(2):
                    mm(p4[:, hs[d]], [q4], kdec[i][:, hs[d]], vnew[i][:, hs[d]], True, True, [kdec[i], vnew[i]])
                for d in range(2):
                    vstt(S2[:, hs[d]], S2[:, hs[d]], cl[:, d * 8 + 3:d * 8 + 4], p4[:, hs[d]], ALU.mult, ALU.add, [S2, cl, q4], [S2])
                act(Sb2[:], S2[:], AF.Copy, [S2], [Sb2])
                yield
                p3, q3 = nextq2()
                for d in range(2):
                    mm(p3[:, hs[d]], [q3], attnT[i][:, hs[d]], vnew[i][:, hs[d]], True, True, [attnT[i], vnew[i]])
                vtt(tmpo[i][:], tmpo[i][:], p3, ALU.add, [q3, tmpo[i]], [tmpo[i]])
                for d in range(2):
                    vtt(otok[:, ns[d], :], otok[:, ns[d], :], tmpo[i][:, hs[d]], ALU.add, [otok, tmpo[i]], [otok])

            nxt = 0
            free_s = list(range(G))
            free_h = list(range(NH))
            hmap = {}
            prep_done = set()
            step_done = -1
            step_started = -1
            active = []
            while step_done < NTL - 1:
                while nxt < NTL and free_s and free_h:
                    it = nxt
                    nxt += 1
                    si = free_s.pop(0)
                    hi = free_h.pop(0)
                    hmap[it] = hi
                    active.append(("P", it, prep_gen(it, si, hi), si))
                it = step_started + 1
                if it < NTL and it in prep_done and step_done == it - 1:
                    active.append(("S", it, step_gen(it, hmap[it]), None))
                    step_started = it
                for task in list(active):
                    kind, it, gen, si = task
                    try:
                        next(gen)
                    except StopIteration:
                        active.remove(task)
                        if kind == "P":
                            prep_done.add(it)
                            free_s.append(si)
                        else:
                            step_done = it
                            free_h.append(hmap[it])
            P.barrier()
        sl = wslot()
        wz = sl[:, 0:1024].rearrange("p (k n) -> p k n", k=8)
        wload(sl, wz, wA, wA.t[l, :, :, C_GZ + hh * 128:C_GZ + (hh + 1) * 128])
        def outA(tl):
            sz, on, ssq = szs[tl % 3], ons[tl % 3], ssqs[tl % 3]
            pt, pq = nextbank()
            for k in range(8):
                mm(pt[:, 0:128], pq, hT[:, k, tl * 128:(tl + 1) * 128], wz[:, k, :], k == 0, k == 7, [hT, sl])
            act(sz[:], pt[:, 0:128], AF.Silu, pq, [sz])
            P.op("dve", lambda: nc.vector.tensor_tensor(out=on[:], in0=otok[:, tl, :], in1=otok[:, tl, :], op=ALU.mult), [otok], [on])
            P.op("dve", lambda: nc.vector.tensor_reduce(out=ssq[:, 0:1], in_=on[:], axis=AX.X, op=ALU.add), [on], [ssq])
            act(ssq[:, 1:2], ssq[:, 0:1], AF.Ln, [ssq], [ssq], bias=epsc[:, 0:1], scale=1.0 / 128)
            act(ssq[:, 1:2], ssq[:, 1:2], AF.Exp, [ssq], [ssq], scale=-0.5)
            vstt(on[:], otok[:, tl, :], ssq[:, 1:2], vec(l, "gng", 0, 128), ALU.mult, ALU.mult, [otok, ssq, vecs], [on])
            vtt(on[:], on[:], sz[:], ALU.mult, [on, sz], [on])

        def outB(tl):
            on = ons[tl % 3]
            qa, qk = nextq()
            mm(qa, [qk], on[:], ident, True, True, [on, consts])
            act(ob[:, hh, tl * 128:(tl + 1) * 128], qa, AF.Copy, [qk], [ob])

        for tl in range(NTL + 2):
            if tl < NTL:
                outA(tl)
            if tl >= 2:
                outB(tl - 2)


def emit_merge(P, nc, env, sc, l, s):
    e = _h(env)
    hT, obT, consts = e.hT, e.obT, e.consts
    mm, act, vtt, vts, vstt, vcopy = e.mm, e.act, e.vtt, e.vts, e.vstt, e.vcopy
    nextbank, cst, vec, wslot, wload, hrhs = e.nextbank, e.cst, e.vec, e.wslot, e.wload, e.hrhs
    wA, wBR, wO, mod = e.wA, e.wBR, e.wO, e.mod
    mT = P.sbuf("mT", [128, 8, T], BF16, sc)
    Gs = [P.sbuf("Gs", [128, 512], F32, sc) for _ in range(2)]
    macc = P.sbuf("macc", [128, 512], F32, sc)
    mtmp = P.sbuf("mtmp", [128, 512], F32, sc)
    xb = P.sbuf("xb", [128, 8, 512], F32, sc)
    sq = P.sbuf("sq", [128, 8, 512], BF16, sc)
    rstd = P.sbuf("rstd", [128, 512], F32, sc)
    tmp = P.sbuf("tmp", [128, 512], F32, sc)
    gi = 0
    for j in range(8):
        sl = wslot()
        wg = sl[:, 0:3072].rearrange("p (b k n) -> p b k n", b=3, k=8)
        for b in range(3):
            wload(sl, wg[:, b], wA, wA.t[l, :, :, C_GATE + b * 1024 + j * 128:C_GATE + b * 1024 + (j + 1) * 128])
        sl2 = wslot()
        wb = sl2[:, 0:1536].rearrange("p (k n) -> p k n", k=12)
        wload(sl2, wb, wBR, wBR.t[l, :, :, j * 128:(j + 1) * 128])
        for tb in range(NTB):
            blk = slice(tb * 512, (tb + 1) * 512)
            for b in range(3):
                pg, qg = nextbank()
                for k in range(8):
                    mm(pg[:, :], qg, wg[:, b, k, :], hrhs(k, tb), k == 0, k == 7, [sl, hT])
                Gt = Gs[gi % 2]
                gi += 1
                act(Gt[:], pg[:, :], AF.Sigmoid, qg, [Gt])
                py, qy = nextbank()
                for k in range(4):
                    mm(py[:, :], qy, wb[:, b * 4 + k, :], obT[b][:, k, blk], k == 0, k == 3, [sl2, obT[b]])
                if b == 0:
                    vtt(macc[:], py[:, :], Gt[:], ALU.mult, qy + [Gt], [macc])
                elif b == 1:
                    vtt(mtmp[:], py[:, :], Gt[:], ALU.mult, qy + [Gt], [mtmp])
                    vtt(macc[:], macc[:], mtmp[:], ALU.add, [macc, mtmp], [macc])
                else:
                    vtt(mtmp[:], py[:, :], Gt[:], ALU.mult, qy + [Gt], [mtmp])
                    vtt(mT[:, j, blk], macc[:], mtmp[:], ALU.add, [macc, mtmp], [mT])
    for tb in range(NTB):
        blk = slice(tb * 512, (tb + 1) * 512)
        e.load_xblock(xb, tb)
        for half in range(2):
            sl = wslot()
            wv = sl[:, 0:4096].rearrange("p (k n) -> p k n", k=8)
            wload(sl, wv, wO, wO.t[l, :, :, half * 512:(half + 1) * 512])
            for ii in range(4):
                i = half * 4 + ii
                pt, pq = nextbank()
                for k in range(8):
                    mm(pt[:, :], pq, wv[:, k, ii * 128:(ii + 1) * 128], mT[:, k, blk], k == 0, k == 7, [sl, mT])
                vstt(xb[:, i, :], pt[:, :], mod[:, l, 16 + i, s:s + 1], xb[:, i, :], ALU.mult, ALU.add, pq + [mod, xb], [xb])
        e.store_xblock(xb, tb)
        e.norm_block(xb, tb, l, s, 2, sq, rstd, tmp)


def emit_ffn(P, nc, env, sc, l, s):
    e = _h(env)
    hT, consts = e.hT, e.consts
    mm, act, vtt, vts, vstt, vcopy, vmemset = e.mm, e.act, e.vtt, e.vts, e.vstt, e.vcopy, e.vmemset
    nextbank, cst, vec, wslot, wload, hrhs = e.nextbank, e.cst, e.vec, e.wslot, e.wload, e.hrhs
    wUP, wDN, mod, vecs, xT_d, xT_tk = e.wUP, e.wDN, e.mod, e.vecs, e.xT_d, e.xT_tk
    actT = P.sbuf("actT", [128, NJ, T], BF16, sc)
    upad = [P.sbuf("upad", [128, T + 2], F32, sc) for _ in range(2)]
    cv = [P.sbuf("cv", [128, T], F32, sc) for _ in range(2)]
    xs = [P.sbuf("xs", [128, 512], F32, sc) for _ in range(2)]
    for u in upad:
        vmemset(u[:], 0.0, [u])
    for jp in range(NJ // 2):
        sl = wslot()
        wv = sl[:, 0:4096].rearrange("p (k n) -> p k n", k=8)
        for jj in range(2):
            j = jp * 2 + jj
            wload(sl, wv[:, :, jj * 128:(jj + 1) * 128], wUP, wUP.t[l, :, :, j * 128:(j + 1) * 128])
            wload(sl, wv[:, :, 256 + jj * 128:256 + (jj + 1) * 128], wUP, wUP.t[l, :, :, DFF + j * 128:DFF + (j + 1) * 128])
        for jj in range(2):
            j = jp * 2 + jj
            for lg in range(2):
                ch = lg * NJ + j
                wc = lg * 256 + jj * 128
                for tb in range(NTB):
                    pt, pq = nextbank()
                    for k in range(8):
                        mm(pt[:, :], pq, wv[:, k, wc:wc + 128], hrhs(k, tb), k == 0, k == 7, [sl, hT])
                    act(upad[lg][:, 1 + tb * 512:1 + (tb + 1) * 512], pt[:, :], AF.Copy, pq, [upad[lg]])
                act(cv[lg][:], upad[lg][:, 0:T], AF.Identity, [upad[lg], vecs], [cv[lg]],
                    bias=vec(l, "fcb", ch), scale=vec(l, "fcw", ch * 3 + 0))
                for k in range(1, 3):
                    vstt(cv[lg][:], upad[lg][:, k:k + T], vec(l, "fcw", ch * 3 + k), cv[lg][:], ALU.mult, ALU.add,
                         [upad[lg], cv[lg], vecs], [cv[lg]])
            act(cv[1][:], cv[1][:], AF.Silu, [cv[1]], [cv[1]])
            vtt(actT[:, j, :], cv[1][:], cv[0][:], ALU.mult, [cv[0], cv[1]], [actT])
    xi = 0
    for i in range(8):
        sl = wslot()
        wv = sl[:, 0:NJ * 128].rearrange("p (k n) -> p k n", k=NJ)
        wload(sl, wv, wDN, wDN.t[l, :, :, i * 128:(i + 1) * 128])
        for tb in range(NTB):
            blk = slice(tb * 512, (tb + 1) * 512)
            pt, pq = nextbank()
            for k in range(NJ):
                mm(pt[:, :], pq, wv[:, k, :], actT[:, k, blk], k == 0, k == NJ - 1, [sl, actT])
            x1 = xs[xi % 2]
            xi += 1
            P.dma([x1], x1[:], [xT_tk[i][tb]], xT_d[:, i, blk])
            vstt(x1[:], pt[:, :], mod[:, l, 40 + i, s:s + 1], x1[:], ALU.mult, ALU.add, pq + [mod, x1], [x1])
            P.dma([xT_tk[i][tb]], xT_d[:, i, blk], [x1], x1[:], sem_t=x1)


def _fm(w, kc):
    K, N = w.shape
    return np.ascontiguousarray(w.reshape(kc, 128, N).transpose(1, 0, 2))


def _colv(v, nch):
    return np.ascontiguousarray(np.asarray(v).reshape(nch, 128).T)


def _consts():
    c = np.zeros((128, NC), np.float32)
    i = np.arange(128)
    ident = np.eye(128, dtype=np.float32)
    c[:, COFF["ident"]:COFF["ident"] + 128] = ident
    c[:, COFF["ident_b"]:COFF["ident_b"] + 128] = ident
    c[:, COFF["ones"]:COFF["ones"] + 256] = 1.0
    c[:, COFF["negones"]:COFF["negones"] + 256] = -1.0
    m, ii = np.meshgrid(i, i, indexing="ij")
    c[:, COFF["mt0"]:COFF["mt0"] + 128] = (m <= ii)
    c[:, COFF["mt1"]:COFF["mt1"] + 128] = (m >= ii)
    a, b = np.meshgrid(i, i, indexing="ij")
    v0 = (b < a)
    v1 = (b > a)
    c[:, COFF["negs0"]:COFF["negs0"] + 128] = np.where(v0, 0.0, -BIG)
    c[:, COFF["negs1"]:COFF["negs1"] + 128] = np.where(v1, 0.0, -BIG)
    eye = np.eye(128, dtype=bool)
    c[:, COFF["negst0"]:COFF["negst0"] + 128] = np.where(v0.T | eye, 0.0, -BIG)
    c[:, COFF["negst1"]:COFF["negst1"] + 128] = np.where(v1.T | eye, 0.0, -BIG)
    bdm = lambda n: ((a // n) == (b // n)).astype(np.float32)
    for sfx in ("", "_b"):
        c[:, COFF["bd16" + sfx]:COFF["bd16" + sfx] + 128] = bdm(16)
        c[:, COFF["m32" + sfx]:COFF["m32" + sfx] + 128] = bdm(32) - bdm(16)
        c[:, COFF["m64" + sfx]:COFF["m64" + sfx] + 128] = bdm(64) - bdm(32)
        c[:, COFF["m128" + sfx]:COFF["m128" + sfx] + 128] = 1.0 - bdm(64)
    return c


def _rope_tables():
    inv = 1.0 / (10000.0 ** (np.arange(0, 32, 2, dtype=np.float32) / 32.0))
    ang = np.arange(T, dtype=np.float32)[None, :] * inv[:, None].astype(np.float32)
    cos, sin = np.cos(ang).astype(np.float32), np.sin(ang).astype(np.float32)
    r = np.zeros((2, 96, T), np.float32)
    r[0, 0:64] = 1.0
    r[0, 64:80] = cos
    r[0, 80:96] = cos
    r[1, 64:80] = -sin
    r[1, 80:96] = sin
    return r


def prep_weights(inp):
    L = L_ALL
    f = lambda k: np.asarray(inp[k], np.float32)
    w_in = f("w_in")
    wA = np.zeros((L, 128, 8, NA), np.float32)
    sw = np.concatenate([np.arange(16, 32), np.arange(0, 16)])
    wUQ = np.zeros((L, 128, 3, 1536), np.float32)
    wKN = np.zeros((L, 128, 2, 768), np.float32)
    wV = np.zeros((L, 128, 2, 512), np.float32)
    wRG = np.zeros((L, 128, 16, 128), np.float32)
    vecs = np.zeros((128, L, NV), np.float32)
    for l in range(L):
        ext = np.zeros((D, 192), np.float32)
        kr = w_in[l][:, 640:672]
        ext[:, 64:96] = kr
        ext[:, 96 + 64:96 + 96] = kr[:, sw]
        wA[l] = _fm(np.concatenate([w_in[l], ext], axis=1), 8)
        uq = f("mla_w_uq")[l]
        uqs = uq.reshape(384, 8, 96).copy()
        uqs[:, :, 64:96] = uqs[:, :, 64:96][:, :, sw]
        wUQ[l] = _fm(np.concatenate([uq, uqs.reshape(384, 768)], axis=1), 3)
        ukv = f("mla_w_ukv")[l].reshape(256, 8, 128)
        kn = np.zeros((256, 8, 96), np.float32)
        kn[:, :, 0:64] = ukv[:, :, 0:64]
        wKN[l] = _fm(kn.reshape(256, 768), 2)
        wV[l] = _fm(np.ascontiguousarray(ukv[:, :, 64:128]).reshape(256, 512), 2)
        for d in range(2):
            for ai, nm in enumerate(("rg_w_a", "rg_w_i")):
                w = f(nm)[l, d]
                for c in range(4):
                    m = np.zeros((128, 128), np.float32)
                    m[0:64, 0:64] = w[2 * c]
                    m[64:128, 64:128] = w[2 * c + 1]
                    wRG[l, :, (d * 2 + ai) * 4 + c, :] = m

        def put(name, arr):
            o, w = VOFF[name]
            vecs[:, l, o:o + w] = arr
        put("ln1", _colv(f("ln1_g")[l], 8))
        put("ln2", _colv(f("ln2_g")[l], 8))
        put("bmod", _colv(f("b_mod")[l], 48))
        put("qg", _colv(f("mla_q_norm_g")[l], 3))
        put("kvg", _colv(f("mla_kv_norm_g")[l], 2))
        put("rgcw", np.ascontiguousarray(f("rg_conv_w")[l].reshape(4, 4, 128).transpose(2, 1, 0)).reshape(128, 16))
        put("rgcb", _colv(f("rg_conv_b")[l], 4))
        put("rgba", np.ascontiguousarray(f("rg_b_a")[l].reshape(2, 4, 128).transpose(2, 0, 1)).reshape(128, 8))
        put("rgbi", np.ascontiguousarray(f("rg_b_i")[l].reshape(2, 4, 128).transpose(2, 0, 1)).reshape(128, 8))
        put("rglam", np.ascontiguousarray(f("rg_lam")[l].reshape(2, 4, 128).transpose(2, 0, 1)).reshape(128, 8))
        put("gcw", np.ascontiguousarray(f("gdn_conv_w")[l].reshape(4, 12, 128).transpose(2, 1, 0)).reshape(128, 48))
        put("gng", np.broadcast_to(f("gdn_norm_g")[l][None, :], (128, 128)))
        put("galog", np.broadcast_to(np.tile(f("gdn_a_log")[l].reshape(8), 16)[None, :], (128, 128)))
        put("gdtb", np.broadcast_to(np.tile(f("gdn_dt_bias")[l].reshape(8), 16)[None, :], (128, 128)))
        put("fcw", np.ascontiguousarray(f("ffn_conv_w")[l].reshape(3, 44, 128).transpose(2, 1, 0)).reshape(128, 132))
        put("fcb", _colv(f("ffn_conv_b")[l], 44))
        put("fng", _colv(f("final_norm_g"), 8))
    wd = {
        "wA": wA, "wUQ": wUQ, "wKN": wKN, "wV": wV, "wRG": wRG,
        "wBR": np.stack([_fm(f("w_branch")[l].reshape(1536, 1024), 12) for l in range(L)]),
        "wO": np.stack([_fm(f("w_out")[l], 8) for l in range(L)]),
        "wUP": np.stack([_fm(f("ffn_w_up")[l], 8) for l in range(L)]),
        "wDN": np.stack([_fm(f("ffn_w_down")[l], NJ) for l in range(L)]),
        "wMOD": np.stack([_fm(f("w_mod")[l], 8) for l in range(L)]),
        "vecs": vecs, "consts": _consts(), "ropet": _rope_tables(),
    }
    return wd


def core_inputs(wd, xs, cs):
    m = dict(wd)
    m["xin"] = np.ascontiguousarray(xs, dtype=np.float32)
    nseq = xs.shape[0]
    m["cT"] = np.ascontiguousarray(np.asarray(cs, np.float32).reshape(nseq, 8, 128).transpose(2, 1, 0))
    return m


_CACHE = {}


def kernel(**inputs):
    wd = prep_weights(inputs)
    xp = np.asarray(inputs["x_prompt"], np.float32)
    xsm = np.asarray(inputs["x_sample"], np.float32)
    cp = np.asarray(inputs["c_prompt"], np.float32)
    csm = np.asarray(inputs["c_sample"], np.float32)
    xall = np.concatenate([xp, xsm], axis=0)
    call = np.concatenate([cp, csm], axis=0)
    nseq = xall.shape[0] // NCORES
    if "nc" not in _CACHE:
        _CACHE["nc"] = build(nseq)[0]
    nc = _CACHE["nc"]
    in_maps = [core_inputs(wd, xall[i * nseq:(i + 1) * nseq], call[i * nseq:(i + 1) * nseq]) for i in range(NCORES)]
    res = run_bass_kernel_spmd(nc, in_maps, core_ids=list(range(NCORES)))
    y = np.concatenate([np.asarray(r["yout"], np.float32) for r in res.results], axis=0)
    nb = xp.shape[0]
    return (np.ascontiguousarray(y[:nb]), np.ascontiguousarray(y[nb:]))
```

```python
import contextlib
import math
import numpy as np
import concourse.bass as bass
import concourse.mybir as mybir
from concourse.bass_utils import run_bass_kernel_spmd

F32 = mybir.dt.float32
BF16 = mybir.dt.bfloat16
AF = mybir.ActivationFunctionType
ALU = mybir.AluOpType
AX = mybir.AxisListType

D = 1024
T = 2048
NTB = 4
NTL = 16
L_ALL = 2
NCORES = 8
EPS = 1e-6
N_IN = 6832
KR0 = N_IN
NA = N_IN + 192
C_RGX, C_RGG, C_GQKV, C_GZ, C_AB, C_GATE = 672, 1184, 1696, 3232, 3744, 3760
DFF = 2816
NJ = 22
BIG = 30000.0
SEM_LIMIT = 30000

VOFF = {}
_o = 0
for _n, _w in [("ln1", 8), ("ln2", 8), ("bmod", 48), ("qg", 3), ("kvg", 2), ("rgcw", 16), ("rgcb", 4),
               ("rgba", 8), ("rgbi", 8), ("rglam", 8), ("gcw", 48), ("gng", 128), ("galog", 128), ("gdtb", 128),
               ("fcw", 132), ("fcb", 44), ("fng", 8)]:
    VOFF[_n] = (_o, _w)
    _o += _w
NV = _o
COFF = {}
_o = 0
for _n in ["ident", "ident_b", "ones", "ones_b", "negones", "negones_b", "mt0", "mt1", "negs0", "negs1", "negst0", "negst1",
           "bd16", "bd16_b", "m32", "m32_b", "m64", "m64_b", "m128", "m128_b"]:
    COFF[_n] = _o
    _o += 128
NC = _o


class Tk:
    __slots__ = ("name", "w", "r", "t", "dk")

    def __init__(self, name, t=None):
        self.name = name
        self.w = None
        self.r = {}
        self.t = t
        self.dk = name

    def __getitem__(self, idx):
        return self.t[idx]


class Prog:
    def __init__(self, nc):
        self.nc = nc
        self.es = contextlib.ExitStack()
        self.eng = {"pe": nc.tensor, "act": nc.scalar, "dve": nc.vector, "pool": nc.gpsimd, "sp": nc.sync}
        self.sems = {}
        self.cnt = {}
        self.cur = {}
        self.epoch = {e: 0 for e in self.eng}
        self.seen = {e: {} for e in self.eng}
        self.ninst = 0
        self.uid = 0
        self.namectr = {}
        for e in self.eng:
            self._new_epoch(e)

    def _mk_sem(self, key):
        s = self.es.enter_context(self.nc.semaphore("s_%s" % (key,)))
        self.sems[key] = s
        self.cnt[key] = 0
        return s

    def _new_epoch(self, e):
        key = "%s%d" % (e, self.epoch[e])
        self.epoch[e] += 1
        self._mk_sem(key)
        self.cur[e] = key

    def sbuf(self, name, shape, dt, es=None):
        self.uid += 1
        t = (es or self.es).enter_context(self.nc.sbuf_tensor("%s_%d" % (name, self.uid), list(shape), dt))
        tk = Tk(name + str(self.uid), t)
        c = self.namectr.get(name, 0)
        self.namectr[name] = c + 1
        tk.dk = "%s_%d" % (name, c % 4)
        return tk

    def dram(self, name, shape, dt, kind="Internal"):
        t = self.nc.dram_tensor(name, list(shape), dt, kind=kind).ap()
        return Tk(name, t)

    def _deps(self, reads, writes):
        deps = {}
        for t in reads:
            if t.w is not None:
                k, v = t.w
                if deps.get(k, 0) < v:
                    deps[k] = v
        for t in writes:
            if t.w is not None:
                k, v = t.w
                if deps.get(k, 0) < v:
                    deps[k] = v
            for k, v in t.r.items():
                if deps.get(k, 0) < v:
                    deps[k] = v
        return deps

    def _wait(self, e, deps):
        eng = self.eng[e]
        seen = self.seen[e]
        for k, v in deps.items():
            isdma = k.startswith("dma")
            if (not isdma) and k.startswith(e) and e in ("pe", "sp"):
                continue
            if isdma:
                v = self.cnt[k]
            if seen.get(k, 0) < v:
                eng.wait_ge(self.sems[k], v)
                seen[k] = v
                self.ninst += 1

    def op(self, e, fn, reads=(), writes=()):
        self._wait(e, self._deps(reads, writes))
        if self.cnt[self.cur[e]] >= SEM_LIMIT:
            self._new_epoch(e)
        key = self.cur[e]
        inst = fn()
        self.cnt[key] += 1
        v = self.cnt[key]
        inst.then_inc(self.sems[key], 1)
        self.ninst += 1
        for t in reads:
            t.r[key] = v
        for t in writes:
            t.w = (key, v)
            t.r = {}
        return inst

    def dma(self, out_ts, out_ap, in_ts, in_ap, q="sp", sem_t=None):
        st = sem_t if sem_t is not None else out_ts[0]
        key = "dma_" + st.dk
        if key not in self.sems:
            self._mk_sem(key)
        self._wait(q, self._deps(in_ts, out_ts))
        inst = self.eng[q].dma_start(out=out_ap, in_=in_ap)
        self.cnt[key] += 16
        v = self.cnt[key]
        inst.then_inc(self.sems[key], 16)
        self.ninst += 1
        for t in in_ts:
            t.r[key] = v
        for t in out_ts:
            t.w = (key, v)
            t.r = {}
        return inst

    def barrier(self):
        allk = {k: v for k, v in self.cnt.items() if v > 0}
        for e in self.eng:
            self._wait(e, allk)

    def finish(self, outs, q="sp"):
        deps = {}
        for t in outs:
            if t.w is not None:
                k, v = t.w
                deps[k] = max(deps.get(k, 0), v)
        self._wait(q, deps)


OUTSTAGE = 3
GDNSTAGE = 9
GDN_G = 3
GDN_POOL = False
GDN_BF16INV = True
GSUB = 9
RGDBG = 0
PHASES = set(['N1', 'MLA', 'RG', 'GDN', 'MERGE', 'FFN'])


def build(nseq, nlayer=L_ALL, dbg=False):
    nc = bass.Bass("TRN2", target_bir_lowering=False)
    P = Prog(nc)
    L = L_ALL
    xin = P.dram("xin", [nseq, T, D], F32, "ExternalInput")
    cT = P.dram("cT", [128, 8, nseq], F32, "ExternalInput")
    wA = P.dram("wA", [L, 128, 8, NA], F32, "ExternalInput")
    wUQ = P.dram("wUQ", [L, 128, 3, 1536], F32, "ExternalInput")
    wKN = P.dram("wKN", [L, 128, 2, 768], F32, "ExternalInput")
    wV = P.dram("wV", [L, 128, 2, 512], F32, "ExternalInput")
    wRG = P.dram("wRG", [L, 128, 16, 128], F32, "ExternalInput")
    wBR = P.dram("wBR", [L, 128, 12, 1024], F32, "ExternalInput")
    wO = P.dram("wO", [L, 128, 8, 1024], F32, "ExternalInput")
    wUP = P.dram("wUP", [L, 128, 8, 2 * DFF], F32, "ExternalInput")
    wDN = P.dram("wDN", [L, 128, NJ, 1024], F32, "ExternalInput")
    wMOD = P.dram("wMOD", [L, 128, 8, 6144], F32, "ExternalInput")
    vecs_d = P.dram("vecs", [128, L, NV], F32, "ExternalInput")
    consts_d = P.dram("consts", [128, NC], F32, "ExternalInput")
    ropet = P.dram("ropet", [2, 96, T], F32, "ExternalInput")
    yout = P.dram("yout", [nseq, T, D], F32, "ExternalOutput")
    xT_d = nc.dram_tensor("xT_scr", [128, 8, T], F32, kind="Internal").ap()
    xT_tk = [[Tk("xT_%d_%d" % (c, tb)) for tb in range(NTB)] for c in range(8)]
    dbg_d = None
    if dbg:
        dbg_d = P.dram("dbg", [3, 128, 4, T], F32, "ExternalOutput")

    V = nc.vector
    A = nc.scalar
    PE = nc.tensor
    G = nc.gpsimd

    with P.es:
        base = P.es
        consts = P.sbuf("consts", [128, NC], F32)
        vecs = P.sbuf("vecs", [128, L, NV], F32)
        identb = P.sbuf("identb", [128, 256], BF16)
        negidentb = P.sbuf("negidentb", [128, 128], BF16)
        onesb = P.sbuf("onesb", [128, 128], BF16)
        hT = P.sbuf("hT", [128, 8, T], BF16)
        mod = P.sbuf("mod", [128, L, 48, nseq], F32)
        gs1 = P.sbuf("gs1", [128, L, 8, nseq], F32)
        gs2 = P.sbuf("gs2", [128, L, 8, nseq], F32)
        cneg = P.sbuf("cneg", [128, L, 8], F32)
        cneg2 = P.sbuf("cneg2", [128, L, 8], F32)
        negA = P.sbuf("negA", [128, L, 128], F32)
        NSLOT = 3
        wslots = [P.sbuf("wslot%d" % i, [128, 4096], BF16) for i in range(NSLOT)]
        wctr = [0]
        pbank = []
        for i in range(8):
            t = P.es.enter_context(nc.psum_tensor("psb%d" % i, [128, 512], F32))
            pbank.append((t, [Tk("psq%d_%d" % (i, q)) for q in range(4)]))
        pctr = [0]
        qctr = [0]

        def nextbank(pool=None):
            if pool is None:
                b = pbank[pctr[0] % 8]
            else:
                b = pbank[pool[pctr[0] % len(pool)]]
            pctr[0] += 1
            return b

        def nextq():
            t, qs = pbank[pctr[0] % 8]
            pctr[0] += 1
            return t[:, 0:128], qs[0]

        def nextq2():
            t, qs = pbank[pctr[0] % 8]
            pctr[0] += 1
            return t[:, 0:256], qs[0]

        def cst(name, w=128):
            o = COFF[name]
            return consts[:, o:o + w]

        def vec(l, name, j=0, w=1):
            o, _ = VOFF[name]
            return vecs[:, l, o + j:o + j + w]

        def wslot():
            s = wslots[wctr[0] % NSLOT]
            wctr[0] += 1
            return s

        def wload(slot, dst_ap, src_tk, src_ap):
            P.dma([slot], dst_ap, [src_tk], src_ap, q="pool", sem_t=slot)

        def mm(ps_ap, ps_tks, lhsT, rhs, start, stop, reads):
            P.op("pe", lambda: PE.matmul(ps_ap, lhsT=lhsT, rhs=rhs, start=start, stop=stop), reads, ps_tks)

        def act(out_ap, in_ap, func, reads, writes, bias=None, scale=None):
            kw = {}
            if bias is not None:
                kw["bias"] = bias
            if scale is not None:
                kw["scale"] = scale
            P.op("act", lambda: A.activation(out=out_ap, in_=in_ap, func=func, **kw), reads, writes)

        def vtt(out_ap, in0, in1, op, reads, writes, e="dve"):
            eng = V if e == "dve" else G
            P.op(e, lambda: eng.tensor_tensor(out=out_ap, in0=in0, in1=in1, op=op), reads, writes)

        def vts(out_ap, in0, s1, s2, op0, op1, reads, writes, e="dve"):
            eng = V if e == "dve" else G
            if op1 is None:
                P.op(e, lambda: eng.tensor_scalar(out=out_ap, in0=in0, scalar1=s1, scalar2=None, op0=op0), reads, writes)
            else:
                P.op(e, lambda: eng.tensor_scalar(out=out_ap, in0=in0, scalar1=s1, scalar2=s2, op0=op0, op1=op1), reads, writes)

        def vstt(out_ap, in0, scalar, in1, op0, op1, reads, writes, e="dve"):
            eng = V if e == "dve" else G
            P.op(e, lambda: eng.scalar_tensor_tensor(out=out_ap, in0=in0, scalar=scalar, in1=in1, op0=op0, op1=op1), reads, writes)

        def vcopy(out_ap, in_ap, reads, writes, e="dve"):
            eng = V if e == "dve" else G
            P.op(e, lambda: eng.tensor_copy(out=out_ap, in_=in_ap), reads, writes)

        def vmemset(ap, val, writes, e="dve"):
            eng = V if e == "dve" else G
            P.op(e, lambda: eng.memset(ap, val), [], writes)

        def rsqrt_from(out_ap, in_ap, scale, reads, writes):
            act(out_ap, in_ap, AF.Ln, reads, writes, bias=epsc[:, 0:1], scale=scale)
            act(out_ap, out_ap, AF.Exp, writes, writes, scale=-0.5)

        def softplus_acc(out_ap, x_ap, t1, t2, t3, rd, wr):
            vts(t1, x_ap, -1.0, None, ALU.mult, None, rd, wr)
            vtt(t1, t1, x_ap, ALU.max, rd + wr, wr)
            act(t1, t1, AF.Exp, wr, wr, scale=-1.0)
            vts(t2, t1, 2.0, None, ALU.add, None, wr, wr)
            P.op("dve", lambda: V.reciprocal(out=t2, in_=t2), wr, wr)
            vtt(t1, t1, t2, ALU.mult, wr, wr)
            vtt(t2, t1, t1, ALU.mult, wr, wr)
            vts(t3, t2, 1.0 / 11, 1.0 / 9, ALU.mult, ALU.add, wr, wr)
            for cf in (1.0 / 7, 1.0 / 5, 1.0 / 3, 1.0):
                vtt(t3, t3, t2, ALU.mult, wr, wr)
                vts(t3, t3, cf, None, ALU.add, None, wr, wr)
            vtt(t3, t3, t1, ALU.mult, wr, wr)
            vts(t1, x_ap, 0.0, None, ALU.max, None, rd + wr, wr)
            vstt(out_ap, t3, 2.0, t1, ALU.mult, ALU.add, wr, wr)

        epsc = P.sbuf("epsc", [128, 1], F32)
        vmemset(epsc[:], EPS, [epsc])
        P.dma([consts], consts[:], [consts_d], consts_d.t)
        P.dma([vecs], vecs[:], [vecs_d], vecs_d.t)
        vcopy(identb[:], cst("ident", 256), [consts], [identb])
        vts(negidentb[:], cst("ident"), -1.0, None, ALU.mult, None, [consts], [negidentb])
        vcopy(onesb[:], cst("ones"), [consts], [onesb])
        with contextlib.ExitStack() as sc:
            craw = P.sbuf("craw", [128, 8, nseq], F32, sc)
            csil = P.sbuf("csil", [128, 8, nseq], BF16, sc)
            tmpm = P.sbuf("tmpm", [128, 8, nseq], F32, sc)
            spx = P.sbuf("spx", [128, 8], F32, sc)
            spt = P.sbuf("spt", [128, 3, 8], F32, sc)
            P.dma([craw], craw[:], [cT], cT.t)
            act(csil[:], craw[:], AF.Silu, [craw], [csil])
            for l in range(nlayer):
                for pn in range(12):
                    sl = wslot()
                    wv = sl[:, 0:4096].rearrange("p (k n) -> p k n", k=8)
                    wload(sl, wv, wMOD, wMOD.t[l, :, :, pn * 512:(pn + 1) * 512])
                    for jj in range(4):
                        j = pn * 4 + jj
                        pt, pq = nextbank()
                        for k in range(8):
                            mm(pt[:, 0:nseq], pq, wv[:, k, jj * 128:(jj + 1) * 128], csil[:, k, :], k == 0, k == 7, [sl, csil])
                        act(mod[:, l, j, :], pt[:, 0:nseq], AF.Identity, pq, [mod], bias=vec(l, "bmod", j))
                for (gs, lnn, j0) in ((gs1, "ln1", 8), (gs2, "ln2", 32)):
                    vts(tmpm[:], mod[:, l, j0:j0 + 8, :], 1.0, None, ALU.add, None, [mod], [tmpm])
                    for s in range(nseq):
                        vtt(gs[:, l, :, s], tmpm[:, :, s], vec(l, lnn, 0, 8), ALU.mult, [tmpm, vecs], [gs])
                vts(spx[:], vec(l, "rglam", 0, 8), -1.0, None, ALU.mult, None, [vecs], [spx])
                softplus_acc(cneg[:, l, :], spx[:], spt[:, 0, :], spt[:, 1, :], spt[:, 2, :], [spx], [spt, cneg])
                vts(cneg2[:, l, :], cneg[:, l, :], -16.0, None, ALU.mult, None, [cneg], [cneg2])
                vts(cneg[:, l, :], cneg[:, l, :], -8.0, None, ALU.mult, None, [cneg], [cneg])
                act(negA[:, l, :], vec(l, "galog", 0, 128), AF.Exp, [vecs], [negA])
                vts(negA[:, l, :], negA[:, l, :], -1.0, None, ALU.mult, None, [negA], [negA])
            P.barrier()

        def proj_fm(wv, kc, M, rhs_fn, rhs_tks, slot, consume):
            for tb in range(NTB):
                pt, pq = nextbank()
                for k in range(kc):
                    mm(pt[0:M, :], pq, wv[:, k, 0:M], rhs_fn(k, tb), k == 0, k == kc - 1, [slot] + rhs_tks)
                consume(tb, pt, pq)

        def hrhs(k, tb):
            return hT[:, k, tb * 512:(tb + 1) * 512]

        def load_xblock(xb, tb):
            for c in range(8):
                pass
            P.dma([xb], xb[:], [xT_tk[c][tb] for c in range(8)], xT_d[:, :, tb * 512:(tb + 1) * 512])

        def store_xblock(xb, tb):
            P.dma([xT_tk[c][tb] for c in range(8)], xT_d[:, :, tb * 512:(tb + 1) * 512], [xb], xb[:], sem_t=xb)

        def norm_block(xb, tb, l, s, which, sq, rstd, tmp):
            gs = gs1 if which == 1 else gs2
            shj = 0 if which == 1 else 24
            vtt(sq[:], xb[:], xb[:], ALU.mult, [xb], [sq])
            pt, pq = nextbank()
            for c in range(8):
                mm(pt[:, :], pq, onesb[:], sq[:, c, :], c == 0, c == 7, [onesb, sq])
            rsqrt_from(rstd[:], pt[:, :], 1.0 / D, pq, [rstd])
            for c in range(8):
                vtt(tmp[:], xb[:, c, :], rstd[:], ALU.mult, [xb, rstd], [tmp])
                act(hT[:, c, tb * 512:(tb + 1) * 512], tmp[:], AF.Identity, [tmp, gs, mod], [hT],
                    bias=mod[:, l, shj + c, s:s + 1], scale=gs[:, l, c, s:s + 1])

        for s in range(nseq if 'NOSEQ' not in PHASES else 0):
            with contextlib.ExitStack() as sc:
                xtok = [P.sbuf("xtok", [128, D], F32, sc) for _ in range(2)]
                xblk = [P.sbuf("xblk", [128, 8, 128], F32, sc) for _ in range(2)]
                for tl in range(NTL if 'NOL' not in PHASES else 0):
                    xt = xtok[tl % 2]
                    xk = xblk[tl % 2]
                    P.dma([xt], xt[:], [xin], xin.t[s, tl * 128:(tl + 1) * 128, :])
                    for half in range(2):
                        pt, pq = nextbank()
                        for q in range(4):
                            c = half * 4 + q
                            P.op("pe", lambda: PE.transpose(out=pt[:, q * 128:(q + 1) * 128], in_=xt[:, c * 128:(c + 1) * 128],
                                                            identity=cst("ident")), [xt, consts], pq)
                        act(xk[:, half * 4:half * 4 + 4, :], pt[:, :].rearrange("p (q n) -> p q n", q=4), AF.Copy, pq, [xk])
                    tb = tl // 4
                    P.dma([xT_tk[c][tb] for c in range(8)], xT_d[:, :, tl * 128:(tl + 1) * 128], [xk], xk[:], sem_t=xk)
                P.barrier()

            for l in range(nlayer):
                with contextlib.ExitStack() as sc:
                    xbs = [P.sbuf("xb", [128, 8, 512], F32, sc) for _ in range(2)]
                    sq = P.sbuf("sq", [128, 8, 512], BF16, sc)
                    rstd = P.sbuf("rstd", [128, 512], F32, sc)
                    tmp = P.sbuf("tmp", [128, 512], F32, sc)
                    for tb in range(NTB if 'N1' in PHASES else 0):
                        load_xblock(xbs[tb % 2], tb)
                        norm_block(xbs[tb % 2], tb, l, s, 1, sq, rstd, tmp)
                    P.barrier()

                mixsc = contextlib.ExitStack()
                obT = []
                obT.append(P.sbuf("obT_mla", [128, 4, T], BF16, mixsc))
                with contextlib.ExitStack() as sc:
                    if 'MLA' in PHASES:
                        emit_mla(P, nc, locals(), sc, l, s)
                    P.barrier()
                obT.append(P.sbuf("obT_rg", [128, 4, T], BF16, mixsc))
                with contextlib.ExitStack() as sc:
                    if 'RG' in PHASES:
                        emit_rg(P, nc, locals(), sc, l, s)
                    P.barrier()
                obT.append(P.sbuf("obT_gdn", [128, 4, T], BF16, mixsc))
                with contextlib.ExitStack() as sc:
                    if 'GDN' in PHASES:
                        emit_gdn(P, nc, locals(), sc, l, s)
                    P.barrier()
                if dbg and l == 0 and s == 0:
                    with contextlib.ExitStack() as sc:
                        dtmp = P.sbuf("dtmp", [128, 4, T], F32, sc)
                        for b in range(3):
                            vcopy(dtmp[:], obT[b][:], [obT[b]], [dtmp])
                            P.dma([dbg_d], dbg_d.t[b], [dtmp], dtmp[:], sem_t=dtmp)
                        P.barrier()
                with contextlib.ExitStack() as sc:
                    if 'MERGE' in PHASES:
                        emit_merge(P, nc, locals(), sc, l, s)
                    P.barrier()
                mixsc.close()
                with contextlib.ExitStack() as sc:
                    if 'FFN' in PHASES:
                        emit_ffn(P, nc, locals(), sc, l, s)
                    P.barrier()

            with contextlib.ExitStack() as sc:
                xbs = [P.sbuf("xb", [128, 8, 512], F32, sc) for _ in range(2)]
                sqs_o = [P.sbuf("sq", [128, 8, 512], F32, sc) for _ in range(2)]
                rstd = P.sbuf("rstd", [128, 512], F32, sc)
                ytok = [P.sbuf("ytok", [128, D], F32, sc) for _ in range(2)]
                yc = 0
                for tb in range(NTB if 'NOOUT' not in PHASES else 0):
                    xb, sq = xbs[tb % 2], sqs_o[tb % 2]
                    load_xblock(xb, tb)
                    vtt(sq[:], xb[:], xb[:], ALU.mult, [xb], [sq])
                    if OUTSTAGE >= 1:
                        pt, pq = nextbank()
                        for c in range(8):
                            mm(pt[:, :], pq, cst("ones"), sq[:, c, :], c == 0, c == 7, [consts, sq])
                        if OUTSTAGE >= 2:
                            rsqrt_from(rstd[:], pt[:, :], 1.0 / D, pq, [rstd])
                        else:
                            act(rstd[:], pt[:, :], AF.Copy, pq, [rstd])
                        if OUTSTAGE >= 3:
                            for c in range(8):
                                vstt(sq[:, c, :], xb[:, c, :], vec(0, "fng", c), rstd[:], ALU.mult, ALU.mult, [xb, rstd, vecs], [sq])
                    for t4 in range(4):
                        yt = ytok[yc % 2]
                        yc += 1
                        for half in range(2):
                            pt, pq = nextbank()
                            for q in range(4):
                                c = half * 4 + q
                                P.op("pe", lambda: PE.transpose(out=pt[:, q * 128:(q + 1) * 128], in_=sq[:, c, t4 * 128:(t4 + 1) * 128],
                                                                identity=cst("ident")), [sq, consts], pq)
                            act(yt[:, half * 512:(half + 1) * 512], pt[:, :], AF.Copy, pq, [yt])
                        tl = tb * 4 + t4
                        P.dma([yout], yout.t[s, tl * 128:(tl + 1) * 128, :], [yt], yt[:], sem_t=yt)
                P.barrier()
        outs = [yout] + ([dbg_d] if dbg else [])
        P.finish(outs)
        P.barrier()
    return nc, P


def _h(env):
    class E:
        pass
    e = E()
    e.__dict__.update(env)
    return e


def emit_mla(P, nc, env, sc, l, s):
    e = _h(env)
    hT, obT, consts, identb = e.hT, e.obT, e.consts, e.identb
    mm, act, vtt, vts, vstt, vcopy, vmemset = e.mm, e.act, e.vtt, e.vts, e.vstt, e.vcopy, e.vmemset
    nextbank, cst, vec, wslot, wload, rsqrt_from, hrhs = e.nextbank, e.cst, e.vec, e.wslot, e.wload, e.rsqrt_from, e.hrhs
    wA, wUQ, wKN, wV, ropet, vecs = e.wA, e.wUQ, e.wKN, e.wV, e.ropet, e.vecs
    ob = obT[0]
    cosT = P.sbuf("cosT", [96, T], BF16, sc)
    sinT = P.sbuf("sinT", [96, T], BF16, sc)
    cq = P.sbuf("cq", [128, 3, T], BF16, sc)
    ckv = P.sbuf("ckv", [128, 2, T], BF16, sc)
    krT = P.sbuf("krT", [96, T], BF16, sc)
    Vall = P.sbuf("Vall", [128, NTL, 8, 128], BF16, sc)
    qhs = [P.sbuf("qh", [96, T], BF16, sc) for _ in range(2)]
    khs = [P.sbuf("kh", [96, T], BF16, sc) for _ in range(2)]
    PTs = [P.sbuf("PT", [128, 512], BF16, sc) for _ in range(4)]
    sqs = [P.sbuf("sqm", [128, 512], BF16, sc) for _ in range(3)]
    onesb = e.onesb
    rstd = P.sbuf("rstdm", [128, 512], F32, sc)
    t1 = P.sbuf("t1m", [128, 512], F32, sc)
    t2 = P.sbuf("t2m", [128, 512], F32, sc)
    dsh = P.sbuf("dsh", [128, 512], F32, sc)
    P.dma([cosT], cosT[:], [ropet], ropet.t[0], q="pool")
    P.dma([sinT], sinT[:], [ropet], ropet.t[1], q="pool")
    vmemset(Vall[:], 1.0, [Vall])

    for (dst, col0, nch, gname) in ((cq, 0, 3, "qg"), (ckv, 384, 2, "kvg")):
        sl = wslot()
        wv = sl[:, 0:8 * nch * 128].rearrange("p (k n) -> p k n", k=8)
        wload(sl, wv, wA, wA.t[l, :, :, col0:col0 + nch * 128])
        for tb in range(NTB):
            pcs = []
            for c in range(nch):
                pt, pq = nextbank()
                for k in range(8):
                    mm(pt[:, :], pq, wv[:, k, c * 128:(c + 1) * 128], hrhs(k, tb), k == 0, k == 7, [sl, hT])
                act(sqs[c][:], pt[:, :], AF.Square, pq, [sqs[c]])
                pcs.append((pt, pq))
            pt2, pq2 = nextbank()
            for c in range(nch):
                mm(pt2[:, :], pq2, onesb[:], sqs[c][:], c == 0, c == nch - 1, [onesb, sqs[c]])
            rsqrt_from(rstd[:], pt2[:, :], 1.0 / (nch * 128), pq2, [rstd])
            for c in range(nch):
                pt, pq = pcs[c]
                vtt(t1[:], pt[:, :], rstd[:], ALU.mult, pq + [rstd], [t1])
                act(dst[:, c, tb * 512:(tb + 1) * 512], t1[:], AF.Identity, [t1, vecs], [dst], scale=vec(l, gname, c))
    sl = wslot()
    wv = sl[:, 0:8 * 192].rearrange("p (k n) -> p k n", k=8)
    wload(sl, wv, wA, wA.t[l, :, :, KR0:KR0 + 192])
    for tb in range(NTB):
        blk = slice(tb * 512, (tb + 1) * 512)
        pa, qa = nextbank()
        pb, qb_ = nextbank()
        for k in range(8):
            mm(pa[0:96, :], qa, wv[:, k, 0:96], hrhs(k, tb), k == 0, k == 7, [sl, hT])
        for k in range(8):
            mm(pb[0:96, :], qb_, wv[:, k, 96:192], hrhs(k, tb), k == 0, k == 7, [sl, hT])
        vtt(t1[0:96, :], pa[0:96, :], cosT[:, blk], ALU.mult, qa + [cosT], [t1])
        vtt(t2[0:96, :], pb[0:96, :], sinT[:, blk], ALU.mult, qb_ + [sinT], [t2])
        vtt(krT[:, blk], t1[0:96, :], t2[0:96, :], ALU.add, [t1, t2], [krT])
    sl = wslot()
    wv = sl[:, 0:1024].rearrange("p (k n) -> p k n", k=2)
    wload(sl, wv, wV, wV.t[l])
    for tl in range(NTL):
        pt, pq = nextbank()
        for c in range(2):
            mm(pt[:, :], pq, ckv[:, c, tl * 128:(tl + 1) * 128], wv[:, c, :], c == 0, c == 1, [ckv, sl])
        pv = pt[:, :].rearrange("p (h d) -> p h d", h=8)
        for par in range(2):
            act(Vall[:, tl, par::2, par * 64:par * 64 + 64], pv[:, par::2, :], AF.Copy, pq, [Vall])
    scale = 96.0 ** -0.5
    for h in range(8):
        qh, kh = qhs[h % 2], khs[h % 2]
        sl = wslot()
        wq = sl[:, 0:3 * 192].rearrange("p (k n) -> p k n", k=3)
        wload(sl, wq[:, :, 0:96], wUQ, wUQ.t[l, :, :, h * 96:(h + 1) * 96])
        wload(sl, wq[:, :, 96:192], wUQ, wUQ.t[l, :, :, 768 + h * 96:768 + (h + 1) * 96])
        wk = sl[:, 1024:1024 + 2 * 96].rearrange("p (k n) -> p k n", k=2)
        wload(sl, wk, wKN, wKN.t[l, :, :, h * 96:(h + 1) * 96])
        for tb in range(NTB):
            blk = slice(tb * 512, (tb + 1) * 512)
            pa, qa = nextbank()
            pb, qb_ = nextbank()
            for c in range(3):
                mm(pa[0:96, :], qa, wq[:, c, 0:96], cq[:, c, blk], c == 0, c == 2, [sl, cq])
            for c in range(3):
                mm(pb[0:96, :], qb_, wq[:, c, 96:192], cq[:, c, blk], c == 0, c == 2, [sl, cq])
            vtt(t1[0:96, :], pa[0:96, :], cosT[:, blk], ALU.mult, qa + [cosT], [t1])
            vtt(t2[0:96, :], pb[0:96, :], sinT[:, blk], ALU.mult, qb_ + [sinT], [t2])
            vtt(qh[:, blk], t1[0:96, :], t2[0:96, :], ALU.add, [t1, t2], [qh])
            pk, qk = nextbank()
            for c in range(2):
                mm(pk[0:96, :], qk, wk[:, c, 0:96], ckv[:, c, blk], c == 0, False, [sl, ckv])
            mm(pk[0:96, :], qk, identb[0:96, 0:96], krT[:, blk], False, True, [identb, krT])
            act(kh[:, blk], pk[0:96, :], AF.Copy, qk, [kh])
        par = h % 2
        num = slice(par * 64, par * 64 + 64)
        den = slice((1 - par) * 64, (1 - par) * 64 + 64)
        pti = 0
        for qb in range(NTB):
            qblk = slice(qb * 512, (qb + 1) * 512)
            po, qo = nextbank([0, 1])
            LOOK = 2
            pend = []
            for kt in range(NTL + LOOK):
                if kt < NTL:
                    ps_, qs_ = nextbank([2, 3, 4, 5, 6, 7])
                    mm(ps_[:, :], qs_, kh[:, kt * 128:(kt + 1) * 128], qh[:, qblk], True, True, [kh, qh])
                    PT = PTs[pti % 4]
                    pti += 1
                    act(PT[:], ps_[:, :], AF.Exp, qs_, [PT], scale=scale)
                    pend.append(PT)
                if kt >= LOOK:
                    k2 = kt - LOOK
                    PT2 = pend[k2]
                    mm(po[:, :], qo, Vall[:, k2, h, :], PT2[:], k2 == 0, k2 == NTL - 1, [Vall, PT2])
            act(dsh[num, :], po[den, :], AF.Copy, qo, [dsh])
            P.op("dve", lambda: nc.vector.reciprocal(out=dsh[num, :], in_=dsh[num, :]), [dsh], [dsh])
            vtt(ob[num, h // 2, qblk], po[num, :], dsh[num, :], ALU.mult, qo + [dsh], [ob])


def emit_rg(P, nc, env, sc, l, s):
    e = _h(env)
    hT, obT, consts = e.hT, e.obT, e.consts
    mm, act, vtt, vts, vstt, vcopy, vmemset = e.mm, e.act, e.vtt, e.vts, e.vstt, e.vcopy, e.vmemset
    nextbank, cst, vec, wslot, wload, hrhs = e.nextbank, e.cst, e.vec, e.wslot, e.wload, e.hrhs
    wA, wRG, vecs, cneg, cneg2 = e.wA, e.wRG, e.vecs, e.cneg, e.cneg2
    ob = obT[1]
    xpads = [P.sbuf("xpad", [128, T + 4], F32, sc) for _ in range(2)]
    xc = P.sbuf("xc", [128, T], F32, sc)
    xcb = P.sbuf("xcb", [128, T], BF16, sc)
    R = P.sbuf("R", [128, T], F32, sc)
    I = P.sbuf("I", [128, T], F32, sc)
    Ab = P.sbuf("Ab", [128, T], F32, sc)
    HF = P.sbuf("HF", [128, T], F32, sc)
    HB = P.sbuf("HB", [128, T], F32, sc)
    gts = [P.sbuf("gt", [128, T], F32, sc) for _ in range(2)]
    wrg = P.sbuf("wrg", [128, 16, 128], BF16, sc)
    P.dma([wrg], wrg[:], [wRG], wRG.t[l], q="pool")
    for xp in xpads:
        vmemset(xp[:], 0.0, [xp])

    def rg_proj(c):
        xpad, gt = xpads[c % 2], gts[c % 2]
        sl = wslot()
        wv = sl[:, 0:2048].rearrange("p (k n) -> p k n", k=8)
        wload(sl, wv[:, :, 0:128], wA, wA.t[l, :, :, C_RGX + c * 128:C_RGX + (c + 1) * 128])
        wload(sl, wv[:, :, 128:256], wA, wA.t[l, :, :, C_RGG + c * 128:C_RGG + (c + 1) * 128])
        for tb in range(NTB):
            pt, pq = nextbank()
            for k in range(8):
                mm(pt[:, :], pq, wv[:, k, 0:128], hrhs(k, tb), k == 0, k == 7, [sl, hT])
            act(xpad[:, 1 + tb * 512:1 + (tb + 1) * 512], pt[:, :], AF.Copy, pq, [xpad])
        for tb in range(NTB):
            pt, pq = nextbank()
            for k in range(8):
                mm(pt[:, :], pq, wv[:, k, 128:256], hrhs(k, tb), k == 0, k == 7, [sl, hT])
            act(gt[:, tb * 512:(tb + 1) * 512], pt[:, :], AF.Copy, pq, [gt])

    rg_proj(0)
    for c in range(4):
        if c + 1 < 4:
            rg_proj(c + 1)
        xpad, gt = xpads[c % 2], gts[c % 2]
        vtt(HB[:], gt[:], gt[:], ALU.mult, [gt], [HB])
        vts(HB[:], HB[:], 0.044715, 1.0, ALU.mult, ALU.add, [HB], [HB])
        vtt(HB[:], HB[:], gt[:], ALU.mult, [HB, gt], [HB])
        act(HB[:], HB[:], AF.Sigmoid, [HB], [HB], scale=2.0 * math.sqrt(2.0 / math.pi))
        vtt(gt[:], gt[:], HB[:], ALU.mult, [gt, HB], [gt])
        act(xc[:], xpad[:, 0:T], AF.Identity, [xpad, vecs], [xc], bias=vec(l, "rgcb", c), scale=vec(l, "rgcw", c * 4 + 0))
        for k in range(1, 4):
            vstt(xc[:], xpad[:, k:k + T], vec(l, "rgcw", c * 4 + k), xc[:], ALU.mult, ALU.add, [xpad, xc, vecs], [xc])
        vcopy(xcb[:], xc[:], [xc], [xcb])
        if RGDBG == 1:
            vcopy(ob[:, c, :], xc[:], [xc], [ob])
            continue
        if RGDBG == 2:
            vcopy(ob[:, c, :], gt[:], [gt], [ob])
            continue
        for d in range(2):
            for (ai, dst, bname) in ((0, R, "rgba"), (1, I, "rgbi")):
                wi = (d * 2 + ai) * 4 + c
                for tb in range(NTB):
                    blk = slice(tb * 512, (tb + 1) * 512)
                    pt, pq = nextbank()
                    mm(pt[:, :], pq, wrg[:, wi, :], xcb[:, blk], True, True, [wrg, xcb])
                    act(dst[:, blk], pt[:, :], AF.Sigmoid, pq + [vecs], [dst], bias=vec(l, bname, d * 4 + c))
            if RGDBG == 3 and d == 0:
                vcopy(ob[:, c, :], R[:], [R], [ob])
            if RGDBG == 4 and d == 0:
                vcopy(ob[:, c, :], I[:], [I], [ob])
            act(Ab[:], R[:], AF.Exp, [R, cneg], [Ab], scale=cneg[:, l, d * 4 + c:d * 4 + c + 1])
            if RGDBG == 5 and d == 0:
                vcopy(ob[:, c, :], Ab[:], [Ab], [ob])
            act(R[:], R[:], AF.Exp, [R, cneg2], [R], scale=cneg2[:, l, d * 4 + c:d * 4 + c + 1])
            vts(R[:], R[:], -1.0, 1.0, ALU.mult, ALU.add, [R], [R])
            vts(R[:], R[:], 0.0, None, ALU.max, None, [R], [R])
            act(R[:], R[:], AF.Sqrt, [R], [R])
            edge = 0 if d == 0 else T - 1
            vmemset(R[:, edge:edge + 1], 1.0, [R])
            vtt(I[:], I[:], xc[:], ALU.mult, [I, xc], [I])
            vtt(I[:], I[:], R[:], ALU.mult, [I, R], [I])
            if d == 0:
                P.op("dve", lambda: nc.vector.tensor_tensor_scan(out=HF[:], data0=Ab[:], data1=I[:], initial=0.0,
                                                                 op0=ALU.mult, op1=ALU.add), [Ab, I], [HF])
            else:
                P.op("dve", lambda: nc.vector.tensor_tensor_scan(out=HB[:, ::-1], data0=Ab[:, ::-1], data1=I[:, ::-1], initial=0.0,
                                                                 op0=ALU.mult, op1=ALU.add), [Ab, I], [HB])
        if RGDBG in (3, 4, 5):
            continue
        if RGDBG == 6:
            vcopy(ob[:, c, :], HF[:], [HF], [ob])
            continue
        if RGDBG == 7:
            vcopy(ob[:, c, :], HB[:], [HB], [ob])
            continue
        vtt(HF[:], HF[:], HB[:], ALU.add, [HF, HB], [HF])
        vtt(ob[:, c, :], HF[:], gt[:], ALU.mult, [HF, gt], [ob])


def emit_gdn(P, nc, env, sc, l, s):
    e = _h(env)
    hT, obT, consts, identb = e.hT, e.obT, e.consts, e.identb
    mm, act, vtt, vts, vstt, vcopy, vmemset = e.mm, e.act, e.vtt, e.vts, e.vstt, e.vcopy, e.vmemset
    nextbank, nextq, cst, vec, wslot, wload, rsqrt_from, hrhs = e.nextbank, e.nextq, e.cst, e.vec, e.wslot, e.wload, e.rsqrt_from, e.hrhs
    nextq2 = e.nextq2
    negidentb = e.negidentb
    wA, vecs, negA, epsc = e.wA, e.vecs, e.negA, e.epsc
    ob = obT[2]
    qT = P.sbuf("gqT", [128, T], BF16, sc)
    kT = P.sbuf("gkT", [128, T], BF16, sc)
    ktok = P.sbuf("gktok", [128, NTL, 128], BF16, sc)
    vtok = P.sbuf("gvtok", [128, NTL, 128], BF16, sc)
    otok = P.sbuf("gotok", [128, NTL, 128], F32, sc)
    gtok = P.sbuf("ggtok", [128, NTL, 8], F32, sc)
    btok = P.sbuf("gbtok", [128, NTL, 8], F32, sc)
    rtmp = P.sbuf("grtmp", [128, 512], F32, sc)
    szs = [P.sbuf("gsz", [128, 128], F32, sc) for _ in range(3)]
    ons = [P.sbuf("gon", [128, 128], F32, sc) for _ in range(3)]
    ssqs = [P.sbuf("gssq", [128, 4], F32, sc) for _ in range(3)]

    sl = wslot()
    wab = sl[:, 0:128].rearrange("p (k n) -> p k n", k=8)
    wload(sl, wab, wA, wA.t[l, :, :, C_AB:C_AB + 16])
    for tl in range(NTL):
        pt, pq = nextbank()
        for k in range(8):
            mm(pt[:, 0:16], pq, hT[:, k, tl * 128:(tl + 1) * 128], wab[:, k, :], k == 0, k == 7, [hT, sl])
        vtt(gtok[:, tl, :], pt[:, 0:8], vec(l, "gdtb", 0, 8), ALU.add, pq + [vecs], [gtok])
        act(btok[:, tl, :], pt[:, 8:16], AF.Sigmoid, pq, [btok])
    gfl = gtok[:].rearrange("p a b -> p (a b)")
    e.softplus_acc(rtmp[:, 0:128], gfl, rtmp[:, 128:256], rtmp[:, 256:384], rtmp[:, 384:512], [gtok], [rtmp])
    vtt(gfl, rtmp[:, 0:128], negA[:, l, :], ALU.mult, [rtmp, negA], [gtok])

    ident, ones, negones, bd = cst("ident"), cst("ones"), cst("negones"), cst("bd16")
    offmasks = [cst("m32"), cst("m64"), cst("m128")]
    G = GDN_G
    NH = G + 1
    PL = "pool" if GDN_POOL else "dve"

    for hh in range(4):
        with contextlib.ExitStack() as sc2:
            xpads = [P.sbuf("gxpad", [128, T + 4], F32, sc2) for _ in range(2)]
            xc = P.sbuf("gxc", [128, T], F32, sc2)
            rts = [rtmp] + [P.sbuf("grt", [128, 512], F32, sc2) for _ in range(3)]
            rsq = [P.sbuf("grsq", [128, 512], BF16, sc2) for _ in range(4)]
            for xp in xpads:
                vmemset(xp[:], 0.0, [xp])

            def qkv_proj(role):
                ch = role * 4 + hh
                col0 = C_GQKV + ch * 128
                xpad = xpads[role % 2]
                sl = wslot()
                wv = sl[:, 0:1024].rearrange("p (k n) -> p k n", k=8)
                wload(sl, wv, wA, wA.t[l, :, :, col0:col0 + 128])
                for tb in range(NTB):
                    pt, pq = nextbank()
                    for k in range(8):
                        mm(pt[:, :], pq, wv[:, k, :], hrhs(k, tb), k == 0, k == 7, [sl, hT])
                    act(xpad[:, 1 + tb * 512:1 + (tb + 1) * 512], pt[:, :], AF.Copy, pq, [xpad])

            def qkv_post(role):
                ch = role * 4 + hh
                xpad = xpads[role % 2]
                act(xc[:], xpad[:, 0:T], AF.Identity, [xpad, vecs], [xc], scale=vec(l, "gcw", ch * 4 + 0))
                for k in range(1, 4):
                    vstt(xc[:], xpad[:, k:k + T], vec(l, "gcw", ch * 4 + k), xc[:], ALU.mult, ALU.add, [xpad, xc, vecs], [xc])
                act(xc[:], xc[:], AF.Silu, [xc], [xc])
                if role < 2:
                    blks = [slice(tb * 512, (tb + 1) * 512) for tb in range(NTB)]
                    for tb in range(NTB):
                        vtt(rsq[tb][:], xc[:, blks[tb]], xc[:, blks[tb]], ALU.mult, [xc], [rsq[tb]])
                    pts = []
                    for tb in range(NTB):
                        pt, pq = nextbank()
                        mm(pt[:, :], pq, e.onesb[:], rsq[tb][:], True, True, [e.onesb, rsq[tb]])
                        pts.append((pt, pq))
                    for tb in range(NTB):
                        pt, pq = pts[tb]
                        act(rts[tb][:], pt[:, :], AF.Ln, pq, [rts[tb]], bias=epsc[:, 0:1], scale=1.0)
                    for tb in range(NTB):
                        act(rts[tb][:], rts[tb][:], AF.Exp, [rts[tb]], [rts[tb]], scale=-0.5)
                    for tb in range(NTB):
                        if role == 0:
                            vstt(xc[:, blks[tb]], xc[:, blks[tb]], 128.0 ** -0.5, rts[tb][:], ALU.mult, ALU.mult, [xc, rts[tb]], [xc])
                        else:
                            vtt(xc[:, blks[tb]], xc[:, blks[tb]], rts[tb][:], ALU.mult, [xc, rts[tb]], [xc])
                    vcopy((qT if role == 0 else kT)[:], xc[:], [xc], [qT if role == 0 else kT])
                if role >= 1:
                    dst = ktok if role == 1 else vtok
                    for tl in range(NTL):
                        qa, qk = nextq()
                        mm(qa, [qk], xc[:, tl * 128:(tl + 1) * 128], ident, True, True, [xc, consts])
                        act(dst[:, tl, :], qa, AF.Copy, [qk], [dst])

            qkv_proj(0)
            for role in range(3):
                if role + 1 < 3:
                    qkv_proj(role + 1)
                qkv_post(role)
            P.barrier()
        vmemset(otok[:], 0.0, [otok])

        with contextlib.ExitStack() as sc3:
            W2 = 256
            def mk(name, n, dt=F32, w=W2):
                return [P.sbuf(name, [128, w], dt, sc3) for _ in range(n)]
            GM, decS, decI, Af = (mk(nm, G) for nm in ("GM", "decS", "decI", "Af"))
            IDT = BF16
            Ad, Aoff, Bm = (mk(nm, G, IDT) for nm in ("Ad", "Aoff", "Bm"))
            Y = [mk("Ya", G, IDT), mk("Yb", G, IDT)]
            YT = [mk("YTa", G, IDT), mk("YTb", G, IDT)]
            Pm, PTm, Zm, TTb = mk("Pm", G, IDT), mk("PTm", G, IDT), mk("Zm", G, IDT), mk("TTb", G, BF16)
            vb, kbg = mk("vb", G, BF16), mk("kbg", G, BF16)
            um, wTm, kdec, attnT = mk("um", NH), mk("wTm", NH, BF16), mk("kdec", NH, BF16), mk("attnT", NH, BF16)
            cols = mk("cols", NH, F32, 16)
            vnew = mk("vnew", NH, BF16)
            tmpo = mk("tmpo", NH)
            S2 = P.sbuf("gS2", [128, W2], F32, sc3)
            Sb2 = P.sbuf("gSb2", [128, W2], BF16, sc3)
            vmemset(S2[:], 0.0, [S2])
            vmemset(Sb2[:], 0.0, [Sb2])
            ident2, identb2 = cst("ident", 256), identb[:, 0:256]
            ones2, negones2 = cst("ones", 256), cst("negones", 256)
            mt01, negs01, negst01 = cst("mt0", 256), cst("negs0", 256), cst("negst0", 256)
            bd2 = cst("bd16", 256)
            offmasks2 = [cst("m32", 256), cst("m64", 256), cst("m128", 256)]
            hs = [slice(0, 128), slice(128, 256)]
            tn = lambda d, it: it if d == 0 else NTL - 1 - it

            def prep_gen(it, i, hi):
                ns = [tn(0, it), tn(1, it)]
                colj = [0 * 4 + hh, 1 * 4 + hh]
                tsl = [slice(n * 128, (n + 1) * 128) for n in ns]
                g_col = [gtok[:, ns[d], colj[d]:colj[d] + 1] for d in range(2)]
                b_col = [btok[:, ns[d], colj[d]:colj[d] + 1] for d in range(2)]
                cl = cols[hi]
                for d in range(2):
                    vts(GM[i][:, hs[d]], mt01[:, hs[d]], g_col[d], None, ALU.mult, None, [consts, gtok], [GM[i]])
                    vts(cl[:, d * 8 + 6:d * 8 + 8], ones[:, 0:2], g_col[d], None, ALU.mult, None, [consts, gtok], [cl])
                yield
                pD, qD = nextq2()
                for d in range(2):
                    mm(pD[:, hs[d]], [qD], GM[i][:, hs[d]], ones, d == 0, False, [GM[i], consts])
                mm(pD, [qD], negones, GM[i][:], False, False, [GM[i], consts])
                mm(pD, [qD], ident, negs01, False, True, [consts])
                act(decS[i][:], pD, AF.Exp, [qD], [decS[i]])
                yield
                pDT, qDT = nextq2()
                mm(pDT, [qDT], ones, GM[i][:], True, False, [GM[i], consts])
                for d in range(2):
                    mm(pDT[:, hs[d]], [qDT], GM[i][:, hs[d]], negones, False, False, [GM[i], consts])
                mm(pDT, [qDT], ident, negst01, False, True, [consts])
                act(decI[i][:], pDT, AF.Exp, [qDT], [decI[i]])
                yield
                pc, qc = nextq2()
                for d in range(2):
                    mm(pc[:, d * 4:d * 4 + 2], [qc], GM[i][:, hs[d]], ones[:, 0:2], True, True, [GM[i], consts])
                    mm(pc[:, d * 4 + 2:d * 4 + 4], [qc], ones, cl[:, d * 8 + 6:d * 8 + 8], True, True, [cl, consts])
                cl3 = cl[:, 0:16].rearrange("p (d c) -> p d c", d=2)
                pc3 = pc[:, 0:8].rearrange("p (d c) -> p d c", d=2)
                vcopy(cl3[:, :, 0:2], pc3[:, :, 1:3], [qc], [cl])
                act(cl3[:, :, 2:4], cl3[:, :, 0:2], AF.Exp, [cl], [cl])
                for d in range(2):
                    act(cl[:, d * 8 + 4:d * 8 + 5], cl[:, d * 8:d * 8 + 1], AF.Exp, [cl], [cl], bias=cl[:, d * 8 + 1:d * 8 + 2], scale=-1.0)
                    vtt(cl[:, d * 8 + 5:d * 8 + 6], cl[:, d * 8 + 2:d * 8 + 3], b_col[d], ALU.mult, [cl, btok], [cl])
                yield
                pK, qK = nextq2()
                for d in range(2):
                    mm(pK[:, hs[d]], [qK], kT[:, tsl[d]], kT[:, tsl[d]], True, True, [kT])
                for d in range(2):
                    vstt(Af[i][:, hs[d]], pK[:, hs[d]], b_col[d], decS[i][:, hs[d]], ALU.mult, ALU.mult, [qK, btok, decS[i]], [Af[i]])
                vtt(Ad[i][:], Af[i][:], bd2, ALU.mult, [Af[i], consts], [Ad[i]])
                yield
                pQ, qQ = nextq2()
                for d in range(2):
                    mm(pQ[:, hs[d]], [qQ], kT[:, tsl[d]], qT[:, tsl[d]], True, True, [kT, qT])
                vtt(attnT[hi][:], pQ, decI[i][:], ALU.mult, [qQ, decI[i]], [attnT[hi]])
                yield
                pB, qB = nextq2()
                for d in range(2):
                    mm(pB[:, hs[d]], [qB], Ad[i][:, hs[d]], identb2[:, 0:128], True, True, [Ad[i], identb])
                act(Bm[i][:], pB, AF.Copy, [qB], [Bm[i]])
                pP, qP = nextq2()
                mm(pP, [qP], identb2[:, 0:128], identb2, True, False, [identb])
                for d in range(2):
                    mm(pP[:, hs[d]], [qP], Ad[i][:, hs[d]], negidentb[:], False, d == 1, [Ad[i], negidentb])
                act(Pm[i][:], pP, AF.Copy, [qP], [Pm[i]])
                yield
                yprev, ytprev = Bm[i], Ad[i]
                for lev in range(3):
                    ycur, ytcur = Y[lev % 2][i], YT[lev % 2][i]
                    pyt, qyt = nextq2()
                    for d in range(2):
                        mm(pyt[:, hs[d]], [qyt], yprev[:, hs[d]], ytprev[:, hs[d]], True, True, [yprev, ytprev])
                    act(ytcur[:], pyt, AF.Copy, [qyt], [ytcur])
                    if lev < 2:
                        py, qy = nextq2()
                        for d in range(2):
                            mm(py[:, hs[d]], [qy], ytprev[:, hs[d]], yprev[:, hs[d]], True, True, [yprev, ytprev])
                        act(ycur[:], py, AF.Copy, [qy], [ycur])
                    yield
                    pp, qp = nextq2()
                    mm(pp, [qp], identb2[:, 0:128], Pm[i][:], True, False, [identb, Pm[i]])
                    for d in range(2):
                        mm(pp[:, hs[d]], [qp], ytcur[:, hs[d]], Pm[i][:, hs[d]], False, d == 1, [ytcur, Pm[i]])
                    act(Pm[i][:], pp, AF.Copy, [qp], [Pm[i]])
                    yield
                    yprev, ytprev = ycur, ytcur
                for mi, msk in enumerate(offmasks2):
                    ppt, qpt = nextq2()
                    for d in range(2):
                        mm(ppt[:, hs[d]], [qpt], Pm[i][:, hs[d]], identb2[:, 0:128], True, True, [Pm[i], identb])
                    act(PTm[i][:], ppt, AF.Copy, [qpt], [PTm[i]])
                    vtt(Aoff[i][:], Af[i][:], msk, ALU.mult, [Af[i], consts], [Aoff[i]])
                    pz, qz = nextq2()
                    for d in range(2):
                        mm(pz[:, hs[d]], [qz], Aoff[i][:, hs[d]], Pm[i][:, hs[d]], True, True, [Aoff[i], Pm[i]])
                    act(Zm[i][:], pz, AF.Copy, [qz], [Zm[i]], scale=-1.0)
                    yield
                    pt2, qt2 = nextq2()
                    mm(pt2, [qt2], identb2[:, 0:128], Pm[i][:], True, False, [identb, Pm[i]])
                    for d in range(2):
                        mm(pt2[:, hs[d]], [qt2], PTm[i][:, hs[d]], Zm[i][:, hs[d]], False, d == 1, [PTm[i], Zm[i]])
                    if mi < 2:
                        act(Pm[i][:], pt2, AF.Copy, [qt2], [Pm[i]])
                    else:
                        act(TTb[i][:], pt2, AF.Copy, [qt2], [TTb[i]])
                    yield
                for d in range(2):
                    vts(vb[i][:, hs[d]], vtok[:, ns[d], :], b_col[d], None, ALU.mult, None, [vtok, btok], [vb[i]])
                    vts(kbg[i][:, hs[d]], ktok[:, ns[d], :], cl[:, d * 8 + 5:d * 8 + 6], None, ALU.mult, None, [ktok, cl], [kbg[i]])
                    vts(kdec[hi][:, hs[d]], ktok[:, ns[d], :], cl[:, d * 8 + 4:d * 8 + 5], None, ALU.mult, None, [ktok, cl], [kdec[hi]])
                pu, qu = nextq2()
                for d in range(2):
                    mm(pu[:, hs[d]], [qu], TTb[i][:, hs[d]], vb[i][:, hs[d]], True, True, [TTb[i], vb[i]])
                act(um[hi][:], pu, AF.Copy, [qu], [um[hi]])
                yield
                pw, qw = nextq2()
                for d in range(2):
                    mm(pw[:, hs[d]], [qw], kbg[i][:, hs[d]], TTb[i][:, hs[d]], True, True, [TTb[i], kbg[i]])
                act(wTm[hi][:], pw, AF.Copy, [qw], [wTm[hi]])

            def step_gen(it, i):
                ns = [tn(0, it), tn(1, it)]
                tsl = [slice(n * 128, (n + 1) * 128) for n in ns]
                cl = cols[i]
                p1, q1 = nextq2()
                for d in range(2):
                    mm(p1[:, hs[d]], [q1], wTm[i][:, hs[d]], Sb2[:, hs[d]], True, True, [wTm[i], Sb2])
                vstt(vnew[i][:], p1, -1.0, um[i][:], ALU.mult, ALU.add, [q1, um[i]], [vnew[i]])
                p2, q2 = nextq2()
                for d in range(2):
                    mm(p2[:, hs[d]], [q2], qT[:, tsl[d]], Sb2[:, hs[d]], True, True, [qT, Sb2])
                for d in range(2):
                    vts(tmpo[i][:, hs[d]], p2[:, hs[d]], cl[:, d * 8 + 2:d * 8 + 3], None, ALU.mult, None, [q2, cl], [tmpo[i]])
                yield
                p4, q4 = nextq2()
                for d in range(2):
                    mm(p4[:, hs[d]], [q4], kdec[i][:, hs[d]], vnew[i][:, hs[d]], True, True, [kdec[i], vnew[i]])
                for d in range(2):
                    vstt(S2[:, hs[d]], S2[:, hs[d]], cl[:, d * 8 + 3:d * 8 + 4], p4[:, hs[d]], ALU.mult, ALU.add, [S2, cl, q4], [S2])
                act(Sb2[:], S2[:], AF.Copy, [S2], [Sb2])
                yield
                p3, q3 = nextq2()
                for d in range(2):
                    mm(p3[:, hs[d]], [q3], attnT[i][:, hs[d]], vnew[i][:, hs[d]], True, True, [attnT[i], vnew[i]])
                vtt(tmpo[i][:], tmpo[i][:], p3, ALU.add, [q3, tmpo[i]], [tmpo[i]])
                for d in range(2):
                    vtt(otok[:, ns[d], :], otok[:, ns[d], :], tmpo[i][:, hs[d]], ALU.add, [otok, tmpo[i]], [otok])

            nxt = 0
            free_s = list(range(G))
            free_h = list(range(NH))
            hmap = {}
            prep_done = set()
            step_done = -1
            step_started = -1
            active = []
            while step_done < NTL - 1:
                while nxt < NTL and free_s and free_h:
                    it = nxt
                    nxt += 1
                    si = free_s.pop(0)
                    hi = free_h.pop(0)
                    hmap[it] = hi
                    active.append(("P", it, prep_gen(it, si, hi), si))
                it = step_started + 1
                if it < NTL and it in prep_done and step_done == it - 1:
                    active.append(("S", it, step_gen(it, hmap[it]), None))
                    step_started = it
                for task in list(active):
                    kind, it, gen, si = task
                    try:
                        next(gen)
                    except StopIteration:
                        active.remove(task)
                        if kind == "P":
                            prep_done.add(it)
                            free_s.append(si)
                        else:
                            step_done = it
                            free_h.append(hmap[it])
            P.barrier()
        sl = wslot()
        wz = sl[:, 0:1024].rearrange("p (k n) -> p k n", k=8)
        wload(sl, wz, wA, wA.t[l, :, :, C_GZ + hh * 128:C_GZ + (hh + 1) * 128])
        def outA(tl):
            sz, on, ssq = szs[tl % 3], ons[tl % 3], ssqs[tl % 3]
            pt, pq = nextbank()
            for k in range(8):
                mm(pt[:, 0:128], pq, hT[:, k, tl * 128:(tl + 1) * 128], wz[:, k, :], k == 0, k == 7, [hT, sl])
            act(sz[:], pt[:, 0:128], AF.Silu, pq, [sz])
            P.op("dve", lambda: nc.vector.tensor_tensor(out=on[:], in0=otok[:, tl, :], in1=otok[:, tl, :], op=ALU.mult), [otok], [on])
            P.op("dve", lambda: nc.vector.tensor_reduce(out=ssq[:, 0:1], in_=on[:], axis=AX.X, op=ALU.add), [on], [ssq])
            act(ssq[:, 1:2], ssq[:, 0:1], AF.Ln, [ssq], [ssq], bias=epsc[:, 0:1], scale=1.0 / 128)
            act(ssq[:, 1:2], ssq[:, 1:2], AF.Exp, [ssq], [ssq], scale=-0.5)
            vstt(on[:], otok[:, tl, :], ssq[:, 1:2], vec(l, "gng", 0, 128), ALU.mult, ALU.mult, [otok, ssq, vecs], [on])
            vtt(on[:], on[:], sz[:], ALU.mult, [on, sz], [on])

        def outB(tl):
            on = ons[tl % 3]
            qa, qk = nextq()
            mm(qa, [qk], on[:], ident, True, True, [on, consts])
            act(ob[:, hh, tl * 128:(tl + 1) * 128], qa, AF.Copy, [qk], [ob])

        for tl in range(NTL + 2):
            if tl < NTL:
                outA(tl)
            if tl >= 2:
                outB(tl - 2)


def emit_merge(P, nc, env, sc, l, s):
    e = _h(env)
    hT, obT, consts = e.hT, e.obT, e.consts
    mm, act, vtt, vts, vstt, vcopy = e.mm, e.act, e.vtt, e.vts, e.vstt, e.vcopy
    nextbank, cst, vec, wslot, wload, hrhs = e.nextbank, e.cst, e.vec, e.wslot, e.wload, e.hrhs
    wA, wBR, wO, mod = e.wA, e.wBR, e.wO, e.mod
    mT = P.sbuf("mT", [128, 8, T], BF16, sc)
    Gs = [P.sbuf("Gs", [128, 512], F32, sc) for _ in range(2)]
    macc = P.sbuf("macc", [128, 512], F32, sc)
    mtmp = P.sbuf("mtmp", [128, 512], F32, sc)
    xb = P.sbuf("xb", [128, 8, 512], F32, sc)
    sq = P.sbuf("sq", [128, 8, 512], BF16, sc)
    rstd = P.sbuf("rstd", [128, 512], F32, sc)
    tmp = P.sbuf("tmp", [128, 512], F32, sc)
    gi = 0
    for j in range(8):
        sl = wslot()
        wg = sl[:, 0:3072].rearrange("p (b k n) -> p b k n", b=3, k=8)
        for b in range(3):
            wload(sl, wg[:, b], wA, wA.t[l, :, :, C_GATE + b * 1024 + j * 128:C_GATE + b * 1024 + (j + 1) * 128])
        sl2 = wslot()
        wb = sl2[:, 0:1536].rearrange("p (k n) -> p k n", k=12)
        wload(sl2, wb, wBR, wBR.t[l, :, :, j * 128:(j + 1) * 128])
        for tb in range(NTB):
            blk = slice(tb * 512, (tb + 1) * 512)
            for b in range(3):
                pg, qg = nextbank()
                for k in range(8):
                    mm(pg[:, :], qg, wg[:, b, k, :], hrhs(k, tb), k == 0, k == 7, [sl, hT])
                Gt = Gs[gi % 2]
                gi += 1
                act(Gt[:], pg[:, :], AF.Sigmoid, qg, [Gt])
                py, qy = nextbank()
                for k in range(4):
                    mm(py[:, :], qy, wb[:, b * 4 + k, :], obT[b][:, k, blk], k == 0, k == 3, [sl2, obT[b]])
                if b == 0:
                    vtt(macc[:], py[:, :], Gt[:], ALU.mult, qy + [Gt], [macc])
                elif b == 1:
                    vtt(mtmp[:], py[:, :], Gt[:], ALU.mult, qy + [Gt], [mtmp])
                    vtt(macc[:], macc[:], mtmp[:], ALU.add, [macc, mtmp], [macc])
                else:
                    vtt(mtmp[:], py[:, :], Gt[:], ALU.mult, qy + [Gt], [mtmp])
                    vtt(mT[:, j, blk], macc[:], mtmp[:], ALU.add, [macc, mtmp], [mT])
    for tb in range(NTB):
        blk = slice(tb * 512, (tb + 1) * 512)
        e.load_xblock(xb, tb)
        for half in range(2):
            sl = wslot()
            wv = sl[:, 0:4096].rearrange("p (k n) -> p k n", k=8)
            wload(sl, wv, wO, wO.t[l, :, :, half * 512:(half + 1) * 512])
            for ii in range(4):
                i = half * 4 + ii
                pt, pq = nextbank()
                for k in range(8):
                    mm(pt[:, :], pq, wv[:, k, ii * 128:(ii + 1) * 128], mT[:, k, blk], k == 0, k == 7, [sl, mT])
                vstt(xb[:, i, :], pt[:, :], mod[:, l, 16 + i, s:s + 1], xb[:, i, :], ALU.mult, ALU.add, pq + [mod, xb], [xb])
        e.store_xblock(xb, tb)
        e.norm_block(xb, tb, l, s, 2, sq, rstd, tmp)


def emit_ffn(P, nc, env, sc, l, s):
    e = _h(env)
    hT, consts = e.hT, e.consts
    mm, act, vtt, vts, vstt, vcopy, vmemset = e.mm, e.act, e.vtt, e.vts, e.vstt, e.vcopy, e.vmemset
    nextbank, cst, vec, wslot, wload, hrhs = e.nextbank, e.cst, e.vec, e.wslot, e.wload, e.hrhs
    wUP, wDN, mod, vecs, xT_d, xT_tk = e.wUP, e.wDN, e.mod, e.vecs, e.xT_d, e.xT_tk
    actT = P.sbuf("actT", [128, NJ, T], BF16, sc)
    upad = [P.sbuf("upad", [128, T + 2], F32, sc) for _ in range(2)]
    cv = [P.sbuf("cv", [128, T], F32, sc) for _ in range(2)]
    xs = [P.sbuf("xs", [128, 512], F32, sc) for _ in range(2)]
    for u in upad:
        vmemset(u[:], 0.0, [u])
    for jp in range(NJ // 2):
        sl = wslot()
        wv = sl[:, 0:4096].rearrange("p (k n) -> p k n", k=8)
        for jj in range(2):
            j = jp * 2 + jj
            wload(sl, wv[:, :, jj * 128:(jj + 1) * 128], wUP, wUP.t[l, :, :, j * 128:(j + 1) * 128])
            wload(sl, wv[:, :, 256 + jj * 128:256 + (jj + 1) * 128], wUP, wUP.t[l, :, :, DFF + j * 128:DFF + (j + 1) * 128])
        for jj in range(2):
            j = jp * 2 + jj
            for lg in range(2):
                ch = lg * NJ + j
                wc = lg * 256 + jj * 128
                for tb in range(NTB):
                    pt, pq = nextbank()
                    for k in range(8):
                        mm(pt[:, :], pq, wv[:, k, wc:wc + 128], hrhs(k, tb), k == 0, k == 7, [sl, hT])
                    act(upad[lg][:, 1 + tb * 512:1 + (tb + 1) * 512], pt[:, :], AF.Copy, pq, [upad[lg]])
                act(cv[lg][:], upad[lg][:, 0:T], AF.Identity, [upad[lg], vecs], [cv[lg]],
                    bias=vec(l, "fcb", ch), scale=vec(l, "fcw", ch * 3 + 0))
                for k in range(1, 3):
                    vstt(cv[lg][:], upad[lg][:, k:k + T], vec(l, "fcw", ch * 3 + k), cv[lg][:], ALU.mult, ALU.add,
                         [upad[lg], cv[lg], vecs], [cv[lg]])
            act(cv[1][:], cv[1][:], AF.Silu, [cv[1]], [cv[1]])
            vtt(actT[:, j, :], cv[1][:], cv[0][:], ALU.mult, [cv[0], cv[1]], [actT])
    xi = 0
    for i in range(8):
        sl = wslot()
        wv = sl[:, 0:NJ * 128].rearrange("p (k n) -> p k n", k=NJ)
        wload(sl, wv, wDN, wDN.t[l, :, :, i * 128:(i + 1) * 128])
        for tb in range(NTB):
            blk = slice(tb * 512, (tb + 1) * 512)
            pt, pq = nextbank()
            for k in range(NJ):
                mm(pt[:, :], pq, wv[:, k, :], actT[:, k, blk], k == 0, k == NJ - 1, [sl, actT])
            x1 = xs[xi % 2]
            xi += 1
            P.dma([x1], x1[:], [xT_tk[i][tb]], xT_d[:, i, blk])
            vstt(x1[:], pt[:, :], mod[:, l, 40 + i, s:s + 1], x1[:], ALU.mult, ALU.add, pq + [mod, x1], [x1])
            P.dma([xT_tk[i][tb]], xT_d[:, i, blk], [x1], x1[:], sem_t=x1)


def _fm(w, kc):
    K, N = w.shape
    return np.ascontiguousarray(w.reshape(kc, 128, N).transpose(1, 0, 2))


def _colv(v, nch):
    return np.ascontiguousarray(np.asarray(v).reshape(nch, 128).T)


def _consts():
    c = np.zeros((128, NC), np.float32)
    i = np.arange(128)
    ident = np.eye(128, dtype=np.float32)
    c[:, COFF["ident"]:COFF["ident"] + 128] = ident
    c[:, COFF["ident_b"]:COFF["ident_b"] + 128] = ident
    c[:, COFF["ones"]:COFF["ones"] + 256] = 1.0
    c[:, COFF["negones"]:COFF["negones"] + 256] = -1.0
    m, ii = np.meshgrid(i, i, indexing="ij")
    c[:, COFF["mt0"]:COFF["mt0"] + 128] = (m <= ii)
    c[:, COFF["mt1"]:COFF["mt1"] + 128] = (m >= ii)
    a, b = np.meshgrid(i, i, indexing="ij")
    v0 = (b < a)
    v1 = (b > a)
    c[:, COFF["negs0"]:COFF["negs0"] + 128] = np.where(v0, 0.0, -BIG)
    c[:, COFF["negs1"]:COFF["negs1"] + 128] = np.where(v1, 0.0, -BIG)
    eye = np.eye(128, dtype=bool)
    c[:, COFF["negst0"]:COFF["negst0"] + 128] = np.where(v0.T | eye, 0.0, -BIG)
    c[:, COFF["negst1"]:COFF["negst1"] + 128] = np.where(v1.T | eye, 0.0, -BIG)
    bdm = lambda n: ((a // n) == (b // n)).astype(np.float32)
    for sfx in ("", "_b"):
        c[:, COFF["bd16" + sfx]:COFF["bd16" + sfx] + 128] = bdm(16)
        c[:, COFF["m32" + sfx]:COFF["m32" + sfx] + 128] = bdm(32) - bdm(16)
        c[:, COFF["m64" + sfx]:COFF["m64" + sfx] + 128] = bdm(64) - bdm(32)
        c[:, COFF["m128" + sfx]:COFF["m128" + sfx] + 128] = 1.0 - bdm(64)
    return c


def _rope_tables():
    inv = 1.0 / (10000.0 ** (np.arange(0, 32, 2, dtype=np.float32) / 32.0))
    ang = np.arange(T, dtype=np.float32)[None, :] * inv[:, None].astype(np.float32)
    cos, sin = np.cos(ang).astype(np.float32), np.sin(ang).astype(np.float32)
    r = np.zeros((2, 96, T), np.float32)
    r[0, 0:64] = 1.0
    r[0, 64:80] = cos
    r[0, 80:96] = cos
    r[1, 64:80] = -sin
    r[1, 80:96] = sin
    return r


def prep_weights(inp):
    L = L_ALL
    f = lambda k: np.asarray(inp[k], np.float32)
    w_in = f("w_in")
    wA = np.zeros((L, 128, 8, NA), np.float32)
    sw = np.concatenate([np.arange(16, 32), np.arange(0, 16)])
    wUQ = np.zeros((L, 128, 3, 1536), np.float32)
    wKN = np.zeros((L, 128, 2, 768), np.float32)
    wV = np.zeros((L, 128, 2, 512), np.float32)
    wRG = np.zeros((L, 128, 16, 128), np.float32)
    vecs = np.zeros((128, L, NV), np.float32)
    for l in range(L):
        ext = np.zeros((D, 192), np.float32)
        kr = w_in[l][:, 640:672]
        ext[:, 64:96] = kr
        ext[:, 96 + 64:96 + 96] = kr[:, sw]
        wA[l] = _fm(np.concatenate([w_in[l], ext], axis=1), 8)
        uq = f("mla_w_uq")[l]
        uqs = uq.reshape(384, 8, 96).copy()
        uqs[:, :, 64:96] = uqs[:, :, 64:96][:, :, sw]
        wUQ[l] = _fm(np.concatenate([uq, uqs.reshape(384, 768)], axis=1), 3)
        ukv = f("mla_w_ukv")[l].reshape(256, 8, 128)
        kn = np.zeros((256, 8, 96), np.float32)
        kn[:, :, 0:64] = ukv[:, :, 0:64]
        wKN[l] = _fm(kn.reshape(256, 768), 2)
        wV[l] = _fm(np.ascontiguousarray(ukv[:, :, 64:128]).reshape(256, 512), 2)
        for d in range(2):
            for ai, nm in enumerate(("rg_w_a", "rg_w_i")):
                w = f(nm)[l, d]
                for c in range(4):
                    m = np.zeros((128, 128), np.float32)
                    m[0:64, 0:64] = w[2 * c]
                    m[64:128, 64:128] = w[2 * c + 1]
                    wRG[l, :, (d * 2 + ai) * 4 + c, :] = m

        def put(name, arr):
            o, w = VOFF[name]
            vecs[:, l, o:o + w] = arr
        put("ln1", _colv(f("ln1_g")[l], 8))
        put("ln2", _colv(f("ln2_g")[l], 8))
        put("bmod", _colv(f("b_mod")[l], 48))
        put("qg", _colv(f("mla_q_norm_g")[l], 3))
        put("kvg", _colv(f("mla_kv_norm_g")[l], 2))
        put("rgcw", np.ascontiguousarray(f("rg_conv_w")[l].reshape(4, 4, 128).transpose(2, 1, 0)).reshape(128, 16))
        put("rgcb", _colv(f("rg_conv_b")[l], 4))
        put("rgba", np.ascontiguousarray(f("rg_b_a")[l].reshape(2, 4, 128).transpose(2, 0, 1)).reshape(128, 8))
        put("rgbi", np.ascontiguousarray(f("rg_b_i")[l].reshape(2, 4, 128).transpose(2, 0, 1)).reshape(128, 8))
        put("rglam", np.ascontiguousarray(f("rg_lam")[l].reshape(2, 4, 128).transpose(2, 0, 1)).reshape(128, 8))
        put("gcw", np.ascontiguousarray(f("gdn_conv_w")[l].reshape(4, 12, 128).transpose(2, 1, 0)).reshape(128, 48))
        put("gng", np.broadcast_to(f("gdn_norm_g")[l][None, :], (128, 128)))
        put("galog", np.broadcast_to(np.tile(f("gdn_a_log")[l].reshape(8), 16)[None, :], (128, 128)))
        put("gdtb", np.broadcast_to(np.tile(f("gdn_dt_bias")[l].reshape(8), 16)[None, :], (128, 128)))
        put("fcw", np.ascontiguousarray(f("ffn_conv_w")[l].reshape(3, 44, 128).transpose(2, 1, 0)).reshape(128, 132))
        put("fcb", _colv(f("ffn_conv_b")[l], 44))
        put("fng", _colv(f("final_norm_g"), 8))
    wd = {
        "wA": wA, "wUQ": wUQ, "wKN": wKN, "wV": wV, "wRG": wRG,
        "wBR": np.stack([_fm(f("w_branch")[l].reshape(1536, 1024), 12) for l in range(L)]),
        "wO": np.stack([_fm(f("w_out")[l], 8) for l in range(L)]),
        "wUP": np.stack([_fm(f("ffn_w_up")[l], 8) for l in range(L)]),
        "wDN": np.stack([_fm(f("ffn_w_down")[l], NJ) for l in range(L)]),
        "wMOD": np.stack([_fm(f("w_mod")[l], 8) for l in range(L)]),
        "vecs": vecs, "consts": _consts(), "ropet": _rope_tables(),
    }
    return wd


def core_inputs(wd, xs, cs):
    m = dict(wd)
    m["xin"] = np.ascontiguousarray(xs, dtype=np.float32)
    nseq = xs.shape[0]
    m["cT"] = np.ascontiguousarray(np.asarray(cs, np.float32).reshape(nseq, 8, 128).transpose(2, 1, 0))
    return m


_CACHE = {}


def kernel(**inputs):
    wd = prep_weights(inputs)
    xp = np.asarray(inputs["x_prompt"], np.float32)
    xsm = np.asarray(inputs["x_sample"], np.float32)
    cp = np.asarray(inputs["c_prompt"], np.float32)
    csm = np.asarray(inputs["c_sample"], np.float32)
    xall = np.concatenate([xp, xsm], axis=0)
    call = np.concatenate([cp, csm], axis=0)
    nseq = xall.shape[0] // NCORES
    if "nc" not in _CACHE:
        _CACHE["nc"] = build(nseq)[0]
    nc = _CACHE["nc"]
    in_maps = [core_inputs(wd, xall[i * nseq:(i + 1) * nseq], call[i * nseq:(i + 1) * nseq]) for i in range(NCORES)]
    res = run_bass_kernel_spmd(nc, in_maps, core_ids=list(range(NCORES)))
    y = np.concatenate([np.asarray(r["yout"], np.float32) for r in res.results], axis=0)
    nb = xp.shape[0]
    return (np.ascontiguousarray(y[:nb]), np.ascontiguousarray(y[nb:]))
```
